# Optimizing a Trainium2 kernel written in Bass

```python
import math
import jax, jax.numpy as jnp
from jax import lax
import numpy as np

D_MODEL = 1024
BATCH = 4
SEQ = 4096
DEPTH = 4
DEC_BATCH = 128
DEC_SEQ = 8
PAST_LEN = 2048
PAGE_SIZE = 128

N_MIXERS = 4
CONV_WIDTH = 31
D_CONV = D_MODEL
DA_HEADS = 8
DA_HEAD_DIM = D_MODEL // (2 * DA_HEADS)
Q_BLOCK = 128
POOL_WINDOWS = (2, 4, 8, 16)
POOL_GROUPS = len(POOL_WINDOWS)
POOL_GROUP_DIM = D_MODEL // POOL_GROUPS
POOL_BUF = max(POOL_WINDOWS) - 1
SG_CHUNK = 128
SG_DIM = 2 * D_MODEL
SG_GROUPS = 4
SG_GROUP_DIM = SG_DIM // SG_GROUPS
D_FF = 2816
FFN_CONV_WIDTH = 3
NORM_EPS = 1e-6

kernel_name = 'hybrid_conv_diffattn_pool_sgmlp_decode_step'


def _rmsnorm(x, g):
    xf = x.astype(jnp.float32)
    y = xf * lax.rsqrt(jnp.mean(xf * xf, axis=-1, keepdims=True) + NORM_EPS)
    return (y * g.astype(jnp.float32)).astype(x.dtype)


def _layernorm(x, g, b):
    xf = x.astype(jnp.float32)
    mu = jnp.mean(xf, axis=-1, keepdims=True)
    xc = xf - mu
    y = xc * lax.rsqrt(jnp.mean(xc * xc, axis=-1, keepdims=True) + NORM_EPS)
    return (y * g.astype(jnp.float32) + b.astype(jnp.float32)).astype(x.dtype)


def _causal_dwconv(xp, w, b):
    c = xp.shape[-1]
    y = lax.conv_general_dilated(xp, w[:, None, :].astype(xp.dtype), window_strides=(1,), padding='VALID',
                                 dimension_numbers=('NWC', 'WIO', 'NWC'), feature_group_count=c)
    return y + b


def _conformer_conv(h, buf, w_in, b_in, w_dw, b_dw, ln_g, ln_b, w_out, b_out):
    a, gate = jnp.split(h @ w_in + b_in, 2, axis=-1)
    g = a * jax.nn.sigmoid(gate)
    gp = jnp.concatenate([buf, g], axis=1)
    c = _causal_dwconv(gp, w_dw, b_dw)
    c = jax.nn.silu(_layernorm(c, ln_g, ln_b))
    return c @ w_out + b_out, gp[:, -(CONV_WIDTH - 1):]


def _diff_lambda(lq1, lk1, lq2, lk2, layer_idx):
    lam_init = 0.8 - 0.6 * math.exp(-0.3 * layer_idx)
    f = jnp.float32
    lam = (jnp.exp(jnp.sum(lq1.astype(f) * lk1.astype(f))) - jnp.exp(jnp.sum(lq2.astype(f) * lk2.astype(f)))
           + lam_init)
    return lam, lam_init


def _qkv(h, w_qkv):
    b, t, _ = h.shape
    q, k, v = jnp.split(h @ w_qkv, 3, axis=-1)
    shp = (b, t, DA_HEADS, 2 * DA_HEAD_DIM)
    return q.reshape(shp), k.reshape(shp), v.reshape(shp)


def _two_scores(q, k):
    d = DA_HEAD_DIM
    scale = d ** -0.5
    s1 = jnp.einsum('bqhd,bkhd->bhqk', q[..., :d], k[..., :d], preferred_element_type=jnp.float32) * scale
    s2 = jnp.einsum('bqhd,bkhd->bhqk', q[..., d:], k[..., d:], preferred_element_type=jnp.float32) * scale
    return s1, s2


def _diff_weights(s1, s2, lam):
    return jax.nn.softmax(s1, axis=-1) - lam * jax.nn.softmax(s2, axis=-1)


def _diff_attn_prompt(q, k, v, lam):
    b, s, h, e = q.shape
    nb = s // Q_BLOCK
    qb = q.reshape(b, nb, Q_BLOCK, h, e).swapaxes(0, 1)
    starts = jnp.arange(nb, dtype=jnp.int32) * Q_BLOCK
    k_pos = jnp.arange(s, dtype=jnp.int32)

    def block(args):
        qi, start = args
        s1, s2 = _two_scores(qi, k)
        q_pos = start + jnp.arange(Q_BLOCK, dtype=jnp.int32)
        mask = k_pos[None, :] <= q_pos[:, None]
        s1 = jnp.where(mask, s1, -jnp.inf)
        s2 = jnp.where(mask, s2, -jnp.inf)
        w = _diff_weights(s1, s2, lam).astype(v.dtype)
        return jnp.einsum('bhqk,bkhe->bqhe', w, v)

    o = lax.map(block, (qb, starts))
    return o.swapaxes(0, 1).reshape(b, s, h, e)


def _diff_attn_sample(q, k_new, v_new, k_past, v_past, lam):
    t = q.shape[1]
    p = k_past.shape[1]
    s1p, s2p = _two_scores(q, k_past)
    s1n, s2n = _two_scores(q, k_new)
    causal = jnp.tril(jnp.ones((t, t), dtype=bool))
    s1 = jnp.concatenate([s1p, jnp.where(causal, s1n, -jnp.inf)], axis=-1)
    s2 = jnp.concatenate([s2p, jnp.where(causal, s2n, -jnp.inf)], axis=-1)
    w = _diff_weights(s1, s2, lam).astype(v_new.dtype)
    return (jnp.einsum('bhqk,bkhe->bqhe', w[..., :p], v_past)
            + jnp.einsum('bhqk,bkhe->bqhe', w[..., p:], v_new))


def _diff_out(o, lam_init, norm_g, w_o):
    b, t, h, e = o.shape
    of = o.astype(jnp.float32)
    of = of * lax.rsqrt(jnp.mean(of * of, axis=-1, keepdims=True) + NORM_EPS)
    of = of * norm_g.astype(jnp.float32).reshape(h, e) * (1.0 - lam_init)
    return of.astype(o.dtype).reshape(b, t, h * e) @ w_o


def _pool_mixer(h, buf, pos0, w_grp, scale):
    b, t, d = h.shape
    hcat = jnp.concatenate([buf, h], axis=1)
    hf = hcat.astype(jnp.float32)
    cs = jnp.concatenate([jnp.zeros((b, 1, d), jnp.float32), jnp.cumsum(hf, axis=1)], axis=1)
    end = cs[:, POOL_BUF + 1:]
    xs = hf[:, POOL_BUF:]
    pos = pos0 + jnp.arange(t, dtype=jnp.int32)
    outs = []
    for gi, win in enumerate(POOL_WINDOWS):
        sl = slice(gi * POOL_GROUP_DIM, (gi + 1) * POOL_GROUP_DIM)
        start = cs[:, POOL_BUF + 1 - win: POOL_BUF + 1 - win + t, sl]
        cnt = jnp.minimum(win, pos + 1).astype(jnp.float32)[None, :, None]
        outs.append((end[..., sl] - start) / cnt - xs[..., sl])
    pooled = jnp.stack(outs, axis=2).astype(h.dtype)
    y = jnp.einsum('btgc,gce->btge', pooled, w_grp).reshape(b, t, d) * scale
    return y, hcat[:, -POOL_BUF:]


def _sg_mixer(h, w_in, b_in, ln_g, ln_b, w_s, b_s, w_out):
    b, t, _ = h.shape
    z = jax.nn.gelu(h @ w_in + b_in)
    u, v = jnp.split(z, 2, axis=-1)
    v = _layernorm(v, ln_g, ln_b)
    L = SG_CHUNK if t >= SG_CHUNK else t
    ws = w_s[:, :L, :L] * jnp.tril(jnp.ones((L, L), w_s.dtype))
    vc = v.reshape(b, t // L, L, SG_GROUPS, SG_GROUP_DIM)
    s = jnp.einsum('gts,bnsgc->bntgc', ws, vc) + b_s[:, :L].T[:, :, None]
    y = (u * s.reshape(b, t, SG_DIM)) @ w_out
    return y, v


def _conv_ffn(h, buf, w_gate, w_up, w_dw, b_dw, w_down):
    g = h @ w_gate
    u = h @ w_up
    gp = jnp.concatenate([buf, g], axis=1)
    gc = _causal_dwconv(gp, w_dw, b_dw)
    return (jax.nn.silu(gc) * u) @ w_down, gp[:, -(FFN_CONV_WIDTH - 1):]


def setup_inputs(seed: int = 0) -> dict:
    key = jax.random.key(seed)
    keys = jax.random.split(key, 48)
    ctr = [0]

    def nk():
        ctr[0] += 1
        return keys[ctr[0] - 1]

    def nrm(shape, scale):
        return jax.random.normal(nk(), shape, jnp.float32) * scale

    def gain(shape):
        return 1.0 + nrm(shape, 0.05)

    n_pages = PAST_LEN // PAGE_SIZE
    n_used = DEC_BATCH * n_pages
    n_phys = n_used + n_used // 4
    kv_shape = (n_phys, PAGE_SIZE, DA_HEADS, 2 * DA_HEAD_DIM)
    d = D_MODEL
    inp = {}
    inp['x_prompt'] = nrm((BATCH, SEQ, d), 1.0)
    inp['x_sample'] = nrm((DEC_BATCH, DEC_SEQ, d), 1.0)
    inp['state_conv'] = nrm((DEC_BATCH, CONV_WIDTH - 1, D_CONV), 0.5)
    inp['cache_k'] = nrm(kv_shape, 1.0)
    inp['cache_v'] = nrm(kv_shape, 1.0)
    inp['page_table'] = jax.random.permutation(nk(), n_phys)[:n_used].reshape(DEC_BATCH, n_pages).astype(jnp.int32)
    inp['state_pool'] = nrm((DEC_BATCH, POOL_BUF, d), 1.0)
    inp['state_ffn'] = nrm((DEPTH, DEC_BATCH, FFN_CONV_WIDTH - 1, D_FF), 1.0)
    inp['norm_mix'] = gain((DEPTH, d))
    inp['norm_ffn'] = gain((DEPTH, d))
    inp['norm_final'] = gain((d,))
    inp['cv_w_in'] = nrm((d, 2 * D_CONV), d ** -0.5)
    inp['cv_b_in'] = nrm((2 * D_CONV,), 0.02)
    inp['cv_w_dw'] = nrm((CONV_WIDTH, D_CONV), CONV_WIDTH ** -0.5)
    inp['cv_b_dw'] = nrm((D_CONV,), 0.02)
    inp['cv_ln_g'] = gain((D_CONV,))
    inp['cv_ln_b'] = nrm((D_CONV,), 0.02)
    inp['cv_w_out'] = nrm((D_CONV, d), D_CONV ** -0.5)
    inp['cv_b_out'] = nrm((d,), 0.02)
    inp['da_w_qkv'] = nrm((d, 3 * DA_HEADS * 2 * DA_HEAD_DIM), d ** -0.5)
    inp['da_lq1'] = nrm((DA_HEAD_DIM,), 0.1)
    inp['da_lk1'] = nrm((DA_HEAD_DIM,), 0.1)
    inp['da_lq2'] = nrm((DA_HEAD_DIM,), 0.1)
    inp['da_lk2'] = nrm((DA_HEAD_DIM,), 0.1)
    inp['da_norm_g'] = gain((DA_HEADS * 2 * DA_HEAD_DIM,))
    inp['da_w_o'] = nrm((DA_HEADS * 2 * DA_HEAD_DIM, d), d ** -0.5)
    inp['pl_w'] = nrm((POOL_GROUPS, POOL_GROUP_DIM, POOL_GROUP_DIM), POOL_GROUP_DIM ** -0.5)
    inp['pl_scale'] = 1.0 + nrm((d,), 0.1)
    inp['sg_w_in'] = nrm((d, 2 * SG_DIM), d ** -0.5)
    inp['sg_b_in'] = nrm((2 * SG_DIM,), 0.02)
    inp['sg_ln_g'] = gain((SG_DIM,))
    inp['sg_ln_b'] = nrm((SG_DIM,), 0.02)
    inp['sg_w_s'] = nrm((SG_GROUPS, SG_CHUNK, SG_CHUNK), SG_CHUNK ** -0.5)
    inp['sg_b_s'] = 1.0 + nrm((SG_GROUPS, SG_CHUNK), 0.1)
    inp['sg_w_out'] = nrm((SG_DIM, d), SG_DIM ** -0.5)
    inp['ff_w_gate'] = nrm((DEPTH, d, D_FF), d ** -0.5)
    inp['ff_w_up'] = nrm((DEPTH, d, D_FF), d ** -0.5)
    inp['ff_w_dw'] = nrm((DEPTH, FFN_CONV_WIDTH, D_FF), FFN_CONV_WIDTH ** -0.5)
    inp['ff_b_dw'] = nrm((DEPTH, D_FF), 0.02)
    inp['ff_w_down'] = nrm((DEPTH, D_FF, d), D_FF ** -0.5)
    return inp


def reference(x_prompt, x_sample, state_conv, cache_k, cache_v, page_table, state_pool, state_ffn,
              norm_mix, norm_ffn, norm_final,
              cv_w_in, cv_b_in, cv_w_dw, cv_b_dw, cv_ln_g, cv_ln_b, cv_w_out, cv_b_out,
              da_w_qkv, da_lq1, da_lk1, da_lq2, da_lk2, da_norm_g, da_w_o,
              pl_w, pl_scale,
              sg_w_in, sg_b_in, sg_ln_g, sg_ln_b, sg_w_s, sg_b_s, sg_w_out,
              ff_w_gate, ff_w_up, ff_w_dw, ff_b_dw, ff_w_down):
    xp, xs = x_prompt, x_sample
    bp, bs = xp.shape[0], xs.shape[0]
    kv_row = (bs, -1, DA_HEADS, 2 * DA_HEAD_DIM)
    ffn_p_list, ffn_s_list = [], []
    for i in range(DEPTH):
        kind = i % N_MIXERS
        hp = _rmsnorm(xp, norm_mix[i])
        hs = _rmsnorm(xs, norm_mix[i])
        if kind == 0:
            cw = (cv_w_in, cv_b_in, cv_w_dw, cv_b_dw, cv_ln_g, cv_ln_b, cv_w_out, cv_b_out)
            mp, conv_p = _conformer_conv(hp, jnp.zeros((bp, CONV_WIDTH - 1, D_CONV), hp.dtype), *cw)
            ms, conv_s = _conformer_conv(hs, state_conv, *cw)
        elif kind == 1:
            lam, lam_init = _diff_lambda(da_lq1, da_lk1, da_lq2, da_lk2, i)
            q_p, k_rows_p, v_rows_p = _qkv(hp, da_w_qkv)
            mp = _diff_out(_diff_attn_prompt(q_p, k_rows_p, v_rows_p, lam), lam_init, da_norm_g, da_w_o)
            q_s, k_rows_s, v_rows_s = _qkv(hs, da_w_qkv)
            k_past = cache_k[page_table].reshape(kv_row)
            v_past = cache_v[page_table].reshape(kv_row)
            ms = _diff_out(_diff_attn_sample(q_s, k_rows_s, v_rows_s, k_past, v_past, lam), lam_init,
                           da_norm_g, da_w_o)
        elif kind == 2:
            mp, pool_p = _pool_mixer(hp, jnp.zeros((bp, POOL_BUF, D_MODEL), hp.dtype), 0, pl_w, pl_scale)
            ms, pool_s = _pool_mixer(hs, state_pool, PAST_LEN, pl_w, pl_scale)
        else:
            sw = (sg_w_in, sg_b_in, sg_ln_g, sg_ln_b, sg_w_s, sg_b_s, sg_w_out)
            mp, _ = _sg_mixer(hp, *sw)
            ms, sg_v_s = _sg_mixer(hs, *sw)
        xp = xp + mp
        xs = xs + ms
        fw = (ff_w_gate[i], ff_w_up[i], ff_w_dw[i], ff_b_dw[i], ff_w_down[i])
        fp, st_p = _conv_ffn(_rmsnorm(xp, norm_ffn[i]), jnp.zeros((bp, FFN_CONV_WIDTH - 1, D_FF), xp.dtype), *fw)
        fs, st_s = _conv_ffn(_rmsnorm(xs, norm_ffn[i]), state_ffn[i], *fw)
        xp = xp + fp
        xs = xs + fs
        ffn_p_list.append(st_p)
        ffn_s_list.append(st_s)
    y_prompt = _rmsnorm(xp, norm_final)
    y_sample = _rmsnorm(xs, norm_final)
    ffn_p = jnp.stack(ffn_p_list, axis=0)
    ffn_s = jnp.stack(ffn_s_list, axis=0)
    return (y_prompt, y_sample, conv_p, conv_s, k_rows_p, v_rows_p, k_rows_s, v_rows_s,
            pool_p, pool_s, sg_v_s, ffn_p, ffn_s)
```

```python
import math
from contextlib import ExitStack

import numpy as np
import concourse.bass as bass
import concourse.mybir as mybir
from concourse.bass_utils import run_bass_kernel_spmd

F32 = mybir.dt.float32
BF16 = mybir.dt.bfloat16
I32 = mybir.dt.int32
AF = mybir.ActivationFunctionType
ALU = mybir.AluOpType
AX = mybir.AxisListType

D = 1024
DFF = 2816
NFF = 22
EPS = 1e-6
ENGS = ("pe", "act", "dve", "pool", "sp")


class Ins:
    __slots__ = ("eng", "emit", "deps", "dsem", "dwaits", "signal", "cnt")

    def __init__(self, eng, emit, dsem):
        self.eng = eng
        self.emit = emit
        self.dsem = dsem
        self.deps = []
        self.dwaits = {}
        self.signal = False
        self.cnt = 0


class Prog:
    def __init__(self, nc):
        self.nc = nc
        self.st = {e: [] for e in ENGS}
        self.lastw = {}
        self.rd = {}
        self.dcnt = {}

    def op(self, eng, emit, r=(), w=(), dsem=None):
        ins = Ins(eng, emit, dsem)
        deps = {}

        def need(d, kind):
            if d is None or d is ins:
                return
            if d.dsem is None and d.eng == eng and (eng == "pe" or kind == "WAR"):
                return
            deps[id(d)] = d

        for k in r:
            need(self.lastw.get(k), "RAW")
        for k in w:
            need(self.lastw.get(k), "WAW")
            for d in self.rd.get(k, {}).values():
                need(d, "WAR")
        ins.deps = list(deps.values())
        for d in ins.deps:
            if d.dsem is not None:
                ins.dwaits[d.dsem] = self.dcnt[d.dsem]
        rkey = eng if dsem is None else ("d", dsem)
        for k in r:
            self.rd.setdefault(k, {})[rkey] = ins
        for k in w:
            self.lastw[k] = ins
            self.rd[k] = {}
        if dsem is not None:
            self.dcnt[dsem] = self.dcnt.get(dsem, 0) + 16
        self.st[eng].append(ins)
        return ins

    def emit_all(self, stack):
        nc = self.nc
        for e in ENGS:
            for ins in self.st[e]:
                for d in ins.deps:
                    if d.dsem is None:
                        d.signal = True
        for e in ENGS:
            c = 0
            for ins in self.st[e]:
                if ins.dsem is None and ins.signal:
                    c += 1
                ins.cnt = c
        esem = {e: stack.enter_context(nc.semaphore("e_" + e)) for e in ENGS if e != "sp"}
        dsem = {k: stack.enter_context(nc.semaphore("d_%s" % str(k))) for k in self.dcnt}
        block = stack.enter_context(nc.Block())
        final = dict(self.dcnt)

        def run(e, eng):
            waited = {}
            for ins in self.st[e]:
                waits = {}
                for d in ins.deps:
                    if d.dsem is None:
                        key, val = ("e", d.eng), d.cnt
                    else:
                        key, val = ("d", d.dsem), ins.dwaits[d.dsem]
                    if waits.get(key, 0) < val:
                        waits[key] = val
                for key, val in waits.items():
                    if waited.get(key, 0) >= val:
                        continue
                    eng.wait_ge(esem[key[1]] if key[0] == "e" else dsem[key[1]], val)
                    waited[key] = val
                bi = ins.emit(eng)
                if ins.dsem is not None:
                    bi.then_inc(dsem[ins.dsem], 16)
                elif ins.signal:
                    bi.then_inc(esem[e], 1)
            if e == "sp":
                for k, v in final.items():
                    eng.wait_ge(dsem[k], v)

        block.tensor(lambda eng: run("pe", eng))
        block.scalar(lambda eng: run("act", eng))
        block.vector(lambda eng: run("dve", eng))
        block.gpsimd(lambda eng: run("pool", eng))
        block.sync(lambda eng: run("sp", eng))


def split_tiles(n, maxn=512):
    out = []
    t = 0
    while t < n:
        m = min(maxn, n - t)
        out.append((t, m))
        t += m
    return out


class Ctx:
    def __init__(self, nc, stack, n_wslots=4, wslot_elems=4096):
        self.nc = nc
        self.stack = stack
        self.p = Prog(nc)
        self.ps = [stack.enter_context(nc.psum_tensor("ps%d" % i, [128, 512], F32)) for i in range(8)]
        self.ps_i = 0
        self.wslots = [stack.enter_context(nc.sbuf_tensor("wslot%d" % i, [128, wslot_elems], BF16))
                       for i in range(n_wslots)]
        self.w_i = 0
        self.ones = self.sb("ones_bf", [128, 128], BF16)
        self.p.op("pool", lambda e: e.memset(self.ones[:], 1.0), w=["ones"])
        self.epsc = self.sb("epsc", [128, 1], F32)
        self.p.op("pool", lambda e: e.memset(self.epsc[:], EPS), w=["epsc"])
        self.uid = 0
        self.scr = None

    def sq_next(self):
        self.sq_i = (getattr(self, "sq_i", -1) + 1) % len(self.scr_sq)
        return self.sq_i

    def sb(self, name, shape, dt):
        return self.stack.enter_context(self.nc.sbuf_tensor("sb_" + name, shape, dt))

    def alloc(self, name, shape, dt):
        if self.scr is None:
            return self.sb(name, shape, dt)
        n = 1
        for d_ in shape[1:]:
            n *= d_
        nwords = (n * (4 if dt == F32 or dt == I32 else 2) + 3) // 4
        nwords = (nwords + 7) // 8 * 8
        assert self.scr_off + nwords <= self.scr_words, ("scratch overflow", name, self.scr_off, nwords, self.scr_words)
        v = self.scr[:, self.scr_off:self.scr_off + nwords]
        self.scr_off += nwords
        if dt != F32:
            v = v.bitcast(dt)
        v = v[:, 0:n]
        if len(shape) == 3:
            v = v.rearrange("p (a b) -> p a b", a=shape[1])
        elif len(shape) == 4:
            v = v.rearrange("p (a b c) -> p a b c", a=shape[1], b=shape[2])
        return v

    def scratch_init(self, words):
        self.scr = self.sb("scratch", [128, words], F32)
        self.scr_words = words
        self.scr_off = 0
        self.dmy = {e: self.sb("dmy_" + e, [128, 4], F32) for e in ("act", "dve", "pool")}

    def new_scope(self):
        p = self.p
        self.scr_off = 0
        self.nbar = getattr(self, "nbar", 0) + 1
        b = self.nbar
        p.op("act", lambda e: e.activation(out=self.dmy["act"][:, 0:1], in_=self.epsc[:, 0:1], func=AF.Copy),
             r=["epsc"], w=[("bar", b, "act")])
        p.op("dve", lambda e: e.memset(self.dmy["dve"][:, 0:1], 0.0), w=[("bar", b, "dve")])
        p.op("pool", lambda e: e.memset(self.dmy["pool"][:, 0:1], 0.0), w=[("bar", b, "pool")])
        allb = [("bar", b, e_) for e_ in ("act", "dve", "pool")]
        p.op("act", lambda e: e.activation(out=self.dmy["act"][:, 1:2], in_=self.epsc[:, 0:1], func=AF.Copy),
             r=["epsc"] + allb, w=[("bar2", b, "act")])
        p.op("dve", lambda e: e.memset(self.dmy["dve"][:, 1:2], 0.0), r=allb, w=[("bar2", b, "dve")])
        p.op("pool", lambda e: e.memset(self.dmy["pool"][:, 1:2], 0.0), r=allb, w=[("bar2", b, "pool"), "scr"])

    def load_s(self, dst, src, key, dsem, eng="sp"):
        self.p.op(eng, lambda e, o=dst, i=src: e.dma_start(out=o, in_=i), r=["scr"], w=[key], dsem=dsem)

    def store_s(self, dst, src, key, dsem, eng="sp"):
        self.p.op(eng, lambda e, o=dst, i=src: e.dma_start(out=o, in_=i), r=[key, "scr"], dsem=dsem)

    def dram_in(self, name, shape, dt=F32):
        return self.nc.dram_tensor(name, list(shape), dt, kind="ExternalInput").ap()

    def dram_out(self, name, shape, dt=F32):
        return self.nc.dram_tensor(name, list(shape), dt, kind="ExternalOutput").ap()

    def psum(self, exclude=0):
        i = self.ps_i
        self.ps_i = (self.ps_i + 1) % (8 - exclude)
        return self.ps[exclude + i], ("ps", exclude + i)

    def wload(self, src_ap, kc, ncols):
        s = self.w_i
        self.w_i = (self.w_i + 1) % len(self.wslots)
        view = self.wslots[s][:, 0:kc * ncols].rearrange("p (k n) -> p k n", k=kc)
        src = src_ap.rearrange("(k p) n -> p k n", p=128)
        self.p.op("pool", lambda e, o=view, i=src: e.dma_start(out=o, in_=i), w=[("w", s)], dsem=("w", s))
        return view, ("w", s)

    def load(self, dst, src, key, dsem, eng="sp"):
        self.p.op(eng, lambda e, o=dst, i=src: e.dma_start(out=o, in_=i), w=[key], dsem=dsem)

    def store(self, dst, src, key, dsem, eng="sp"):
        self.p.op(eng, lambda e, o=dst, i=src: e.dma_start(out=o, in_=i), r=[key], dsem=dsem)

    def const_cols(self, name, dram_ap, ncols):
        t = self.sb(name, [128, ncols], F32)
        self.load(t[:], dram_ap, name, "const")
        return t


def rmsnorm(cx, X, gcol, tiles, out_fn, tag, nch=8, dim=D):
    p = cx.p
    sq = cx.scr_sq
    R = cx.scr_r
    for ti, (t0, n) in enumerate(tiles):
        par = ti % 2
        ps, pk = cx.psum()
        for kc in range(nch):
            sqi = cx.sq_next()
            p.op("act", lambda e, o=sq[sqi][:, 0:n], i=X[:, kc, t0:t0 + n]:
                 e.activation(out=o, in_=i, func=AF.Square), r=[("X", ti)], w=[("sq", sqi)])
            p.op("pe", lambda e, o=ps[:, 0:n], r_=sq[sqi][:, 0:n], s=(kc == 0), t=(kc == nch - 1):
                 e.matmul(o, lhsT=cx.ones[:], rhs=r_, start=s, stop=t), r=[("sq", sqi), "ones"], w=[pk])
        p.op("act", lambda e, o=R[par][:, 0:n], i=ps[:, 0:n]:
             e.activation(out=o, in_=i, func=AF.Sqrt, bias=cx.epsc[:, 0:1], scale=1.0 / dim),
             r=[pk, "epsc"], w=[("R", par)])
        p.op("dve", lambda e, o=R[par][:, 0:n]: e.reciprocal(out=o, in_=o),
             r=[("R", par)], w=[("R", par)])
        for kc in range(nch):
            o_ap, wk = out_fn(kc, ti, t0, n)
            p.op("dve", lambda e, o=o_ap, i=X[:, kc, t0:t0 + n], g=gcol[:, kc:kc + 1], r_=R[par][:, 0:n]:
                 e.scalar_tensor_tensor(out=o, in0=i, scalar=g, in1=r_, op0=ALU.mult, op1=ALU.mult),
                 r=[("X", ti), ("R", par), gcol_key(gcol)], w=[wk])


_gk = {}


def gcol_key(t):
    return _gk.get(id(t), "const")


def xn_out(cx):
    def f(kc, ti, t0, n):
        return cx.XN[:, kc, t0:t0 + n], ("XN", ti)
    return f


def proj(cx, W, kc_n, col0, ncols_total, src, src_key, tiles, consume, group_cols=None):
    p = cx.p
    if group_cols is None:
        group_cols = max(128, (4096 // kc_n) // 128 * 128)
    c = 0
    while c < ncols_total:
        gc = min(group_cols, ncols_total - c)
        wv, wk = cx.wload(W[:, col0 + c:col0 + c + gc], kc_n, gc)
        for mm in range(gc // 128):
            for ti, (t0, n) in enumerate(tiles):
                ps, pk = cx.psum()
                for kc in range(kc_n):
                    p.op("pe", lambda e, o=ps[:, 0:n], l=wv[:, kc, mm * 128:(mm + 1) * 128],
                         r_=src[:, kc, t0:t0 + n], s=(kc == 0), t=(kc == kc_n - 1):
                         e.matmul(o, lhsT=l, rhs=r_, start=s, stop=t),
                         r=[wk, (src_key, ti)], w=[pk])
                consume(c // 128 + mm, ti, t0, n, ps, pk)
        c += gc


def conv_ffn(cx, li, W, tiles, np_tok, ns, halo, part=3):
    p = cx.p
    X, XN = cx.X, cx.XN
    p_dw = cx.ffn_dw[li]
    p_b = cx.ffn_b[li]
    stf = cx.ffn_state[li]
    outst = cx.ffn_out[li]
    j = 0
    parts = []
    while j < NFF:
        parts.append((j, min(part, NFF - j)))
        j += part
    for (j0, nj) in parts:
        wg, wgk = cx.wload(W["gate"][:, j0 * 128:(j0 + nj) * 128], 8, nj * 128)
        wu, wuk = cx.wload(W["up"][:, j0 * 128:(j0 + nj) * 128], 8, nj * 128)
        wd, wdk = cx.wload(W["down"][j0 * 128:(j0 + nj) * 128, :], nj, D)
        for jj in range(nj):
            jg = j0 + jj
            Gs = cx.Gs[jg % 2]
            p.op("pool", lambda e, o=Gs[:, :, 0:2], i=stf[:, jg, :, :]: e.tensor_copy(out=o, in_=i),
                 r=[("stf", li)], w=[("Gs", jg % 2)])
            prev = None
            for ti, (t0, n) in enumerate(tiles):
                samp = t0 >= np_tok
                psg, pgk = cx.psum()
                for kc in range(8):
                    p.op("pe", lambda e, o=psg[:, 0:n], l=wg[:, kc, jj * 128:(jj + 1) * 128],
                         r_=XN[:, kc, t0:t0 + n], s=(kc == 0), t=(kc == 7):
                         e.matmul(o, lhsT=l, rhs=r_, start=s, stop=t), r=[wgk, ("XN", ti)], w=[pgk])
                psu, puk = cx.psum()
                for kc in range(8):
                    p.op("pe", lambda e, o=psu[:, 0:n], l=wu[:, kc, jj * 128:(jj + 1) * 128],
                         r_=XN[:, kc, t0:t0 + n], s=(kc == 0), t=(kc == 7):
                         e.matmul(o, lhsT=l, rhs=r_, start=s, stop=t), r=[wuk, ("XN", ti)], w=[puk])
                cx.gt_i = (cx.gt_i + 1) % 2
                gp = cx.gt_i
                acc = cx.facc[gp]
                sil = cx.fsil[gp]
                if not samp:
                    Gt = cx.gt[gp]
                    gk = ("gt", gp)
                    if prev is None:
                        p.op("pool", lambda e, o=Gt[:, 0:2]: e.memset(o, 0.0), w=[gk])
                    else:
                        pg, pn = prev
                        p.op("pool", lambda e, o=Gt[:, 0:2], i=cx.gt[pg][:, pn:pn + 2]: e.tensor_copy(out=o, in_=i),
                             r=[("gt", pg)], w=[gk])
                    p.op("act", lambda e, o=Gt[:, 2:2 + n], i=psg[:, 0:n]:
                         e.activation(out=o, in_=i, func=AF.Copy), r=[pgk], w=[gk])
                    if ti == 0 and halo > 0:
                        assert halo <= n
                        p.op("dve", lambda e, o=Gt[:, 2:2 + halo]:
                             e.tensor_scalar(out=o, in0=o, scalar1=cx.hmask[:, 0:1], scalar2=None, op0=ALU.mult),
                             r=[gk, "const"], w=[gk])
                    prev = (gp, n)
                    v0 = Gt[:, 0:n]
                    v1 = Gt[:, 1:1 + n]
                    v2 = Gt[:, 2:2 + n]
                    a_ = acc[:, 0:n]
                    s_ = sil[:, 0:n]
                    u_ = psu[:, 0:n]
                    h_ = cx.Hb[:, jj, t0:t0 + n]
                    if t0 + n == np_tok:
                        p.op("pool", lambda e, o=outst[:, jg, 0:2], i=Gt[:, n:n + 2]: e.tensor_copy(out=o, in_=i),
                             r=[gk], w=[("ffo", li)])
                else:
                    gk = ("Gs", jg % 2)
                    p.op("act", lambda e, o=Gs[:, :, 2:10], i=psg[:, 0:n].rearrange("p (s t) -> p s t", t=8):
                         e.activation(out=o, in_=i, func=AF.Copy), r=[pgk], w=[gk])
                    v0 = Gs[:, :, 0:8]
                    v1 = Gs[:, :, 1:9]
                    v2 = Gs[:, :, 2:10]
                    a_ = acc[:, 0:n].rearrange("p (s t) -> p s t", t=8)
                    s_ = sil[:, 0:n].rearrange("p (s t) -> p s t", t=8)
                    u_ = psu[:, 0:n].rearrange("p (s t) -> p s t", t=8)
                    h_ = cx.Hb[:, jj, t0:t0 + n].rearrange("p (s t) -> p s t", t=8)
                    p.op("pool", lambda e, o=outst[:, jg, 2:2 + 2 * ns].rearrange("p (s t) -> p s t", t=2),
                         i=Gs[:, :, 8:10]: e.tensor_copy(out=o, in_=i), r=[gk], w=[("ffo", li)])
                ak = ("facc", gp)
                sk = ("fsil", gp)
                p.op("act", lambda e, o=a_, i=v0, sc=p_dw[:, jg, 0:1], b=p_b[:, jg:jg + 1]:
                     e.activation(out=o, in_=i, func=AF.Identity, bias=b, scale=sc),
                     r=[gk, "const"], w=[ak])
                p.op("dve", lambda e, o=a_, i=v1, sc=p_dw[:, jg, 1:2]:
                     e.scalar_tensor_tensor(out=o, in0=i, scalar=sc, in1=o, op0=ALU.mult, op1=ALU.add),
                     r=[gk, ak, "const"], w=[ak])
                p.op("dve", lambda e, o=a_, i=v2, sc=p_dw[:, jg, 2:3]:
                     e.scalar_tensor_tensor(out=o, in0=i, scalar=sc, in1=o, op0=ALU.mult, op1=ALU.add),
                     r=[gk, ak, "const"], w=[ak])
                p.op("act", lambda e, o=s_, i=a_: e.activation(out=o, in_=i, func=AF.Silu), r=[ak], w=[sk])
                p.op("dve", lambda e, o=h_, a=s_, b=u_: e.tensor_tensor(out=o, in0=a, in1=b, op=ALU.mult),
                     r=[sk, puk], w=[("Hb", ti)])
        for m in range(8):
            for ti, (t0, n) in enumerate(tiles):
                ps, pk = cx.psum()
                for jj in range(nj):
                    p.op("pe", lambda e, o=ps[:, 0:n], l=wd[:, jj, m * 128:(m + 1) * 128],
                         r_=cx.Hb[:, jj, t0:t0 + n], s=(jj == 0), t=(jj == nj - 1):
                         e.matmul(o, lhsT=l, rhs=r_, start=s, stop=t), r=[wdk, ("Hb", ti)], w=[pk])
                p.op("dve", lambda e, o=X[:, m, t0:t0 + n], i=ps[:, 0:n]:
                     e.tensor_tensor(out=o, in0=i, in1=o, op=ALU.add), r=[pk, ("X", ti)], w=[("X", ti)])


def ffn_setup(cx, n_layers, T, ns, dw_d, b_d, st_d, part=3):
    cx.ffn_dw, cx.ffn_b, cx.ffn_state, cx.ffn_out = [], [], [], []
    for li in range(n_layers):
        t = cx.alloc("ffdw%d" % li, [128, NFF, 3], F32)
        cx.load_s(t[:], dw_d[li], "const", "const")
        cx.ffn_dw.append(t)
        t = cx.alloc("ffb%d" % li, [128, NFF], F32)
        cx.load_s(t[:], b_d[li], "const", "const")
        cx.ffn_b.append(t)
        t = cx.alloc("ffst%d" % li, [128, NFF, ns, 2], F32)
        cx.load_s(t[:], st_d[li], ("stf", li), "const")
        cx.ffn_state.append(t)
        cx.ffn_out.append(cx.alloc("ffo%d" % li, [128, NFF, 2 + 2 * ns], F32))
    cx.gt = [cx.alloc("gt%d" % i, [128, 514], F32) for i in range(2)]
    cx.gt_i = 0
    cx.Gs = [cx.alloc("Gs%d" % i, [128, ns, 10], F32) for i in range(2)]
    cx.facc = [cx.alloc("facc%d" % i, [128, 512], F32) for i in range(2)]
    cx.fsil = [cx.alloc("fsil%d" % i, [128, 512], F32) for i in range(2)]
    cx.Hb = cx.alloc("Hb", [128, part, T], BF16)


HA = 32


def build_A(own, ns):
    np_tok = HA + own
    T = np_tok + ns * 8
    nc = bass.Bass("TRN2", target_bir_lowering=False)
    stack = ExitStack()
    with stack:
        cx = Ctx(nc, stack, n_wslots=4, wslot_elems=3072)
        p = cx.p
        d_x = cx.dram_in("xT", [D, T])
        d_hm = cx.dram_in("hmask", [128, 1])
        d_stc = cx.dram_in("stconv", [128, 8, ns, 30])
        d_vec = cx.dram_in("vecA", [128, 72])
        d_wdw = cx.dram_in("cv_wdw", [128, 8, 31])
        d_ident = cx.dram_in("identA", [128, 128])
        d_win = cx.dram_in("cv_w_in", [D, 2 * D])
        d_wout = cx.dram_in("cv_w_out", [D, D])
        d_fdw = cx.dram_in("ff_dw", [1, 128, NFF, 3])
        d_fb = cx.dram_in("ff_b", [1, 128, NFF])
        d_fst = cx.dram_in("ff_st", [1, 128, NFF, ns, 2])
        d_wg = cx.dram_in("ff_w_gate", [D, DFF])
        d_wu = cx.dram_in("ff_w_up", [D, DFF])
        d_wd = cx.dram_in("ff_w_down", [DFF, D])
        d_wqkv = cx.dram_in("w_qkv", [D, 3 * D])
        o_x1 = cx.dram_out("x1T", [D, T])
        o_qkv = cx.dram_out("qkvT", [3 * D, T])
        o_conv = cx.dram_out("convT", [128, 8, 30 + ns * 30])
        o_ffn = cx.dram_out("ffnT", [128, NFF, 2 + 2 * ns])

        tiles = split_tiles(np_tok) + [(np_tok, ns * 8)]
        ptiles = tiles[:-1]
        cx.X = cx.sb("X", [128, 8, T], F32)
        cx.XN = cx.sb("XN", [128, 8, T], BF16)
        cx.scr_sq = [cx.sb("sq%d" % i, [128, 512], BF16) for i in range(4)]
        cx.scr_r = [cx.sb("R%d" % i, [128, 512], F32) for i in range(2)]
        cx.hmask = cx.const_cols("hmask", d_hm, 1)
        vec = cx.const_cols("vecA", d_vec, 72)
        X, XN = cx.X, cx.XN
        for ti, (t0, n) in enumerate(tiles):
            for kc in range(8):
                cx.load(X[:, kc, t0:t0 + n], d_x[kc * 128:(kc + 1) * 128, t0:t0 + n], ("X", ti), ("xin", ti % 2))
        wdw = cx.sb("wdw", [128, 8, 31], F32)
        cx.load(wdw[:], d_wdw, "const", "const")
        identb = cx.sb("identb", [128, 128], BF16)
        p.op("pool", lambda e: e.dma_start(out=identb[:], in_=d_ident), w=["identb"], dsem="const2")
        cx.scratch_init(17000)
        G0 = cx.alloc("G0", [128, 8, 30 + np_tok], BF16)
        GS = cx.alloc("GS", [128, 8, ns, 38], F32)
        cx.load_s(GS[:, :, :, 0:30], d_stc, "GS", "const_s")
        glast = cx.alloc("glast", [128, 8, 32], F32)
        for c in range(8):
            p.op("pool", lambda e, o=G0[:, c, 0:30]: e.memset(o, 0.0), w=[("G0", c)])

        rmsnorm(cx, X, vec[:, 0:8], tiles, xn_out(cx), "n0")
        s1 = [cx.alloc("s1_%d" % i, [128, 512], F32) for i in range(2)]
        for (c0, ncg) in ((0, 3), (3, 3), (6, 2)):
            wa, wak = cx.wload(d_win[:, c0 * 128:(c0 + ncg) * 128], 8, ncg * 128)
            wg_, wgk = cx.wload(d_win[:, D + c0 * 128:D + (c0 + ncg) * 128], 8, ncg * 128)
            for cc in range(ncg):
                c = c0 + cc
                for ti, (t0, n) in enumerate(tiles):
                    samp = t0 >= np_tok
                    psa, pak = cx.psum()
                    for kc in range(8):
                        p.op("pe", lambda e, o=psa[:, 0:n], l=wa[:, kc, cc * 128:(cc + 1) * 128],
                             r_=XN[:, kc, t0:t0 + n], s=(kc == 0), t=(kc == 7):
                             e.matmul(o, lhsT=l, rhs=r_, start=s, stop=t), r=[wak, ("XN", ti)], w=[pak])
                    psg, pgk = cx.psum()
                    for kc in range(8):
                        p.op("pe", lambda e, o=psg[:, 0:n], l=wg_[:, kc, cc * 128:(cc + 1) * 128],
                             r_=XN[:, kc, t0:t0 + n], s=(kc == 0), t=(kc == 7):
                             e.matmul(o, lhsT=l, rhs=r_, start=s, stop=t), r=[wgk, ("XN", ti)], w=[pgk])
                    sp_ = ti % 2
                    p.op("act", lambda e, o=s1[sp_][:, 0:n], i=psg[:, 0:n], b=vec[:, 8 + 8 + c:8 + 8 + c + 1]:
                         e.activation(out=o, in_=i, func=AF.Sigmoid, bias=b), r=[pgk, "const"], w=[("s1", sp_)])
                    if not samp:
                        p.op("dve", lambda e, o=G0[:, c, 30 + t0:30 + t0 + n], i=psa[:, 0:n],
                             b=vec[:, 8 + c:8 + c + 1], s=s1[sp_][:, 0:n]:
                             e.scalar_tensor_tensor(out=o, in0=i, scalar=b, in1=s, op0=ALU.add, op1=ALU.mult),
                             r=[pak, ("s1", sp_), "const"], w=[("G0", c)])
                        if t0 + n == np_tok:
                            nl = min(n, 30)
                            p.op("dve", lambda e, o=glast[:, c, 30 - nl:30], i=psa[:, n - nl:n],
                                 b=vec[:, 8 + c:8 + c + 1], s=s1[sp_][:, n - nl:n]:
                                 e.scalar_tensor_tensor(out=o, in0=i, scalar=b, in1=s, op0=ALU.add, op1=ALU.mult),
                                 r=[pak, ("s1", sp_), "const"], w=["glast"])
                    else:
                        p.op("dve", lambda e, o=GS[:, c, :, 30:38], i=psa[:, 0:n].rearrange("p (s t) -> p s t", t=8),
                             b=vec[:, 8 + c:8 + c + 1], s=s1[sp_][:, 0:n].rearrange("p (s t) -> p s t", t=8):
                             e.scalar_tensor_tensor(out=o, in0=i, scalar=b, in1=s, op0=ALU.add, op1=ALU.mult),
                             r=[pak, ("s1", sp_), "const"], w=["GS"])
        assert ptiles[-1][1] >= 30
        cx.store_s(o_conv[:, :, 0:30], glast[:, :, 0:30], "glast", "outs")
        for c in range(8):
            cx.store_s(o_conv[:, c, 30:30 + ns * 30].rearrange("p (s t) -> p s t", t=30), GS[:, c, :, 8:38], "GS", "outs")
        accs = cx.alloc("caccs", [128, ns, 8], F32)
        DG = cx.alloc("DG", [128, 31, 128], BF16)
        for c in range(8):
            p.op("dve", lambda e, o=G0[:, c, 30:30 + HA]:
                 e.tensor_scalar(out=o, in0=o, scalar1=cx.hmask[:, 0:1], scalar2=None, op0=ALU.mult),
                 r=[("G0", c), "const"], w=[("G0", c)])
            for k in range(31):
                p.op("act", lambda e, o=DG[:, k, :], sc=wdw[:, c, k:k + 1]:
                     e.activation(out=o, in_=identb[:, :], func=AF.Copy, scale=sc), r=["identb", "const"], w=["DG"])
            for ti, (t0, n) in enumerate(ptiles):
                ps, pk = cx.psum()
                for k in range(31):
                    p.op("pe", lambda e, o=ps[:, 0:n], l=DG[:, k, :], r_=G0[:, c, t0 + k:t0 + k + n], s=(k == 0), t=(k == 30):
                         e.matmul(o, lhsT=l, rhs=r_, start=s, stop=t), r=["DG", ("G0", c)], w=[pk])
                p.op("act", lambda e, o=XN[:, c, t0:t0 + n], i=ps[:, 0:n], b=vec[:, 24 + c:25 + c]:
                     e.activation(out=o, in_=i, func=AF.Identity, bias=b), r=[pk, "const"], w=[("XN", ti)])
            for k in range(31):
                last = k == 30
                o_s = XN[:, c, np_tok:T].rearrange("p (s t) -> p s t", t=8) if last else accs[:, :, :]
                if k == 0:
                    p.op("dve", lambda e, o=o_s, i=GS[:, c, :, 0:8], sc=wdw[:, c, 0:1], b=vec[:, 24 + c:25 + c]:
                         e.tensor_scalar(out=o, in0=i, scalar1=sc, scalar2=b, op0=ALU.mult, op1=ALU.add),
                         r=["GS", "const"], w=["caccs"])
                else:
                    p.op("dve", lambda e, o=o_s, i=GS[:, c, :, k:k + 8], sc=wdw[:, c, k:k + 1], a=accs[:, :, :]:
                         e.scalar_tensor_tensor(out=o, in0=i, scalar=sc, in1=a, op0=ALU.mult, op1=ALU.add),
                         r=["GS", "caccs", "const"], w=["caccs"] + ([("XN", len(tiles) - 1)] if last else []))
        cx.new_scope()
        mu = [cx.alloc("mu%d" % i, [128, 512], F32) for i in range(2)]
        var = [cx.alloc("var%d" % i, [128, 512], F32) for i in range(2)]
        tmpc = [cx.alloc("tmpc%d" % i, [128, 512], F32) for i in range(2)]
        for ti, (t0, n) in enumerate(tiles):
            par = ti % 2
            ps1, pk1 = cx.psum()
            for kc in range(8):
                p.op("pe", lambda e, o=ps1[:, 0:n], r_=XN[:, kc, t0:t0 + n], s=(kc == 0), t=(kc == 7):
                     e.matmul(o, lhsT=cx.ones[:], rhs=r_, start=s, stop=t), r=[("XN", ti), "ones"], w=[pk1])
            ps2, pk2 = cx.psum()
            for kc in range(8):
                sqi = cx.sq_next()
                p.op("act", lambda e, o=cx.scr_sq[sqi][:, 0:n], i=XN[:, kc, t0:t0 + n]:
                     e.activation(out=o, in_=i, func=AF.Square), r=[("XN", ti)], w=[("sq", sqi)])
                p.op("pe", lambda e, o=ps2[:, 0:n], r_=cx.scr_sq[sqi][:, 0:n], s=(kc == 0), t=(kc == 7):
                     e.matmul(o, lhsT=cx.ones[:], rhs=r_, start=s, stop=t), r=[("sq", sqi), "ones"], w=[pk2])
            m_ = mu[par][:, 0:n]
            v_ = var[par][:, 0:n]
            p.op("dve", lambda e, o=m_, i=ps1[:, 0:n]:
                 e.tensor_scalar(out=o, in0=i, scalar1=1.0 / D, scalar2=None, op0=ALU.mult), r=[pk1], w=[("mu", par)])
            p.op("dve", lambda e, o=v_, a=m_: e.tensor_tensor(out=o, in0=a, in1=a, op=ALU.mult),
                 r=[("mu", par)], w=[("var", par)])
            p.op("dve", lambda e, o=v_, i=ps2[:, 0:n]:
                 e.scalar_tensor_tensor(out=o, in0=i, scalar=1.0 / D, in1=o, op0=ALU.mult, op1=ALU.subtract),
                 r=[pk2, ("var", par)], w=[("var", par)])
            p.op("act", lambda e, o=v_: e.activation(out=o, in_=o, func=AF.Sqrt, bias=cx.epsc[:, 0:1], scale=1.0),
                 r=[("var", par), "epsc"], w=[("var", par)])
            p.op("dve", lambda e, o=v_: e.reciprocal(out=o, in_=o), r=[("var", par)], w=[("var", par)])
            for kc in range(8):
                tp = (ti * 8 + kc) % 2
                t_ = tmpc[tp][:, 0:n]
                p.op("dve", lambda e, o=t_, a=XN[:, kc, t0:t0 + n], b=m_: e.tensor_tensor(out=o, in0=a, in1=b, op=ALU.subtract),
                     r=[("XN", ti), ("mu", par)], w=[("tmpc", tp)])
                p.op("dve", lambda e, o=t_, b=v_: e.tensor_tensor(out=o, in0=o, in1=b, op=ALU.mult),
                     r=[("tmpc", tp), ("var", par)], w=[("tmpc", tp)])
                p.op("act", lambda e, o=XN[:, kc, t0:t0 + n], i=t_, sc=vec[:, 32 + kc:33 + kc], b=vec[:, 40 + kc:41 + kc]:
                     e.activation(out=o, in_=i, func=AF.Silu, bias=b, scale=sc),
                     r=[("tmpc", tp), "const"], w=[("XN", ti)])
        def cons_out(m, ti, t0, n, ps, pk):
            p.op("dve", lambda e, o=X[:, m, t0:t0 + n], i=ps[:, 0:n], b=vec[:, 48 + m:49 + m]:
                 e.scalar_tensor_tensor(out=o, in0=i, scalar=b, in1=o, op0=ALU.add, op1=ALU.add),
                 r=[pk, ("X", ti), "const"], w=[("X", ti)])
        proj(cx, d_wout, 8, 0, D, XN, "XN", tiles, cons_out, group_cols=384)
        cx.new_scope()
        ffn_setup(cx, 1, T, ns, d_fdw, d_fb, d_fst)
        rmsnorm(cx, X, vec[:, 56:64], tiles, xn_out(cx), "nf0")
        conv_ffn(cx, 0, {"gate": d_wg, "up": d_wu, "down": d_wd}, tiles, np_tok, ns, HA)
        cx.store_s(o_ffn, cx.ffn_out[0], ("ffo", 0), "outs")
        for ti, (t0, n) in enumerate(tiles):
            for kc in range(8):
                cx.store(o_x1[kc * 128:(kc + 1) * 128, t0:t0 + n], X[:, kc, t0:t0 + n], ("X", ti), "outs")
        cx.new_scope()
        rmsnorm(cx, X, vec[:, 64:72], tiles, xn_out(cx), "n1")
        ost = [cx.alloc("ost%d" % i, [128, 512], F32) for i in range(3)]
        cnt = [0]

        def cons_qkv(m, ti, t0, n, ps, pk):
            s = cnt[0] % 3
            cnt[0] += 1
            p.op("act", lambda e, o=ost[s][:, 0:n], i=ps[:, 0:n]: e.activation(out=o, in_=i, func=AF.Copy),
                 r=[pk], w=[("ost", s)])
            cx.store_s(o_qkv[m * 128:(m + 1) * 128, t0:t0 + n], ost[s][:, 0:n], ("ost", s), ("ost", s))
        proj(cx, d_wqkv, 8, 0, 3 * D, XN, "XN", tiles, cons_qkv, group_cols=384)
        cx.p.emit_all(stack)
    return nc


def lay_cols(v):
    v = np.asarray(v, np.float32)
    return np.ascontiguousarray(v.reshape(-1, 128).T)


def run_A(inp, own, ns, n_cores, seq_of_core, nc_cache={}):
    key = (own, ns)
    if key not in nc_cache:
        nc_cache[key] = build_A(own, ns)
    nc = nc_cache[key]
    xp = np.asarray(inp["x_prompt"], np.float32)
    xs = np.asarray(inp["x_sample"], np.float32)
    vec = np.concatenate([
        lay_cols(inp["norm_mix"][0]), lay_cols(inp["cv_b_in"]), lay_cols(inp["cv_b_dw"]),
        lay_cols(inp["cv_ln_g"]), lay_cols(inp["cv_ln_b"]), lay_cols(inp["cv_b_out"]),
        lay_cols(inp["norm_ffn"][0]), lay_cols(inp["norm_mix"][1])], axis=1)
    wdw = np.ascontiguousarray(np.asarray(inp["cv_w_dw"], np.float32).T.reshape(8, 128, 31).transpose(1, 0, 2))
    fdw = np.ascontiguousarray(np.asarray(inp["ff_w_dw"][0], np.float32).T.reshape(NFF, 128, 3).transpose(1, 0, 2))[None]
    fb = lay_cols(inp["ff_b_dw"][0])[None]
    in_maps = []
    for c in range(n_cores):
        b, h = seq_of_core(c)
        seg = xp[b, h * own:(h + 1) * own]
        if h == 0:
            halo = np.zeros((HA, D), np.float32)
        else:
            halo = xp[b, h * own - HA:h * own]
        sm = xs[c * ns:(c + 1) * ns].reshape(ns * 8, D)
        xT = np.ascontiguousarray(np.concatenate([halo, seg, sm], 0).T)
        stc = np.asarray(inp["state_conv"][c * ns:(c + 1) * ns], np.float32)
        stc = np.ascontiguousarray(stc.transpose(2, 0, 1).reshape(8, 128, ns, 30).transpose(1, 0, 2, 3))
        stf = np.asarray(inp["state_ffn"][0, c * ns:(c + 1) * ns], np.float32)
        stf = np.ascontiguousarray(stf.transpose(2, 0, 1).reshape(NFF, 128, ns, 2).transpose(1, 0, 2, 3))[None]
        in_maps.append({
            "xT": xT, "hmask": np.full((128, 1), float(h), np.float32), "stconv": stc, "vecA": vec,
            "cv_wdw": wdw, "identA": np.eye(128, dtype=np.float32), "cv_w_in": np.asarray(inp["cv_w_in"], np.float32),
            "cv_w_out": np.asarray(inp["cv_w_out"], np.float32),
            "ff_dw": fdw, "ff_b": fb, "ff_st": stf,
            "ff_w_gate": np.asarray(inp["ff_w_gate"][0], np.float32),
            "ff_w_up": np.asarray(inp["ff_w_up"][0], np.float32),
            "ff_w_down": np.asarray(inp["ff_w_down"][0], np.float32),
            "w_qkv": np.asarray(inp["da_w_qkv"], np.float32),
        })
    res = run_bass_kernel_spmd(nc, in_maps, core_ids=list(range(n_cores)))
    return res.results


LAM_INIT = 0.8 - 0.6 * math.exp(-0.3 * 1)


def build_B(nseq, S, nss, npg, n_phys):
    nc = bass.Bass("TRN2", target_bir_lowering=False)
    stack = ExitStack()
    NG = S // 512
    NB = S // 128
    with stack:
        cx = Ctx(nc, stack, n_wslots=1, wslot_elems=64)
        p = cx.p
        d_q = cx.dram_in("qT", [nseq, 128, S])
        d_k = cx.dram_in("kT", [nseq, 128, S])
        d_v = cx.dram_in("v", [nseq, S, 128])
        d_qs = cx.dram_in("qsT", [128, nss * 8])
        d_ks = cx.dram_in("ksT", [128, nss * 8])
        d_vs = cx.dram_in("vs", [nss * 8, 128])
        d_ck = cx.dram_in("ck", [n_phys, 128, 128])
        d_cv = cx.dram_in("cv", [n_phys, 128, 128])
        d_tabr = cx.dram_in("ptabr", [128, nss], I32)
        d_iota = cx.dram_in("iota", [128, 1])
        d_ident = cx.dram_in("ident", [128, 128])
        d_lam = cx.dram_in("lamp", [1, 256])
        d_g = cx.dram_in("gcol", [128, 1])
        d_grow = cx.dram_in("grow", [8, 128])
        d_mask = cx.dram_in("masks", [128, 4, 512])
        d_smask = cx.dram_in("smask", [8, 8])
        o_p = cx.dram_out("oT", [nseq, 128, S])
        o_s = cx.dram_out("os", [nss * 8, 128])

        masks = cx.sb("masks", [128, 4, 512], BF16)
        p.op("pool", lambda e: e.dma_start(out=masks[:], in_=d_mask), w=["masks"], dsem="const2")
        smask = cx.sb("smask", [8, 8], F32)
        cx.load(smask[:], d_smask, "smask", "const")
        gcol = cx.sb("gcol", [128, 1], F32)
        cx.load(gcol[:], d_g, "gcol", "const")
        grow = cx.sb("grow", [8, 128], F32)
        cx.load(grow[:], d_grow, "grow", "const")
        lamp = cx.sb("lamp", [1, 256], F32)
        cx.load(lamp[:], d_lam, "lamp", "const")
        onesf = cx.sb("onesf", [1, 128], F32)
        p.op("pool", lambda e: e.memset(onesf[:], 1.0), w=["onesf"])
        lt = cx.sb("lt", [1, 128], F32)
        lsum = cx.sb("lsum", [1, 4], F32)
        p.op("dve", lambda e: e.tensor_tensor(out=lt[:, 0:64], in0=lamp[:, 0:64], in1=lamp[:, 64:128], op=ALU.mult),
             r=["lamp"], w=["lt"])
        p.op("dve", lambda e: e.tensor_tensor(out=lt[:, 64:128], in0=lamp[:, 128:192], in1=lamp[:, 192:256], op=ALU.mult),
             r=["lamp"], w=["lt"])
        p.op("dve", lambda e: e.reduce_sum(out=lsum[:, 0:1], in_=lt[:, 0:64], axis=AX.X), r=["lt"], w=["lsum"])
        p.op("dve", lambda e: e.reduce_sum(out=lsum[:, 1:2], in_=lt[:, 64:128], axis=AX.X), r=["lt"], w=["lsum"])
        p.op("act", lambda e: e.activation(out=lsum[:, 0:2], in_=lsum[:, 0:2], func=AF.Exp), r=["lsum"], w=["lsum"])
        p.op("dve", lambda e: e.tensor_tensor(out=lsum[:, 2:3], in0=lsum[:, 1:2], in1=lsum[:, 0:1], op=ALU.subtract),
             r=["lsum"], w=["lsum"])
        p.op("dve", lambda e: e.tensor_scalar(out=lsum[:, 2:3], in0=lsum[:, 2:3], scalar1=-LAM_INIT, scalar2=None, op0=ALU.add),
             r=["lsum"], w=["lsum"])
        neglam = cx.sb("neglam", [128, 1], F32)
        psl, plk = cx.psum()
        p.op("pe", lambda e: e.matmul(psl[:, 0:1], lhsT=onesf[:, :], rhs=lsum[:, 2:3], start=True, stop=True),
             r=["onesf", "lsum"], w=[plk])
        p.op("dve", lambda e: e.tensor_copy(out=neglam[:], in_=psl[:, 0:1]), r=[plk], w=["neglam"])
        gsc = cx.sb("gsc", [128, 1], F32)
        p.op("dve", lambda e: e.tensor_scalar(out=gsc[:], in0=gcol[:], scalar1=1.0 - LAM_INIT, scalar2=None, op0=ALU.mult),
             r=["gcol"], w=["gsc"])
        grs = cx.sb("grs", [8, 128], F32)
        p.op("dve", lambda e: e.tensor_scalar(out=grs[:], in0=grow[:], scalar1=1.0 - LAM_INIT, scalar2=None, op0=ALU.mult),
             r=["grow"], w=["grs"])

        Q = [cx.sb("Q%d" % i, [128, S], BF16) for i in range(2)]
        Kt = [cx.sb("K%d" % i, [128, S], BF16) for i in range(2)]
        V = [cx.sb("V%d" % i, [128, NB, 128], BF16) for i in range(2)]
        PT = [[cx.sb("PT%d_%d" % (m, i), [128, 512], BF16) for i in range(2)] for m in range(2)]
        ep = {n_: cx.sb("ep_" + n_, [128, 512], F32) for n_ in ("r1", "r2", "t1", "o", "rs")}
        epq = cx.sb("ep_sq", [128, 512], BF16)
        ost = [cx.sb("ostB%d" % i, [128, 512], F32) for i in range(2)]
        O1, O2, L1, L2 = cx.ps[0], cx.ps[1], cx.ps[2], cx.ps[3]
        gi = 0
        for b in range(nseq):
            bp = b % 2
            for hh in range(0, S, 2048):
                he = min(S, hh + 2048)
                p.op("pool", lambda e, o=Q[bp][:, hh:he], i=d_q[b][:, hh:he]: e.dma_start(out=o, in_=i), w=[("Q", bp)], dsem=("qkv", bp))
                p.op("pool", lambda e, o=Kt[bp][:, hh:he], i=d_k[b][:, hh:he]: e.dma_start(out=o, in_=i), w=[("K", bp)], dsem=("qkv", bp))
            p.op("pool", lambda e, o=V[bp][:], i=d_v[b].rearrange("(n p) e -> p n e", p=128): e.dma_start(out=o, in_=i),
                 w=[("V", bp)], dsem=("qkv", bp))
            for G in range(NG):
                nkb = 4 * G + 4
                qs = slice(G * 512, (G + 1) * 512)

                def scores(kb):
                    par = kb % 2
                    ks = slice(kb * 128, (kb + 1) * 128)
                    p.op("pe", lambda e, o=cx.ps[4 + 2 * par][:, :], l=Kt[bp][0:64, ks], r_=Q[bp][0:64, qs]:
                         e.matmul(o, lhsT=l, rhs=r_, start=True, stop=True),
                         r=[("K", bp), ("Q", bp)], w=[("ps", 4 + 2 * par)])
                    p.op("pe", lambda e, o=cx.ps[5 + 2 * par][:, :], l=Kt[bp][64:128, ks], r_=Q[bp][64:128, qs]:
                         e.matmul(o, lhsT=l, rhs=r_, start=True, stop=True),
                         r=[("K", bp), ("Q", bp)], w=[("ps", 5 + 2 * par)])

                scores(0)
                for kb in range(nkb):
                    par = kb % 2
                    if kb + 1 < nkb:
                        scores(kb + 1)
                    for m in range(2):
                        p.op("act", lambda e, o=PT[m][par][:, :], i=cx.ps[4 + m + 2 * par][:, :]:
                             e.activation(out=o, in_=i, func=AF.Exp, scale=0.125),
                             r=[("ps", 4 + m + 2 * par)], w=[("pt", m, par)])
                        if kb >= 4 * G:
                            p.op("dve", lambda e, o=PT[m][par][:, :], mk=masks[:, kb - 4 * G, :]:
                                 e.tensor_tensor(out=o, in0=o, in1=mk, op=ALU.mult),
                                 r=[("pt", m, par), "masks"], w=[("pt", m, par)])
                    for m, (Ob, Lb) in enumerate(((0, 2), (1, 3))):
                        p.op("pe", lambda e, o=cx.ps[Ob][:, :], l=V[bp][:, kb, :], r_=PT[m][par][:, :], s=(kb == 0), t=(kb == nkb - 1):
                             e.matmul(o, lhsT=l, rhs=r_, start=s, stop=t), r=[("V", bp), ("pt", m, par)], w=[("ps", Ob)])
                        p.op("pe", lambda e, o=cx.ps[Lb][:, :], r_=PT[m][par][:, :], s=(kb == 0), t=(kb == nkb - 1):
                             e.matmul(o, lhsT=cx.ones[:], rhs=r_, start=s, stop=t), r=["ones", ("pt", m, par)], w=[("ps", Lb)])
                p.op("dve", lambda e: e.reciprocal(out=ep["r1"][:], in_=L1[:, :]), r=[("ps", 2)], w=["ep_r1"])
                p.op("dve", lambda e: e.reciprocal(out=ep["r2"][:], in_=L2[:, :]), r=[("ps", 3)], w=["ep_r2"])
                p.op("dve", lambda e: e.tensor_tensor(out=ep["t1"][:], in0=O1[:, :], in1=ep["r1"][:], op=ALU.mult),
                     r=[("ps", 0), "ep_r1"], w=["ep_t1"])
                p.op("dve", lambda e: e.tensor_tensor(out=ep["r2"][:], in0=O2[:, :], in1=ep["r2"][:], op=ALU.mult),
                     r=[("ps", 1), "ep_r2"], w=["ep_r2"])
                p.op("dve", lambda e: e.scalar_tensor_tensor(out=ep["o"][:], in0=ep["r2"][:], scalar=neglam[:, 0:1], in1=ep["t1"][:],
                                                             op0=ALU.mult, op1=ALU.add),
                     r=["ep_r2", "ep_t1", "neglam"], w=["ep_o"])
                p.op("act", lambda e: e.activation(out=epq[:], in_=ep["o"][:], func=AF.Square), r=["ep_o"], w=["ep_sq"])
                sp_ = 4 + 2 * (nkb % 2)
                p.op("pe", lambda e, o=cx.ps[sp_][:, :]: e.matmul(o, lhsT=cx.ones[:], rhs=epq[:], start=True, stop=True),
                     r=["ones", "ep_sq"], w=[("ps", sp_)])
                p.op("act", lambda e, i=cx.ps[sp_][:, :]: e.activation(out=ep["rs"][:], in_=i, func=AF.Sqrt, bias=cx.epsc[:, 0:1], scale=1.0 / 128),
                     r=[("ps", sp_), "epsc"], w=["ep_rs"])
                p.op("dve", lambda e: e.reciprocal(out=ep["rs"][:], in_=ep["rs"][:]), r=["ep_rs"], w=["ep_rs"])
                so = gi % 2
                gi += 1
                p.op("dve", lambda e, o=ost[so][:]: e.scalar_tensor_tensor(out=o, in0=ep["o"][:], scalar=gsc[:, 0:1], in1=ep["rs"][:],
                                                                          op0=ALU.mult, op1=ALU.mult),
                     r=["ep_o", "ep_rs", "gsc"], w=[("ostB", so)])
                cx.store(o_p[b][:, qs], ost[so][:], ("ostB", so), ("ostB", so))

        NT = nss * 8
        QS = cx.sb("QS", [128, NT], BF16)
        KN = cx.sb("KN", [128, NT], BF16)
        p.op("pool", lambda e: e.dma_start(out=QS[:], in_=d_qs), w=["QS"], dsem="const2")
        p.op("pool", lambda e: e.dma_start(out=KN[:], in_=d_ks), w=["KN"], dsem="const2")
        assert npg * 8 == 128
        NBUF = 4
        KTk = [cx.sb("KTk%d" % i, [128, 16, 128], F32) for i in range(NBUF)]
        KB = [cx.sb("KB%d" % i, [128, 16, 128], BF16) for i in range(NBUF)]
        VB = [cx.sb("VB%d" % i, [128, 17, 128], BF16) for i in range(NBUF)]
        for i in range(NBUF):
            p.op("pool", lambda e, o=VB[i][:, 16, :]: e.memset(o, 0.0), w=[("VB", i)])
        PS_ = [[cx.sb("PS%d_%d" % (m, i), [128, 17 * 8], BF16) for i in range(NBUF)] for m in range(2)]
        OS = [cx.sb("OS%d" % i, [8, 128], F32) for i in range(4)]
        sm = {n_: cx.sb("sm_" + n_, [8, 2], F32) for n_ in ("r", "ss")}
        smt = cx.sb("sm_t1", [8, 128], F32)
        smo = cx.sb("sm_o", [8, 128], F32)
        smq = cx.sb("sm_q", [8, 128], F32)
        NC_ = 17 * 8
        tabi = cx.sb("tabi", [128, nss], I32)
        tabf = cx.sb("tabf", [128, nss], F32)
        idx = cx.sb("idx", [128, nss], I32)
        iot = cx.sb("iot", [128, 1], F32)
        ident = cx.sb("ident", [128, 128], F32)
        cx.load(tabi[:], d_tabr, "tabi", "const")
        cx.load(iot[:], d_iota, "iot", "const")
        cx.load(ident[:], d_ident, "ident", "const")
        p.op("dve", lambda e: e.tensor_copy(out=tabf[:], in_=tabi[:]), r=["tabi"], w=["tabf"])
        p.op("dve", lambda e: e.tensor_scalar(out=tabf[:], in0=tabf[:], scalar1=8.0, scalar2=iot[:, 0:1], op0=ALU.mult, op1=ALU.add),
             r=["tabf", "iot"], w=["tabf"])
        p.op("dve", lambda e: e.tensor_copy(out=idx[:], in_=tabf[:]), r=["tabf"], w=["idx"])
        ckf = d_ck.rearrange("n (a b) d -> (n a) (b d)", a=8)
        cvf = d_cv.rearrange("n (a b) d -> (n a) (b d)", a=8)
        npg = 16
        for i in range(nss):
            par = i % NBUF
            p.op("pool", lambda e, o=KTk[par][:, :, :].rearrange("p a b -> p (a b)"), c_=i: e.indirect_dma_start(
                out=o, out_offset=None, in_=ckf, in_offset=bass.IndirectOffsetOnAxis(ap=idx[:, c_:c_ + 1], axis=0)),
                r=["idx"], w=[("KTk", par)], dsem=("kb", par))
            p.op("pool", lambda e, o=VB[par][:, 0:16, :].rearrange("p a b -> p (a b)"), c_=i: e.indirect_dma_start(
                out=o, out_offset=None, in_=cvf, in_offset=bass.IndirectOffsetOnAxis(ap=idx[:, c_:c_ + 1], axis=0)),
                r=["idx"], w=[("VB", par)], dsem=("vb", par))
            p.op("pool", lambda e, o=VB[par][0:8, 16, :], i_=d_vs[i * 8:(i + 1) * 8, :]: e.dma_start(out=o, in_=i_),
                 w=[("VB", par)], dsem=("vb", par))
            for q4 in range(4):
                pst, ptk = cx.psum()
                for uu in range(4):
                    u = q4 * 4 + uu
                    p.op("pe", lambda e, o=pst[:, uu * 128:(uu + 1) * 128], a=KTk[par][:, u, :]:
                         e.transpose(o, a, ident[:, :]), r=[("KTk", par), "ident"], w=[ptk])
                eng_ = "act" if q4 % 2 == 0 else "dve"
                if eng_ == "act":
                    p.op("act", lambda e, o=KB[par][:, q4 * 4:(q4 + 1) * 4, :], a=pst[:, :].rearrange("p (u k) -> p u k", u=4):
                         e.activation(out=o, in_=a, func=AF.Copy), r=[ptk], w=[("KB", par)])
                else:
                    p.op("dve", lambda e, o=KB[par][:, q4 * 4:(q4 + 1) * 4, :], a=pst[:, :].rearrange("p (u k) -> p u k", u=4):
                         e.tensor_copy(out=o, in_=a), r=[ptk], w=[("KB", par)])
            qsl = slice(i * 8, (i + 1) * 8)
            sps = []
            for m in range(2):
                ps, pk = cx.psum()
                sps.append((ps, pk))
                pr = slice(64 * m, 64 * m + 64)
                for j in range(npg):
                    p.op("pe", lambda e, o=ps[:, j * 8:(j + 1) * 8], l=KB[par][pr, j, :], r_=QS[pr, qsl]:
                         e.matmul(o, lhsT=l, rhs=r_, start=True, stop=True), r=[("KB", par), "QS"], w=[pk])
                p.op("pe", lambda e, o=ps[0:8, npg * 8:NC_], l=KN[pr, qsl], r_=QS[pr, qsl]:
                     e.matmul(o, lhsT=l, rhs=r_, start=True, stop=True), r=["KN", "QS"], w=[pk])
            for m in range(2):
                ps, pk = sps[m]
                p.op("act", lambda e, o=PS_[m][par][:, 0:npg * 8], i_=ps[:, 0:npg * 8]:
                     e.activation(out=o, in_=i_, func=AF.Exp, scale=0.125), r=[pk], w=[("PS", m, par)])
                p.op("act", lambda e, o=PS_[m][par][0:8, npg * 8:NC_], i_=ps[0:8, npg * 8:NC_]:
                     e.activation(out=o, in_=i_, func=AF.Exp, scale=0.125), r=[pk], w=[("PS", m, par)])
                p.op("dve", lambda e, o=PS_[m][par][0:8, npg * 8:NC_]: e.tensor_tensor(out=o, in0=o, in1=smask[:, :], op=ALU.mult),
                     r=[("PS", m, par), "smask"], w=[("PS", m, par)])
            ops_ = []
            for m in range(2):
                ps, pk = cx.psum()
                ops_.append((ps, pk))
                for j in range(npg):
                    p.op("pe", lambda e, o=ps[0:8, 128:129], l=PS_[m][par][:, j * 8:(j + 1) * 8], s=(j == 0):
                         e.matmul(o, lhsT=l, rhs=cx.ones[:, 0:1], start=s, stop=False), r=[("PS", m, par), "ones"], w=[pk])
                p.op("pe", lambda e, o=ps[0:8, 128:129], l=PS_[m][par][0:8, npg * 8:NC_]:
                     e.matmul(o, lhsT=l, rhs=cx.ones[0:8, 0:1], start=False, stop=True), r=[("PS", m, par), "ones"], w=[pk])
                for j in range(npg):
                    p.op("pe", lambda e, o=ps[0:8, 0:128], l=PS_[m][par][:, j * 8:(j + 1) * 8], r_=VB[par][:, j, :], s=(j == 0):
                         e.matmul(o, lhsT=l, rhs=r_, start=s, stop=False), r=[("PS", m, par), ("VB", par)], w=[pk])
                p.op("pe", lambda e, o=ps[0:8, 0:128], l=PS_[m][par][0:8, npg * 8:NC_], r_=VB[par][0:8, npg, :]:
                     e.matmul(o, lhsT=l, rhs=r_, start=False, stop=True), r=[("PS", m, par), ("VB", par)], w=[pk])
            (p1, k1), (p2, k2) = ops_
            p.op("dve", lambda e, a=p1[0:8, 128:129]: e.reciprocal(out=sm["r"][:, 0:1], in_=a), r=[k1], w=["sm_r"])
            p.op("dve", lambda e, a=p2[0:8, 128:129]: e.reciprocal(out=sm["r"][:, 1:2], in_=a), r=[k2], w=["sm_r"])
            p.op("dve", lambda e: e.tensor_tensor(out=sm["r"][:, 1:2], in0=sm["r"][:, 1:2], in1=neglam[0:8, 0:1], op=ALU.mult),
                 r=["sm_r", "neglam"], w=["sm_r"])
            p.op("dve", lambda e, a=p1[0:8, 0:128]: e.tensor_scalar(out=smt[:], in0=a, scalar1=sm["r"][:, 0:1], scalar2=None, op0=ALU.mult),
                 r=[k1, "sm_r"], w=["sm_t1"])
            p.op("dve", lambda e, a=p2[0:8, 0:128]: e.scalar_tensor_tensor(out=smo[:], in0=a, scalar=sm["r"][:, 1:2], in1=smt[:],
                                                                            op0=ALU.mult, op1=ALU.add),
                 r=[k2, "sm_r", "sm_t1"], w=["sm_o"])
            p.op("dve", lambda e: e.tensor_tensor(out=smq[:], in0=smo[:], in1=smo[:], op=ALU.mult), r=["sm_o"], w=["sm_q"])
            p.op("dve", lambda e: e.reduce_sum(out=sm["ss"][:, 0:1], in_=smq[:], axis=AX.X), r=["sm_q"], w=["sm_ss"])
            p.op("act", lambda e: e.activation(out=sm["ss"][:, 1:2], in_=sm["ss"][:, 0:1], func=AF.Sqrt, bias=cx.epsc[0:8, 0:1], scale=1.0 / 128),
                 r=["sm_ss", "epsc"], w=["sm_ss"])
            p.op("dve", lambda e: e.reciprocal(out=sm["ss"][:, 1:2], in_=sm["ss"][:, 1:2]), r=["sm_ss"], w=["sm_ss"])
            osl = i % 4
            p.op("dve", lambda e, o=OS[osl][:, :]: e.scalar_tensor_tensor(out=o, in0=smo[:], scalar=sm["ss"][:, 1:2], in1=grs[:],
                                                                         op0=ALU.mult, op1=ALU.mult),
                 r=["sm_o", "sm_ss", "grs"], w=[("OS", osl)])
            cx.store(o_s[i * 8:(i + 1) * 8, :], OS[osl][:, :], ("OS", osl), ("OS", osl))
        cx.p.emit_all(stack)
    return nc


def make_masks():
    pidx = np.arange(128)[:, None]
    j = np.arange(512)[None, :]
    m = np.stack([(j >= o * 128 + pidx) for o in range(4)], axis=1).astype(np.float32)
    sm = (np.arange(8)[:, None] <= np.arange(8)[None, :]).astype(np.float32)
    return m, sm


HC = 256
POOL_WIN = (2, 2, 4, 4, 8, 8, 16, 16)
GELU_C = 1.5957691216057308


def gelu_tile(cx, x_ap, out_ap, n_shape_keys, rk, wk):
    cx.p.op("act", lambda e: e.activation(out=out_ap, in_=x_ap, func=AF.Gelu_apprx_tanh), r=rk, w=wk)


def build_C(own, ns, debug=None):
    NP = HC + own
    NSB = ns * 8
    T = NP + NSB
    assert own % 128 == 0 and NSB <= 128
    nc = bass.Bass("TRN2", target_bir_lowering=False)
    stack = ExitStack()
    with stack:
        cx = Ctx(nc, stack, n_wslots=4, wslot_elems=3072)
        p = cx.p
        d_x = cx.dram_in("x1T", [D, T])
        d_o = cx.dram_in("oT", [D, T])
        d_hm = cx.dram_in("hmask", [128, 1])
        d_vec = cx.dram_in("vecC", [128, 88])
        d_wo = cx.dram_in("w_o", [D, D])
        d_plw = cx.dram_in("pl_w", [D, 256])
        d_stp = cx.dram_in("stpool", [128, 8, ns, 15])
        d_invc = cx.dram_in("invc", [128, 4, 16])
        d_fdw = cx.dram_in("ff_dw", [3, 128, NFF, 3])
        d_fb = cx.dram_in("ff_b", [3, 128, NFF])
        d_fst = cx.dram_in("ff_st", [3, 128, NFF, ns, 2])
        d_wg = cx.dram_in("ff_w_gate", [3, D, DFF])
        d_wu = cx.dram_in("ff_w_up", [3, D, DFF])
        d_wd = cx.dram_in("ff_w_down", [3, DFF, D])
        d_sgin = cx.dram_in("sg_w_in", [D, 4 * D])
        d_sgout = cx.dram_in("sg_w_out", [2 * D, D])
        d_sgrow = cx.dram_in("sg_rows", [3, 128, 2 * D])
        d_wst = cx.dram_in("sg_wsT", [2, 128, 4, 128])
        d_wsm = cx.dram_in("sg_mask", [2, 128, 128])
        d_bs = cx.dram_in("sg_bs", [2, 128, 4, 128])
        o_y = cx.dram_out("yT", [D, T])
        o_pool = cx.dram_out("poolT", [128, 8, 15 + ns * 15])
        o_sgv = cx.dram_out("sgv", [NSB, 2 * D])
        o_ffn = cx.dram_out("ffnT", [3, 128, NFF, 2 + 2 * ns])

        tiles = split_tiles(NP) + [(NP, NSB)]
        nt = len(tiles)
        cx.X = cx.sb("X", [128, 8, T], F32)
        cx.XN = cx.sb("XN", [128, 8, T], BF16)
        X, XN = cx.X, cx.XN
        cx.scr_sq = [cx.sb("sq%d" % i, [128, 512], BF16) for i in range(4)]
        cx.scr_r = [cx.sb("R%d" % i, [128, 512], F32) for i in range(2)]
        cx.hmask = cx.const_cols("hmask", d_hm, 1)
        vec = cx.const_cols("vecC", d_vec, 88)
        cx.scratch_init(13800)
        for ti, (t0, n) in enumerate(tiles):
            for kc in range(8):
                cx.load(X[:, kc, t0:t0 + n], d_x[kc * 128:(kc + 1) * 128, t0:t0 + n], ("X", ti), ("xin", ti % 2))
        for ti, (t0, n) in enumerate(tiles):
            p.op("pool", lambda e, o=XN[:, :, t0:t0 + n], i=d_o[:, t0:t0 + n].rearrange("(k p) n -> p k n", p=128):
                 e.dma_start(out=o, in_=i), w=[("XN", ti)], dsem=("oin", ti % 2))

        def cons_add(m, ti, t0, n, ps, pk):
            p.op("dve", lambda e, o=X[:, m, t0:t0 + n], i=ps[:, 0:n]:
                 e.tensor_tensor(out=o, in0=i, in1=o, op=ALU.add), r=[pk, ("X", ti)], w=[("X", ti)])
        proj(cx, d_wo, 8, 0, D, XN, "XN", tiles, cons_add, group_cols=384)

        def run_ffn(li, ncol):
            cx.new_scope()
            cx.ffn_dw, cx.ffn_b, cx.ffn_state, cx.ffn_out = {}, {}, {}, {}
            t = cx.alloc("ffdw", [128, NFF, 3], F32)
            cx.load_s(t, d_fdw[li], "const_s", "const_s")
            cx.ffn_dw[li] = t
            t = cx.alloc("ffb", [128, NFF], F32)
            cx.load_s(t, d_fb[li], "const_s", "const_s")
            cx.ffn_b[li] = t
            t = cx.alloc("ffst", [128, NFF, ns, 2], F32)
            cx.load_s(t, d_fst[li], ("stf", li), "const_s")
            cx.ffn_state[li] = t
            cx.ffn_out[li] = cx.alloc("ffo", [128, NFF, 2 + 2 * ns], F32)
            cx.gt = [cx.alloc("gt%d" % i, [128, 514], F32) for i in range(2)]
            cx.gt_i = 0
            cx.Gs = [cx.alloc("Gs%d" % i, [128, ns, 10], F32) for i in range(2)]
            cx.facc = [cx.alloc("facc%d" % i, [128, 512], F32) for i in range(2)]
            cx.fsil = [cx.alloc("fsil%d" % i, [128, 512], F32) for i in range(2)]
            cx.Hb = cx.alloc("Hb", [128, 3, T], BF16)
            rmsnorm(cx, X, vec[:, ncol:ncol + 8], tiles, xn_out(cx), "nf%d" % li)
            conv_ffn(cx, li, {"gate": d_wg[li], "up": d_wu[li], "down": d_wd[li]}, tiles, NP, ns, HC)
            cx.store_s(o_ffn[li], cx.ffn_out[li], ("ffo", li), "outs")

        run_ffn(0, 0)

        cx.new_scope()
        Rall = cx.alloc("Rall", [128, T], F32)
        HN = cx.alloc("HN", [128, 15 + NP], F32)
        P0 = cx.alloc("P0", [128, 15 + NP], F32)
        P1 = cx.alloc("P1", [128, 15 + NP], F32)
        HS = cx.alloc("HS", [128, ns, 23], F32)
        Q0 = cx.alloc("Q0", [128, ns, 23], F32)
        Q1 = cx.alloc("Q1", [128, ns, 23], F32)
        invc = cx.alloc("invc", [128, 4, 16], F32)
        ptm = cx.alloc("ptm", [128, 16], F32)
        cx.load_s(invc, d_invc, "invc", "const_s")
        for ti, (t0, n) in enumerate(tiles):
            ps, pk = cx.psum()
            for kc in range(8):
                sqi = cx.sq_next()
                p.op("act", lambda e, o=cx.scr_sq[sqi][:, 0:n], i=X[:, kc, t0:t0 + n]:
                     e.activation(out=o, in_=i, func=AF.Square), r=[("X", ti)], w=[("sq", sqi)])
                p.op("pe", lambda e, o=ps[:, 0:n], r_=cx.scr_sq[sqi][:, 0:n], s=(kc == 0), t=(kc == 7):
                     e.matmul(o, lhsT=cx.ones[:], rhs=r_, start=s, stop=t), r=[("sq", sqi), "ones"], w=[pk])
            p.op("act", lambda e, o=Rall[:, t0:t0 + n], i=ps[:, 0:n]:
                 e.activation(out=o, in_=i, func=AF.Sqrt, bias=cx.epsc[:, 0:1], scale=1.0 / D), r=[pk, "epsc"], w=["Rall"])
            p.op("dve", lambda e, o=Rall[:, t0:t0 + n]: e.reciprocal(out=o, in_=o), r=["Rall"], w=["Rall"])
        p.op("pool", lambda e: e.memset(HN[:, 0:15], 0.0), w=["HN"])
        allXN = [("XN", ti) for ti in range(nt)]
        allX = [("X", ti) for ti in range(nt)]
        for c in range(8):
            win = POOL_WIN[c]
            gcolc = vec[:, 8 + c:9 + c]
            p.op("dve", lambda e, o=HN[:, 15:15 + NP], i=X[:, c, 0:NP], g=gcolc, r_=Rall[:, 0:NP]:
                 e.scalar_tensor_tensor(out=o, in0=i, scalar=g, in1=r_, op0=ALU.mult, op1=ALU.mult),
                 r=allX + ["Rall", "const"], w=["HN"])
            p.op("dve", lambda e, o=HN[:, 15:15 + HC]:
                 e.tensor_scalar(out=o, in0=o, scalar1=cx.hmask[:, 0:1], scalar2=None, op0=ALU.mult), r=["HN", "const"], w=["HN"])
            p.op("dve", lambda e, o=HS[:, :, 15:23], i=X[:, c, NP:T].rearrange("p (s t) -> p s t", t=8), g=gcolc,
                 r_=Rall[:, NP:T].rearrange("p (s t) -> p s t", t=8):
                 e.scalar_tensor_tensor(out=o, in0=i, scalar=g, in1=r_, op0=ALU.mult, op1=ALU.mult),
                 r=allX + ["Rall", "const"], w=["HS"])
            p.op("sp", lambda e, o=HS[:, :, 0:15], i=d_stp[:, c, :, :]: e.dma_start(out=o, in_=i), r=["scr"], w=["HS"], dsem="stp")
            cx.store_s(o_pool[:, c, 0:15], HN[:, NP:NP + 15], "HN", "pout")
            cx.store_s(o_pool[:, c, 15:15 + ns * 15].rearrange("p (s t) -> p s t", t=15), HS[:, :, 8:23], "HS", "pout")
            src_p, src_s = HN, HS
            bufs_p, bufs_s = [P0, P1], [Q0, Q1]
            k = 1
            bi = 0
            while k < win:
                dp, ds_ = bufs_p[bi], bufs_s[bi]
                lo = 2 * k - 1
                p.op("dve", lambda e, o=dp[:, lo:15 + NP], a=src_p[:, lo:15 + NP], b=src_p[:, lo - k:15 + NP - k]:
                     e.tensor_tensor(out=o, in0=a, in1=b, op=ALU.add), r=["HN", "P0", "P1"], w=["P%d" % bi])
                p.op("dve", lambda e, o=ds_[:, :, lo:23], a=src_s[:, :, lo:23], b=src_s[:, :, lo - k:23 - k]:
                     e.tensor_tensor(out=o, in0=a, in1=b, op=ALU.add), r=["HS", "Q0", "Q1"], w=["Q%d" % bi])
                src_p, src_s = dp, ds_
                k *= 2
                bi ^= 1
            widx = {2: 0, 4: 1, 8: 2, 16: 3}[win]
            p.op("dve", lambda e, o=XN[:, c, 0:NP], a=src_p[:, 15:15 + NP], h=HN[:, 15:15 + NP], iw=1.0 / win:
                 e.scalar_tensor_tensor(out=o, in0=a, scalar=iw, in1=h, op0=ALU.mult, op1=ALU.subtract),
                 r=["HN", "P0", "P1"], w=allXN)
            p.op("dve", lambda e, a=src_p[:, 15 + HC:15 + HC + 16], iv=invc[:, widx, :]:
                 e.tensor_tensor(out=ptm[:, :], in0=a, in1=iv, op=ALU.mult), r=["P0", "P1", "invc"], w=["ptm"])
            p.op("dve", lambda e, o=XN[:, c, HC:HC + 16], h=HN[:, 15 + HC:15 + HC + 16]:
                 e.tensor_tensor(out=o, in0=ptm[:, :], in1=h, op=ALU.subtract), r=["ptm", "HN"], w=allXN)
            p.op("dve", lambda e, o=XN[:, c, NP:T].rearrange("p (s t) -> p s t", t=8), a=src_s[:, :, 15:23], h=HS[:, :, 15:23], iw=1.0 / win:
                 e.scalar_tensor_tensor(out=o, in0=a, scalar=iw, in1=h, op0=ALU.mult, op1=ALU.subtract),
                 r=["HS", "Q0", "Q1"], w=allXN)
        wv, wk = cx.wload(d_plw, 8, 256)
        for m in range(8):
            g = m // 2
            for ti, (t0, n) in enumerate(tiles):
                ps, pk = cx.psum()
                for kk in range(2):
                    p.op("pe", lambda e, o=ps[:, 0:n], l=wv[:, g * 2 + kk, (m % 2) * 128:(m % 2 + 1) * 128],
                         r_=XN[:, g * 2 + kk, t0:t0 + n], s=(kk == 0), t=(kk == 1):
                         e.matmul(o, lhsT=l, rhs=r_, start=s, stop=t), r=[wk, ("XN", ti)], w=[pk])
                p.op("dve", lambda e, o=X[:, m, t0:t0 + n], i=ps[:, 0:n], sc=vec[:, 16 + m:17 + m]:
                     e.scalar_tensor_tensor(out=o, in0=i, scalar=sc, in1=o, op0=ALU.mult, op1=ALU.add),
                     r=[pk, ("X", ti), "const"], w=[("X", ti)])
        if debug == "pool":
            for ti, (t0, n) in enumerate(tiles):
                for kc in range(8):
                    cx.store(o_y[kc * 128:(kc + 1) * 128, t0:t0 + n], X[:, kc, t0:t0 + n], ("X", ti), "outs")
            cx.p.emit_all(stack)
            return nc
        run_ffn(1, 24)

        cx.new_scope()
        rmsnorm(cx, X, vec[:, 32:40], tiles, xn_out(cx), "n3")
        NBK = NP // 128
        blocks = [(i * 128, 128) for i in range(NBK)] + [(NP, NSB)]
        nb = len(blocks)
        U = cx.alloc("U", [128, 4, T], BF16)
        rows = cx.alloc("rows", [128, 3, 512], F32)
        wst = cx.alloc("wst", [128, 2, 4, 128], F32)
        wsb = cx.alloc("wsb", [128, 2, 4, 128], BF16)
        wsm = cx.alloc("wsm", [128, 2, 128], F32)
        bsr = cx.alloc("bsr", [128, 2, 4, 128], F32)
        stat = cx.alloc("stat", [128, nb, 4, 2], F32)
        mur = cx.alloc("mur", [128, nb, 2], F32)
        xv = [cx.alloc("xv%d" % i, [128, 512], F32) for i in range(2)]
        zvs = [cx.alloc("zv%d" % i, [128, 512], F32) for i in range(2)]
        zqs = [cx.alloc("zq%d" % i, [128, 512], F32) for i in range(2)]
        vnb = [cx.alloc("vnb%d" % i, [128, 512], BF16) for i in range(2)]
        ssts = [cx.alloc("sst%d" % i, [128, 128], F32) for i in range(4)]
        for i in range(2):
            cx.load_s(wst[:, i], d_wst[i], "wst", "const_s")
            cx.load_s(wsm[:, i], d_wsm[i], "wsm", "const_s")
            cx.load_s(bsr[:, i], d_bs[i], "bsr", "const_s")
        for i in range(2):
            for g in range(4):
                p.op("dve", lambda e, o=wsb[:, i, g, :], a=wst[:, i, g, :], b=wsm[:, i, :]:
                     e.tensor_tensor(out=o, in0=a, in1=b, op=ALU.mult), r=["wst", "wsm"], w=["wsb"])
        for g in range(4):
            halves = []
            for hh in range(2):
                halves.append(cx.wload(d_sgin[:, 2 * D + g * 512 + hh * 256:2 * D + g * 512 + (hh + 1) * 256], 8, 256))
            p.op("sp", lambda e, o=rows[:, 0, :], i=d_sgrow[0][:, g * 512:(g + 1) * 512]: e.dma_start(out=o, in_=i),
                 r=["scr"], w=["rows"], dsem="rows")
            for bi_, (t0, n) in enumerate(blocks):
                ti = min(t0 // 512, nt - 1) if t0 < NP else nt - 1
                ps, pk = cx.psum()
                for hh in range(2):
                    wv_, wvk = halves[hh]
                    for kc in range(8):
                        p.op("pe", lambda e, o=ps[0:n, hh * 256:(hh + 1) * 256], l=XN[:, kc, t0:t0 + n], r_=wv_[:, kc, :],
                             s=(kc == 0), t=(kc == 7): e.matmul(o, lhsT=l, rhs=r_, start=s, stop=t),
                             r=[wvk, ("XN", ti)], w=[pk])
                xb = xv[bi_ % 2]
                p.op("dve", lambda e, o=xb[0:n, :], a=ps[0:n, :], b=rows[0:n, 0, :]: e.tensor_tensor(out=o, in0=a, in1=b, op=ALU.add),
                     r=[pk, "rows"], w=[("xv", bi_ % 2)])
                zv = zvs[bi_ % 2]
                zq = zqs[bi_ % 2]
                zk = ("zv", bi_ % 2)
                qk = ("zq", bi_ % 2)
                gelu_tile(cx, xb[0:n, :], zv[0:n, :], (slice(0, n), slice(0, 512)), [("xv", bi_ % 2)], [zk])
                p.op("dve", lambda e, o=stat[0:n, bi_, g, 0:1], a=zv[0:n, :]: e.reduce_sum(out=o, in_=a, axis=AX.X), r=[zk], w=["stat"])
                p.op("act", lambda e, o=zq[0:n, :], a=zv[0:n, :]: e.activation(out=o, in_=a, func=AF.Square), r=[zk], w=[qk])
                p.op("dve", lambda e, o=stat[0:n, bi_, g, 1:2], a=zq[0:n, :]: e.reduce_sum(out=o, in_=a, axis=AX.X), r=[qk], w=["stat"])
        for bi_, (t0, n) in enumerate(blocks):
            p.op("dve", lambda e, o=mur[0:n, bi_, :], a=stat[0:n, bi_, :, :].rearrange("p g s -> p s g"):
                 e.reduce_sum(out=o, in_=a, axis=AX.X), r=["stat"], w=["mur"])
        p.op("dve", lambda e: e.tensor_scalar(out=mur[:, :, :], in0=mur[:, :, :], scalar1=1.0 / (2 * D), scalar2=None, op0=ALU.mult),
             r=["mur"], w=["mur"])
        msq = cx.alloc("msq", [128, nb, 1], F32)
        p.op("dve", lambda e: e.tensor_tensor(out=msq[:, :, :], in0=mur[:, :, 0:1], in1=mur[:, :, 0:1], op=ALU.mult), r=["mur"], w=["msq"])
        p.op("dve", lambda e: e.tensor_tensor(out=mur[:, :, 1:2], in0=mur[:, :, 1:2], in1=msq[:, :, :], op=ALU.subtract),
             r=["mur", "msq"], w=["mur"])
        p.op("act", lambda e: e.activation(out=mur[:, :, 1:2], in_=mur[:, :, 1:2], func=AF.Sqrt, bias=cx.epsc[:, 0:1], scale=1.0),
             r=["mur", "epsc"], w=["mur"])
        p.op("dve", lambda e: e.reciprocal(out=mur[:, :, 1:2], in_=mur[:, :, 1:2]), r=["mur"], w=["mur"])
        for g in range(4):
            def cons_u(m, ti, t0, n, ps, pk, g=g):
                p.op("act", lambda e, o=U[:, m, t0:t0 + n], i=ps[:, 0:n], b=vec[:, 56 + g * 4 + m:57 + g * 4 + m]:
                     e.activation(out=o, in_=i, func=AF.Gelu_apprx_tanh, bias=b), r=[pk, "const"], w=[("U", ti)])
            proj(cx, d_sgin, 8, g * 512, 512, XN, "XN", tiles, cons_u, group_cols=256)
            halves = []
            for hh in range(2):
                halves.append(cx.wload(d_sgin[:, 2 * D + g * 512 + hh * 256:2 * D + g * 512 + (hh + 1) * 256], 8, 256))
            for r_i in range(3):
                p.op("sp", lambda e, o=rows[:, r_i, :], i=d_sgrow[r_i][:, g * 512:(g + 1) * 512]: e.dma_start(out=o, in_=i),
                     r=["scr"], w=["rows"], dsem="rows")
            for bi_, (t0, n) in enumerate(blocks):
                ti = min(t0 // 512, nt - 1) if t0 < NP else nt - 1
                samp = t0 >= NP
                ps, pk = cx.psum()
                for hh in range(2):
                    wv_, wvk = halves[hh]
                    for kc in range(8):
                        p.op("pe", lambda e, o=ps[0:n, hh * 256:(hh + 1) * 256], l=XN[:, kc, t0:t0 + n], r_=wv_[:, kc, :],
                             s=(kc == 0), t=(kc == 7): e.matmul(o, lhsT=l, rhs=r_, start=s, stop=t),
                             r=[wvk, ("XN", ti)], w=[pk])
                xb = xv[bi_ % 2]
                p.op("dve", lambda e, o=xb[0:n, :], a=ps[0:n, :], b=rows[0:n, 0, :]: e.tensor_tensor(out=o, in0=a, in1=b, op=ALU.add),
                     r=[pk, "rows"], w=[("xv", bi_ % 2)])
                zv = zvs[bi_ % 2]
                zq = zqs[bi_ % 2]
                zk = ("zv", bi_ % 2)
                qk = ("zq", bi_ % 2)
                gelu_tile(cx, xb[0:n, :], zv[0:n, :], (slice(0, n), slice(0, 512)), [("xv", bi_ % 2)], [zk])
                p.op("dve", lambda e, o=zv[0:n, :], mu_=mur[0:n, bi_, 0:1], rs_=mur[0:n, bi_, 1:2]:
                     e.tensor_scalar(out=o, in0=o, scalar1=mu_, scalar2=rs_, op0=ALU.subtract, op1=ALU.mult), r=[zk, "mur"], w=[zk])
                p.op("dve", lambda e, o=zv[0:n, :], a=rows[0:n, 1, :]: e.tensor_tensor(out=o, in0=o, in1=a, op=ALU.mult), r=[zk, "rows"], w=[zk])
                if samp:
                    p.op("dve", lambda e, o=zq[0:n, :], a=zv[0:n, :], b=rows[0:n, 2, :]: e.tensor_tensor(out=o, in0=a, in1=b, op=ALU.add),
                         r=[zk, "rows"], w=[qk])
                    cx.store_s(o_sgv[:, g * 512:(g + 1) * 512], zq[0:n, :], qk, "sgv")
                vb = vnb[bi_ % 2]
                p.op("dve", lambda e, o=vb[0:n, :], a=zv[0:n, :], b=rows[0:n, 2, :]: e.tensor_tensor(out=o, in0=a, in1=b, op=ALU.add),
                     r=[zk, "rows"], w=[("vnb", bi_ % 2)])
                wi = 1 if samp else 0
                for cc in range(4):
                    ps2, pk2 = cx.psum()
                    p.op("pe", lambda e, o=ps2[:, 0:n], l=vb[0:n, cc * 128:(cc + 1) * 128], r_=wsb[0:n, wi, g, 0:n]:
                         e.matmul(o, lhsT=l, rhs=r_, start=True, stop=True), r=[("vnb", bi_ % 2), "wsb"], w=[pk2])
                    sst = ssts[cc]
                    p.op("dve", lambda e, o=sst[:, 0:n], a=ps2[:, 0:n], b=bsr[:, wi, g, 0:n]: e.tensor_tensor(out=o, in0=a, in1=b, op=ALU.add),
                         r=[pk2, "bsr"], w=[("sst", cc)])
                    p.op("dve", lambda e, o=U[:, cc, t0:t0 + n], a=sst[:, 0:n]: e.tensor_tensor(out=o, in0=o, in1=a, op=ALU.mult),
                         r=[("sst", cc), ("U", ti)], w=[("U", ti)])
            for half in range(2):
                if half == 1:
                    wo_, wok = cx.wload(d_sgout[g * 512:(g + 1) * 512, 512:1024], 4, 512)
                else:
                    wo_, wok = cx.wload(d_sgout[g * 512:(g + 1) * 512, 0:512], 4, 512)
                for mm in range(4):
                    m = half * 4 + mm
                    for ti, (t0, n) in enumerate(tiles):
                        ps, pk = cx.psum()
                        for kk in range(4):
                            p.op("pe", lambda e, o=ps[:, 0:n], l=wo_[:, kk, mm * 128:(mm + 1) * 128], r_=U[:, kk, t0:t0 + n],
                                 s=(kk == 0), t=(kk == 3): e.matmul(o, lhsT=l, rhs=r_, start=s, stop=t), r=[wok, ("U", ti)], w=[pk])
                        p.op("dve", lambda e, o=X[:, m, t0:t0 + n], i=ps[:, 0:n]:
                             e.tensor_tensor(out=o, in0=i, in1=o, op=ALU.add), r=[pk, ("X", ti)], w=[("X", ti)])
        run_ffn(2, 40)
        cx.new_scope()
        yst = [cx.alloc("yst%d" % i, [128, 512], F32) for i in range(4)]
        ycnt = [0]

        def y_out(kc, ti, t0, n):
            s_ = ycnt[0] % 4
            ycnt[0] += 1
            y_out.last = (s_, kc, t0, n)
            return yst[s_][:, 0:n], ("yst", s_)
        p_op_orig = p.op

        def hooked(eng, emit, r=(), w=(), dsem=None):
            ins = p_op_orig(eng, emit, r=r, w=w, dsem=dsem)
            if eng == "dve" and len(w) == 1 and isinstance(w[0], tuple) and w[0][0] == "yst":
                s_, kc, t0, n = y_out.last
                p_op_orig("sp", lambda e, o=o_y[kc * 128:(kc + 1) * 128, t0:t0 + n], i=yst[s_][:, 0:n]: e.dma_start(out=o, in_=i),
                          r=[("yst", s_), "scr"], dsem=("yst", s_))
            return ins
        p.op = hooked
        rmsnorm(cx, X, vec[:, 48:56], tiles, y_out, "nfin")
        p.op = p_op_orig
        cx.p.emit_all(stack)
    return nc


def run_C(inp, x1p, x1s, op, os_, own, ns, n_cores, seq_of_core, nc_cache={}, debug=None):
    key = (own, ns, debug)
    if key not in nc_cache:
        nc_cache[key] = build_C(own, ns, debug)
    nc = nc_cache[key]
    f = lambda a: np.asarray(a, np.float32)
    vec = np.concatenate([
        lay_cols(inp["norm_ffn"][1]), lay_cols(inp["norm_mix"][2]), lay_cols(inp["pl_scale"]),
        lay_cols(inp["norm_ffn"][2]), lay_cols(inp["norm_mix"][3]), lay_cols(inp["norm_ffn"][3]),
        lay_cols(inp["norm_final"]), lay_cols(inp["sg_b_in"])], axis=1)
    fdw = np.ascontiguousarray(f(inp["ff_w_dw"])[1:4].transpose(0, 2, 1).reshape(3, NFF, 128, 3).transpose(0, 2, 1, 3))
    fb = np.stack([lay_cols(inp["ff_b_dw"][i]) for i in (1, 2, 3)])
    ws = f(inp["sg_w_s"])
    wst_p = np.ascontiguousarray(ws.transpose(2, 0, 1))
    wst_s = np.zeros((128, 4, 128), np.float32)
    msk_p = (np.arange(128)[:, None] <= np.arange(128)[None, :]).astype(np.float32)
    msk_s = np.zeros((128, 128), np.float32)
    for b in range(16):
        wst_s[b * 8:(b + 1) * 8, :, b * 8:(b + 1) * 8] = ws[:, :8, :8].transpose(2, 0, 1)
        msk_s[b * 8:(b + 1) * 8, b * 8:(b + 1) * 8] = msk_p[:8, :8]
    bs = f(inp["sg_b_s"])
    bs_p = np.broadcast_to(bs[None], (128, 4, 128))
    bs_s = np.broadcast_to(np.tile(bs[:, :8], (1, 16))[None], (128, 4, 128))
    sgrow = np.stack([np.broadcast_to(f(inp["sg_b_in"])[2 * D:][None], (128, 2 * D)),
                      np.broadcast_to(f(inp["sg_ln_g"])[None], (128, 2 * D)),
                      np.broadcast_to(f(inp["sg_ln_b"])[None], (128, 2 * D))]).astype(np.float32)
    in_maps = []
    for c in range(n_cores):
        b, h = seq_of_core(c)

        def seg(a, a_s):
            own_ = a[b, h * own:(h + 1) * own]
            halo = np.zeros((HC, D), np.float32) if h == 0 else a[b, h * own - HC:h * own]
            return np.ascontiguousarray(np.concatenate([halo, own_, a_s[c * ns:(c + 1) * ns].reshape(ns * 8, D)], 0).T)
        stp = f(inp["state_pool"][c * ns:(c + 1) * ns])
        stp = np.ascontiguousarray(stp.transpose(2, 0, 1).reshape(8, 128, ns, 15).transpose(1, 0, 2, 3))
        stf = f(inp["state_ffn"])[1:4, c * ns:(c + 1) * ns]
        stf = np.ascontiguousarray(stf.transpose(0, 3, 1, 2).reshape(3, NFF, 128, ns, 2).transpose(0, 2, 1, 3, 4))
        pos = h * own + np.arange(16)
        invc = np.stack([1.0 / np.minimum(w, pos + 1) for w in (2, 4, 8, 16)]).astype(np.float32)
        in_maps.append({
            "x1T": seg(x1p, x1s), "oT": seg(op, os_), "hmask": np.full((128, 1), float(h), np.float32),
            "vecC": vec, "w_o": f(inp["da_w_o"]), "pl_w": f(inp["pl_w"]).reshape(D, 256),
            "stpool": stp, "invc": np.ascontiguousarray(np.broadcast_to(invc[None], (128, 4, 16))),
            "ff_dw": fdw, "ff_b": fb, "ff_st": stf,
            "ff_w_gate": f(inp["ff_w_gate"])[1:4], "ff_w_up": f(inp["ff_w_up"])[1:4], "ff_w_down": f(inp["ff_w_down"])[1:4],
            "sg_w_in": f(inp["sg_w_in"]), "sg_w_out": f(inp["sg_w_out"]), "sg_rows": sgrow,
            "sg_wsT": np.stack([wst_p, wst_s]), "sg_mask": np.stack([msk_p, msk_s]),
            "sg_bs": np.ascontiguousarray(np.stack([bs_p, bs_s])),
        })
    res = run_bass_kernel_spmd(nc, in_maps, core_ids=list(range(n_cores)))
    return res.results


_B_CACHE = {}


def run_B(inp, q_p, k_p, v_p, q_s, k_s, v_s, n_heads=8):
    f = lambda a: np.asarray(a, np.float32)
    nseq, S, _ = q_p.shape
    nss = q_s.shape[0]
    ck_all = f(inp["cache_k"])
    cv_all = f(inp["cache_v"])
    n_phys = ck_all.shape[0]
    ptab = np.asarray(inp["page_table"], np.int32)
    npg = ptab.shape[1]
    key = (nseq, S, nss, npg, n_phys)
    if key not in _B_CACHE:
        _B_CACHE[key] = build_B(nseq, S, nss, npg, n_phys)
    nc = _B_CACHE[key]
    masks, smask = make_masks()
    lamp = np.concatenate([f(inp["da_lq1"]), f(inp["da_lk1"]), f(inp["da_lq2"]), f(inp["da_lk2"])]).reshape(1, 256)
    assert npg == 16
    ptabr = np.ascontiguousarray(np.repeat(ptab.T, 8, axis=0))
    iota = (np.arange(128) % 8).astype(np.float32).reshape(128, 1)
    ident = np.eye(128, dtype=np.float32)
    ng = f(inp["da_norm_g"])
    in_maps = []
    for c in range(n_heads):
        hs = slice(c * 128, (c + 1) * 128)
        in_maps.append({
            "qT": np.ascontiguousarray(q_p[:, :, hs].transpose(0, 2, 1)),
            "kT": np.ascontiguousarray(k_p[:, :, hs].transpose(0, 2, 1)),
            "v": np.ascontiguousarray(v_p[:, :, hs]),
            "qsT": np.ascontiguousarray(q_s.reshape(nss * 8, -1)[:, hs].T),
            "ksT": np.ascontiguousarray(k_s.reshape(nss * 8, -1)[:, hs].T),
            "vs": np.ascontiguousarray(v_s.reshape(nss * 8, -1)[:, hs]),
            "ck": np.ascontiguousarray(ck_all[:, :, c, :]),
            "cv": np.ascontiguousarray(cv_all[:, :, c, :]),
            "ptabr": ptabr, "iota": iota, "ident": ident, "lamp": lamp,
            "gcol": np.ascontiguousarray(ng[hs].reshape(128, 1)),
            "grow": np.ascontiguousarray(np.broadcast_to(ng[hs].reshape(1, 128), (8, 128))),
            "masks": masks, "smask": smask,
        })
    res = run_bass_kernel_spmd(nc, in_maps, core_ids=list(range(n_heads))).results
    op = np.empty((nseq, S, n_heads * 128), np.float32)
    os_ = np.empty((nss, 8, n_heads * 128), np.float32)
    for c in range(n_heads):
        hs = slice(c * 128, (c + 1) * 128)
        op[:, :, hs] = res[c]["oT"].transpose(0, 2, 1)
        os_[:, :, hs] = res[c]["os"].reshape(nss, 8, 128)
    return op, os_


def kernel(**inp):
    f = lambda a: np.asarray(a, np.float32)
    xp = f(inp["x_prompt"])
    xs = f(inp["x_sample"])
    Bn, S, _ = xp.shape
    NSS = xs.shape[0]
    n_cores = 8
    own = S * Bn // n_cores
    ns = NSS // n_cores
    halves = S // own

    def seq_of(c):
        return (c // halves, c % halves)

    rA = run_A(inp, own, ns, n_cores, seq_of)
    x1p = np.empty((Bn, S, D), np.float32)
    x1s = np.empty((NSS, 8, D), np.float32)
    qkv_p = np.empty((Bn, S, 3 * D), np.float32)
    qkv_s = np.empty((NSS, 8, 3 * D), np.float32)
    conv_p = np.empty((Bn, 30, D), np.float32)
    conv_s = np.empty((NSS, 30, D), np.float32)
    ffn_p = np.empty((4, Bn, 2, DFF), np.float32)
    ffn_s = np.empty((4, NSS, 2, DFF), np.float32)

    def put_ffn(li, c, b, h, ff):
        if h == halves - 1:
            ffn_p[li, b] = ff[:, :, :2].transpose(2, 1, 0).reshape(2, DFF)
        ffn_s[li, c * ns:(c + 1) * ns] = ff[:, :, 2:].reshape(128, NFF, ns, 2).transpose(2, 3, 1, 0).reshape(ns, 2, DFF)

    for c in range(n_cores):
        b, h = seq_of(c)
        r = rA[c]
        x1 = r["x1T"].T
        x1p[b, h * own:(h + 1) * own] = x1[HA:HA + own]
        x1s[c * ns:(c + 1) * ns] = x1[HA + own:].reshape(ns, 8, D)
        q = r["qkvT"].T
        qkv_p[b, h * own:(h + 1) * own] = q[HA:HA + own]
        qkv_s[c * ns:(c + 1) * ns] = q[HA + own:].reshape(ns, 8, 3 * D)
        cv = r["convT"]
        if h == halves - 1:
            conv_p[b] = cv[:, :, :30].transpose(2, 1, 0).reshape(30, D)
        conv_s[c * ns:(c + 1) * ns] = cv[:, :, 30:].reshape(128, 8, ns, 30).transpose(2, 3, 1, 0).reshape(ns, 30, D)
        put_ffn(0, c, b, h, r["ffnT"])
    del rA
    k_rows_p = np.ascontiguousarray(qkv_p[:, :, D:2 * D]).reshape(Bn, S, 8, 128)
    v_rows_p = np.ascontiguousarray(qkv_p[:, :, 2 * D:]).reshape(Bn, S, 8, 128)
    k_rows_s = np.ascontiguousarray(qkv_s[:, :, D:2 * D]).reshape(NSS, 8, 8, 128)
    v_rows_s = np.ascontiguousarray(qkv_s[:, :, 2 * D:]).reshape(NSS, 8, 8, 128)

    op, os_ = run_B(inp, qkv_p[:, :, :D], qkv_p[:, :, D:2 * D], qkv_p[:, :, 2 * D:],
                    qkv_s[:, :, :D], qkv_s[:, :, D:2 * D], qkv_s[:, :, 2 * D:])

    rC = run_C(inp, x1p, x1s, op, os_, own, ns, n_cores, seq_of)
    y_p = np.empty((Bn, S, D), np.float32)
    y_s = np.empty((NSS, 8, D), np.float32)
    pool_p = np.empty((Bn, 15, D), np.float32)
    pool_s = np.empty((NSS, 15, D), np.float32)
    sgv = np.empty((NSS, 8, 2 * D), np.float32)
    for c in range(n_cores):
        b, h = seq_of(c)
        r = rC[c]
        y = r["yT"].T
        y_p[b, h * own:(h + 1) * own] = y[HC:HC + own]
        y_s[c * ns:(c + 1) * ns] = y[HC + own:].reshape(ns, 8, D)
        pl = r["poolT"]
        if h == halves - 1:
            pool_p[b] = pl[:, :, :15].transpose(2, 1, 0).reshape(15, D)
        pool_s[c * ns:(c + 1) * ns] = pl[:, :, 15:].reshape(128, 8, ns, 15).transpose(2, 3, 1, 0).reshape(ns, 15, D)
        sgv[c * ns:(c + 1) * ns] = r["sgv"].reshape(ns, 8, 2 * D)
        for li in range(3):
            put_ffn(li + 1, c, b, h, r["ffnT"][li])
    return (y_p, y_s, conv_p, conv_s, k_rows_p, v_rows_p, k_rows_s, v_rows_s, pool_p, pool_s, sgv, ffn_p, ffn_s)
```

```python
import math
from contextlib import ExitStack

import numpy as np
import concourse.bass as bass
import concourse.mybir as mybir
from concourse.bass_utils import run_bass_kernel_spmd

F32 = mybir.dt.float32
BF16 = mybir.dt.bfloat16
I32 = mybir.dt.int32
AF = mybir.ActivationFunctionType
ALU = mybir.AluOpType
AX = mybir.AxisListType

D = 1024
DFF = 2816
NFF = 22
EPS = 1e-6
ENGS = ("pe", "act", "dve", "pool", "sp")


class Ins:
    __slots__ = ("eng", "emit", "deps", "dsem", "dwaits", "signal", "cnt")

    def __init__(self, eng, emit, dsem):
        self.eng = eng
        self.emit = emit
        self.dsem = dsem
        self.deps = []
        self.dwaits = {}
        self.signal = False
        self.cnt = 0


class Prog:
    def __init__(self, nc):
        self.nc = nc
        self.st = {e: [] for e in ENGS}
        self.lastw = {}
        self.rd = {}
        self.dcnt = {}

    def op(self, eng, emit, r=(), w=(), dsem=None):
        ins = Ins(eng, emit, dsem)
        deps = {}

        def need(d, kind):
            if d is None or d is ins:
                return
            if d.dsem is None and d.eng == eng and (eng == "pe" or kind == "WAR"):
                return
            deps[id(d)] = d

        for k in r:
            need(self.lastw.get(k), "RAW")
        for k in w:
            need(self.lastw.get(k), "WAW")
            for d in self.rd.get(k, {}).values():
                need(d, "WAR")
        ins.deps = list(deps.values())
        for d in ins.deps:
            if d.dsem is not None:
                ins.dwaits[d.dsem] = self.dcnt[d.dsem]
        rkey = eng if dsem is None else ("d", dsem)
        for k in r:
            self.rd.setdefault(k, {})[rkey] = ins
        for k in w:
            self.lastw[k] = ins
            self.rd[k] = {}
        if dsem is not None:
            self.dcnt[dsem] = self.dcnt.get(dsem, 0) + 16
        self.st[eng].append(ins)
        return ins

    def emit_all(self, stack):
        nc = self.nc
        for e in ENGS:
            for ins in self.st[e]:
                for d in ins.deps:
                    if d.dsem is None:
                        d.signal = True
        for e in ENGS:
            c = 0
            for ins in self.st[e]:
                if ins.dsem is None and ins.signal:
                    c += 1
                ins.cnt = c
        esem = {e: stack.enter_context(nc.semaphore("e_" + e)) for e in ENGS if e != "sp"}
        dsem = {k: stack.enter_context(nc.semaphore("d_%s" % str(k))) for k in self.dcnt}
        block = stack.enter_context(nc.Block())
        final = dict(self.dcnt)

        def run(e, eng):
            waited = {}
            for ins in self.st[e]:
                waits = {}
                for d in ins.deps:
                    if d.dsem is None:
                        key, val = ("e", d.eng), d.cnt
                    else:
                        key, val = ("d", d.dsem), ins.dwaits[d.dsem]
                    if waits.get(key, 0) < val:
                        waits[key] = val
                for key, val in waits.items():
                    if waited.get(key, 0) >= val:
                        continue
                    eng.wait_ge(esem[key[1]] if key[0] == "e" else dsem[key[1]], val)
                    waited[key] = val
                bi = ins.emit(eng)
                if ins.dsem is not None:
                    bi.then_inc(dsem[ins.dsem], 16)
                elif ins.signal:
                    bi.then_inc(esem[e], 1)
            if e == "sp":
                for k, v in final.items():
                    eng.wait_ge(dsem[k], v)

        block.tensor(lambda eng: run("pe", eng))
        block.scalar(lambda eng: run("act", eng))
        block.vector(lambda eng: run("dve", eng))
        block.gpsimd(lambda eng: run("pool", eng))
        block.sync(lambda eng: run("sp", eng))


def split_tiles(n, maxn=512):
    out = []
    t = 0
    while t < n:
        m = min(maxn, n - t)
        out.append((t, m))
        t += m
    return out


class Ctx:
    def __init__(self, nc, stack, n_wslots=4, wslot_elems=4096):
        self.nc = nc
        self.stack = stack
        self.p = Prog(nc)
        self.ps = [stack.enter_context(nc.psum_tensor("ps%d" % i, [128, 512], F32)) for i in range(8)]
        self.ps_i = 0
        self.wslots = [stack.enter_context(nc.sbuf_tensor("wslot%d" % i, [128, wslot_elems], BF16))
                       for i in range(n_wslots)]
        self.w_i = 0
        self.ones = self.sb("ones_bf", [128, 128], BF16)
        self.p.op("pool", lambda e: e.memset(self.ones[:], 1.0), w=["ones"])
        self.epsc = self.sb("epsc", [128, 1], F32)
        self.p.op("pool", lambda e: e.memset(self.epsc[:], EPS), w=["epsc"])
        self.uid = 0
        self.scr = None

    def sq_next(self):
        self.sq_i = (getattr(self, "sq_i", -1) + 1) % len(self.scr_sq)
        return self.sq_i

    def sb(self, name, shape, dt):
        return self.stack.enter_context(self.nc.sbuf_tensor("sb_" + name, shape, dt))

    def alloc(self, name, shape, dt):
        if self.scr is None:
            return self.sb(name, shape, dt)
        n = 1
        for d_ in shape[1:]:
            n *= d_
        nwords = (n * (4 if dt == F32 or dt == I32 else 2) + 3) // 4
        nwords = (nwords + 7) // 8 * 8
        assert self.scr_off + nwords <= self.scr_words, ("scratch overflow", name, self.scr_off, nwords, self.scr_words)
        v = self.scr[:, self.scr_off:self.scr_off + nwords]
        self.scr_off += nwords
        if dt != F32:
            v = v.bitcast(dt)
        v = v[:, 0:n]
        if len(shape) == 3:
            v = v.rearrange("p (a b) -> p a b", a=shape[1])
        elif len(shape) == 4:
            v = v.rearrange("p (a b c) -> p a b c", a=shape[1], b=shape[2])
        return v

    def scratch_init(self, words):
        self.scr = self.sb("scratch", [128, words], F32)
        self.scr_words = words
        self.scr_off = 0
        self.dmy = {e: self.sb("dmy_" + e, [128, 4], F32) for e in ("act", "dve", "pool")}

    def new_scope(self):
        p = self.p
        self.scr_off = 0
        self.nbar = getattr(self, "nbar", 0) + 1
        b = self.nbar
        p.op("act", lambda e: e.activation(out=self.dmy["act"][:, 0:1], in_=self.epsc[:, 0:1], func=AF.Copy),
             r=["epsc"], w=[("bar", b, "act")])
        p.op("dve", lambda e: e.memset(self.dmy["dve"][:, 0:1], 0.0), w=[("bar", b, "dve")])
        p.op("pool", lambda e: e.memset(self.dmy["pool"][:, 0:1], 0.0), w=[("bar", b, "pool")])
        allb = [("bar", b, e_) for e_ in ("act", "dve", "pool")]
        p.op("act", lambda e: e.activation(out=self.dmy["act"][:, 1:2], in_=self.epsc[:, 0:1], func=AF.Copy),
             r=["epsc"] + allb, w=[("bar2", b, "act")])
        p.op("dve", lambda e: e.memset(self.dmy["dve"][:, 1:2], 0.0), r=allb, w=[("bar2", b, "dve")])
        p.op("pool", lambda e: e.memset(self.dmy["pool"][:, 1:2], 0.0), r=allb, w=[("bar2", b, "pool"), "scr"])

    def load_s(self, dst, src, key, dsem, eng="sp"):
        self.p.op(eng, lambda e, o=dst, i=src: e.dma_start(out=o, in_=i), r=["scr"], w=[key], dsem=dsem)

    def store_s(self, dst, src, key, dsem, eng="sp"):
        self.p.op(eng, lambda e, o=dst, i=src: e.dma_start(out=o, in_=i), r=[key, "scr"], dsem=dsem)

    def dram_in(self, name, shape, dt=F32):
        return self.nc.dram_tensor(name, list(shape), dt, kind="ExternalInput").ap()

    def dram_out(self, name, shape, dt=F32):
        return self.nc.dram_tensor(name, list(shape), dt, kind="ExternalOutput").ap()

    def psum(self, exclude=0):
        i = self.ps_i
        self.ps_i = (self.ps_i + 1) % (8 - exclude)
        return self.ps[exclude + i], ("ps", exclude + i)

    def wload(self, src_ap, kc, ncols):
        s = self.w_i
        self.w_i = (self.w_i + 1) % len(self.wslots)
        view = self.wslots[s][:, 0:kc * ncols].rearrange("p (k n) -> p k n", k=kc)
        src = src_ap.rearrange("(k p) n -> p k n", p=128)
        self.p.op("pool", lambda e, o=view, i=src: e.dma_start(out=o, in_=i), w=[("w", s)], dsem=("w", s))
        return view, ("w", s)

    def load(self, dst, src, key, dsem, eng="sp"):
        self.p.op(eng, lambda e, o=dst, i=src: e.dma_start(out=o, in_=i), w=[key], dsem=dsem)

    def store(self, dst, src, key, dsem, eng="sp"):
        self.p.op(eng, lambda e, o=dst, i=src: e.dma_start(out=o, in_=i), r=[key], dsem=dsem)

    def const_cols(self, name, dram_ap, ncols):
        t = self.sb(name, [128, ncols], F32)
        self.load(t[:], dram_ap, name, "const")
        return t


def rmsnorm(cx, X, gcol, tiles, out_fn, tag, nch=8, dim=D):
    p = cx.p
    sq = cx.scr_sq
    R = cx.scr_r
    for ti, (t0, n) in enumerate(tiles):
        par = ti % 2
        ps, pk = cx.psum()
        for kc in range(nch):
            sqi = cx.sq_next()
            p.op("act", lambda e, o=sq[sqi][:, 0:n], i=X[:, kc, t0:t0 + n]:
                 e.activation(out=o, in_=i, func=AF.Square), r=[("X", ti)], w=[("sq", sqi)])
            p.op("pe", lambda e, o=ps[:, 0:n], r_=sq[sqi][:, 0:n], s=(kc == 0), t=(kc == nch - 1):
                 e.matmul(o, lhsT=cx.ones[:], rhs=r_, start=s, stop=t), r=[("sq", sqi), "ones"], w=[pk])
        p.op("act", lambda e, o=R[par][:, 0:n], i=ps[:, 0:n]:
             e.activation(out=o, in_=i, func=AF.Sqrt, bias=cx.epsc[:, 0:1], scale=1.0 / dim),
             r=[pk, "epsc"], w=[("R", par)])
        p.op("dve", lambda e, o=R[par][:, 0:n]: e.reciprocal(out=o, in_=o),
             r=[("R", par)], w=[("R", par)])
        for kc in range(nch):
            o_ap, wk = out_fn(kc, ti, t0, n)
            p.op("dve", lambda e, o=o_ap, i=X[:, kc, t0:t0 + n], g=gcol[:, kc:kc + 1], r_=R[par][:, 0:n]:
                 e.scalar_tensor_tensor(out=o, in0=i, scalar=g, in1=r_, op0=ALU.mult, op1=ALU.mult),
                 r=[("X", ti), ("R", par), gcol_key(gcol)], w=[wk])


_gk = {}


def gcol_key(t):
    return _gk.get(id(t), "const")


def xn_out(cx):
    def f(kc, ti, t0, n):
        return cx.XN[:, kc, t0:t0 + n], ("XN", ti)
    return f


def proj(cx, W, kc_n, col0, ncols_total, src, src_key, tiles, consume, group_cols=None):
    p = cx.p
    if group_cols is None:
        group_cols = max(128, (4096 // kc_n) // 128 * 128)
    c = 0
    while c < ncols_total:
        gc = min(group_cols, ncols_total - c)
        wv, wk = cx.wload(W[:, col0 + c:col0 + c + gc], kc_n, gc)
        for mm in range(gc // 128):
            for ti, (t0, n) in enumerate(tiles):
                ps, pk = cx.psum()
                for kc in range(kc_n):
                    p.op("pe", lambda e, o=ps[:, 0:n], l=wv[:, kc, mm * 128:(mm + 1) * 128],
                         r_=src[:, kc, t0:t0 + n], s=(kc == 0), t=(kc == kc_n - 1):
                         e.matmul(o, lhsT=l, rhs=r_, start=s, stop=t),
                         r=[wk, (src_key, ti)], w=[pk])
                consume(c // 128 + mm, ti, t0, n, ps, pk)
        c += gc


def conv_ffn(cx, li, W, tiles, np_tok, ns, halo, part=3):
    p = cx.p
    X, XN = cx.X, cx.XN
    p_dw = cx.ffn_dw[li]
    p_b = cx.ffn_b[li]
    stf = cx.ffn_state[li]
    outst = cx.ffn_out[li]
    j = 0
    parts = []
    while j < NFF:
        parts.append((j, min(part, NFF - j)))
        j += part
    for (j0, nj) in parts:
        wg, wgk = cx.wload(W["gate"][:, j0 * 128:(j0 + nj) * 128], 8, nj * 128)
        wu, wuk = cx.wload(W["up"][:, j0 * 128:(j0 + nj) * 128], 8, nj * 128)
        wd, wdk = cx.wload(W["down"][j0 * 128:(j0 + nj) * 128, :], nj, D)
        for jj in range(nj):
            jg = j0 + jj
            Gs = cx.Gs[jg % 2]
            p.op("pool", lambda e, o=Gs[:, :, 0:2], i=stf[:, jg, :, :]: e.tensor_copy(out=o, in_=i),
                 r=[("stf", li)], w=[("Gs", jg % 2)])
            prev = None
            for ti, (t0, n) in enumerate(tiles):
                samp = t0 >= np_tok
                psg, pgk = cx.psum()
                for kc in range(8):
                    p.op("pe", lambda e, o=psg[:, 0:n], l=wg[:, kc, jj * 128:(jj + 1) * 128],
                         r_=XN[:, kc, t0:t0 + n], s=(kc == 0), t=(kc == 7):
                         e.matmul(o, lhsT=l, rhs=r_, start=s, stop=t), r=[wgk, ("XN", ti)], w=[pgk])
                psu, puk = cx.psum()
                for kc in range(8):
                    p.op("pe", lambda e, o=psu[:, 0:n], l=wu[:, kc, jj * 128:(jj + 1) * 128],
                         r_=XN[:, kc, t0:t0 + n], s=(kc == 0), t=(kc == 7):
                         e.matmul(o, lhsT=l, rhs=r_, start=s, stop=t), r=[wuk, ("XN", ti)], w=[puk])
                cx.gt_i = (cx.gt_i + 1) % 2
                gp = cx.gt_i
                acc = cx.facc[gp]
                sil = cx.fsil[gp]
                if not samp:
                    Gt = cx.gt[gp]
                    gk = ("gt", gp)
                    if prev is None:
                        p.op("pool", lambda e, o=Gt[:, 0:2]: e.memset(o, 0.0), w=[gk])
                    else:
                        pg, pn = prev
                        p.op("pool", lambda e, o=Gt[:, 0:2], i=cx.gt[pg][:, pn:pn + 2]: e.tensor_copy(out=o, in_=i),
                             r=[("gt", pg)], w=[gk])
                    p.op("act", lambda e, o=Gt[:, 2:2 + n], i=psg[:, 0:n]:
                         e.activation(out=o, in_=i, func=AF.Copy), r=[pgk], w=[gk])
                    if ti == 0 and halo > 0:
                        assert halo <= n
                        p.op("dve", lambda e, o=Gt[:, 2:2 + halo]:
                             e.tensor_scalar(out=o, in0=o, scalar1=cx.hmask[:, 0:1], scalar2=None, op0=ALU.mult),
                             r=[gk, "const"], w=[gk])
                    prev = (gp, n)
                    v0 = Gt[:, 0:n]
                    v1 = Gt[:, 1:1 + n]
                    v2 = Gt[:, 2:2 + n]
                    a_ = acc[:, 0:n]
                    s_ = sil[:, 0:n]
                    u_ = psu[:, 0:n]
                    h_ = cx.Hb[:, jj, t0:t0 + n]
                    if t0 + n == np_tok:
                        p.op("pool", lambda e, o=outst[:, jg, 0:2], i=Gt[:, n:n + 2]: e.tensor_copy(out=o, in_=i),
                             r=[gk], w=[("ffo", li)])
                else:
                    gk = ("Gs", jg % 2)
                    p.op("act", lambda e, o=Gs[:, :, 2:10], i=psg[:, 0:n].rearrange("p (s t) -> p s t", t=8):
                         e.activation(out=o, in_=i, func=AF.Copy), r=[pgk], w=[gk])
                    v0 = Gs[:, :, 0:8]
                    v1 = Gs[:, :, 1:9]
                    v2 = Gs[:, :, 2:10]
                    a_ = acc[:, 0:n].rearrange("p (s t) -> p s t", t=8)
                    s_ = sil[:, 0:n].rearrange("p (s t) -> p s t", t=8)
                    u_ = psu[:, 0:n].rearrange("p (s t) -> p s t", t=8)
                    h_ = cx.Hb[:, jj, t0:t0 + n].rearrange("p (s t) -> p s t", t=8)
                    p.op("pool", lambda e, o=outst[:, jg, 2:2 + 2 * ns].rearrange("p (s t) -> p s t", t=2),
                         i=Gs[:, :, 8:10]: e.tensor_copy(out=o, in_=i), r=[gk], w=[("ffo", li)])
                ak = ("facc", gp)
                sk = ("fsil", gp)
                p.op("act", lambda e, o=a_, i=v0, sc=p_dw[:, jg, 0:1], b=p_b[:, jg:jg + 1]:
                     e.activation(out=o, in_=i, func=AF.Identity, bias=b, scale=sc),
                     r=[gk, "const"], w=[ak])
                p.op("dve", lambda e, o=a_, i=v1, sc=p_dw[:, jg, 1:2]:
                     e.scalar_tensor_tensor(out=o, in0=i, scalar=sc, in1=o, op0=ALU.mult, op1=ALU.add),
                     r=[gk, ak, "const"], w=[ak])
                p.op("dve", lambda e, o=a_, i=v2, sc=p_dw[:, jg, 2:3]:
                     e.scalar_tensor_tensor(out=o, in0=i, scalar=sc, in1=o, op0=ALU.mult, op1=ALU.add),
                     r=[gk, ak, "const"], w=[ak])
                p.op("act", lambda e, o=s_, i=a_: e.activation(out=o, in_=i, func=AF.Silu), r=[ak], w=[sk])
                p.op("dve", lambda e, o=h_, a=s_, b=u_: e.tensor_tensor(out=o, in0=a, in1=b, op=ALU.mult),
                     r=[sk, puk], w=[("Hb", ti)])
        for m in range(8):
            for ti, (t0, n) in enumerate(tiles):
                ps, pk = cx.psum()
                for jj in range(nj):
                    p.op("pe", lambda e, o=ps[:, 0:n], l=wd[:, jj, m * 128:(m + 1) * 128],
                         r_=cx.Hb[:, jj, t0:t0 + n], s=(jj == 0), t=(jj == nj - 1):
                         e.matmul(o, lhsT=l, rhs=r_, start=s, stop=t), r=[wdk, ("Hb", ti)], w=[pk])
                p.op("dve", lambda e, o=X[:, m, t0:t0 + n], i=ps[:, 0:n]:
                     e.tensor_tensor(out=o, in0=i, in1=o, op=ALU.add), r=[pk, ("X", ti)], w=[("X", ti)])


def ffn_setup(cx, n_layers, T, ns, dw_d, b_d, st_d, part=3):
    cx.ffn_dw, cx.ffn_b, cx.ffn_state, cx.ffn_out = [], [], [], []
    for li in range(n_layers):
        t = cx.alloc("ffdw%d" % li, [128, NFF, 3], F32)
        cx.load_s(t[:], dw_d[li], "const", "const")
        cx.ffn_dw.append(t)
        t = cx.alloc("ffb%d" % li, [128, NFF], F32)
        cx.load_s(t[:], b_d[li], "const", "const")
        cx.ffn_b.append(t)
        t = cx.alloc("ffst%d" % li, [128, NFF, ns, 2], F32)
        cx.load_s(t[:], st_d[li], ("stf", li), "const")
        cx.ffn_state.append(t)
        cx.ffn_out.append(cx.alloc("ffo%d" % li, [128, NFF, 2 + 2 * ns], F32))
    cx.gt = [cx.alloc("gt%d" % i, [128, 514], F32) for i in range(2)]
    cx.gt_i = 0
    cx.Gs = [cx.alloc("Gs%d" % i, [128, ns, 10], F32) for i in range(2)]
    cx.facc = [cx.alloc("facc%d" % i, [128, 512], F32) for i in range(2)]
    cx.fsil = [cx.alloc("fsil%d" % i, [128, 512], F32) for i in range(2)]
    cx.Hb = cx.alloc("Hb", [128, part, T], BF16)


HA = 32


def build_A(own, ns):
    np_tok = HA + own
    T = np_tok + ns * 8
    nc = bass.Bass("TRN2", target_bir_lowering=False)
    stack = ExitStack()
    with stack:
        cx = Ctx(nc, stack, n_wslots=4, wslot_elems=3072)
        p = cx.p
        d_x = cx.dram_in("xT", [D, T])
        d_hm = cx.dram_in("hmask", [128, 1])
        d_stc = cx.dram_in("stconv", [128, 8, ns, 30])
        d_vec = cx.dram_in("vecA", [128, 72])
        d_wdw = cx.dram_in("cv_wdw", [128, 8, 31])
        d_ident = cx.dram_in("identA", [128, 128])
        d_win = cx.dram_in("cv_w_in", [D, 2 * D])
        d_wout = cx.dram_in("cv_w_out", [D, D])
        d_fdw = cx.dram_in("ff_dw", [1, 128, NFF, 3])
        d_fb = cx.dram_in("ff_b", [1, 128, NFF])
        d_fst = cx.dram_in("ff_st", [1, 128, NFF, ns, 2])
        d_wg = cx.dram_in("ff_w_gate", [D, DFF])
        d_wu = cx.dram_in("ff_w_up", [D, DFF])
        d_wd = cx.dram_in("ff_w_down", [DFF, D])
        d_wqkv = cx.dram_in("w_qkv", [D, 3 * D])
        o_x1 = cx.dram_out("x1T", [D, T])
        o_qkv = cx.dram_out("qkvT", [3 * D, T])
        o_conv = cx.dram_out("convT", [128, 8, 30 + ns * 30])
        o_ffn = cx.dram_out("ffnT", [128, NFF, 2 + 2 * ns])

        tiles = split_tiles(np_tok) + [(np_tok, ns * 8)]
        ptiles = tiles[:-1]
        cx.X = cx.sb("X", [128, 8, T], F32)
        cx.XN = cx.sb("XN", [128, 8, T], BF16)
        cx.scr_sq = [cx.sb("sq%d" % i, [128, 512], BF16) for i in range(4)]
        cx.scr_r = [cx.sb("R%d" % i, [128, 512], F32) for i in range(2)]
        cx.hmask = cx.const_cols("hmask", d_hm, 1)
        vec = cx.const_cols("vecA", d_vec, 72)
        X, XN = cx.X, cx.XN
        for ti, (t0, n) in enumerate(tiles):
            for kc in range(8):
                cx.load(X[:, kc, t0:t0 + n], d_x[kc * 128:(kc + 1) * 128, t0:t0 + n], ("X", ti), ("xin", ti % 2))
        wdw = cx.sb("wdw", [128, 8, 31], F32)
        cx.load(wdw[:], d_wdw, "const", "const")
        identb = cx.sb("identb", [128, 128], BF16)
        p.op("pool", lambda e: e.dma_start(out=identb[:], in_=d_ident), w=["identb"], dsem="const2")
        cx.scratch_init(17000)
        G0 = cx.alloc("G0", [128, 8, 30 + np_tok], BF16)
        GS = cx.alloc("GS", [128, 8, ns, 38], F32)
        cx.load_s(GS[:, :, :, 0:30], d_stc, "GS", "const_s")
        glast = cx.alloc("glast", [128, 8, 32], F32)
        for c in range(8):
            p.op("pool", lambda e, o=G0[:, c, 0:30]: e.memset(o, 0.0), w=[("G0", c)])

        rmsnorm(cx, X, vec[:, 0:8], tiles, xn_out(cx), "n0")
        s1 = [cx.alloc("s1_%d" % i, [128, 512], F32) for i in range(2)]
        for (c0, ncg) in ((0, 3), (3, 3), (6, 2)):
            wa, wak = cx.wload(d_win[:, c0 * 128:(c0 + ncg) * 128], 8, ncg * 128)
            wg_, wgk = cx.wload(d_win[:, D + c0 * 128:D + (c0 + ncg) * 128], 8, ncg * 128)
            for cc in range(ncg):
                c = c0 + cc
                for ti, (t0, n) in enumerate(tiles):
                    samp = t0 >= np_tok
                    psa, pak = cx.psum()
                    for kc in range(8):
                        p.op("pe", lambda e, o=psa[:, 0:n], l=wa[:, kc, cc * 128:(cc + 1) * 128],
                             r_=XN[:, kc, t0:t0 + n], s=(kc == 0), t=(kc == 7):
                             e.matmul(o, lhsT=l, rhs=r_, start=s, stop=t), r=[wak, ("XN", ti)], w=[pak])
                    psg, pgk = cx.psum()
                    for kc in range(8):
                        p.op("pe", lambda e, o=psg[:, 0:n], l=wg_[:, kc, cc * 128:(cc + 1) * 128],
                             r_=XN[:, kc, t0:t0 + n], s=(kc == 0), t=(kc == 7):
                             e.matmul(o, lhsT=l, rhs=r_, start=s, stop=t), r=[wgk, ("XN", ti)], w=[pgk])
                    sp_ = ti % 2
                    p.op("act", lambda e, o=s1[sp_][:, 0:n], i=psg[:, 0:n], b=vec[:, 8 + 8 + c:8 + 8 + c + 1]:
                         e.activation(out=o, in_=i, func=AF.Sigmoid, bias=b), r=[pgk, "const"], w=[("s1", sp_)])
                    if not samp:
                        p.op("dve", lambda e, o=G0[:, c, 30 + t0:30 + t0 + n], i=psa[:, 0:n],
                             b=vec[:, 8 + c:8 + c + 1], s=s1[sp_][:, 0:n]:
                             e.scalar_tensor_tensor(out=o, in0=i, scalar=b, in1=s, op0=ALU.add, op1=ALU.mult),
                             r=[pak, ("s1", sp_), "const"], w=[("G0", c)])
                        if t0 + n == np_tok:
                            nl = min(n, 30)
                            p.op("dve", lambda e, o=glast[:, c, 30 - nl:30], i=psa[:, n - nl:n],
                                 b=vec[:, 8 + c:8 + c + 1], s=s1[sp_][:, n - nl:n]:
                                 e.scalar_tensor_tensor(out=o, in0=i, scalar=b, in1=s, op0=ALU.add, op1=ALU.mult),
                                 r=[pak, ("s1", sp_), "const"], w=["glast"])
                    else:
                        p.op("dve", lambda e, o=GS[:, c, :, 30:38], i=psa[:, 0:n].rearrange("p (s t) -> p s t", t=8),
                             b=vec[:, 8 + c:8 + c + 1], s=s1[sp_][:, 0:n].rearrange("p (s t) -> p s t", t=8):
                             e.scalar_tensor_tensor(out=o, in0=i, scalar=b, in1=s, op0=ALU.add, op1=ALU.mult),
                             r=[pak, ("s1", sp_), "const"], w=["GS"])
        assert ptiles[-1][1] >= 30
        cx.store_s(o_conv[:, :, 0:30], glast[:, :, 0:30], "glast", "outs")
        for c in range(8):
            cx.store_s(o_conv[:, c, 30:30 + ns * 30].rearrange("p (s t) -> p s t", t=30), GS[:, c, :, 8:38], "GS", "outs")
        accs = cx.alloc("caccs", [128, ns, 8], F32)
        DG = cx.alloc("DG", [128, 31, 128], BF16)
        for c in range(8):
            p.op("dve", lambda e, o=G0[:, c, 30:30 + HA]:
                 e.tensor_scalar(out=o, in0=o, scalar1=cx.hmask[:, 0:1], scalar2=None, op0=ALU.mult),
                 r=[("G0", c), "const"], w=[("G0", c)])
            for k in range(31):
                p.op("act", lambda e, o=DG[:, k, :], sc=wdw[:, c, k:k + 1]:
                     e.activation(out=o, in_=identb[:, :], func=AF.Copy, scale=sc), r=["identb", "const"], w=["DG"])
            for ti, (t0, n) in enumerate(ptiles):
                ps, pk = cx.psum()
                for k in range(31):
                    p.op("pe", lambda e, o=ps[:, 0:n], l=DG[:, k, :], r_=G0[:, c, t0 + k:t0 + k + n], s=(k == 0), t=(k == 30):
                         e.matmul(o, lhsT=l, rhs=r_, start=s, stop=t), r=["DG", ("G0", c)], w=[pk])
                p.op("act", lambda e, o=XN[:, c, t0:t0 + n], i=ps[:, 0:n], b=vec[:, 24 + c:25 + c]:
                     e.activation(out=o, in_=i, func=AF.Identity, bias=b), r=[pk, "const"], w=[("XN", ti)])
            for k in range(31):
                last = k == 30
                o_s = XN[:, c, np_tok:T].rearrange("p (s t) -> p s t", t=8) if last else accs[:, :, :]
                if k == 0:
                    p.op("dve", lambda e, o=o_s, i=GS[:, c, :, 0:8], sc=wdw[:, c, 0:1], b=vec[:, 24 + c:25 + c]:
                         e.tensor_scalar(out=o, in0=i, scalar1=sc, scalar2=b, op0=ALU.mult, op1=ALU.add),
                         r=["GS", "const"], w=["caccs"])
                else:
                    p.op("dve", lambda e, o=o_s, i=GS[:, c, :, k:k + 8], sc=wdw[:, c, k:k + 1], a=accs[:, :, :]:
                         e.scalar_tensor_tensor(out=o, in0=i, scalar=sc, in1=a, op0=ALU.mult, op1=ALU.add),
                         r=["GS", "caccs", "const"], w=["caccs"] + ([("XN", len(tiles) - 1)] if last else []))
        cx.new_scope()
        mu = [cx.alloc("mu%d" % i, [128, 512], F32) for i in range(2)]
        var = [cx.alloc("var%d" % i, [128, 512], F32) for i in range(2)]
        tmpc = [cx.alloc("tmpc%d" % i, [128, 512], F32) for i in range(2)]
        for ti, (t0, n) in enumerate(tiles):
            par = ti % 2
            ps1, pk1 = cx.psum()
            for kc in range(8):
                p.op("pe", lambda e, o=ps1[:, 0:n], r_=XN[:, kc, t0:t0 + n], s=(kc == 0), t=(kc == 7):
                     e.matmul(o, lhsT=cx.ones[:], rhs=r_, start=s, stop=t), r=[("XN", ti), "ones"], w=[pk1])
            ps2, pk2 = cx.psum()
            for kc in range(8):
                sqi = cx.sq_next()
                p.op("act", lambda e, o=cx.scr_sq[sqi][:, 0:n], i=XN[:, kc, t0:t0 + n]:
                     e.activation(out=o, in_=i, func=AF.Square), r=[("XN", ti)], w=[("sq", sqi)])
                p.op("pe", lambda e, o=ps2[:, 0:n], r_=cx.scr_sq[sqi][:, 0:n], s=(kc == 0), t=(kc == 7):
                     e.matmul(o, lhsT=cx.ones[:], rhs=r_, start=s, stop=t), r=[("sq", sqi), "ones"], w=[pk2])
            m_ = mu[par][:, 0:n]
            v_ = var[par][:, 0:n]
            p.op("dve", lambda e, o=m_, i=ps1[:, 0:n]:
                 e.tensor_scalar(out=o, in0=i, scalar1=1.0 / D, scalar2=None, op0=ALU.mult), r=[pk1], w=[("mu", par)])
            p.op("dve", lambda e, o=v_, a=m_: e.tensor_tensor(out=o, in0=a, in1=a, op=ALU.mult),
                 r=[("mu", par)], w=[("var", par)])
            p.op("dve", lambda e, o=v_, i=ps2[:, 0:n]:
                 e.scalar_tensor_tensor(out=o, in0=i, scalar=1.0 / D, in1=o, op0=ALU.mult, op1=ALU.subtract),
                 r=[pk2, ("var", par)], w=[("var", par)])
            p.op("act", lambda e, o=v_: e.activation(out=o, in_=o, func=AF.Sqrt, bias=cx.epsc[:, 0:1], scale=1.0),
                 r=[("var", par), "epsc"], w=[("var", par)])
            p.op("dve", lambda e, o=v_: e.reciprocal(out=o, in_=o), r=[("var", par)], w=[("var", par)])
            for kc in range(8):
                tp = (ti * 8 + kc) % 2
                t_ = tmpc[tp][:, 0:n]
                p.op("dve", lambda e, o=t_, a=XN[:, kc, t0:t0 + n], b=m_: e.tensor_tensor(out=o, in0=a, in1=b, op=ALU.subtract),
                     r=[("XN", ti), ("mu", par)], w=[("tmpc", tp)])
                p.op("dve", lambda e, o=t_, b=v_: e.tensor_tensor(out=o, in0=o, in1=b, op=ALU.mult),
                     r=[("tmpc", tp), ("var", par)], w=[("tmpc", tp)])
                p.op("act", lambda e, o=XN[:, kc, t0:t0 + n], i=t_, sc=vec[:, 32 + kc:33 + kc], b=vec[:, 40 + kc:41 + kc]:
                     e.activation(out=o, in_=i, func=AF.Silu, bias=b, scale=sc),
                     r=[("tmpc", tp), "const"], w=[("XN", ti)])
        def cons_out(m, ti, t0, n, ps, pk):
            p.op("dve", lambda e, o=X[:, m, t0:t0 + n], i=ps[:, 0:n], b=vec[:, 48 + m:49 + m]:
                 e.scalar_tensor_tensor(out=o, in0=i, scalar=b, in1=o, op0=ALU.add, op1=ALU.add),
                 r=[pk, ("X", ti), "const"], w=[("X", ti)])
        proj(cx, d_wout, 8, 0, D, XN, "XN", tiles, cons_out, group_cols=384)
        cx.new_scope()
        ffn_setup(cx, 1, T, ns, d_fdw, d_fb, d_fst)
        rmsnorm(cx, X, vec[:, 56:64], tiles, xn_out(cx), "nf0")
        conv_ffn(cx, 0, {"gate": d_wg, "up": d_wu, "down": d_wd}, tiles, np_tok, ns, HA)
        cx.store_s(o_ffn, cx.ffn_out[0], ("ffo", 0), "outs")
        for ti, (t0, n) in enumerate(tiles):
            for kc in range(8):
                cx.store(o_x1[kc * 128:(kc + 1) * 128, t0:t0 + n], X[:, kc, t0:t0 + n], ("X", ti), "outs")
        cx.new_scope()
        rmsnorm(cx, X, vec[:, 64:72], tiles, xn_out(cx), "n1")
        ost = [cx.alloc("ost%d" % i, [128, 512], F32) for i in range(3)]
        cnt = [0]

        def cons_qkv(m, ti, t0, n, ps, pk):
            s = cnt[0] % 3
            cnt[0] += 1
            p.op("act", lambda e, o=ost[s][:, 0:n], i=ps[:, 0:n]: e.activation(out=o, in_=i, func=AF.Copy),
                 r=[pk], w=[("ost", s)])
            cx.store_s(o_qkv[m * 128:(m + 1) * 128, t0:t0 + n], ost[s][:, 0:n], ("ost", s), ("ost", s))
        proj(cx, d_wqkv, 8, 0, 3 * D, XN, "XN", tiles, cons_qkv, group_cols=384)
        cx.p.emit_all(stack)
    return nc


def lay_cols(v):
    v = np.asarray(v, np.float32)
    return np.ascontiguousarray(v.reshape(-1, 128).T)


def run_A(inp, own, ns, n_cores, seq_of_core, nc_cache={}):
    key = (own, ns)
    if key not in nc_cache:
        nc_cache[key] = build_A(own, ns)
    nc = nc_cache[key]
    xp = np.asarray(inp["x_prompt"], np.float32)
    xs = np.asarray(inp["x_sample"], np.float32)
    vec = np.concatenate([
        lay_cols(inp["norm_mix"][0]), lay_cols(inp["cv_b_in"]), lay_cols(inp["cv_b_dw"]),
        lay_cols(inp["cv_ln_g"]), lay_cols(inp["cv_ln_b"]), lay_cols(inp["cv_b_out"]),
        lay_cols(inp["norm_ffn"][0]), lay_cols(inp["norm_mix"][1])], axis=1)
    wdw = np.ascontiguousarray(np.asarray(inp["cv_w_dw"], np.float32).T.reshape(8, 128, 31).transpose(1, 0, 2))
    fdw = np.ascontiguousarray(np.asarray(inp["ff_w_dw"][0], np.float32).T.reshape(NFF, 128, 3).transpose(1, 0, 2))[None]
    fb = lay_cols(inp["ff_b_dw"][0])[None]
    in_maps = []
    for c in range(n_cores):
        b, h = seq_of_core(c)
        seg = xp[b, h * own:(h + 1) * own]
        if h == 0:
            halo = np.zeros((HA, D), np.float32)
        else:
            halo = xp[b, h * own - HA:h * own]
        sm = xs[c * ns:(c + 1) * ns].reshape(ns * 8, D)
        xT = np.ascontiguousarray(np.concatenate([halo, seg, sm], 0).T)
        stc = np.asarray(inp["state_conv"][c * ns:(c + 1) * ns], np.float32)
        stc = np.ascontiguousarray(stc.transpose(2, 0, 1).reshape(8, 128, ns, 30).transpose(1, 0, 2, 3))
        stf = np.asarray(inp["state_ffn"][0, c * ns:(c + 1) * ns], np.float32)
        stf = np.ascontiguousarray(stf.transpose(2, 0, 1).reshape(NFF, 128, ns, 2).transpose(1, 0, 2, 3))[None]
        in_maps.append({
            "xT": xT, "hmask": np.full((128, 1), float(h), np.float32), "stconv": stc, "vecA": vec,
            "cv_wdw": wdw, "identA": np.eye(128, dtype=np.float32), "cv_w_in": np.asarray(inp["cv_w_in"], np.float32),
            "cv_w_out": np.asarray(inp["cv_w_out"], np.float32),
            "ff_dw": fdw, "ff_b": fb, "ff_st": stf,
            "ff_w_gate": np.asarray(inp["ff_w_gate"][0], np.float32),
            "ff_w_up": np.asarray(inp["ff_w_up"][0], np.float32),
            "ff_w_down": np.asarray(inp["ff_w_down"][0], np.float32),
            "w_qkv": np.asarray(inp["da_w_qkv"], np.float32),
        })
    res = run_bass_kernel_spmd(nc, in_maps, core_ids=list(range(n_cores)))
    return res.results


LAM_INIT = 0.8 - 0.6 * math.exp(-0.3 * 1)


def build_B(nseq, S, nss, npg, n_phys):
    nc = bass.Bass("TRN2", target_bir_lowering=False)
    stack = ExitStack()
    NG = S // 512
    NB = S // 128
    with stack:
        cx = Ctx(nc, stack, n_wslots=1, wslot_elems=64)
        p = cx.p
        d_q = cx.dram_in("qT", [nseq, 128, S])
        d_k = cx.dram_in("kT", [nseq, 128, S])
        d_v = cx.dram_in("v", [nseq, S, 128])
        d_qs = cx.dram_in("qsT", [128, nss * 8])
        d_ks = cx.dram_in("ksT", [128, nss * 8])
        d_vs = cx.dram_in("vs", [nss * 8, 128])
        d_ck = cx.dram_in("ck", [n_phys, 128, 128])
        d_cv = cx.dram_in("cv", [n_phys, 128, 128])
        d_tabr = cx.dram_in("ptabr", [128, nss], I32)
        d_iota = cx.dram_in("iota", [128, 1])
        d_ident = cx.dram_in("ident", [128, 128])
        d_lam = cx.dram_in("lamp", [1, 256])
        d_g = cx.dram_in("gcol", [128, 1])
        d_grow = cx.dram_in("grow", [8, 128])
        d_mask = cx.dram_in("masks", [128, 4, 512])
        d_smask = cx.dram_in("smask", [8, 8])
        o_p = cx.dram_out("oT", [nseq, 128, S])
        o_s = cx.dram_out("os", [nss * 8, 128])

        masks = cx.sb("masks", [128, 4, 512], BF16)
        p.op("pool", lambda e: e.dma_start(out=masks[:], in_=d_mask), w=["masks"], dsem="const2")
        smask = cx.sb("smask", [8, 8], F32)
        cx.load(smask[:], d_smask, "smask", "const")
        gcol = cx.sb("gcol", [128, 1], F32)
        cx.load(gcol[:], d_g, "gcol", "const")
        grow = cx.sb("grow", [8, 128], F32)
        cx.load(grow[:], d_grow, "grow", "const")
        lamp = cx.sb("lamp", [1, 256], F32)
        cx.load(lamp[:], d_lam, "lamp", "const")
        onesf = cx.sb("onesf", [1, 128], F32)
        p.op("pool", lambda e: e.memset(onesf[:], 1.0), w=["onesf"])
        lt = cx.sb("lt", [1, 128], F32)
        lsum = cx.sb("lsum", [1, 4], F32)
        p.op("dve", lambda e: e.tensor_tensor(out=lt[:, 0:64], in0=lamp[:, 0:64], in1=lamp[:, 64:128], op=ALU.mult),
             r=["lamp"], w=["lt"])
        p.op("dve", lambda e: e.tensor_tensor(out=lt[:, 64:128], in0=lamp[:, 128:192], in1=lamp[:, 192:256], op=ALU.mult),
             r=["lamp"], w=["lt"])
        p.op("dve", lambda e: e.reduce_sum(out=lsum[:, 0:1], in_=lt[:, 0:64], axis=AX.X), r=["lt"], w=["lsum"])
        p.op("dve", lambda e: e.reduce_sum(out=lsum[:, 1:2], in_=lt[:, 64:128], axis=AX.X), r=["lt"], w=["lsum"])
        p.op("act", lambda e: e.activation(out=lsum[:, 0:2], in_=lsum[:, 0:2], func=AF.Exp), r=["lsum"], w=["lsum"])
        p.op("dve", lambda e: e.tensor_tensor(out=lsum[:, 2:3], in0=lsum[:, 1:2], in1=lsum[:, 0:1], op=ALU.subtract),
             r=["lsum"], w=["lsum"])
        p.op("dve", lambda e: e.tensor_scalar(out=lsum[:, 2:3], in0=lsum[:, 2:3], scalar1=-LAM_INIT, scalar2=None, op0=ALU.add),
             r=["lsum"], w=["lsum"])
        neglam = cx.sb("neglam", [128, 1], F32)
        psl, plk = cx.psum()
        p.op("pe", lambda e: e.matmul(psl[:, 0:1], lhsT=onesf[:, :], rhs=lsum[:, 2:3], start=True, stop=True),
             r=["onesf", "lsum"], w=[plk])
        p.op("dve", lambda e: e.tensor_copy(out=neglam[:], in_=psl[:, 0:1]), r=[plk], w=["neglam"])
        gsc = cx.sb("gsc", [128, 1], F32)
        p.op("dve", lambda e: e.tensor_scalar(out=gsc[:], in0=gcol[:], scalar1=1.0 - LAM_INIT, scalar2=None, op0=ALU.mult),
             r=["gcol"], w=["gsc"])
        grs = cx.sb("grs", [8, 128], F32)
        p.op("dve", lambda e: e.tensor_scalar(out=grs[:], in0=grow[:], scalar1=1.0 - LAM_INIT, scalar2=None, op0=ALU.mult),
             r=["grow"], w=["grs"])

        Q = [cx.sb("Q%d" % i, [128, S], BF16) for i in range(2)]
        Kt = [cx.sb("K%d" % i, [128, S], BF16) for i in range(2)]
        V = [cx.sb("V%d" % i, [128, NB, 128], BF16) for i in range(2)]
        PT = [[cx.sb("PT%d_%d" % (m, i), [128, 512], BF16) for i in range(2)] for m in range(2)]
        ep = {n_: cx.sb("ep_" + n_, [128, 512], F32) for n_ in ("r1", "r2", "t1", "o", "rs")}
        epq = cx.sb("ep_sq", [128, 512], BF16)
        ost = [cx.sb("ostB%d" % i, [128, 512], F32) for i in range(2)]
        O1, O2, L1, L2 = cx.ps[0], cx.ps[1], cx.ps[2], cx.ps[3]
        gi = 0
        for b in range(nseq):
            bp = b % 2
            for hh in range(0, S, 2048):
                he = min(S, hh + 2048)
                p.op("pool", lambda e, o=Q[bp][:, hh:he], i=d_q[b][:, hh:he]: e.dma_start(out=o, in_=i), w=[("Q", bp)], dsem=("qkv", bp))
                p.op("pool", lambda e, o=Kt[bp][:, hh:he], i=d_k[b][:, hh:he]: e.dma_start(out=o, in_=i), w=[("K", bp)], dsem=("qkv", bp))
            p.op("pool", lambda e, o=V[bp][:], i=d_v[b].rearrange("(n p) e -> p n e", p=128): e.dma_start(out=o, in_=i),
                 w=[("V", bp)], dsem=("qkv", bp))
            for G in range(NG):
                nkb = 4 * G + 4
                qs = slice(G * 512, (G + 1) * 512)

                def scores(kb):
                    par = kb % 2
                    ks = slice(kb * 128, (kb + 1) * 128)
                    p.op("pe", lambda e, o=cx.ps[4 + 2 * par][:, :], l=Kt[bp][0:64, ks], r_=Q[bp][0:64, qs]:
                         e.matmul(o, lhsT=l, rhs=r_, start=True, stop=True),
                         r=[("K", bp), ("Q", bp)], w=[("ps", 4 + 2 * par)])
                    p.op("pe", lambda e, o=cx.ps[5 + 2 * par][:, :], l=Kt[bp][64:128, ks], r_=Q[bp][64:128, qs]:
                         e.matmul(o, lhsT=l, rhs=r_, start=True, stop=True),
                         r=[("K", bp), ("Q", bp)], w=[("ps", 5 + 2 * par)])

                scores(0)
                for kb in range(nkb):
                    par = kb % 2
                    if kb + 1 < nkb:
                        scores(kb + 1)
                    for m in range(2):
                        p.op("act", lambda e, o=PT[m][par][:, :], i=cx.ps[4 + m + 2 * par][:, :]:
                             e.activation(out=o, in_=i, func=AF.Exp, scale=0.125),
                             r=[("ps", 4 + m + 2 * par)], w=[("pt", m, par)])
                        if kb >= 4 * G:
                            p.op("dve", lambda e, o=PT[m][par][:, :], mk=masks[:, kb - 4 * G, :]:
                                 e.tensor_tensor(out=o, in0=o, in1=mk, op=ALU.mult),
                                 r=[("pt", m, par), "masks"], w=[("pt", m, par)])
                    for m, (Ob, Lb) in enumerate(((0, 2), (1, 3))):
                        p.op("pe", lambda e, o=cx.ps[Ob][:, :], l=V[bp][:, kb, :], r_=PT[m][par][:, :], s=(kb == 0), t=(kb == nkb - 1):
                             e.matmul(o, lhsT=l, rhs=r_, start=s, stop=t), r=[("V", bp), ("pt", m, par)], w=[("ps", Ob)])
                        p.op("pe", lambda e, o=cx.ps[Lb][:, :], r_=PT[m][par][:, :], s=(kb == 0), t=(kb == nkb - 1):
                             e.matmul(o, lhsT=cx.ones[:], rhs=r_, start=s, stop=t), r=["ones", ("pt", m, par)], w=[("ps", Lb)])
                p.op("dve", lambda e: e.reciprocal(out=ep["r1"][:], in_=L1[:, :]), r=[("ps", 2)], w=["ep_r1"])
                p.op("dve", lambda e: e.reciprocal(out=ep["r2"][:], in_=L2[:, :]), r=[("ps", 3)], w=["ep_r2"])
                p.op("dve", lambda e: e.tensor_tensor(out=ep["t1"][:], in0=O1[:, :], in1=ep["r1"][:], op=ALU.mult),
                     r=[("ps", 0), "ep_r1"], w=["ep_t1"])
                p.op("dve", lambda e: e.tensor_tensor(out=ep["r2"][:], in0=O2[:, :], in1=ep["r2"][:], op=ALU.mult),
                     r=[("ps", 1), "ep_r2"], w=["ep_r2"])
                p.op("dve", lambda e: e.scalar_tensor_tensor(out=ep["o"][:], in0=ep["r2"][:], scalar=neglam[:, 0:1], in1=ep["t1"][:],
                                                             op0=ALU.mult, op1=ALU.add),
                     r=["ep_r2", "ep_t1", "neglam"], w=["ep_o"])
                p.op("act", lambda e: e.activation(out=epq[:], in_=ep["o"][:], func=AF.Square), r=["ep_o"], w=["ep_sq"])
                sp_ = 4 + 2 * (nkb % 2)
                p.op("pe", lambda e, o=cx.ps[sp_][:, :]: e.matmul(o, lhsT=cx.ones[:], rhs=epq[:], start=True, stop=True),
                     r=["ones", "ep_sq"], w=[("ps", sp_)])
                p.op("act", lambda e, i=cx.ps[sp_][:, :]: e.activation(out=ep["rs"][:], in_=i, func=AF.Sqrt, bias=cx.epsc[:, 0:1], scale=1.0 / 128),
                     r=[("ps", sp_), "epsc"], w=["ep_rs"])
                p.op("dve", lambda e: e.reciprocal(out=ep["rs"][:], in_=ep["rs"][:]), r=["ep_rs"], w=["ep_rs"])
                so = gi % 2
                gi += 1
                p.op("dve", lambda e, o=ost[so][:]: e.scalar_tensor_tensor(out=o, in0=ep["o"][:], scalar=gsc[:, 0:1], in1=ep["rs"][:],
                                                                          op0=ALU.mult, op1=ALU.mult),
                     r=["ep_o", "ep_rs", "gsc"], w=[("ostB", so)])
                cx.store(o_p[b][:, qs], ost[so][:], ("ostB", so), ("ostB", so))

        NT = nss * 8
        QS = cx.sb("QS", [128, NT], BF16)
        KN = cx.sb("KN", [128, NT], BF16)
        p.op("pool", lambda e: e.dma_start(out=QS[:], in_=d_qs), w=["QS"], dsem="const2")
        p.op("pool", lambda e: e.dma_start(out=KN[:], in_=d_ks), w=["KN"], dsem="const2")
        assert npg * 8 == 128
        NBUF = 5
        KTk = [cx.sb("KTk%d" % i, [128, 16, 128], F32) for i in range(NBUF)]
        KB = [cx.sb("KB%d" % i, [128, 16, 128], BF16) for i in range(NBUF)]
        VB = [cx.sb("VB%d" % i, [128, 17, 128], BF16) for i in range(NBUF)]
        for i in range(NBUF):
            p.op("pool", lambda e, o=VB[i][:, 16, :]: e.memset(o, 0.0), w=[("VB", i)])
        PS_ = [[cx.sb("PS%d_%d" % (m, i), [128, 17 * 8], BF16) for i in range(NBUF)] for m in range(2)]
        OS = [cx.sb("OS%d" % i, [8, 128], F32) for i in range(4)]
        sm = {n_: cx.sb("sm_" + n_, [8, 2], F32) for n_ in ("r", "ss")}
        smt = cx.sb("sm_t1", [8, 128], F32)
        smo = cx.sb("sm_o", [8, 128], F32)
        smq = cx.sb("sm_q", [8, 128], F32)
        NC_ = 17 * 8
        tabi = cx.sb("tabi", [128, nss], I32)
        tabf = cx.sb("tabf", [128, nss], F32)
        idx = cx.sb("idx", [128, nss], I32)
        iot = cx.sb("iot", [128, 1], F32)
        ident = cx.sb("ident", [128, 128], F32)
        cx.load(tabi[:], d_tabr, "tabi", "const")
        cx.load(iot[:], d_iota, "iot", "const")
        cx.load(ident[:], d_ident, "ident", "const")
        p.op("dve", lambda e: e.tensor_copy(out=tabf[:], in_=tabi[:]), r=["tabi"], w=["tabf"])
        p.op("dve", lambda e: e.tensor_scalar(out=tabf[:], in0=tabf[:], scalar1=8.0, scalar2=iot[:, 0:1], op0=ALU.mult, op1=ALU.add),
             r=["tabf", "iot"], w=["tabf"])
        p.op("dve", lambda e: e.tensor_copy(out=idx[:], in_=tabf[:]), r=["tabf"], w=["idx"])
        ckf = d_ck.rearrange("n (a b) d -> (n a) (b d)", a=8)
        cvf = d_cv.rearrange("n (a b) d -> (n a) (b d)", a=8)
        npg = 16
        def seq_gen(i):
            par = i % NBUF
            p.op("pool", lambda e, o=KTk[par][:, :, :].rearrange("p a b -> p (a b)"), c_=i: e.indirect_dma_start(
                out=o, out_offset=None, in_=ckf, in_offset=bass.IndirectOffsetOnAxis(ap=idx[:, c_:c_ + 1], axis=0)),
                r=["idx"], w=[("KTk", par)], dsem=("kb", par))
            p.op("pool", lambda e, o=VB[par][:, 0:16, :].rearrange("p a b -> p (a b)"), c_=i: e.indirect_dma_start(
                out=o, out_offset=None, in_=cvf, in_offset=bass.IndirectOffsetOnAxis(ap=idx[:, c_:c_ + 1], axis=0)),
                r=["idx"], w=[("VB", par)], dsem=("vb", par))
            p.op("pool", lambda e, o=VB[par][0:8, 16, :], i_=d_vs[i * 8:(i + 1) * 8, :]: e.dma_start(out=o, in_=i_),
                 w=[("VB", par)], dsem=("vb", par))
            yield
            for q4 in range(4):
                pst, ptk = cx.psum()
                for uu in range(4):
                    u = q4 * 4 + uu
                    p.op("pe", lambda e, o=pst[:, uu * 128:(uu + 1) * 128], a=KTk[par][:, u, :]:
                         e.transpose(o, a, ident[:, :]), r=[("KTk", par), "ident"], w=[ptk])
                eng_ = "act" if q4 % 2 == 0 else "dve"
                if eng_ == "act":
                    p.op("act", lambda e, o=KB[par][:, q4 * 4:(q4 + 1) * 4, :], a=pst[:, :].rearrange("p (u k) -> p u k", u=4):
                         e.activation(out=o, in_=a, func=AF.Copy), r=[ptk], w=[("KB", par)])
                else:
                    p.op("dve", lambda e, o=KB[par][:, q4 * 4:(q4 + 1) * 4, :], a=pst[:, :].rearrange("p (u k) -> p u k", u=4):
                         e.tensor_copy(out=o, in_=a), r=[ptk], w=[("KB", par)])
            yield
            qsl = slice(i * 8, (i + 1) * 8)
            sps = []
            for m in range(2):
                ps, pk = cx.psum()
                sps.append((ps, pk))
                pr = slice(64 * m, 64 * m + 64)
                for j in range(npg):
                    p.op("pe", lambda e, o=ps[:, j * 8:(j + 1) * 8], l=KB[par][pr, j, :], r_=QS[pr, qsl]:
                         e.matmul(o, lhsT=l, rhs=r_, start=True, stop=True), r=[("KB", par), "QS"], w=[pk])
                p.op("pe", lambda e, o=ps[0:8, npg * 8:NC_], l=KN[pr, qsl], r_=QS[pr, qsl]:
                     e.matmul(o, lhsT=l, rhs=r_, start=True, stop=True), r=["KN", "QS"], w=[pk])
            for m in range(2):
                ps, pk = sps[m]
                p.op("act", lambda e, o=PS_[m][par][:, 0:npg * 8], i_=ps[:, 0:npg * 8]:
                     e.activation(out=o, in_=i_, func=AF.Exp, scale=0.125), r=[pk], w=[("PS", m, par)])
                p.op("act", lambda e, o=PS_[m][par][0:8, npg * 8:NC_], i_=ps[0:8, npg * 8:NC_]:
                     e.activation(out=o, in_=i_, func=AF.Exp, scale=0.125), r=[pk], w=[("PS", m, par)])
                p.op("dve", lambda e, o=PS_[m][par][0:8, npg * 8:NC_]: e.tensor_tensor(out=o, in0=o, in1=smask[:, :], op=ALU.mult),
                     r=[("PS", m, par), "smask"], w=[("PS", m, par)])
            yield
            ops_ = []
            for m in range(2):
                ps, pk = cx.psum()
                ops_.append((ps, pk))
                for j in range(npg):
                    p.op("pe", lambda e, o=ps[0:8, 128:129], l=PS_[m][par][:, j * 8:(j + 1) * 8], s=(j == 0):
                         e.matmul(o, lhsT=l, rhs=cx.ones[:, 0:1], start=s, stop=False), r=[("PS", m, par), "ones"], w=[pk])
                p.op("pe", lambda e, o=ps[0:8, 128:129], l=PS_[m][par][0:8, npg * 8:NC_]:
                     e.matmul(o, lhsT=l, rhs=cx.ones[0:8, 0:1], start=False, stop=True), r=[("PS", m, par), "ones"], w=[pk])
                for j in range(npg):
                    p.op("pe", lambda e, o=ps[0:8, 0:128], l=PS_[m][par][:, j * 8:(j + 1) * 8], r_=VB[par][:, j, :], s=(j == 0):
                         e.matmul(o, lhsT=l, rhs=r_, start=s, stop=False), r=[("PS", m, par), ("VB", par)], w=[pk])
                p.op("pe", lambda e, o=ps[0:8, 0:128], l=PS_[m][par][0:8, npg * 8:NC_], r_=VB[par][0:8, npg, :]:
                     e.matmul(o, lhsT=l, rhs=r_, start=False, stop=True), r=[("PS", m, par), ("VB", par)], w=[pk])
            (p1, k1), (p2, k2) = ops_
            p.op("dve", lambda e, a=p1[0:8, 128:129]: e.reciprocal(out=sm["r"][:, 0:1], in_=a), r=[k1], w=["sm_r"])
            p.op("dve", lambda e, a=p2[0:8, 128:129]: e.reciprocal(out=sm["r"][:, 1:2], in_=a), r=[k2], w=["sm_r"])
            p.op("dve", lambda e: e.tensor_tensor(out=sm["r"][:, 1:2], in0=sm["r"][:, 1:2], in1=neglam[0:8, 0:1], op=ALU.mult),
                 r=["sm_r", "neglam"], w=["sm_r"])
            p.op("dve", lambda e, a=p1[0:8, 0:128]: e.tensor_scalar(out=smt[:], in0=a, scalar1=sm["r"][:, 0:1], scalar2=None, op0=ALU.mult),
                 r=[k1, "sm_r"], w=["sm_t1"])
            p.op("dve", lambda e, a=p2[0:8, 0:128]: e.scalar_tensor_tensor(out=smo[:], in0=a, scalar=sm["r"][:, 1:2], in1=smt[:],
                                                                            op0=ALU.mult, op1=ALU.add),
                 r=[k2, "sm_r", "sm_t1"], w=["sm_o"])
            p.op("dve", lambda e: e.tensor_tensor(out=smq[:], in0=smo[:], in1=smo[:], op=ALU.mult), r=["sm_o"], w=["sm_q"])
            p.op("dve", lambda e: e.reduce_sum(out=sm["ss"][:, 0:1], in_=smq[:], axis=AX.X), r=["sm_q"], w=["sm_ss"])
            p.op("act", lambda e: e.activation(out=sm["ss"][:, 1:2], in_=sm["ss"][:, 0:1], func=AF.Sqrt, bias=cx.epsc[0:8, 0:1], scale=1.0 / 128),
                 r=["sm_ss", "epsc"], w=["sm_ss"])
            p.op("dve", lambda e: e.reciprocal(out=sm["ss"][:, 1:2], in_=sm["ss"][:, 1:2]), r=["sm_ss"], w=["sm_ss"])
            osl = i % 4
            p.op("dve", lambda e, o=OS[osl][:, :]: e.scalar_tensor_tensor(out=o, in0=smo[:], scalar=sm["ss"][:, 1:2], in1=grs[:],
                                                                         op0=ALU.mult, op1=ALU.mult),
                 r=["sm_o", "sm_ss", "grs"], w=[("OS", osl)])
            cx.store(o_s[i * 8:(i + 1) * 8, :], OS[osl][:, :], ("OS", osl), ("OS", osl))

        gens = [seq_gen(i) for i in range(nss)]
        for step in range(nss + 3):
            for off in range(4):
                i = step - off
                if 0 <= i < nss:
                    try:
                        next(gens[i])
                    except StopIteration:
                        pass
        cx.p.emit_all(stack)
    return nc


def make_masks():
    pidx = np.arange(128)[:, None]
    j = np.arange(512)[None, :]
    m = np.stack([(j >= o * 128 + pidx) for o in range(4)], axis=1).astype(np.float32)
    sm = (np.arange(8)[:, None] <= np.arange(8)[None, :]).astype(np.float32)
    return m, sm


HC = 256
POOL_WIN = (2, 2, 4, 4, 8, 8, 16, 16)
GELU_C = 1.5957691216057308


def gelu_tile(cx, x_ap, out_ap, n_shape_keys, rk, wk):
    cx.p.op("act", lambda e: e.activation(out=out_ap, in_=x_ap, func=AF.Gelu_apprx_tanh), r=rk, w=wk)


def build_C(own, ns, debug=None):
    NP = HC + own
    NSB = ns * 8
    T = NP + NSB
    assert own % 128 == 0 and NSB <= 128
    nc = bass.Bass("TRN2", target_bir_lowering=False)
    stack = ExitStack()
    with stack:
        cx = Ctx(nc, stack, n_wslots=4, wslot_elems=3072)
        p = cx.p
        d_x = cx.dram_in("x1T", [D, T])
        d_o = cx.dram_in("oT", [D, T])
        d_hm = cx.dram_in("hmask", [128, 1])
        d_vec = cx.dram_in("vecC", [128, 88])
        d_wo = cx.dram_in("w_o", [D, D])
        d_plw = cx.dram_in("pl_w", [D, 256])
        d_stp = cx.dram_in("stpool", [128, 8, ns, 15])
        d_invc = cx.dram_in("invc", [128, 4, 16])
        d_fdw = cx.dram_in("ff_dw", [3, 128, NFF, 3])
        d_fb = cx.dram_in("ff_b", [3, 128, NFF])
        d_fst = cx.dram_in("ff_st", [3, 128, NFF, ns, 2])
        d_wg = cx.dram_in("ff_w_gate", [3, D, DFF])
        d_wu = cx.dram_in("ff_w_up", [3, D, DFF])
        d_wd = cx.dram_in("ff_w_down", [3, DFF, D])
        d_sgin = cx.dram_in("sg_w_in", [D, 4 * D])
        d_sgout = cx.dram_in("sg_w_out", [2 * D, D])
        d_sgrow = cx.dram_in("sg_rows", [3, 128, 2 * D])
        d_wst = cx.dram_in("sg_wsT", [2, 128, 4, 128])
        d_wsm = cx.dram_in("sg_mask", [2, 128, 128])
        d_bs = cx.dram_in("sg_bs", [2, 128, 4, 128])
        o_y = cx.dram_out("yT", [D, T])
        o_pool = cx.dram_out("poolT", [128, 8, 15 + ns * 15])
        o_sgv = cx.dram_out("sgv", [NSB, 2 * D])
        o_ffn = cx.dram_out("ffnT", [3, 128, NFF, 2 + 2 * ns])

        tiles = split_tiles(NP) + [(NP, NSB)]
        nt = len(tiles)
        cx.X = cx.sb("X", [128, 8, T], F32)
        cx.XN = cx.sb("XN", [128, 8, T], BF16)
        X, XN = cx.X, cx.XN
        cx.scr_sq = [cx.sb("sq%d" % i, [128, 512], BF16) for i in range(4)]
        cx.scr_r = [cx.sb("R%d" % i, [128, 512], F32) for i in range(2)]
        cx.hmask = cx.const_cols("hmask", d_hm, 1)
        vec = cx.const_cols("vecC", d_vec, 88)
        cx.scratch_init(13800)
        for ti, (t0, n) in enumerate(tiles):
            for kc in range(8):
                cx.load(X[:, kc, t0:t0 + n], d_x[kc * 128:(kc + 1) * 128, t0:t0 + n], ("X", ti), ("xin", ti % 2))
        for ti, (t0, n) in enumerate(tiles):
            p.op("pool", lambda e, o=XN[:, :, t0:t0 + n], i=d_o[:, t0:t0 + n].rearrange("(k p) n -> p k n", p=128):
                 e.dma_start(out=o, in_=i), w=[("XN", ti)], dsem=("oin", ti % 2))

        def cons_add(m, ti, t0, n, ps, pk):
            p.op("dve", lambda e, o=X[:, m, t0:t0 + n], i=ps[:, 0:n]:
                 e.tensor_tensor(out=o, in0=i, in1=o, op=ALU.add), r=[pk, ("X", ti)], w=[("X", ti)])
        proj(cx, d_wo, 8, 0, D, XN, "XN", tiles, cons_add, group_cols=384)

        def run_ffn(li, ncol):
            cx.new_scope()
            cx.ffn_dw, cx.ffn_b, cx.ffn_state, cx.ffn_out = {}, {}, {}, {}
            t = cx.alloc("ffdw", [128, NFF, 3], F32)
            cx.load_s(t, d_fdw[li], "const_s", "const_s")
            cx.ffn_dw[li] = t
            t = cx.alloc("ffb", [128, NFF], F32)
            cx.load_s(t, d_fb[li], "const_s", "const_s")
            cx.ffn_b[li] = t
            t = cx.alloc("ffst", [128, NFF, ns, 2], F32)
            cx.load_s(t, d_fst[li], ("stf", li), "const_s")
            cx.ffn_state[li] = t
            cx.ffn_out[li] = cx.alloc("ffo", [128, NFF, 2 + 2 * ns], F32)
            cx.gt = [cx.alloc("gt%d" % i, [128, 514], F32) for i in range(2)]
            cx.gt_i = 0
            cx.Gs = [cx.alloc("Gs%d" % i, [128, ns, 10], F32) for i in range(2)]
            cx.facc = [cx.alloc("facc%d" % i, [128, 512], F32) for i in range(2)]
            cx.fsil = [cx.alloc("fsil%d" % i, [128, 512], F32) for i in range(2)]
            cx.Hb = cx.alloc("Hb", [128, 3, T], BF16)
            rmsnorm(cx, X, vec[:, ncol:ncol + 8], tiles, xn_out(cx), "nf%d" % li)
            conv_ffn(cx, li, {"gate": d_wg[li], "up": d_wu[li], "down": d_wd[li]}, tiles, NP, ns, HC)
            cx.store_s(o_ffn[li], cx.ffn_out[li], ("ffo", li), "outs")

        run_ffn(0, 0)

        cx.new_scope()
        Rall = cx.alloc("Rall", [128, T], F32)
        HN = cx.alloc("HN", [128, 15 + NP], F32)
        P0 = cx.alloc("P0", [128, 15 + NP], F32)
        P1 = cx.alloc("P1", [128, 15 + NP], F32)
        HS = cx.alloc("HS", [128, ns, 23], F32)
        Q0 = cx.alloc("Q0", [128, ns, 23], F32)
        Q1 = cx.alloc("Q1", [128, ns, 23], F32)
        invc = cx.alloc("invc", [128, 4, 16], F32)
        ptm = cx.alloc("ptm", [128, 16], F32)
        cx.load_s(invc, d_invc, "invc", "const_s")
        for ti, (t0, n) in enumerate(tiles):
            ps, pk = cx.psum()
            for kc in range(8):
                sqi = cx.sq_next()
                p.op("act", lambda e, o=cx.scr_sq[sqi][:, 0:n], i=X[:, kc, t0:t0 + n]:
                     e.activation(out=o, in_=i, func=AF.Square), r=[("X", ti)], w=[("sq", sqi)])
                p.op("pe", lambda e, o=ps[:, 0:n], r_=cx.scr_sq[sqi][:, 0:n], s=(kc == 0), t=(kc == 7):
                     e.matmul(o, lhsT=cx.ones[:], rhs=r_, start=s, stop=t), r=[("sq", sqi), "ones"], w=[pk])
            p.op("act", lambda e, o=Rall[:, t0:t0 + n], i=ps[:, 0:n]:
                 e.activation(out=o, in_=i, func=AF.Sqrt, bias=cx.epsc[:, 0:1], scale=1.0 / D), r=[pk, "epsc"], w=["Rall"])
            p.op("dve", lambda e, o=Rall[:, t0:t0 + n]: e.reciprocal(out=o, in_=o), r=["Rall"], w=["Rall"])
        p.op("pool", lambda e: e.memset(HN[:, 0:15], 0.0), w=["HN"])
        allXN = [("XN", ti) for ti in range(nt)]
        allX = [("X", ti) for ti in range(nt)]
        for c in range(8):
            win = POOL_WIN[c]
            gcolc = vec[:, 8 + c:9 + c]
            p.op("dve", lambda e, o=HN[:, 15:15 + NP], i=X[:, c, 0:NP], g=gcolc, r_=Rall[:, 0:NP]:
                 e.scalar_tensor_tensor(out=o, in0=i, scalar=g, in1=r_, op0=ALU.mult, op1=ALU.mult),
                 r=allX + ["Rall", "const"], w=["HN"])
            p.op("dve", lambda e, o=HN[:, 15:15 + HC]:
                 e.tensor_scalar(out=o, in0=o, scalar1=cx.hmask[:, 0:1], scalar2=None, op0=ALU.mult), r=["HN", "const"], w=["HN"])
            p.op("dve", lambda e, o=HS[:, :, 15:23], i=X[:, c, NP:T].rearrange("p (s t) -> p s t", t=8), g=gcolc,
                 r_=Rall[:, NP:T].rearrange("p (s t) -> p s t", t=8):
                 e.scalar_tensor_tensor(out=o, in0=i, scalar=g, in1=r_, op0=ALU.mult, op1=ALU.mult),
                 r=allX + ["Rall", "const"], w=["HS"])
            p.op("sp", lambda e, o=HS[:, :, 0:15], i=d_stp[:, c, :, :]: e.dma_start(out=o, in_=i), r=["scr"], w=["HS"], dsem="stp")
            cx.store_s(o_pool[:, c, 0:15], HN[:, NP:NP + 15], "HN", "pout")
            cx.store_s(o_pool[:, c, 15:15 + ns * 15].rearrange("p (s t) -> p s t", t=15), HS[:, :, 8:23], "HS", "pout")
            src_p, src_s = HN, HS
            bufs_p, bufs_s = [P0, P1], [Q0, Q1]
            k = 1
            bi = 0
            while k < win:
                dp, ds_ = bufs_p[bi], bufs_s[bi]
                lo = 2 * k - 1
                p.op("dve", lambda e, o=dp[:, lo:15 + NP], a=src_p[:, lo:15 + NP], b=src_p[:, lo - k:15 + NP - k]:
                     e.tensor_tensor(out=o, in0=a, in1=b, op=ALU.add), r=["HN", "P0", "P1"], w=["P%d" % bi])
                p.op("dve", lambda e, o=ds_[:, :, lo:23], a=src_s[:, :, lo:23], b=src_s[:, :, lo - k:23 - k]:
                     e.tensor_tensor(out=o, in0=a, in1=b, op=ALU.add), r=["HS", "Q0", "Q1"], w=["Q%d" % bi])
                src_p, src_s = dp, ds_
                k *= 2
                bi ^= 1
            widx = {2: 0, 4: 1, 8: 2, 16: 3}[win]
            p.op("dve", lambda e, o=XN[:, c, 0:NP], a=src_p[:, 15:15 + NP], h=HN[:, 15:15 + NP], iw=1.0 / win:
                 e.scalar_tensor_tensor(out=o, in0=a, scalar=iw, in1=h, op0=ALU.mult, op1=ALU.subtract),
                 r=["HN", "P0", "P1"], w=allXN)
            p.op("dve", lambda e, a=src_p[:, 15 + HC:15 + HC + 16], iv=invc[:, widx, :]:
                 e.tensor_tensor(out=ptm[:, :], in0=a, in1=iv, op=ALU.mult), r=["P0", "P1", "invc"], w=["ptm"])
            p.op("dve", lambda e, o=XN[:, c, HC:HC + 16], h=HN[:, 15 + HC:15 + HC + 16]:
                 e.tensor_tensor(out=o, in0=ptm[:, :], in1=h, op=ALU.subtract), r=["ptm", "HN"], w=allXN)
            p.op("dve", lambda e, o=XN[:, c, NP:T].rearrange("p (s t) -> p s t", t=8), a=src_s[:, :, 15:23], h=HS[:, :, 15:23], iw=1.0 / win:
                 e.scalar_tensor_tensor(out=o, in0=a, scalar=iw, in1=h, op0=ALU.mult, op1=ALU.subtract),
                 r=["HS", "Q0", "Q1"], w=allXN)
        wv, wk = cx.wload(d_plw, 8, 256)
        for m in range(8):
            g = m // 2
            for ti, (t0, n) in enumerate(tiles):
                ps, pk = cx.psum()
                for kk in range(2):
                    p.op("pe", lambda e, o=ps[:, 0:n], l=wv[:, g * 2 + kk, (m % 2) * 128:(m % 2 + 1) * 128],
                         r_=XN[:, g * 2 + kk, t0:t0 + n], s=(kk == 0), t=(kk == 1):
                         e.matmul(o, lhsT=l, rhs=r_, start=s, stop=t), r=[wk, ("XN", ti)], w=[pk])
                p.op("dve", lambda e, o=X[:, m, t0:t0 + n], i=ps[:, 0:n], sc=vec[:, 16 + m:17 + m]:
                     e.scalar_tensor_tensor(out=o, in0=i, scalar=sc, in1=o, op0=ALU.mult, op1=ALU.add),
                     r=[pk, ("X", ti), "const"], w=[("X", ti)])
        if debug == "pool":
            for ti, (t0, n) in enumerate(tiles):
                for kc in range(8):
                    cx.store(o_y[kc * 128:(kc + 1) * 128, t0:t0 + n], X[:, kc, t0:t0 + n], ("X", ti), "outs")
            cx.p.emit_all(stack)
            return nc
        run_ffn(1, 24)

        cx.new_scope()
        rmsnorm(cx, X, vec[:, 32:40], tiles, xn_out(cx), "n3")
        NBK = NP // 128
        blocks = [(i * 128, 128) for i in range(NBK)] + [(NP, NSB)]
        nb = len(blocks)
        U = cx.alloc("U", [128, 4, T], BF16)
        rows = cx.alloc("rows", [128, 3, 512], F32)
        wst = cx.alloc("wst", [128, 2, 4, 128], F32)
        wsb = cx.alloc("wsb", [128, 2, 4, 128], BF16)
        wsm = cx.alloc("wsm", [128, 2, 128], F32)
        bsr = cx.alloc("bsr", [128, 2, 4, 128], F32)
        stat = cx.alloc("stat", [128, nb, 4, 2], F32)
        mur = cx.alloc("mur", [128, nb, 2], F32)
        rowb = cx.alloc("rowb", [1, 512], BF16)
        nmr = cx.alloc("nmr", [128, nb, 1], F32)
        zvs = [cx.alloc("zv%d" % i, [128, 512], F32) for i in range(2)]
        zqs = [cx.alloc("zq%d" % i, [128, 512], F32) for i in range(2)]
        vnb = [cx.alloc("vnb%d" % i, [128, 512], BF16) for i in range(2)]
        ssts = [cx.alloc("sst%d" % i, [128, 128], F32) for i in range(4)]
        for i in range(2):
            cx.load_s(wst[:, i], d_wst[i], "wst", "const_s")
            cx.load_s(wsm[:, i], d_wsm[i], "wsm", "const_s")
            cx.load_s(bsr[:, i], d_bs[i], "bsr", "const_s")
        for i in range(2):
            for g in range(4):
                p.op("dve", lambda e, o=wsb[:, i, g, :], a=wst[:, i, g, :], b=wsm[:, i, :]:
                     e.tensor_tensor(out=o, in0=a, in1=b, op=ALU.mult), r=["wst", "wsm"], w=["wsb"])
        for g in range(4):
            halves = []
            for hh in range(2):
                halves.append(cx.wload(d_sgin[:, 2 * D + g * 512 + hh * 256:2 * D + g * 512 + (hh + 1) * 256], 8, 256))
            p.op("sp", lambda e, o=rows[:, 0, :], i=d_sgrow[0][:, g * 512:(g + 1) * 512]: e.dma_start(out=o, in_=i),
                 r=["scr"], w=["rows"], dsem="rows")
            p.op("act", lambda e: e.activation(out=rowb[0:1, :], in_=rows[0:1, 0, :], func=AF.Copy), r=["rows"], w=["rowb"])
            for bi_, (t0, n) in enumerate(blocks):
                ti = min(t0 // 512, nt - 1) if t0 < NP else nt - 1
                ps, pk = cx.psum()
                for hh in range(2):
                    wv_, wvk = halves[hh]
                    for kc in range(8):
                        p.op("pe", lambda e, o=ps[0:n, hh * 256:(hh + 1) * 256], l=XN[:, kc, t0:t0 + n], r_=wv_[:, kc, :],
                             s=(kc == 0): e.matmul(o, lhsT=l, rhs=r_, start=s, stop=False),
                             r=[wvk, ("XN", ti)], w=[pk])
                    p.op("pe", lambda e, o=ps[0:n, hh * 256:(hh + 1) * 256], l=cx.ones[0:1, 0:n], r_=rowb[0:1, hh * 256:(hh + 1) * 256]:
                         e.matmul(o, lhsT=l, rhs=r_, start=False, stop=True), r=["ones", "rowb"], w=[pk])
                zv = zvs[bi_ % 2]
                zq = zqs[bi_ % 2]
                zk = ("zv", bi_ % 2)
                qk = ("zq", bi_ % 2)
                gelu_tile(cx, ps[0:n, :], zv[0:n, :], None, [pk], [zk])
                p.op("dve", lambda e, o=stat[0:n, bi_, g, 0:1], a=zv[0:n, :]: e.reduce_sum(out=o, in_=a, axis=AX.X), r=[zk], w=["stat"])
                p.op("act", lambda e, o=zq[0:n, :], a=zv[0:n, :]: e.activation(out=o, in_=a, func=AF.Square), r=[zk], w=[qk])
                p.op("dve", lambda e, o=stat[0:n, bi_, g, 1:2], a=zq[0:n, :]: e.reduce_sum(out=o, in_=a, axis=AX.X), r=[qk], w=["stat"])
        for bi_, (t0, n) in enumerate(blocks):
            p.op("dve", lambda e, o=mur[0:n, bi_, :], a=stat[0:n, bi_, :, :].rearrange("p g s -> p s g"):
                 e.reduce_sum(out=o, in_=a, axis=AX.X), r=["stat"], w=["mur"])
        p.op("dve", lambda e: e.tensor_scalar(out=mur[:, :, :], in0=mur[:, :, :], scalar1=1.0 / (2 * D), scalar2=None, op0=ALU.mult),
             r=["mur"], w=["mur"])
        msq = cx.alloc("msq", [128, nb, 1], F32)
        p.op("dve", lambda e: e.tensor_tensor(out=msq[:, :, :], in0=mur[:, :, 0:1], in1=mur[:, :, 0:1], op=ALU.mult), r=["mur"], w=["msq"])
        p.op("dve", lambda e: e.tensor_tensor(out=mur[:, :, 1:2], in0=mur[:, :, 1:2], in1=msq[:, :, :], op=ALU.subtract),
             r=["mur", "msq"], w=["mur"])
        p.op("act", lambda e: e.activation(out=mur[:, :, 1:2], in_=mur[:, :, 1:2], func=AF.Sqrt, bias=cx.epsc[:, 0:1], scale=1.0),
             r=["mur", "epsc"], w=["mur"])
        p.op("dve", lambda e: e.reciprocal(out=mur[:, :, 1:2], in_=mur[:, :, 1:2]), r=["mur"], w=["mur"])
        p.op("dve", lambda e: e.scalar_tensor_tensor(out=nmr[:, :, :], in0=mur[:, :, 0:1], scalar=-1.0, in1=mur[:, :, 1:2],
                                                     op0=ALU.mult, op1=ALU.mult), r=["mur"], w=["nmr"])
        for g in range(4):
            def cons_u(m, ti, t0, n, ps, pk, g=g):
                p.op("act", lambda e, o=U[:, m, t0:t0 + n], i=ps[:, 0:n], b=vec[:, 56 + g * 4 + m:57 + g * 4 + m]:
                     e.activation(out=o, in_=i, func=AF.Gelu_apprx_tanh, bias=b), r=[pk, "const"], w=[("U", ti)])
            proj(cx, d_sgin, 8, g * 512, 512, XN, "XN", tiles, cons_u, group_cols=256)
            halves = []
            for hh in range(2):
                halves.append(cx.wload(d_sgin[:, 2 * D + g * 512 + hh * 256:2 * D + g * 512 + (hh + 1) * 256], 8, 256))
            for r_i in range(3):
                p.op("sp", lambda e, o=rows[:, r_i, :], i=d_sgrow[r_i][:, g * 512:(g + 1) * 512]: e.dma_start(out=o, in_=i),
                     r=["scr"], w=["rows"], dsem="rows")
            p.op("act", lambda e: e.activation(out=rowb[0:1, :], in_=rows[0:1, 0, :], func=AF.Copy), r=["rows"], w=["rowb"])
            for bi_, (t0, n) in enumerate(blocks):
                ti = min(t0 // 512, nt - 1) if t0 < NP else nt - 1
                samp = t0 >= NP
                ps, pk = cx.psum()
                for hh in range(2):
                    wv_, wvk = halves[hh]
                    for kc in range(8):
                        p.op("pe", lambda e, o=ps[0:n, hh * 256:(hh + 1) * 256], l=XN[:, kc, t0:t0 + n], r_=wv_[:, kc, :],
                             s=(kc == 0): e.matmul(o, lhsT=l, rhs=r_, start=s, stop=False),
                             r=[wvk, ("XN", ti)], w=[pk])
                    p.op("pe", lambda e, o=ps[0:n, hh * 256:(hh + 1) * 256], l=cx.ones[0:1, 0:n], r_=rowb[0:1, hh * 256:(hh + 1) * 256]:
                         e.matmul(o, lhsT=l, rhs=r_, start=False, stop=True), r=["ones", "rowb"], w=[pk])
                zv = zvs[bi_ % 2]
                zq = zqs[bi_ % 2]
                zk = ("zv", bi_ % 2)
                qk = ("zq", bi_ % 2)
                gelu_tile(cx, ps[0:n, :], zv[0:n, :], None, [pk], [zk])
                p.op("act", lambda e, o=zv[0:n, :], nm_=nmr[0:n, bi_, :], rs_=mur[0:n, bi_, 1:2]:
                     e.activation(out=o, in_=o, func=AF.Identity, bias=nm_, scale=rs_), r=[zk, "mur", "nmr"], w=[zk])
                p.op("dve", lambda e, o=zv[0:n, :], a=rows[0:n, 1, :]: e.tensor_tensor(out=o, in0=o, in1=a, op=ALU.mult), r=[zk, "rows"], w=[zk])
                if samp:
                    p.op("dve", lambda e, o=zq[0:n, :], a=zv[0:n, :], b=rows[0:n, 2, :]: e.tensor_tensor(out=o, in0=a, in1=b, op=ALU.add),
                         r=[zk, "rows"], w=[qk])
                    cx.store_s(o_sgv[:, g * 512:(g + 1) * 512], zq[0:n, :], qk, "sgv")
                vb = vnb[bi_ % 2]
                p.op("dve", lambda e, o=vb[0:n, :], a=zv[0:n, :], b=rows[0:n, 2, :]: e.tensor_tensor(out=o, in0=a, in1=b, op=ALU.add),
                     r=[zk, "rows"], w=[("vnb", bi_ % 2)])
                wi = 1 if samp else 0
                for cc in range(4):
                    ps2, pk2 = cx.psum()
                    p.op("pe", lambda e, o=ps2[:, 0:n], l=vb[0:n, cc * 128:(cc + 1) * 128], r_=wsb[0:n, wi, g, 0:n]:
                         e.matmul(o, lhsT=l, rhs=r_, start=True, stop=True), r=[("vnb", bi_ % 2), "wsb"], w=[pk2])
                    sst = ssts[cc]
                    p.op("dve", lambda e, o=sst[:, 0:n], a=ps2[:, 0:n], b=bsr[:, wi, g, 0:n]: e.tensor_tensor(out=o, in0=a, in1=b, op=ALU.add),
                         r=[pk2, "bsr"], w=[("sst", cc)])
                    p.op("dve", lambda e, o=U[:, cc, t0:t0 + n], a=sst[:, 0:n]: e.tensor_tensor(out=o, in0=o, in1=a, op=ALU.mult),
                         r=[("sst", cc), ("U", ti)], w=[("U", ti)])
            for half in range(2):
                if half == 1:
                    wo_, wok = cx.wload(d_sgout[g * 512:(g + 1) * 512, 512:1024], 4, 512)
                else:
                    wo_, wok = cx.wload(d_sgout[g * 512:(g + 1) * 512, 0:512], 4, 512)
                for mm in range(4):
                    m = half * 4 + mm
                    for ti, (t0, n) in enumerate(tiles):
                        ps, pk = cx.psum()
                        for kk in range(4):
                            p.op("pe", lambda e, o=ps[:, 0:n], l=wo_[:, kk, mm * 128:(mm + 1) * 128], r_=U[:, kk, t0:t0 + n],
                                 s=(kk == 0), t=(kk == 3): e.matmul(o, lhsT=l, rhs=r_, start=s, stop=t), r=[wok, ("U", ti)], w=[pk])
                        p.op("dve", lambda e, o=X[:, m, t0:t0 + n], i=ps[:, 0:n]:
                             e.tensor_tensor(out=o, in0=i, in1=o, op=ALU.add), r=[pk, ("X", ti)], w=[("X", ti)])
        run_ffn(2, 40)
        cx.new_scope()
        yst = [cx.alloc("yst%d" % i, [128, 512], F32) for i in range(4)]
        ycnt = [0]

        def y_out(kc, ti, t0, n):
            s_ = ycnt[0] % 4
            ycnt[0] += 1
            y_out.last = (s_, kc, t0, n)
            return yst[s_][:, 0:n], ("yst", s_)
        p_op_orig = p.op

        def hooked(eng, emit, r=(), w=(), dsem=None):
            ins = p_op_orig(eng, emit, r=r, w=w, dsem=dsem)
            if eng == "dve" and len(w) == 1 and isinstance(w[0], tuple) and w[0][0] == "yst":
                s_, kc, t0, n = y_out.last
                p_op_orig("sp", lambda e, o=o_y[kc * 128:(kc + 1) * 128, t0:t0 + n], i=yst[s_][:, 0:n]: e.dma_start(out=o, in_=i),
                          r=[("yst", s_), "scr"], dsem=("yst", s_))
            return ins
        p.op = hooked
        rmsnorm(cx, X, vec[:, 48:56], tiles, y_out, "nfin")
        p.op = p_op_orig
        cx.p.emit_all(stack)
    return nc


def run_C(inp, x1p, x1s, op, os_, own, ns, n_cores, seq_of_core, nc_cache={}, debug=None):
    key = (own, ns, debug)
    if key not in nc_cache:
        nc_cache[key] = build_C(own, ns, debug)
    nc = nc_cache[key]
    f = lambda a: np.asarray(a, np.float32)
    vec = np.concatenate([
        lay_cols(inp["norm_ffn"][1]), lay_cols(inp["norm_mix"][2]), lay_cols(inp["pl_scale"]),
        lay_cols(inp["norm_ffn"][2]), lay_cols(inp["norm_mix"][3]), lay_cols(inp["norm_ffn"][3]),
        lay_cols(inp["norm_final"]), lay_cols(inp["sg_b_in"])], axis=1)
    fdw = np.ascontiguousarray(f(inp["ff_w_dw"])[1:4].transpose(0, 2, 1).reshape(3, NFF, 128, 3).transpose(0, 2, 1, 3))
    fb = np.stack([lay_cols(inp["ff_b_dw"][i]) for i in (1, 2, 3)])
    ws = f(inp["sg_w_s"])
    wst_p = np.ascontiguousarray(ws.transpose(2, 0, 1))
    wst_s = np.zeros((128, 4, 128), np.float32)
    msk_p = (np.arange(128)[:, None] <= np.arange(128)[None, :]).astype(np.float32)
    msk_s = np.zeros((128, 128), np.float32)
    for b in range(16):
        wst_s[b * 8:(b + 1) * 8, :, b * 8:(b + 1) * 8] = ws[:, :8, :8].transpose(2, 0, 1)
        msk_s[b * 8:(b + 1) * 8, b * 8:(b + 1) * 8] = msk_p[:8, :8]
    bs = f(inp["sg_b_s"])
    bs_p = np.broadcast_to(bs[None], (128, 4, 128))
    bs_s = np.broadcast_to(np.tile(bs[:, :8], (1, 16))[None], (128, 4, 128))
    sgrow = np.stack([np.broadcast_to(f(inp["sg_b_in"])[2 * D:][None], (128, 2 * D)),
                      np.broadcast_to(f(inp["sg_ln_g"])[None], (128, 2 * D)),
                      np.broadcast_to(f(inp["sg_ln_b"])[None], (128, 2 * D))]).astype(np.float32)
    in_maps = []
    for c in range(n_cores):
        b, h = seq_of_core(c)

        def seg(a, a_s):
            own_ = a[b, h * own:(h + 1) * own]
            halo = np.zeros((HC, D), np.float32) if h == 0 else a[b, h * own - HC:h * own]
            return np.ascontiguousarray(np.concatenate([halo, own_, a_s[c * ns:(c + 1) * ns].reshape(ns * 8, D)], 0).T)
        stp = f(inp["state_pool"][c * ns:(c + 1) * ns])
        stp = np.ascontiguousarray(stp.transpose(2, 0, 1).reshape(8, 128, ns, 15).transpose(1, 0, 2, 3))
        stf = f(inp["state_ffn"])[1:4, c * ns:(c + 1) * ns]
        stf = np.ascontiguousarray(stf.transpose(0, 3, 1, 2).reshape(3, NFF, 128, ns, 2).transpose(0, 2, 1, 3, 4))
        pos = h * own + np.arange(16)
        invc = np.stack([1.0 / np.minimum(w, pos + 1) for w in (2, 4, 8, 16)]).astype(np.float32)
        in_maps.append({
            "x1T": seg(x1p, x1s), "oT": seg(op, os_), "hmask": np.full((128, 1), float(h), np.float32),
            "vecC": vec, "w_o": f(inp["da_w_o"]), "pl_w": f(inp["pl_w"]).reshape(D, 256),
            "stpool": stp, "invc": np.ascontiguousarray(np.broadcast_to(invc[None], (128, 4, 16))),
            "ff_dw": fdw, "ff_b": fb, "ff_st": stf,
            "ff_w_gate": f(inp["ff_w_gate"])[1:4], "ff_w_up": f(inp["ff_w_up"])[1:4], "ff_w_down": f(inp["ff_w_down"])[1:4],
            "sg_w_in": f(inp["sg_w_in"]), "sg_w_out": f(inp["sg_w_out"]), "sg_rows": sgrow,
            "sg_wsT": np.stack([wst_p, wst_s]), "sg_mask": np.stack([msk_p, msk_s]),
            "sg_bs": np.ascontiguousarray(np.stack([bs_p, bs_s])),
        })
    res = run_bass_kernel_spmd(nc, in_maps, core_ids=list(range(n_cores)))
    return res.results


_B_CACHE = {}


def run_B(inp, q_p, k_p, v_p, q_s, k_s, v_s, n_heads=8):
    f = lambda a: np.asarray(a, np.float32)
    nseq, S, _ = q_p.shape
    nss = q_s.shape[0]
    ck_all = f(inp["cache_k"])
    cv_all = f(inp["cache_v"])
    n_phys = ck_all.shape[0]
    ptab = np.asarray(inp["page_table"], np.int32)
    npg = ptab.shape[1]
    key = (nseq, S, nss, npg, n_phys)
    if key not in _B_CACHE:
        _B_CACHE[key] = build_B(nseq, S, nss, npg, n_phys)
    nc = _B_CACHE[key]
    masks, smask = make_masks()
    lamp = np.concatenate([f(inp["da_lq1"]), f(inp["da_lk1"]), f(inp["da_lq2"]), f(inp["da_lk2"])]).reshape(1, 256)
    assert npg == 16
    ptabr = np.ascontiguousarray(np.repeat(ptab.T, 8, axis=0))
    iota = (np.arange(128) % 8).astype(np.float32).reshape(128, 1)
    ident = np.eye(128, dtype=np.float32)
    ng = f(inp["da_norm_g"])
    in_maps = []
    for c in range(n_heads):
        hs = slice(c * 128, (c + 1) * 128)
        in_maps.append({
            "qT": np.ascontiguousarray(q_p[:, :, hs].transpose(0, 2, 1)),
            "kT": np.ascontiguousarray(k_p[:, :, hs].transpose(0, 2, 1)),
            "v": np.ascontiguousarray(v_p[:, :, hs]),
            "qsT": np.ascontiguousarray(q_s.reshape(nss * 8, -1)[:, hs].T),
            "ksT": np.ascontiguousarray(k_s.reshape(nss * 8, -1)[:, hs].T),
            "vs": np.ascontiguousarray(v_s.reshape(nss * 8, -1)[:, hs]),
            "ck": np.ascontiguousarray(ck_all[:, :, c, :]),
            "cv": np.ascontiguousarray(cv_all[:, :, c, :]),
            "ptabr": ptabr, "iota": iota, "ident": ident, "lamp": lamp,
            "gcol": np.ascontiguousarray(ng[hs].reshape(128, 1)),
            "grow": np.ascontiguousarray(np.broadcast_to(ng[hs].reshape(1, 128), (8, 128))),
            "masks": masks, "smask": smask,
        })
    res = run_bass_kernel_spmd(nc, in_maps, core_ids=list(range(n_heads))).results
    op = np.empty((nseq, S, n_heads * 128), np.float32)
    os_ = np.empty((nss, 8, n_heads * 128), np.float32)
    for c in range(n_heads):
        hs = slice(c * 128, (c + 1) * 128)
        op[:, :, hs] = res[c]["oT"].transpose(0, 2, 1)
        os_[:, :, hs] = res[c]["os"].reshape(nss, 8, 128)
    return op, os_


def kernel(**inp):
    f = lambda a: np.asarray(a, np.float32)
    xp = f(inp["x_prompt"])
    xs = f(inp["x_sample"])
    Bn, S, _ = xp.shape
    NSS = xs.shape[0]
    n_cores = 8
    own = S * Bn // n_cores
    ns = NSS // n_cores
    halves = S // own

    def seq_of(c):
        return (c // halves, c % halves)

    rA = run_A(inp, own, ns, n_cores, seq_of)
    x1p = np.empty((Bn, S, D), np.float32)
    x1s = np.empty((NSS, 8, D), np.float32)
    qkv_p = np.empty((Bn, S, 3 * D), np.float32)
    qkv_s = np.empty((NSS, 8, 3 * D), np.float32)
    conv_p = np.empty((Bn, 30, D), np.float32)
    conv_s = np.empty((NSS, 30, D), np.float32)
    ffn_p = np.empty((4, Bn, 2, DFF), np.float32)
    ffn_s = np.empty((4, NSS, 2, DFF), np.float32)

    def put_ffn(li, c, b, h, ff):
        if h == halves - 1:
            ffn_p[li, b] = ff[:, :, :2].transpose(2, 1, 0).reshape(2, DFF)
        ffn_s[li, c * ns:(c + 1) * ns] = ff[:, :, 2:].reshape(128, NFF, ns, 2).transpose(2, 3, 1, 0).reshape(ns, 2, DFF)

    for c in range(n_cores):
        b, h = seq_of(c)
        r = rA[c]
        x1 = r["x1T"].T
        x1p[b, h * own:(h + 1) * own] = x1[HA:HA + own]
        x1s[c * ns:(c + 1) * ns] = x1[HA + own:].reshape(ns, 8, D)
        q = r["qkvT"].T
        qkv_p[b, h * own:(h + 1) * own] = q[HA:HA + own]
        qkv_s[c * ns:(c + 1) * ns] = q[HA + own:].reshape(ns, 8, 3 * D)
        cv = r["convT"]
        if h == halves - 1:
            conv_p[b] = cv[:, :, :30].transpose(2, 1, 0).reshape(30, D)
        conv_s[c * ns:(c + 1) * ns] = cv[:, :, 30:].reshape(128, 8, ns, 30).transpose(2, 3, 1, 0).reshape(ns, 30, D)
        put_ffn(0, c, b, h, r["ffnT"])
    del rA
    k_rows_p = np.ascontiguousarray(qkv_p[:, :, D:2 * D]).reshape(Bn, S, 8, 128)
    v_rows_p = np.ascontiguousarray(qkv_p[:, :, 2 * D:]).reshape(Bn, S, 8, 128)
    k_rows_s = np.ascontiguousarray(qkv_s[:, :, D:2 * D]).reshape(NSS, 8, 8, 128)
    v_rows_s = np.ascontiguousarray(qkv_s[:, :, 2 * D:]).reshape(NSS, 8, 8, 128)

    op, os_ = run_B(inp, qkv_p[:, :, :D], qkv_p[:, :, D:2 * D], qkv_p[:, :, 2 * D:],
                    qkv_s[:, :, :D], qkv_s[:, :, D:2 * D], qkv_s[:, :, 2 * D:])

    rC = run_C(inp, x1p, x1s, op, os_, own, ns, n_cores, seq_of)
    y_p = np.empty((Bn, S, D), np.float32)
    y_s = np.empty((NSS, 8, D), np.float32)
    pool_p = np.empty((Bn, 15, D), np.float32)
    pool_s = np.empty((NSS, 15, D), np.float32)
    sgv = np.empty((NSS, 8, 2 * D), np.float32)
    for c in range(n_cores):
        b, h = seq_of(c)
        r = rC[c]
        y = r["yT"].T
        y_p[b, h * own:(h + 1) * own] = y[HC:HC + own]
        y_s[c * ns:(c + 1) * ns] = y[HC + own:].reshape(ns, 8, D)
        pl = r["poolT"]
        if h == halves - 1:
            pool_p[b] = pl[:, :, :15].transpose(2, 1, 0).reshape(15, D)
        pool_s[c * ns:(c + 1) * ns] = pl[:, :, 15:].reshape(128, 8, ns, 15).transpose(2, 3, 1, 0).reshape(ns, 15, D)
        sgv[c * ns:(c + 1) * ns] = r["sgv"].reshape(ns, 8, 2 * D)
        for li in range(3):
            put_ffn(li + 1, c, b, h, r["ffnT"][li])
    return (y_p, y_s, conv_p, conv_s, k_rows_p, v_rows_p, k_rows_s, v_rows_s, pool_p, pool_s, sgv, ffn_p, ffn_s)
```

```python
import math
from contextlib import ExitStack

import numpy as np
import concourse.bass as bass
import concourse.mybir as mybir
from concourse.bass_utils import run_bass_kernel_spmd

F32 = mybir.dt.float32
BF16 = mybir.dt.bfloat16
I32 = mybir.dt.int32
AF = mybir.ActivationFunctionType
ALU = mybir.AluOpType
AX = mybir.AxisListType

D = 1024
DFF = 2816
NFF = 22
EPS = 1e-6
ENGS = ("pe", "act", "dve", "pool", "sp")


class Ins:
    __slots__ = ("eng", "emit", "deps", "dsem", "dwaits", "signal", "cnt")

    def __init__(self, eng, emit, dsem):
        self.eng = eng
        self.emit = emit
        self.dsem = dsem
        self.deps = []
        self.dwaits = {}
        self.signal = False
        self.cnt = 0


class Prog:
    def __init__(self, nc):
        self.nc = nc
        self.st = {e: [] for e in ENGS}
        self.lastw = {}
        self.rd = {}
        self.dcnt = {}

    def op(self, eng, emit, r=(), w=(), dsem=None):
        ins = Ins(eng, emit, dsem)
        deps = {}

        def need(d, kind):
            if d is None or d is ins:
                return
            if d.dsem is None and d.eng == eng and (eng == "pe" or kind == "WAR"):
                return
            deps[id(d)] = d

        for k in r:
            need(self.lastw.get(k), "RAW")
        for k in w:
            need(self.lastw.get(k), "WAW")
            for d in self.rd.get(k, {}).values():
                need(d, "WAR")
        ins.deps = list(deps.values())
        for d in ins.deps:
            if d.dsem is not None:
                ins.dwaits[d.dsem] = self.dcnt[d.dsem]
        rkey = eng if dsem is None else ("d", dsem)
        for k in r:
            self.rd.setdefault(k, {})[rkey] = ins
        for k in w:
            self.lastw[k] = ins
            self.rd[k] = {}
        if dsem is not None:
            self.dcnt[dsem] = self.dcnt.get(dsem, 0) + 16
        self.st[eng].append(ins)
        return ins

    def emit_all(self, stack):
        nc = self.nc
        for e in ENGS:
            for ins in self.st[e]:
                for d in ins.deps:
                    if d.dsem is None:
                        d.signal = True
        for e in ENGS:
            c = 0
            for ins in self.st[e]:
                if ins.dsem is None and ins.signal:
                    c += 1
                ins.cnt = c
        esem = {e: stack.enter_context(nc.semaphore("e_" + e)) for e in ENGS if e != "sp"}
        dsem = {k: stack.enter_context(nc.semaphore("d_%s" % str(k))) for k in self.dcnt}
        block = stack.enter_context(nc.Block())
        final = dict(self.dcnt)

        def run(e, eng):
            waited = {}
            for ins in self.st[e]:
                waits = {}
                for d in ins.deps:
                    if d.dsem is None:
                        key, val = ("e", d.eng), d.cnt
                    else:
                        key, val = ("d", d.dsem), ins.dwaits[d.dsem]
                    if waits.get(key, 0) < val:
                        waits[key] = val
                for key, val in waits.items():
                    if waited.get(key, 0) >= val:
                        continue
                    eng.wait_ge(esem[key[1]] if key[0] == "e" else dsem[key[1]], val)
                    waited[key] = val
                bi = ins.emit(eng)
                if ins.dsem is not None:
                    bi.then_inc(dsem[ins.dsem], 16)
                elif ins.signal:
                    bi.then_inc(esem[e], 1)
            if e == "sp":
                for k, v in final.items():
                    eng.wait_ge(dsem[k], v)

        block.tensor(lambda eng: run("pe", eng))
        block.scalar(lambda eng: run("act", eng))
        block.vector(lambda eng: run("dve", eng))
        block.gpsimd(lambda eng: run("pool", eng))
        block.sync(lambda eng: run("sp", eng))


def split_tiles(n, maxn=512):
    out = []
    t = 0
    while t < n:
        m = min(maxn, n - t)
        out.append((t, m))
        t += m
    return out


class Ctx:
    def __init__(self, nc, stack, n_wslots=4, wslot_elems=4096):
        self.nc = nc
        self.stack = stack
        self.p = Prog(nc)
        self.ps = [stack.enter_context(nc.psum_tensor("ps%d" % i, [128, 512], F32)) for i in range(8)]
        self.ps_i = 0
        self.wslots = [stack.enter_context(nc.sbuf_tensor("wslot%d" % i, [128, wslot_elems], BF16))
                       for i in range(n_wslots)]
        self.w_i = 0
        self.ones = self.sb("ones_bf", [128, 128], BF16)
        self.p.op("pool", lambda e: e.memset(self.ones[:], 1.0), w=["ones"])
        self.epsc = self.sb("epsc", [128, 1], F32)
        self.p.op("pool", lambda e: e.memset(self.epsc[:], EPS), w=["epsc"])
        self.uid = 0
        self.scr = None

    def sq_next(self):
        self.sq_i = (getattr(self, "sq_i", -1) + 1) % len(self.scr_sq)
        return self.sq_i

    def sb(self, name, shape, dt):
        return self.stack.enter_context(self.nc.sbuf_tensor("sb_" + name, shape, dt))

    def alloc(self, name, shape, dt):
        if self.scr is None:
            return self.sb(name, shape, dt)
        n = 1
        for d_ in shape[1:]:
            n *= d_
        nwords = (n * (4 if dt == F32 or dt == I32 else 2) + 3) // 4
        nwords = (nwords + 7) // 8 * 8
        assert self.scr_off + nwords <= self.scr_words, ("scratch overflow", name, self.scr_off, nwords, self.scr_words)
        v = self.scr[:, self.scr_off:self.scr_off + nwords]
        self.scr_off += nwords
        if dt != F32:
            v = v.bitcast(dt)
        v = v[:, 0:n]
        if len(shape) == 3:
            v = v.rearrange("p (a b) -> p a b", a=shape[1])
        elif len(shape) == 4:
            v = v.rearrange("p (a b c) -> p a b c", a=shape[1], b=shape[2])
        return v

    def scratch_init(self, words):
        self.scr = self.sb("scratch", [128, words], F32)
        self.scr_words = words
        self.scr_off = 0
        self.dmy = {e: self.sb("dmy_" + e, [128, 4], F32) for e in ("act", "dve", "pool")}

    def new_scope(self):
        p = self.p
        self.scr_off = 0
        self.nbar = getattr(self, "nbar", 0) + 1
        b = self.nbar
        p.op("act", lambda e: e.activation(out=self.dmy["act"][:, 0:1], in_=self.epsc[:, 0:1], func=AF.Copy),
             r=["epsc"], w=[("bar", b, "act")])
        p.op("dve", lambda e: e.memset(self.dmy["dve"][:, 0:1], 0.0), w=[("bar", b, "dve")])
        p.op("pool", lambda e: e.memset(self.dmy["pool"][:, 0:1], 0.0), w=[("bar", b, "pool")])
        allb = [("bar", b, e_) for e_ in ("act", "dve", "pool")]
        p.op("act", lambda e: e.activation(out=self.dmy["act"][:, 1:2], in_=self.epsc[:, 0:1], func=AF.Copy),
             r=["epsc"] + allb, w=[("bar2", b, "act")])
        p.op("dve", lambda e: e.memset(self.dmy["dve"][:, 1:2], 0.0), r=allb, w=[("bar2", b, "dve")])
        p.op("pool", lambda e: e.memset(self.dmy["pool"][:, 1:2], 0.0), r=allb, w=[("bar2", b, "pool"), "scr"])

    def load_s(self, dst, src, key, dsem, eng="sp"):
        self.p.op(eng, lambda e, o=dst, i=src: e.dma_start(out=o, in_=i), r=["scr"], w=[key], dsem=dsem)

    def store_s(self, dst, src, key, dsem, eng="sp"):
        self.p.op(eng, lambda e, o=dst, i=src: e.dma_start(out=o, in_=i), r=[key, "scr"], dsem=dsem)

    def dram_in(self, name, shape, dt=F32):
        return self.nc.dram_tensor(name, list(shape), dt, kind="ExternalInput").ap()

    def dram_out(self, name, shape, dt=F32):
        return self.nc.dram_tensor(name, list(shape), dt, kind="ExternalOutput").ap()

    def psum(self, exclude=0):
        i = self.ps_i
        self.ps_i = (self.ps_i + 1) % (8 - exclude)
        return self.ps[exclude + i], ("ps", exclude + i)

    def wload(self, src_ap, kc, ncols):
        s = self.w_i
        self.w_i = (self.w_i + 1) % len(self.wslots)
        view = self.wslots[s][:, 0:kc * ncols].rearrange("p (k n) -> p k n", k=kc)
        src = src_ap.rearrange("(k p) n -> p k n", p=128)
        self.p.op("pool", lambda e, o=view, i=src: e.dma_start(out=o, in_=i), w=[("w", s)], dsem=("w", s))
        return view, ("w", s)

    def load(self, dst, src, key, dsem, eng="sp"):
        self.p.op(eng, lambda e, o=dst, i=src: e.dma_start(out=o, in_=i), w=[key], dsem=dsem)

    def store(self, dst, src, key, dsem, eng="sp"):
        self.p.op(eng, lambda e, o=dst, i=src: e.dma_start(out=o, in_=i), r=[key], dsem=dsem)

    def const_cols(self, name, dram_ap, ncols):
        t = self.sb(name, [128, ncols], F32)
        self.load(t[:], dram_ap, name, "const")
        return t


def rmsnorm(cx, X, gcol, tiles, out_fn, tag, nch=8, dim=D):
    p = cx.p
    sq = cx.scr_sq
    R = cx.scr_r
    for ti, (t0, n) in enumerate(tiles):
        par = ti % 2
        ps, pk = cx.psum()
        for kc in range(nch):
            sqi = cx.sq_next()
            p.op("act", lambda e, o=sq[sqi][:, 0:n], i=X[:, kc, t0:t0 + n]:
                 e.activation(out=o, in_=i, func=AF.Square), r=[("X", ti)], w=[("sq", sqi)])
            p.op("pe", lambda e, o=ps[:, 0:n], r_=sq[sqi][:, 0:n], s=(kc == 0), t=(kc == nch - 1):
                 e.matmul(o, lhsT=cx.ones[:], rhs=r_, start=s, stop=t), r=[("sq", sqi), "ones"], w=[pk])
        p.op("act", lambda e, o=R[par][:, 0:n], i=ps[:, 0:n]:
             e.activation(out=o, in_=i, func=AF.Sqrt, bias=cx.epsc[:, 0:1], scale=1.0 / dim),
             r=[pk, "epsc"], w=[("R", par)])
        p.op("dve", lambda e, o=R[par][:, 0:n]: e.reciprocal(out=o, in_=o),
             r=[("R", par)], w=[("R", par)])
        for kc in range(nch):
            o_ap, wk = out_fn(kc, ti, t0, n)
            p.op("dve", lambda e, o=o_ap, i=X[:, kc, t0:t0 + n], g=gcol[:, kc:kc + 1], r_=R[par][:, 0:n]:
                 e.scalar_tensor_tensor(out=o, in0=i, scalar=g, in1=r_, op0=ALU.mult, op1=ALU.mult),
                 r=[("X", ti), ("R", par), gcol_key(gcol)], w=[wk])


_gk = {}


def gcol_key(t):
    return _gk.get(id(t), "const")


def xn_out(cx):
    def f(kc, ti, t0, n):
        return cx.XN[:, kc, t0:t0 + n], ("XN", ti)
    return f


def proj(cx, W, kc_n, col0, ncols_total, src, src_key, tiles, consume, group_cols=None):
    p = cx.p
    if group_cols is None:
        group_cols = max(128, (4096 // kc_n) // 128 * 128)
    c = 0
    while c < ncols_total:
        gc = min(group_cols, ncols_total - c)
        wv, wk = cx.wload(W[:, col0 + c:col0 + c + gc], kc_n, gc)
        for mm in range(gc // 128):
            for ti, (t0, n) in enumerate(tiles):
                ps, pk = cx.psum()
                for kc in range(kc_n):
                    p.op("pe", lambda e, o=ps[:, 0:n], l=wv[:, kc, mm * 128:(mm + 1) * 128],
                         r_=src[:, kc, t0:t0 + n], s=(kc == 0), t=(kc == kc_n - 1):
                         e.matmul(o, lhsT=l, rhs=r_, start=s, stop=t),
                         r=[wk, (src_key, ti)], w=[pk])
                consume(c // 128 + mm, ti, t0, n, ps, pk)
        c += gc


def conv_ffn(cx, li, W, tiles, np_tok, ns, halo, part=3):
    p = cx.p
    X, XN = cx.X, cx.XN
    p_dw = cx.ffn_dw[li]
    p_b = cx.ffn_b[li]
    stf = cx.ffn_state[li]
    outst = cx.ffn_out[li]
    j = 0
    parts = []
    while j < NFF:
        parts.append((j, min(part, NFF - j)))
        j += part
    for (j0, nj) in parts:
        wg, wgk = cx.wload(W["gate"][:, j0 * 128:(j0 + nj) * 128], 8, nj * 128)
        wu, wuk = cx.wload(W["up"][:, j0 * 128:(j0 + nj) * 128], 8, nj * 128)
        wd, wdk = cx.wload(W["down"][j0 * 128:(j0 + nj) * 128, :], nj, D)
        for jj in range(nj):
            jg = j0 + jj
            Gs = cx.Gs[jg % 2]
            p.op("pool", lambda e, o=Gs[:, :, 0:2], i=stf[:, jg, :, :]: e.tensor_copy(out=o, in_=i),
                 r=[("stf", li)], w=[("Gs", jg % 2)])
            prev = None
            for ti, (t0, n) in enumerate(tiles):
                samp = t0 >= np_tok
                psg, pgk = cx.psum()
                for kc in range(8):
                    p.op("pe", lambda e, o=psg[:, 0:n], l=wg[:, kc, jj * 128:(jj + 1) * 128],
                         r_=XN[:, kc, t0:t0 + n], s=(kc == 0), t=(kc == 7):
                         e.matmul(o, lhsT=l, rhs=r_, start=s, stop=t), r=[wgk, ("XN", ti)], w=[pgk])
                psu, puk = cx.psum()
                for kc in range(8):
                    p.op("pe", lambda e, o=psu[:, 0:n], l=wu[:, kc, jj * 128:(jj + 1) * 128],
                         r_=XN[:, kc, t0:t0 + n], s=(kc == 0), t=(kc == 7):
                         e.matmul(o, lhsT=l, rhs=r_, start=s, stop=t), r=[wuk, ("XN", ti)], w=[puk])
                cx.gt_i = (cx.gt_i + 1) % 2
                gp = cx.gt_i
                acc = cx.facc[gp]
                sil = cx.fsil[gp]
                if not samp:
                    Gt = cx.gt[gp]
                    gk = ("gt", gp)
                    if prev is None:
                        p.op("pool", lambda e, o=Gt[:, 0:2]: e.memset(o, 0.0), w=[gk])
                    else:
                        pg, pn = prev
                        p.op("pool", lambda e, o=Gt[:, 0:2], i=cx.gt[pg][:, pn:pn + 2]: e.tensor_copy(out=o, in_=i),
                             r=[("gt", pg)], w=[gk])
                    p.op("act", lambda e, o=Gt[:, 2:2 + n], i=psg[:, 0:n]:
                         e.activation(out=o, in_=i, func=AF.Copy), r=[pgk], w=[gk])
                    if ti == 0 and halo > 0:
                        assert halo <= n
                        p.op("dve", lambda e, o=Gt[:, 2:2 + halo]:
                             e.tensor_scalar(out=o, in0=o, scalar1=cx.hmask[:, 0:1], scalar2=None, op0=ALU.mult),
                             r=[gk, "const"], w=[gk])
                    prev = (gp, n)
                    v0 = Gt[:, 0:n]
                    v1 = Gt[:, 1:1 + n]
                    v2 = Gt[:, 2:2 + n]
                    a_ = acc[:, 0:n]
                    s_ = sil[:, 0:n]
                    u_ = psu[:, 0:n]
                    h_ = cx.Hb[:, jj, t0:t0 + n]
                    if t0 + n == np_tok:
                        p.op("pool", lambda e, o=outst[:, jg, 0:2], i=Gt[:, n:n + 2]: e.tensor_copy(out=o, in_=i),
                             r=[gk], w=[("ffo", li)])
                else:
                    gk = ("Gs", jg % 2)
                    p.op("act", lambda e, o=Gs[:, :, 2:10], i=psg[:, 0:n].rearrange("p (s t) -> p s t", t=8):
                         e.activation(out=o, in_=i, func=AF.Copy), r=[pgk], w=[gk])
                    v0 = Gs[:, :, 0:8]
                    v1 = Gs[:, :, 1:9]
                    v2 = Gs[:, :, 2:10]
                    a_ = acc[:, 0:n].rearrange("p (s t) -> p s t", t=8)
                    s_ = sil[:, 0:n].rearrange("p (s t) -> p s t", t=8)
                    u_ = psu[:, 0:n].rearrange("p (s t) -> p s t", t=8)
                    h_ = cx.Hb[:, jj, t0:t0 + n].rearrange("p (s t) -> p s t", t=8)
                    p.op("pool", lambda e, o=outst[:, jg, 2:2 + 2 * ns].rearrange("p (s t) -> p s t", t=2),
                         i=Gs[:, :, 8:10]: e.tensor_copy(out=o, in_=i), r=[gk], w=[("ffo", li)])
                ak = ("facc", gp)
                sk = ("fsil", gp)
                p.op("act", lambda e, o=a_, i=v0, sc=p_dw[:, jg, 0:1], b=p_b[:, jg:jg + 1]:
                     e.activation(out=o, in_=i, func=AF.Identity, bias=b, scale=sc),
                     r=[gk, "const"], w=[ak])
                p.op("dve", lambda e, o=a_, i=v1, sc=p_dw[:, jg, 1:2]:
                     e.scalar_tensor_tensor(out=o, in0=i, scalar=sc, in1=o, op0=ALU.mult, op1=ALU.add),
                     r=[gk, ak, "const"], w=[ak])
                p.op("dve", lambda e, o=a_, i=v2, sc=p_dw[:, jg, 2:3]:
                     e.scalar_tensor_tensor(out=o, in0=i, scalar=sc, in1=o, op0=ALU.mult, op1=ALU.add),
                     r=[gk, ak, "const"], w=[ak])
                p.op("act", lambda e, o=s_, i=a_: e.activation(out=o, in_=i, func=AF.Silu), r=[ak], w=[sk])
                p.op("dve", lambda e, o=h_, a=s_, b=u_: e.tensor_tensor(out=o, in0=a, in1=b, op=ALU.mult),
                     r=[sk, puk], w=[("Hb", ti)])
        for m in range(8):
            for ti, (t0, n) in enumerate(tiles):
                ps, pk = cx.psum()
                for jj in range(nj):
                    p.op("pe", lambda e, o=ps[:, 0:n], l=wd[:, jj, m * 128:(m + 1) * 128],
                         r_=cx.Hb[:, jj, t0:t0 + n], s=(jj == 0), t=(jj == nj - 1):
                         e.matmul(o, lhsT=l, rhs=r_, start=s, stop=t), r=[wdk, ("Hb", ti)], w=[pk])
                p.op("dve", lambda e, o=X[:, m, t0:t0 + n], i=ps[:, 0:n]:
                     e.tensor_tensor(out=o, in0=i, in1=o, op=ALU.add), r=[pk, ("X", ti)], w=[("X", ti)])


def ffn_setup(cx, n_layers, T, ns, dw_d, b_d, st_d, part=3):
    cx.ffn_dw, cx.ffn_b, cx.ffn_state, cx.ffn_out = [], [], [], []
    for li in range(n_layers):
        t = cx.alloc("ffdw%d" % li, [128, NFF, 3], F32)
        cx.load_s(t[:], dw_d[li], "const", "const")
        cx.ffn_dw.append(t)
        t = cx.alloc("ffb%d" % li, [128, NFF], F32)
        cx.load_s(t[:], b_d[li], "const", "const")
        cx.ffn_b.append(t)
        t = cx.alloc("ffst%d" % li, [128, NFF, ns, 2], F32)
        cx.load_s(t[:], st_d[li], ("stf", li), "const")
        cx.ffn_state.append(t)
        cx.ffn_out.append(cx.alloc("ffo%d" % li, [128, NFF, 2 + 2 * ns], F32))
    cx.gt = [cx.alloc("gt%d" % i, [128, 514], F32) for i in range(2)]
    cx.gt_i = 0
    cx.Gs = [cx.alloc("Gs%d" % i, [128, ns, 10], F32) for i in range(2)]
    cx.facc = [cx.alloc("facc%d" % i, [128, 512], F32) for i in range(2)]
    cx.fsil = [cx.alloc("fsil%d" % i, [128, 512], F32) for i in range(2)]
    cx.Hb = cx.alloc("Hb", [128, part, T], BF16)


HA = 32


def build_A(own, ns):
    np_tok = HA + own
    T = np_tok + ns * 8
    nc = bass.Bass("TRN2", target_bir_lowering=False)
    stack = ExitStack()
    with stack:
        cx = Ctx(nc, stack, n_wslots=4, wslot_elems=3072)
        p = cx.p
        d_x = cx.dram_in("xT", [D, T])
        d_hm = cx.dram_in("hmask", [128, 1])
        d_stc = cx.dram_in("stconv", [128, 8, ns, 30])
        d_vec = cx.dram_in("vecA", [128, 72])
        d_wdw = cx.dram_in("cv_wdw", [128, 8, 31])
        d_ident = cx.dram_in("identA", [128, 128])
        d_win = cx.dram_in("cv_w_in", [D, 2 * D])
        d_wout = cx.dram_in("cv_w_out", [D, D])
        d_fdw = cx.dram_in("ff_dw", [1, 128, NFF, 3])
        d_fb = cx.dram_in("ff_b", [1, 128, NFF])
        d_fst = cx.dram_in("ff_st", [1, 128, NFF, ns, 2])
        d_wg = cx.dram_in("ff_w_gate", [D, DFF])
        d_wu = cx.dram_in("ff_w_up", [D, DFF])
        d_wd = cx.dram_in("ff_w_down", [DFF, D])
        d_wqkv = cx.dram_in("w_qkv", [D, 3 * D])
        o_x1 = cx.dram_out("x1T", [D, T])
        o_qkv = cx.dram_out("qkvT", [3 * D, T])
        o_conv = cx.dram_out("convT", [128, 8, 30 + ns * 30])
        o_ffn = cx.dram_out("ffnT", [128, NFF, 2 + 2 * ns])

        tiles = split_tiles(np_tok) + [(np_tok, ns * 8)]
        ptiles = tiles[:-1]
        cx.X = cx.sb("X", [128, 8, T], F32)
        cx.XN = cx.sb("XN", [128, 8, T], BF16)
        cx.scr_sq = [cx.sb("sq%d" % i, [128, 512], BF16) for i in range(4)]
        cx.scr_r = [cx.sb("R%d" % i, [128, 512], F32) for i in range(2)]
        cx.hmask = cx.const_cols("hmask", d_hm, 1)
        vec = cx.const_cols("vecA", d_vec, 72)
        X, XN = cx.X, cx.XN
        for ti, (t0, n) in enumerate(tiles):
            for kc in range(8):
                cx.load(X[:, kc, t0:t0 + n], d_x[kc * 128:(kc + 1) * 128, t0:t0 + n], ("X", ti), ("xin", ti % 2))
        wdw = cx.sb("wdw", [128, 8, 31], F32)
        cx.load(wdw[:], d_wdw, "const", "const")
        identb = cx.sb("identb", [128, 128], BF16)
        p.op("pool", lambda e: e.dma_start(out=identb[:], in_=d_ident), w=["identb"], dsem="const2")
        cx.scratch_init(17000)
        G0 = cx.alloc("G0", [128, 8, 30 + np_tok], BF16)
        GS = cx.alloc("GS", [128, 8, ns, 38], F32)
        cx.load_s(GS[:, :, :, 0:30], d_stc, "GS", "const_s")
        glast = cx.alloc("glast", [128, 8, 32], F32)
        for c in range(8):
            p.op("pool", lambda e, o=G0[:, c, 0:30]: e.memset(o, 0.0), w=[("G0", c)])

        rmsnorm(cx, X, vec[:, 0:8], tiles, xn_out(cx), "n0")
        s1 = [cx.alloc("s1_%d" % i, [128, 512], F32) for i in range(2)]
        for (c0, ncg) in ((0, 3), (3, 3), (6, 2)):
            wa, wak = cx.wload(d_win[:, c0 * 128:(c0 + ncg) * 128], 8, ncg * 128)
            wg_, wgk = cx.wload(d_win[:, D + c0 * 128:D + (c0 + ncg) * 128], 8, ncg * 128)
            for cc in range(ncg):
                c = c0 + cc
                for ti, (t0, n) in enumerate(tiles):
                    samp = t0 >= np_tok
                    psa, pak = cx.psum()
                    for kc in range(8):
                        p.op("pe", lambda e, o=psa[:, 0:n], l=wa[:, kc, cc * 128:(cc + 1) * 128],
                             r_=XN[:, kc, t0:t0 + n], s=(kc == 0), t=(kc == 7):
                             e.matmul(o, lhsT=l, rhs=r_, start=s, stop=t), r=[wak, ("XN", ti)], w=[pak])
                    psg, pgk = cx.psum()
                    for kc in range(8):
                        p.op("pe", lambda e, o=psg[:, 0:n], l=wg_[:, kc, cc * 128:(cc + 1) * 128],
                             r_=XN[:, kc, t0:t0 + n], s=(kc == 0), t=(kc == 7):
                             e.matmul(o, lhsT=l, rhs=r_, start=s, stop=t), r=[wgk, ("XN", ti)], w=[pgk])
                    sp_ = ti % 2
                    p.op("act", lambda e, o=s1[sp_][:, 0:n], i=psg[:, 0:n], b=vec[:, 8 + 8 + c:8 + 8 + c + 1]:
                         e.activation(out=o, in_=i, func=AF.Sigmoid, bias=b), r=[pgk, "const"], w=[("s1", sp_)])
                    if not samp:
                        p.op("dve", lambda e, o=G0[:, c, 30 + t0:30 + t0 + n], i=psa[:, 0:n],
                             b=vec[:, 8 + c:8 + c + 1], s=s1[sp_][:, 0:n]:
                             e.scalar_tensor_tensor(out=o, in0=i, scalar=b, in1=s, op0=ALU.add, op1=ALU.mult),
                             r=[pak, ("s1", sp_), "const"], w=[("G0", c)])
                        if t0 + n == np_tok:
                            nl = min(n, 30)
                            p.op("dve", lambda e, o=glast[:, c, 30 - nl:30], i=psa[:, n - nl:n],
                                 b=vec[:, 8 + c:8 + c + 1], s=s1[sp_][:, n - nl:n]:
                                 e.scalar_tensor_tensor(out=o, in0=i, scalar=b, in1=s, op0=ALU.add, op1=ALU.mult),
                                 r=[pak, ("s1", sp_), "const"], w=["glast"])
                    else:
                        p.op("dve", lambda e, o=GS[:, c, :, 30:38], i=psa[:, 0:n].rearrange("p (s t) -> p s t", t=8),
                             b=vec[:, 8 + c:8 + c + 1], s=s1[sp_][:, 0:n].rearrange("p (s t) -> p s t", t=8):
                             e.scalar_tensor_tensor(out=o, in0=i, scalar=b, in1=s, op0=ALU.add, op1=ALU.mult),
                             r=[pak, ("s1", sp_), "const"], w=["GS"])
        assert ptiles[-1][1] >= 30
        cx.store_s(o_conv[:, :, 0:30], glast[:, :, 0:30], "glast", "outs")
        for c in range(8):
            cx.store_s(o_conv[:, c, 30:30 + ns * 30].rearrange("p (s t) -> p s t", t=30), GS[:, c, :, 8:38], "GS", "outs")
        accs = cx.alloc("caccs", [128, ns, 8], F32)
        DG = cx.alloc("DG", [128, 31, 128], BF16)
        for c in range(8):
            p.op("dve", lambda e, o=G0[:, c, 30:30 + HA]:
                 e.tensor_scalar(out=o, in0=o, scalar1=cx.hmask[:, 0:1], scalar2=None, op0=ALU.mult),
                 r=[("G0", c), "const"], w=[("G0", c)])
            for k in range(31):
                p.op("act", lambda e, o=DG[:, k, :], sc=wdw[:, c, k:k + 1]:
                     e.activation(out=o, in_=identb[:, :], func=AF.Copy, scale=sc), r=["identb", "const"], w=["DG"])
            for ti, (t0, n) in enumerate(ptiles):
                ps, pk = cx.psum()
                for k in range(31):
                    p.op("pe", lambda e, o=ps[:, 0:n], l=DG[:, k, :], r_=G0[:, c, t0 + k:t0 + k + n], s=(k == 0), t=(k == 30):
                         e.matmul(o, lhsT=l, rhs=r_, start=s, stop=t), r=["DG", ("G0", c)], w=[pk])
                p.op("act", lambda e, o=XN[:, c, t0:t0 + n], i=ps[:, 0:n], b=vec[:, 24 + c:25 + c]:
                     e.activation(out=o, in_=i, func=AF.Identity, bias=b), r=[pk, "const"], w=[("XN", ti)])
            for k in range(31):
                last = k == 30
                o_s = XN[:, c, np_tok:T].rearrange("p (s t) -> p s t", t=8) if last else accs[:, :, :]
                if k == 0:
                    p.op("dve", lambda e, o=o_s, i=GS[:, c, :, 0:8], sc=wdw[:, c, 0:1], b=vec[:, 24 + c:25 + c]:
                         e.tensor_scalar(out=o, in0=i, scalar1=sc, scalar2=b, op0=ALU.mult, op1=ALU.add),
                         r=["GS", "const"], w=["caccs"])
                else:
                    p.op("dve", lambda e, o=o_s, i=GS[:, c, :, k:k + 8], sc=wdw[:, c, k:k + 1], a=accs[:, :, :]:
                         e.scalar_tensor_tensor(out=o, in0=i, scalar=sc, in1=a, op0=ALU.mult, op1=ALU.add),
                         r=["GS", "caccs", "const"], w=["caccs"] + ([("XN", len(tiles) - 1)] if last else []))
        cx.new_scope()
        mu = [cx.alloc("mu%d" % i, [128, 512], F32) for i in range(2)]
        var = [cx.alloc("var%d" % i, [128, 512], F32) for i in range(2)]
        tmpc = [cx.alloc("tmpc%d" % i, [128, 512], F32) for i in range(2)]
        for ti, (t0, n) in enumerate(tiles):
            par = ti % 2
            ps1, pk1 = cx.psum()
            for kc in range(8):
                p.op("pe", lambda e, o=ps1[:, 0:n], r_=XN[:, kc, t0:t0 + n], s=(kc == 0), t=(kc == 7):
                     e.matmul(o, lhsT=cx.ones[:], rhs=r_, start=s, stop=t), r=[("XN", ti), "ones"], w=[pk1])
            ps2, pk2 = cx.psum()
            for kc in range(8):
                sqi = cx.sq_next()
                p.op("act", lambda e, o=cx.scr_sq[sqi][:, 0:n], i=XN[:, kc, t0:t0 + n]:
                     e.activation(out=o, in_=i, func=AF.Square), r=[("XN", ti)], w=[("sq", sqi)])
                p.op("pe", lambda e, o=ps2[:, 0:n], r_=cx.scr_sq[sqi][:, 0:n], s=(kc == 0), t=(kc == 7):
                     e.matmul(o, lhsT=cx.ones[:], rhs=r_, start=s, stop=t), r=[("sq", sqi), "ones"], w=[pk2])
            m_ = mu[par][:, 0:n]
            v_ = var[par][:, 0:n]
            p.op("dve", lambda e, o=m_, i=ps1[:, 0:n]:
                 e.tensor_scalar(out=o, in0=i, scalar1=1.0 / D, scalar2=None, op0=ALU.mult), r=[pk1], w=[("mu", par)])
            p.op("dve", lambda e, o=v_, a=m_: e.tensor_tensor(out=o, in0=a, in1=a, op=ALU.mult),
                 r=[("mu", par)], w=[("var", par)])
            p.op("dve", lambda e, o=v_, i=ps2[:, 0:n]:
                 e.scalar_tensor_tensor(out=o, in0=i, scalar=1.0 / D, in1=o, op0=ALU.mult, op1=ALU.subtract),
                 r=[pk2, ("var", par)], w=[("var", par)])
            p.op("act", lambda e, o=v_: e.activation(out=o, in_=o, func=AF.Sqrt, bias=cx.epsc[:, 0:1], scale=1.0),
                 r=[("var", par), "epsc"], w=[("var", par)])
            p.op("dve", lambda e, o=v_: e.reciprocal(out=o, in_=o), r=[("var", par)], w=[("var", par)])
            for kc in range(8):
                tp = (ti * 8 + kc) % 2
                t_ = tmpc[tp][:, 0:n]
                p.op("dve", lambda e, o=t_, a=XN[:, kc, t0:t0 + n], b=m_: e.tensor_tensor(out=o, in0=a, in1=b, op=ALU.subtract),
                     r=[("XN", ti), ("mu", par)], w=[("tmpc", tp)])
                p.op("dve", lambda e, o=t_, b=v_: e.tensor_tensor(out=o, in0=o, in1=b, op=ALU.mult),
                     r=[("tmpc", tp), ("var", par)], w=[("tmpc", tp)])
                p.op("act", lambda e, o=XN[:, kc, t0:t0 + n], i=t_, sc=vec[:, 32 + kc:33 + kc], b=vec[:, 40 + kc:41 + kc]:
                     e.activation(out=o, in_=i, func=AF.Silu, bias=b, scale=sc),
                     r=[("tmpc", tp), "const"], w=[("XN", ti)])
        def cons_out(m, ti, t0, n, ps, pk):
            p.op("dve", lambda e, o=X[:, m, t0:t0 + n], i=ps[:, 0:n], b=vec[:, 48 + m:49 + m]:
                 e.scalar_tensor_tensor(out=o, in0=i, scalar=b, in1=o, op0=ALU.add, op1=ALU.add),
                 r=[pk, ("X", ti), "const"], w=[("X", ti)])
        proj(cx, d_wout, 8, 0, D, XN, "XN", tiles, cons_out, group_cols=384)
        cx.new_scope()
        ffn_setup(cx, 1, T, ns, d_fdw, d_fb, d_fst)
        rmsnorm(cx, X, vec[:, 56:64], tiles, xn_out(cx), "nf0")
        conv_ffn(cx, 0, {"gate": d_wg, "up": d_wu, "down": d_wd}, tiles, np_tok, ns, HA)
        cx.store_s(o_ffn, cx.ffn_out[0], ("ffo", 0), "outs")
        for ti, (t0, n) in enumerate(tiles):
            for kc in range(8):
                cx.store(o_x1[kc * 128:(kc + 1) * 128, t0:t0 + n], X[:, kc, t0:t0 + n], ("X", ti), "outs")
        cx.new_scope()
        rmsnorm(cx, X, vec[:, 64:72], tiles, xn_out(cx), "n1")
        ost = [cx.alloc("ost%d" % i, [128, 512], F32) for i in range(3)]
        cnt = [0]

        def cons_qkv(m, ti, t0, n, ps, pk):
            s = cnt[0] % 3
            cnt[0] += 1
            p.op("act", lambda e, o=ost[s][:, 0:n], i=ps[:, 0:n]: e.activation(out=o, in_=i, func=AF.Copy),
                 r=[pk], w=[("ost", s)])
            cx.store_s(o_qkv[m * 128:(m + 1) * 128, t0:t0 + n], ost[s][:, 0:n], ("ost", s), ("ost", s))
        proj(cx, d_wqkv, 8, 0, 3 * D, XN, "XN", tiles, cons_qkv, group_cols=384)
        cx.p.emit_all(stack)
    return nc


def lay_cols(v):
    v = np.asarray(v, np.float32)
    return np.ascontiguousarray(v.reshape(-1, 128).T)


def run_A(inp, own, ns, n_cores, seq_of_core, nc_cache={}):
    key = (own, ns)
    if key not in nc_cache:
        nc_cache[key] = build_A(own, ns)
    nc = nc_cache[key]
    xp = np.asarray(inp["x_prompt"], np.float32)
    xs = np.asarray(inp["x_sample"], np.float32)
    vec = np.concatenate([
        lay_cols(inp["norm_mix"][0]), lay_cols(inp["cv_b_in"]), lay_cols(inp["cv_b_dw"]),
        lay_cols(inp["cv_ln_g"]), lay_cols(inp["cv_ln_b"]), lay_cols(inp["cv_b_out"]),
        lay_cols(inp["norm_ffn"][0]), lay_cols(inp["norm_mix"][1])], axis=1)
    wdw = np.ascontiguousarray(np.asarray(inp["cv_w_dw"], np.float32).T.reshape(8, 128, 31).transpose(1, 0, 2))
    fdw = np.ascontiguousarray(np.asarray(inp["ff_w_dw"][0], np.float32).T.reshape(NFF, 128, 3).transpose(1, 0, 2))[None]
    fb = lay_cols(inp["ff_b_dw"][0])[None]
    in_maps = []
    for c in range(n_cores):
        b, h = seq_of_core(c)
        seg = xp[b, h * own:(h + 1) * own]
        if h == 0:
            halo = np.zeros((HA, D), np.float32)
        else:
            halo = xp[b, h * own - HA:h * own]
        sm = xs[c * ns:(c + 1) * ns].reshape(ns * 8, D)
        xT = np.ascontiguousarray(np.concatenate([halo, seg, sm], 0).T)
        stc = np.asarray(inp["state_conv"][c * ns:(c + 1) * ns], np.float32)
        stc = np.ascontiguousarray(stc.transpose(2, 0, 1).reshape(8, 128, ns, 30).transpose(1, 0, 2, 3))
        stf = np.asarray(inp["state_ffn"][0, c * ns:(c + 1) * ns], np.float32)
        stf = np.ascontiguousarray(stf.transpose(2, 0, 1).reshape(NFF, 128, ns, 2).transpose(1, 0, 2, 3))[None]
        in_maps.append({
            "xT": xT, "hmask": np.full((128, 1), float(h), np.float32), "stconv": stc, "vecA": vec,
            "cv_wdw": wdw, "identA": np.eye(128, dtype=np.float32), "cv_w_in": np.asarray(inp["cv_w_in"], np.float32),
            "cv_w_out": np.asarray(inp["cv_w_out"], np.float32),
            "ff_dw": fdw, "ff_b": fb, "ff_st": stf,
            "ff_w_gate": np.asarray(inp["ff_w_gate"][0], np.float32),
            "ff_w_up": np.asarray(inp["ff_w_up"][0], np.float32),
            "ff_w_down": np.asarray(inp["ff_w_down"][0], np.float32),
            "w_qkv": np.asarray(inp["da_w_qkv"], np.float32),
        })
    res = run_bass_kernel_spmd(nc, in_maps, core_ids=list(range(n_cores)))
    return res.results


LAM_INIT = 0.8 - 0.6 * math.exp(-0.3 * 1)


def build_B(nseq, S, nss, npg, n_phys):
    nc = bass.Bass("TRN2", target_bir_lowering=False)
    stack = ExitStack()
    NG = S // 512
    NB = S // 128
    with stack:
        cx = Ctx(nc, stack, n_wslots=1, wslot_elems=64)
        p = cx.p
        d_q = cx.dram_in("qT", [nseq, 128, S])
        d_k = cx.dram_in("kT", [nseq, 128, S])
        d_v = cx.dram_in("v", [nseq, S, 128])
        d_qs = cx.dram_in("qsT", [128, nss * 8])
        d_ks = cx.dram_in("ksT", [128, nss * 8])
        d_vs = cx.dram_in("vs", [nss * 8, 128])
        d_ck = cx.dram_in("ck", [n_phys, 128, 128])
        d_cv = cx.dram_in("cv", [n_phys, 128, 128])
        d_tabr = cx.dram_in("ptabr", [128, nss], I32)
        d_iota = cx.dram_in("iota", [128, 1])
        d_ident = cx.dram_in("ident", [128, 128])
        d_lam = cx.dram_in("lamp", [1, 256])
        d_g = cx.dram_in("gcol", [128, 1])
        d_grow = cx.dram_in("grow", [8, 128])
        d_mask = cx.dram_in("masks", [128, 4, 512])
        d_smask = cx.dram_in("smask", [8, 8])
        o_p = cx.dram_out("oT", [nseq, 128, S])
        o_s = cx.dram_out("os", [nss * 8, 128])

        masks = cx.sb("masks", [128, 4, 512], BF16)
        p.op("pool", lambda e: e.dma_start(out=masks[:], in_=d_mask), w=["masks"], dsem="const2")
        smask = cx.sb("smask", [8, 8], F32)
        cx.load(smask[:], d_smask, "smask", "const")
        gcol = cx.sb("gcol", [128, 1], F32)
        cx.load(gcol[:], d_g, "gcol", "const")
        grow = cx.sb("grow", [8, 128], F32)
        cx.load(grow[:], d_grow, "grow", "const")
        lamp = cx.sb("lamp", [1, 256], F32)
        cx.load(lamp[:], d_lam, "lamp", "const")
        onesf = cx.sb("onesf", [1, 128], F32)
        p.op("pool", lambda e: e.memset(onesf[:], 1.0), w=["onesf"])
        lt = cx.sb("lt", [1, 128], F32)
        lsum = cx.sb("lsum", [1, 4], F32)
        p.op("dve", lambda e: e.tensor_tensor(out=lt[:, 0:64], in0=lamp[:, 0:64], in1=lamp[:, 64:128], op=ALU.mult),
             r=["lamp"], w=["lt"])
        p.op("dve", lambda e: e.tensor_tensor(out=lt[:, 64:128], in0=lamp[:, 128:192], in1=lamp[:, 192:256], op=ALU.mult),
             r=["lamp"], w=["lt"])
        p.op("dve", lambda e: e.reduce_sum(out=lsum[:, 0:1], in_=lt[:, 0:64], axis=AX.X), r=["lt"], w=["lsum"])
        p.op("dve", lambda e: e.reduce_sum(out=lsum[:, 1:2], in_=lt[:, 64:128], axis=AX.X), r=["lt"], w=["lsum"])
        p.op("act", lambda e: e.activation(out=lsum[:, 0:2], in_=lsum[:, 0:2], func=AF.Exp), r=["lsum"], w=["lsum"])
        p.op("dve", lambda e: e.tensor_tensor(out=lsum[:, 2:3], in0=lsum[:, 1:2], in1=lsum[:, 0:1], op=ALU.subtract),
             r=["lsum"], w=["lsum"])
        p.op("dve", lambda e: e.tensor_scalar(out=lsum[:, 2:3], in0=lsum[:, 2:3], scalar1=-LAM_INIT, scalar2=None, op0=ALU.add),
             r=["lsum"], w=["lsum"])
        neglam = cx.sb("neglam", [128, 1], F32)
        psl, plk = cx.psum()
        p.op("pe", lambda e: e.matmul(psl[:, 0:1], lhsT=onesf[:, :], rhs=lsum[:, 2:3], start=True, stop=True),
             r=["onesf", "lsum"], w=[plk])
        p.op("dve", lambda e: e.tensor_copy(out=neglam[:], in_=psl[:, 0:1]), r=[plk], w=["neglam"])
        gsc = cx.sb("gsc", [128, 1], F32)
        p.op("dve", lambda e: e.tensor_scalar(out=gsc[:], in0=gcol[:], scalar1=1.0 - LAM_INIT, scalar2=None, op0=ALU.mult),
             r=["gcol"], w=["gsc"])
        grs = cx.sb("grs", [8, 128], F32)
        p.op("dve", lambda e: e.tensor_scalar(out=grs[:], in0=grow[:], scalar1=1.0 - LAM_INIT, scalar2=None, op0=ALU.mult),
             r=["grow"], w=["grs"])

        Q = [cx.sb("Q%d" % i, [128, S], BF16) for i in range(2)]
        Kt = [cx.sb("K%d" % i, [128, S], BF16) for i in range(2)]
        V = [cx.sb("V%d" % i, [128, NB, 128], BF16) for i in range(2)]
        PT = [[cx.sb("PT%d_%d" % (m, i), [128, 512], BF16) for i in range(2)] for m in range(2)]
        ep = {n_: cx.sb("ep_" + n_, [128, 512], F32) for n_ in ("r1", "r2", "t1", "o", "rs")}
        epq = cx.sb("ep_sq", [128, 512], BF16)
        ost = [cx.sb("ostB%d" % i, [128, 512], F32) for i in range(2)]
        O1, O2, L1, L2 = cx.ps[0], cx.ps[1], cx.ps[2], cx.ps[3]
        gi = 0
        for b in range(nseq):
            bp = b % 2
            for hh in range(0, S, 2048):
                he = min(S, hh + 2048)
                p.op("pool", lambda e, o=Q[bp][:, hh:he], i=d_q[b][:, hh:he]: e.dma_start(out=o, in_=i), w=[("Q", bp)], dsem=("qkv", bp))
                p.op("pool", lambda e, o=Kt[bp][:, hh:he], i=d_k[b][:, hh:he]: e.dma_start(out=o, in_=i), w=[("K", bp)], dsem=("qkv", bp))
            p.op("pool", lambda e, o=V[bp][:], i=d_v[b].rearrange("(n p) e -> p n e", p=128): e.dma_start(out=o, in_=i),
                 w=[("V", bp)], dsem=("qkv", bp))
            for G in range(NG):
                nkb = 4 * G + 4
                qs = slice(G * 512, (G + 1) * 512)

                def scores(kb):
                    par = kb % 2
                    ks = slice(kb * 128, (kb + 1) * 128)
                    p.op("pe", lambda e, o=cx.ps[4 + 2 * par][:, :], l=Kt[bp][0:64, ks], r_=Q[bp][0:64, qs]:
                         e.matmul(o, lhsT=l, rhs=r_, start=True, stop=True),
                         r=[("K", bp), ("Q", bp)], w=[("ps", 4 + 2 * par)])
                    p.op("pe", lambda e, o=cx.ps[5 + 2 * par][:, :], l=Kt[bp][64:128, ks], r_=Q[bp][64:128, qs]:
                         e.matmul(o, lhsT=l, rhs=r_, start=True, stop=True),
                         r=[("K", bp), ("Q", bp)], w=[("ps", 5 + 2 * par)])

                scores(0)
                for kb in range(nkb):
                    par = kb % 2
                    if kb + 1 < nkb:
                        scores(kb + 1)
                    for m in range(2):
                        p.op("act", lambda e, o=PT[m][par][:, :], i=cx.ps[4 + m + 2 * par][:, :]:
                             e.activation(out=o, in_=i, func=AF.Exp, scale=0.125),
                             r=[("ps", 4 + m + 2 * par)], w=[("pt", m, par)])
                        if kb >= 4 * G:
                            p.op("dve", lambda e, o=PT[m][par][:, :], mk=masks[:, kb - 4 * G, :]:
                                 e.tensor_tensor(out=o, in0=o, in1=mk, op=ALU.mult),
                                 r=[("pt", m, par), "masks"], w=[("pt", m, par)])
                    for m, (Ob, Lb) in enumerate(((0, 2), (1, 3))):
                        p.op("pe", lambda e, o=cx.ps[Ob][:, :], l=V[bp][:, kb, :], r_=PT[m][par][:, :], s=(kb == 0), t=(kb == nkb - 1):
                             e.matmul(o, lhsT=l, rhs=r_, start=s, stop=t), r=[("V", bp), ("pt", m, par)], w=[("ps", Ob)])
                        p.op("pe", lambda e, o=cx.ps[Lb][:, :], r_=PT[m][par][:, :], s=(kb == 0), t=(kb == nkb - 1):
                             e.matmul(o, lhsT=cx.ones[:], rhs=r_, start=s, stop=t), r=["ones", ("pt", m, par)], w=[("ps", Lb)])
                p.op("dve", lambda e: e.reciprocal(out=ep["r1"][:], in_=L1[:, :]), r=[("ps", 2)], w=["ep_r1"])
                p.op("dve", lambda e: e.reciprocal(out=ep["r2"][:], in_=L2[:, :]), r=[("ps", 3)], w=["ep_r2"])
                p.op("dve", lambda e: e.tensor_tensor(out=ep["t1"][:], in0=O1[:, :], in1=ep["r1"][:], op=ALU.mult),
                     r=[("ps", 0), "ep_r1"], w=["ep_t1"])
                p.op("dve", lambda e: e.tensor_tensor(out=ep["r2"][:], in0=O2[:, :], in1=ep["r2"][:], op=ALU.mult),
                     r=[("ps", 1), "ep_r2"], w=["ep_r2"])
                p.op("dve", lambda e: e.scalar_tensor_tensor(out=ep["o"][:], in0=ep["r2"][:], scalar=neglam[:, 0:1], in1=ep["t1"][:],
                                                             op0=ALU.mult, op1=ALU.add),
                     r=["ep_r2", "ep_t1", "neglam"], w=["ep_o"])
                p.op("act", lambda e: e.activation(out=epq[:], in_=ep["o"][:], func=AF.Square), r=["ep_o"], w=["ep_sq"])
                sp_ = 4 + 2 * (nkb % 2)
                p.op("pe", lambda e, o=cx.ps[sp_][:, :]: e.matmul(o, lhsT=cx.ones[:], rhs=epq[:], start=True, stop=True),
                     r=["ones", "ep_sq"], w=[("ps", sp_)])
                p.op("act", lambda e, i=cx.ps[sp_][:, :]: e.activation(out=ep["rs"][:], in_=i, func=AF.Sqrt, bias=cx.epsc[:, 0:1], scale=1.0 / 128),
                     r=[("ps", sp_), "epsc"], w=["ep_rs"])
                p.op("dve", lambda e: e.reciprocal(out=ep["rs"][:], in_=ep["rs"][:]), r=["ep_rs"], w=["ep_rs"])
                so = gi % 2
                gi += 1
                p.op("dve", lambda e, o=ost[so][:]: e.scalar_tensor_tensor(out=o, in0=ep["o"][:], scalar=gsc[:, 0:1], in1=ep["rs"][:],
                                                                          op0=ALU.mult, op1=ALU.mult),
                     r=["ep_o", "ep_rs", "gsc"], w=[("ostB", so)])
                cx.store(o_p[b][:, qs], ost[so][:], ("ostB", so), ("ostB", so))

        NT = nss * 8
        QS = cx.sb("QS", [128, NT], BF16)
        KN = cx.sb("KN", [128, NT], BF16)
        p.op("pool", lambda e: e.dma_start(out=QS[:], in_=d_qs), w=["QS"], dsem="const2")
        p.op("pool", lambda e: e.dma_start(out=KN[:], in_=d_ks), w=["KN"], dsem="const2")
        assert npg * 8 == 128
        NBUF = 5
        KTk = [cx.sb("KTk%d" % i, [128, 16, 128], F32) for i in range(NBUF)]
        KB = [cx.sb("KB%d" % i, [128, 16, 128], BF16) for i in range(NBUF)]
        VB = [cx.sb("VB%d" % i, [128, 17, 128], BF16) for i in range(NBUF)]
        for i in range(NBUF):
            p.op("pool", lambda e, o=VB[i][:, 16, :]: e.memset(o, 0.0), w=[("VB", i)])
        PS_ = [[cx.sb("PS%d_%d" % (m, i), [128, 17 * 8], BF16) for i in range(NBUF)] for m in range(2)]
        OS = [cx.sb("OS%d" % i, [8, 128], F32) for i in range(4)]
        sm = {n_: cx.sb("sm_" + n_, [8, 2], F32) for n_ in ("r", "ss")}
        smt = cx.sb("sm_t1", [8, 128], F32)
        smo = cx.sb("sm_o", [8, 128], F32)
        smq = cx.sb("sm_q", [8, 128], F32)
        NC_ = 17 * 8
        tabi = cx.sb("tabi", [128, nss], I32)
        tabf = cx.sb("tabf", [128, nss], F32)
        idx = cx.sb("idx", [128, nss], I32)
        iot = cx.sb("iot", [128, 1], F32)
        ident = cx.sb("ident", [128, 128], F32)
        cx.load(tabi[:], d_tabr, "tabi", "const")
        cx.load(iot[:], d_iota, "iot", "const")
        cx.load(ident[:], d_ident, "ident", "const")
        p.op("dve", lambda e: e.tensor_copy(out=tabf[:], in_=tabi[:]), r=["tabi"], w=["tabf"])
        p.op("dve", lambda e: e.tensor_scalar(out=tabf[:], in0=tabf[:], scalar1=8.0, scalar2=iot[:, 0:1], op0=ALU.mult, op1=ALU.add),
             r=["tabf", "iot"], w=["tabf"])
        p.op("dve", lambda e: e.tensor_copy(out=idx[:], in_=tabf[:]), r=["tabf"], w=["idx"])
        ckf = d_ck.rearrange("n (a b) d -> (n a) (b d)", a=8)
        cvf = d_cv.rearrange("n (a b) d -> (n a) (b d)", a=8)
        npg = 16
        def seq_gen(i):
            par = i % NBUF
            p.op("pool", lambda e, o=KTk[par][:, :, :].rearrange("p a b -> p (a b)"), c_=i: e.indirect_dma_start(
                out=o, out_offset=None, in_=ckf, in_offset=bass.IndirectOffsetOnAxis(ap=idx[:, c_:c_ + 1], axis=0)),
                r=["idx"], w=[("KTk", par)], dsem=("kb", par))
            p.op("pool", lambda e, o=VB[par][:, 0:16, :].rearrange("p a b -> p (a b)"), c_=i: e.indirect_dma_start(
                out=o, out_offset=None, in_=cvf, in_offset=bass.IndirectOffsetOnAxis(ap=idx[:, c_:c_ + 1], axis=0)),
                r=["idx"], w=[("VB", par)], dsem=("vb", par))
            p.op("pool", lambda e, o=VB[par][0:8, 16, :], i_=d_vs[i * 8:(i + 1) * 8, :]: e.dma_start(out=o, in_=i_),
                 w=[("VB", par)], dsem=("vb", par))
            yield
            for q4 in range(4):
                pst, ptk = cx.psum()
                for uu in range(4):
                    u = q4 * 4 + uu
                    p.op("pe", lambda e, o=pst[:, uu * 128:(uu + 1) * 128], a=KTk[par][:, u, :]:
                         e.transpose(o, a, ident[:, :]), r=[("KTk", par), "ident"], w=[ptk])
                eng_ = "act" if q4 % 2 == 0 else "dve"
                if eng_ == "act":
                    p.op("act", lambda e, o=KB[par][:, q4 * 4:(q4 + 1) * 4, :], a=pst[:, :].rearrange("p (u k) -> p u k", u=4):
                         e.activation(out=o, in_=a, func=AF.Copy), r=[ptk], w=[("KB", par)])
                else:
                    p.op("dve", lambda e, o=KB[par][:, q4 * 4:(q4 + 1) * 4, :], a=pst[:, :].rearrange("p (u k) -> p u k", u=4):
                         e.tensor_copy(out=o, in_=a), r=[ptk], w=[("KB", par)])
            yield
            qsl = slice(i * 8, (i + 1) * 8)
            sps = []
            for m in range(2):
                ps, pk = cx.psum()
                sps.append((ps, pk))
                pr = slice(64 * m, 64 * m + 64)
                for j in range(npg):
                    p.op("pe", lambda e, o=ps[:, j * 8:(j + 1) * 8], l=KB[par][pr, j, :], r_=QS[pr, qsl]:
                         e.matmul(o, lhsT=l, rhs=r_, start=True, stop=True), r=[("KB", par), "QS"], w=[pk])
                p.op("pe", lambda e, o=ps[0:8, npg * 8:NC_], l=KN[pr, qsl], r_=QS[pr, qsl]:
                     e.matmul(o, lhsT=l, rhs=r_, start=True, stop=True), r=["KN", "QS"], w=[pk])
            for m in range(2):
                ps, pk = sps[m]
                p.op("act", lambda e, o=PS_[m][par][:, 0:npg * 8], i_=ps[:, 0:npg * 8]:
                     e.activation(out=o, in_=i_, func=AF.Exp, scale=0.125), r=[pk], w=[("PS", m, par)])
                p.op("act", lambda e, o=PS_[m][par][0:8, npg * 8:NC_], i_=ps[0:8, npg * 8:NC_]:
                     e.activation(out=o, in_=i_, func=AF.Exp, scale=0.125), r=[pk], w=[("PS", m, par)])
                p.op("dve", lambda e, o=PS_[m][par][0:8, npg * 8:NC_]: e.tensor_tensor(out=o, in0=o, in1=smask[:, :], op=ALU.mult),
                     r=[("PS", m, par), "smask"], w=[("PS", m, par)])
            yield
            ops_ = []
            for m in range(2):
                ps, pk = cx.psum()
                ops_.append((ps, pk))
                for j in range(npg):
                    p.op("pe", lambda e, o=ps[0:8, 128:129], l=PS_[m][par][:, j * 8:(j + 1) * 8], s=(j == 0):
                         e.matmul(o, lhsT=l, rhs=cx.ones[:, 0:1], start=s, stop=False), r=[("PS", m, par), "ones"], w=[pk])
                p.op("pe", lambda e, o=ps[0:8, 128:129], l=PS_[m][par][0:8, npg * 8:NC_]:
                     e.matmul(o, lhsT=l, rhs=cx.ones[0:8, 0:1], start=False, stop=True), r=[("PS", m, par), "ones"], w=[pk])
                for j in range(npg):
                    p.op("pe", lambda e, o=ps[0:8, 0:128], l=PS_[m][par][:, j * 8:(j + 1) * 8], r_=VB[par][:, j, :], s=(j == 0):
                         e.matmul(o, lhsT=l, rhs=r_, start=s, stop=False), r=[("PS", m, par), ("VB", par)], w=[pk])
                p.op("pe", lambda e, o=ps[0:8, 0:128], l=PS_[m][par][0:8, npg * 8:NC_], r_=VB[par][0:8, npg, :]:
                     e.matmul(o, lhsT=l, rhs=r_, start=False, stop=True), r=[("PS", m, par), ("VB", par)], w=[pk])
            (p1, k1), (p2, k2) = ops_
            p.op("dve", lambda e, a=p1[0:8, 128:129]: e.reciprocal(out=sm["r"][:, 0:1], in_=a), r=[k1], w=["sm_r"])
            p.op("dve", lambda e, a=p2[0:8, 128:129]: e.reciprocal(out=sm["r"][:, 1:2], in_=a), r=[k2], w=["sm_r"])
            p.op("dve", lambda e: e.tensor_tensor(out=sm["r"][:, 1:2], in0=sm["r"][:, 1:2], in1=neglam[0:8, 0:1], op=ALU.mult),
                 r=["sm_r", "neglam"], w=["sm_r"])
            p.op("dve", lambda e, a=p1[0:8, 0:128]: e.tensor_scalar(out=smt[:], in0=a, scalar1=sm["r"][:, 0:1], scalar2=None, op0=ALU.mult),
                 r=[k1, "sm_r"], w=["sm_t1"])
            p.op("dve", lambda e, a=p2[0:8, 0:128]: e.scalar_tensor_tensor(out=smo[:], in0=a, scalar=sm["r"][:, 1:2], in1=smt[:],
                                                                            op0=ALU.mult, op1=ALU.add),
                 r=[k2, "sm_r", "sm_t1"], w=["sm_o"])
            p.op("dve", lambda e: e.tensor_tensor(out=smq[:], in0=smo[:], in1=smo[:], op=ALU.mult), r=["sm_o"], w=["sm_q"])
            p.op("dve", lambda e: e.reduce_sum(out=sm["ss"][:, 0:1], in_=smq[:], axis=AX.X), r=["sm_q"], w=["sm_ss"])
            p.op("act", lambda e: e.activation(out=sm["ss"][:, 1:2], in_=sm["ss"][:, 0:1], func=AF.Sqrt, bias=cx.epsc[0:8, 0:1], scale=1.0 / 128),
                 r=["sm_ss", "epsc"], w=["sm_ss"])
            p.op("dve", lambda e: e.reciprocal(out=sm["ss"][:, 1:2], in_=sm["ss"][:, 1:2]), r=["sm_ss"], w=["sm_ss"])
            osl = i % 4
            p.op("dve", lambda e, o=OS[osl][:, :]: e.scalar_tensor_tensor(out=o, in0=smo[:], scalar=sm["ss"][:, 1:2], in1=grs[:],
                                                                         op0=ALU.mult, op1=ALU.mult),
                 r=["sm_o", "sm_ss", "grs"], w=[("OS", osl)])
            cx.store(o_s[i * 8:(i + 1) * 8, :], OS[osl][:, :], ("OS", osl), ("OS", osl))

        gens = [seq_gen(i) for i in range(nss)]
        for step in range(nss + 3):
            for off in range(4):
                i = step - off
                if 0 <= i < nss:
                    try:
                        next(gens[i])
                    except StopIteration:
                        pass
        cx.p.emit_all(stack)
    return nc


def make_masks():
    pidx = np.arange(128)[:, None]
    j = np.arange(512)[None, :]
    m = np.stack([(j >= o * 128 + pidx) for o in range(4)], axis=1).astype(np.float32)
    sm = (np.arange(8)[:, None] <= np.arange(8)[None, :]).astype(np.float32)
    return m, sm


HC = 256
POOL_WIN = (2, 2, 4, 4, 8, 8, 16, 16)
GELU_C = 1.5957691216057308


def gelu_tile(cx, x_ap, out_ap, n_shape_keys, rk, wk):
    cx.p.op("act", lambda e: e.activation(out=out_ap, in_=x_ap, func=AF.Gelu_apprx_tanh), r=rk, w=wk)


def build_C(own, ns, debug=None):
    NP = HC + own
    NSB = ns * 8
    T = NP + NSB
    assert own % 128 == 0 and NSB <= 128
    nc = bass.Bass("TRN2", target_bir_lowering=False)
    stack = ExitStack()
    with stack:
        cx = Ctx(nc, stack, n_wslots=4, wslot_elems=3072)
        p = cx.p
        d_x = cx.dram_in("x1T", [D, T])
        d_o = cx.dram_in("oT", [D, T])
        d_hm = cx.dram_in("hmask", [128, 1])
        d_vec = cx.dram_in("vecC", [128, 88])
        d_wo = cx.dram_in("w_o", [D, D])
        d_plw = cx.dram_in("pl_w", [D, 256])
        d_stp = cx.dram_in("stpool", [128, 8, ns, 15])
        d_invc = cx.dram_in("invc", [128, 4, 16])
        d_fdw = cx.dram_in("ff_dw", [3, 128, NFF, 3])
        d_fb = cx.dram_in("ff_b", [3, 128, NFF])
        d_fst = cx.dram_in("ff_st", [3, 128, NFF, ns, 2])
        d_wg = cx.dram_in("ff_w_gate", [3, D, DFF])
        d_wu = cx.dram_in("ff_w_up", [3, D, DFF])
        d_wd = cx.dram_in("ff_w_down", [3, DFF, D])
        d_sgin = cx.dram_in("sg_w_in", [D, 4 * D])
        d_sgout = cx.dram_in("sg_w_out", [2 * D, D])
        d_sgrow = cx.dram_in("sg_rows", [3, 128, 2 * D])
        d_wst = cx.dram_in("sg_wsT", [2, 128, 4, 128])
        d_wsm = cx.dram_in("sg_mask", [2, 128, 128])
        d_bs = cx.dram_in("sg_bs", [2, 128, 4, 128])
        o_y = cx.dram_out("yT", [D, T])
        o_pool = cx.dram_out("poolT", [128, 8, 15 + ns * 15])
        o_sgv = cx.dram_out("sgv", [NSB, 2 * D])
        o_ffn = cx.dram_out("ffnT", [3, 128, NFF, 2 + 2 * ns])

        tiles = split_tiles(NP) + [(NP, NSB)]
        nt = len(tiles)
        cx.X = cx.sb("X", [128, 8, T], F32)
        cx.XN = cx.sb("XN", [128, 8, T], BF16)
        X, XN = cx.X, cx.XN
        cx.scr_sq = [cx.sb("sq%d" % i, [128, 512], BF16) for i in range(4)]
        cx.scr_r = [cx.sb("R%d" % i, [128, 512], F32) for i in range(2)]
        cx.hmask = cx.const_cols("hmask", d_hm, 1)
        vec = cx.const_cols("vecC", d_vec, 88)
        cx.scratch_init(13800)
        for ti, (t0, n) in enumerate(tiles):
            for kc in range(8):
                cx.load(X[:, kc, t0:t0 + n], d_x[kc * 128:(kc + 1) * 128, t0:t0 + n], ("X", ti), ("xin", ti % 2))
        for ti, (t0, n) in enumerate(tiles):
            p.op("pool", lambda e, o=XN[:, :, t0:t0 + n], i=d_o[:, t0:t0 + n].rearrange("(k p) n -> p k n", p=128):
                 e.dma_start(out=o, in_=i), w=[("XN", ti)], dsem=("oin", ti % 2))

        def cons_add(m, ti, t0, n, ps, pk):
            p.op("dve", lambda e, o=X[:, m, t0:t0 + n], i=ps[:, 0:n]:
                 e.tensor_tensor(out=o, in0=i, in1=o, op=ALU.add), r=[pk, ("X", ti)], w=[("X", ti)])
        proj(cx, d_wo, 8, 0, D, XN, "XN", tiles, cons_add, group_cols=384)

        def run_ffn(li, ncol):
            cx.new_scope()
            cx.ffn_dw, cx.ffn_b, cx.ffn_state, cx.ffn_out = {}, {}, {}, {}
            t = cx.alloc("ffdw", [128, NFF, 3], F32)
            cx.load_s(t, d_fdw[li], "const_s", "const_s")
            cx.ffn_dw[li] = t
            t = cx.alloc("ffb", [128, NFF], F32)
            cx.load_s(t, d_fb[li], "const_s", "const_s")
            cx.ffn_b[li] = t
            t = cx.alloc("ffst", [128, NFF, ns, 2], F32)
            cx.load_s(t, d_fst[li], ("stf", li), "const_s")
            cx.ffn_state[li] = t
            cx.ffn_out[li] = cx.alloc("ffo", [128, NFF, 2 + 2 * ns], F32)
            cx.gt = [cx.alloc("gt%d" % i, [128, 514], F32) for i in range(2)]
            cx.gt_i = 0
            cx.Gs = [cx.alloc("Gs%d" % i, [128, ns, 10], F32) for i in range(2)]
            cx.facc = [cx.alloc("facc%d" % i, [128, 512], F32) for i in range(2)]
            cx.fsil = [cx.alloc("fsil%d" % i, [128, 512], F32) for i in range(2)]
            cx.Hb = cx.alloc("Hb", [128, 3, T], BF16)
            rmsnorm(cx, X, vec[:, ncol:ncol + 8], tiles, xn_out(cx), "nf%d" % li)
            conv_ffn(cx, li, {"gate": d_wg[li], "up": d_wu[li], "down": d_wd[li]}, tiles, NP, ns, HC)
            cx.store_s(o_ffn[li], cx.ffn_out[li], ("ffo", li), "outs")

        run_ffn(0, 0)

        cx.new_scope()
        Rall = cx.alloc("Rall", [128, T], F32)
        HN = cx.alloc("HN", [128, 15 + NP], F32)
        P0 = cx.alloc("P0", [128, 15 + NP], F32)
        P1 = cx.alloc("P1", [128, 15 + NP], F32)
        HS = cx.alloc("HS", [128, ns, 23], F32)
        Q0 = cx.alloc("Q0", [128, ns, 23], F32)
        Q1 = cx.alloc("Q1", [128, ns, 23], F32)
        invc = cx.alloc("invc", [128, 4, 16], F32)
        ptm = cx.alloc("ptm", [128, 16], F32)
        cx.load_s(invc, d_invc, "invc", "const_s")
        for ti, (t0, n) in enumerate(tiles):
            ps, pk = cx.psum()
            for kc in range(8):
                sqi = cx.sq_next()
                p.op("act", lambda e, o=cx.scr_sq[sqi][:, 0:n], i=X[:, kc, t0:t0 + n]:
                     e.activation(out=o, in_=i, func=AF.Square), r=[("X", ti)], w=[("sq", sqi)])
                p.op("pe", lambda e, o=ps[:, 0:n], r_=cx.scr_sq[sqi][:, 0:n], s=(kc == 0), t=(kc == 7):
                     e.matmul(o, lhsT=cx.ones[:], rhs=r_, start=s, stop=t), r=[("sq", sqi), "ones"], w=[pk])
            p.op("act", lambda e, o=Rall[:, t0:t0 + n], i=ps[:, 0:n]:
                 e.activation(out=o, in_=i, func=AF.Sqrt, bias=cx.epsc[:, 0:1], scale=1.0 / D), r=[pk, "epsc"], w=["Rall"])
            p.op("dve", lambda e, o=Rall[:, t0:t0 + n]: e.reciprocal(out=o, in_=o), r=["Rall"], w=["Rall"])
        p.op("pool", lambda e: e.memset(HN[:, 0:15], 0.0), w=["HN"])
        allXN = [("XN", ti) for ti in range(nt)]
        allX = [("X", ti) for ti in range(nt)]
        for c in range(8):
            win = POOL_WIN[c]
            gcolc = vec[:, 8 + c:9 + c]
            p.op("dve", lambda e, o=HN[:, 15:15 + NP], i=X[:, c, 0:NP], g=gcolc, r_=Rall[:, 0:NP]:
                 e.scalar_tensor_tensor(out=o, in0=i, scalar=g, in1=r_, op0=ALU.mult, op1=ALU.mult),
                 r=allX + ["Rall", "const"], w=["HN"])
            p.op("dve", lambda e, o=HN[:, 15:15 + HC]:
                 e.tensor_scalar(out=o, in0=o, scalar1=cx.hmask[:, 0:1], scalar2=None, op0=ALU.mult), r=["HN", "const"], w=["HN"])
            p.op("dve", lambda e, o=HS[:, :, 15:23], i=X[:, c, NP:T].rearrange("p (s t) -> p s t", t=8), g=gcolc,
                 r_=Rall[:, NP:T].rearrange("p (s t) -> p s t", t=8):
                 e.scalar_tensor_tensor(out=o, in0=i, scalar=g, in1=r_, op0=ALU.mult, op1=ALU.mult),
                 r=allX + ["Rall", "const"], w=["HS"])
            p.op("sp", lambda e, o=HS[:, :, 0:15], i=d_stp[:, c, :, :]: e.dma_start(out=o, in_=i), r=["scr"], w=["HS"], dsem="stp")
            cx.store_s(o_pool[:, c, 0:15], HN[:, NP:NP + 15], "HN", "pout")
            cx.store_s(o_pool[:, c, 15:15 + ns * 15].rearrange("p (s t) -> p s t", t=15), HS[:, :, 8:23], "HS", "pout")
            src_p, src_s = HN, HS
            bufs_p, bufs_s = [P0, P1], [Q0, Q1]
            k = 1
            bi = 0
            while k < win:
                dp, ds_ = bufs_p[bi], bufs_s[bi]
                lo = 2 * k - 1
                p.op("dve", lambda e, o=dp[:, lo:15 + NP], a=src_p[:, lo:15 + NP], b=src_p[:, lo - k:15 + NP - k]:
                     e.tensor_tensor(out=o, in0=a, in1=b, op=ALU.add), r=["HN", "P0", "P1"], w=["P%d" % bi])
                p.op("dve", lambda e, o=ds_[:, :, lo:23], a=src_s[:, :, lo:23], b=src_s[:, :, lo - k:23 - k]:
                     e.tensor_tensor(out=o, in0=a, in1=b, op=ALU.add), r=["HS", "Q0", "Q1"], w=["Q%d" % bi])
                src_p, src_s = dp, ds_
                k *= 2
                bi ^= 1
            widx = {2: 0, 4: 1, 8: 2, 16: 3}[win]
            p.op("dve", lambda e, o=XN[:, c, 0:NP], a=src_p[:, 15:15 + NP], h=HN[:, 15:15 + NP], iw=1.0 / win:
                 e.scalar_tensor_tensor(out=o, in0=a, scalar=iw, in1=h, op0=ALU.mult, op1=ALU.subtract),
                 r=["HN", "P0", "P1"], w=allXN)
            p.op("dve", lambda e, a=src_p[:, 15 + HC:15 + HC + 16], iv=invc[:, widx, :]:
                 e.tensor_tensor(out=ptm[:, :], in0=a, in1=iv, op=ALU.mult), r=["P0", "P1", "invc"], w=["ptm"])
            p.op("dve", lambda e, o=XN[:, c, HC:HC + 16], h=HN[:, 15 + HC:15 + HC + 16]:
                 e.tensor_tensor(out=o, in0=ptm[:, :], in1=h, op=ALU.subtract), r=["ptm", "HN"], w=allXN)
            p.op("dve", lambda e, o=XN[:, c, NP:T].rearrange("p (s t) -> p s t", t=8), a=src_s[:, :, 15:23], h=HS[:, :, 15:23], iw=1.0 / win:
                 e.scalar_tensor_tensor(out=o, in0=a, scalar=iw, in1=h, op0=ALU.mult, op1=ALU.subtract),
                 r=["HS", "Q0", "Q1"], w=allXN)
        wv, wk = cx.wload(d_plw, 8, 256)
        for m in range(8):
            g = m // 2
            for ti, (t0, n) in enumerate(tiles):
                ps, pk = cx.psum()
                for kk in range(2):
                    p.op("pe", lambda e, o=ps[:, 0:n], l=wv[:, g * 2 + kk, (m % 2) * 128:(m % 2 + 1) * 128],
                         r_=XN[:, g * 2 + kk, t0:t0 + n], s=(kk == 0), t=(kk == 1):
                         e.matmul(o, lhsT=l, rhs=r_, start=s, stop=t), r=[wk, ("XN", ti)], w=[pk])
                p.op("dve", lambda e, o=X[:, m, t0:t0 + n], i=ps[:, 0:n], sc=vec[:, 16 + m:17 + m]:
                     e.scalar_tensor_tensor(out=o, in0=i, scalar=sc, in1=o, op0=ALU.mult, op1=ALU.add),
                     r=[pk, ("X", ti), "const"], w=[("X", ti)])
        if debug == "pool":
            for ti, (t0, n) in enumerate(tiles):
                for kc in range(8):
                    cx.store(o_y[kc * 128:(kc + 1) * 128, t0:t0 + n], X[:, kc, t0:t0 + n], ("X", ti), "outs")
            cx.p.emit_all(stack)
            return nc
        run_ffn(1, 24)

        cx.new_scope()
        rmsnorm(cx, X, vec[:, 32:40], tiles, xn_out(cx), "n3")
        NBK = NP // 128
        blocks = [(i * 128, 128) for i in range(NBK)] + [(NP, NSB)]
        nb = len(blocks)
        U = cx.alloc("U", [128, 4, T], BF16)
        rows = cx.alloc("rows", [128, 3, 512], F32)
        wst = cx.alloc("wst", [128, 2, 4, 128], F32)
        wsb = cx.alloc("wsb", [128, 2, 4, 128], BF16)
        wsm = cx.alloc("wsm", [128, 2, 128], F32)
        bsr = cx.alloc("bsr", [128, 2, 4, 128], F32)
        stat = cx.alloc("stat", [128, nb, 4, 2], F32)
        mur = cx.alloc("mur", [128, nb, 2], F32)
        rowb = cx.alloc("rowb", [1, 512], BF16)
        nmr = cx.alloc("nmr", [128, nb, 1], F32)
        zvs = [cx.alloc("zv%d" % i, [128, 512], F32) for i in range(2)]
        zqs = [cx.alloc("zq%d" % i, [128, 512], F32) for i in range(2)]
        vnb = [cx.alloc("vnb%d" % i, [128, 512], BF16) for i in range(2)]
        ssts = [cx.alloc("sst%d" % i, [128, 128], F32) for i in range(4)]
        for i in range(2):
            cx.load_s(wst[:, i], d_wst[i], "wst", "const_s")
            cx.load_s(wsm[:, i], d_wsm[i], "wsm", "const_s")
            cx.load_s(bsr[:, i], d_bs[i], "bsr", "const_s")
        for i in range(2):
            for g in range(4):
                p.op("dve", lambda e, o=wsb[:, i, g, :], a=wst[:, i, g, :], b=wsm[:, i, :]:
                     e.tensor_tensor(out=o, in0=a, in1=b, op=ALU.mult), r=["wst", "wsm"], w=["wsb"])
        for g in range(4):
            halves = []
            for hh in range(2):
                halves.append(cx.wload(d_sgin[:, 2 * D + g * 512 + hh * 256:2 * D + g * 512 + (hh + 1) * 256], 8, 256))
            p.op("sp", lambda e, o=rows[:, 0, :], i=d_sgrow[0][:, g * 512:(g + 1) * 512]: e.dma_start(out=o, in_=i),
                 r=["scr"], w=["rows"], dsem="rows")
            p.op("act", lambda e: e.activation(out=rowb[0:1, :], in_=rows[0:1, 0, :], func=AF.Copy), r=["rows"], w=["rowb"])
            for bi_, (t0, n) in enumerate(blocks):
                ti = min(t0 // 512, nt - 1) if t0 < NP else nt - 1
                ps, pk = cx.psum()
                for hh in range(2):
                    wv_, wvk = halves[hh]
                    for kc in range(8):
                        p.op("pe", lambda e, o=ps[0:n, hh * 256:(hh + 1) * 256], l=XN[:, kc, t0:t0 + n], r_=wv_[:, kc, :],
                             s=(kc == 0): e.matmul(o, lhsT=l, rhs=r_, start=s, stop=False),
                             r=[wvk, ("XN", ti)], w=[pk])
                    p.op("pe", lambda e, o=ps[0:n, hh * 256:(hh + 1) * 256], l=cx.ones[0:1, 0:n], r_=rowb[0:1, hh * 256:(hh + 1) * 256]:
                         e.matmul(o, lhsT=l, rhs=r_, start=False, stop=True), r=["ones", "rowb"], w=[pk])
                zv = zvs[bi_ % 2]
                zq = zqs[bi_ % 2]
                zk = ("zv", bi_ % 2)
                qk = ("zq", bi_ % 2)
                gelu_tile(cx, ps[0:n, :], zv[0:n, :], None, [pk], [zk])
                p.op("dve", lambda e, o=stat[0:n, bi_, g, 0:1], a=zv[0:n, :]: e.reduce_sum(out=o, in_=a, axis=AX.X), r=[zk], w=["stat"])
                p.op("act", lambda e, o=zq[0:n, :], a=zv[0:n, :]: e.activation(out=o, in_=a, func=AF.Square), r=[zk], w=[qk])
                p.op("dve", lambda e, o=stat[0:n, bi_, g, 1:2], a=zq[0:n, :]: e.reduce_sum(out=o, in_=a, axis=AX.X), r=[qk], w=["stat"])
        for bi_, (t0, n) in enumerate(blocks):
            p.op("dve", lambda e, o=mur[0:n, bi_, :], a=stat[0:n, bi_, :, :].rearrange("p g s -> p s g"):
                 e.reduce_sum(out=o, in_=a, axis=AX.X), r=["stat"], w=["mur"])
        p.op("dve", lambda e: e.tensor_scalar(out=mur[:, :, :], in0=mur[:, :, :], scalar1=1.0 / (2 * D), scalar2=None, op0=ALU.mult),
             r=["mur"], w=["mur"])
        msq = cx.alloc("msq", [128, nb, 1], F32)
        p.op("dve", lambda e: e.tensor_tensor(out=msq[:, :, :], in0=mur[:, :, 0:1], in1=mur[:, :, 0:1], op=ALU.mult), r=["mur"], w=["msq"])
        p.op("dve", lambda e: e.tensor_tensor(out=mur[:, :, 1:2], in0=mur[:, :, 1:2], in1=msq[:, :, :], op=ALU.subtract),
             r=["mur", "msq"], w=["mur"])
        p.op("act", lambda e: e.activation(out=mur[:, :, 1:2], in_=mur[:, :, 1:2], func=AF.Sqrt, bias=cx.epsc[:, 0:1], scale=1.0),
             r=["mur", "epsc"], w=["mur"])
        p.op("dve", lambda e: e.reciprocal(out=mur[:, :, 1:2], in_=mur[:, :, 1:2]), r=["mur"], w=["mur"])
        p.op("dve", lambda e: e.scalar_tensor_tensor(out=nmr[:, :, :], in0=mur[:, :, 0:1], scalar=-1.0, in1=mur[:, :, 1:2],
                                                     op0=ALU.mult, op1=ALU.mult), r=["mur"], w=["nmr"])
        for g in range(4):
            def cons_u(m, ti, t0, n, ps, pk, g=g):
                p.op("act", lambda e, o=U[:, m, t0:t0 + n], i=ps[:, 0:n], b=vec[:, 56 + g * 4 + m:57 + g * 4 + m]:
                     e.activation(out=o, in_=i, func=AF.Gelu_apprx_tanh, bias=b), r=[pk, "const"], w=[("U", ti)])
            proj(cx, d_sgin, 8, g * 512, 512, XN, "XN", tiles, cons_u, group_cols=256)
            halves = []
            for hh in range(2):
                halves.append(cx.wload(d_sgin[:, 2 * D + g * 512 + hh * 256:2 * D + g * 512 + (hh + 1) * 256], 8, 256))
            for r_i in range(3):
                p.op("sp", lambda e, o=rows[:, r_i, :], i=d_sgrow[r_i][:, g * 512:(g + 1) * 512]: e.dma_start(out=o, in_=i),
                     r=["scr"], w=["rows"], dsem="rows")
            p.op("act", lambda e: e.activation(out=rowb[0:1, :], in_=rows[0:1, 0, :], func=AF.Copy), r=["rows"], w=["rowb"])
            def blk_gen(bi_, t0, n, g=g, halves=halves):
                ti = min(t0 // 512, nt - 1) if t0 < NP else nt - 1
                samp = t0 >= NP
                ps, pk = cx.psum()
                for hh in range(2):
                    wv_, wvk = halves[hh]
                    for kc in range(8):
                        p.op("pe", lambda e, o=ps[0:n, hh * 256:(hh + 1) * 256], l=XN[:, kc, t0:t0 + n], r_=wv_[:, kc, :],
                             s=(kc == 0): e.matmul(o, lhsT=l, rhs=r_, start=s, stop=False),
                             r=[wvk, ("XN", ti)], w=[pk])
                    p.op("pe", lambda e, o=ps[0:n, hh * 256:(hh + 1) * 256], l=cx.ones[0:1, 0:n], r_=rowb[0:1, hh * 256:(hh + 1) * 256]:
                         e.matmul(o, lhsT=l, rhs=r_, start=False, stop=True), r=["ones", "rowb"], w=[pk])
                zv = zvs[bi_ % 2]
                zq = zqs[bi_ % 2]
                zk = ("zv", bi_ % 2)
                qk = ("zq", bi_ % 2)
                gelu_tile(cx, ps[0:n, :], zv[0:n, :], None, [pk], [zk])
                p.op("act", lambda e, o=zv[0:n, :], nm_=nmr[0:n, bi_, :], rs_=mur[0:n, bi_, 1:2]:
                     e.activation(out=o, in_=o, func=AF.Identity, bias=nm_, scale=rs_), r=[zk, "mur", "nmr"], w=[zk])
                p.op("dve", lambda e, o=zv[0:n, :], a=rows[0:n, 1, :]: e.tensor_tensor(out=o, in0=o, in1=a, op=ALU.mult), r=[zk, "rows"], w=[zk])
                if samp:
                    p.op("dve", lambda e, o=zq[0:n, :], a=zv[0:n, :], b=rows[0:n, 2, :]: e.tensor_tensor(out=o, in0=a, in1=b, op=ALU.add),
                         r=[zk, "rows"], w=[qk])
                    cx.store_s(o_sgv[:, g * 512:(g + 1) * 512], zq[0:n, :], qk, "sgv")
                vb = vnb[bi_ % 2]
                p.op("dve", lambda e, o=vb[0:n, :], a=zv[0:n, :], b=rows[0:n, 2, :]: e.tensor_tensor(out=o, in0=a, in1=b, op=ALU.add),
                     r=[zk, "rows"], w=[("vnb", bi_ % 2)])
                yield
                wi = 1 if samp else 0
                for cc in range(4):
                    ps2, pk2 = cx.psum()
                    p.op("pe", lambda e, o=ps2[:, 0:n], l=vb[0:n, cc * 128:(cc + 1) * 128], r_=wsb[0:n, wi, g, 0:n]:
                         e.matmul(o, lhsT=l, rhs=r_, start=True, stop=True), r=[("vnb", bi_ % 2), "wsb"], w=[pk2])
                    sst = ssts[cc]
                    p.op("dve", lambda e, o=sst[:, 0:n], a=ps2[:, 0:n], b=bsr[:, wi, g, 0:n]: e.tensor_tensor(out=o, in0=a, in1=b, op=ALU.add),
                         r=[pk2, "bsr"], w=[("sst", cc)])
                    p.op("dve", lambda e, o=U[:, cc, t0:t0 + n], a=sst[:, 0:n]: e.tensor_tensor(out=o, in0=o, in1=a, op=ALU.mult),
                         r=[("sst", cc), ("U", ti)], w=[("U", ti)])
            bgens = [blk_gen(bi_, t0, n) for bi_, (t0, n) in enumerate(blocks)]
            for step in range(nb + 1):
                for off in range(2):
                    i_ = step - off
                    if 0 <= i_ < nb:
                        try:
                            next(bgens[i_])
                        except StopIteration:
                            pass
            for half in range(2):
                if half == 1:
                    wo_, wok = cx.wload(d_sgout[g * 512:(g + 1) * 512, 512:1024], 4, 512)
                else:
                    wo_, wok = cx.wload(d_sgout[g * 512:(g + 1) * 512, 0:512], 4, 512)
                for mm in range(4):
                    m = half * 4 + mm
                    for ti, (t0, n) in enumerate(tiles):
                        ps, pk = cx.psum()
                        for kk in range(4):
                            p.op("pe", lambda e, o=ps[:, 0:n], l=wo_[:, kk, mm * 128:(mm + 1) * 128], r_=U[:, kk, t0:t0 + n],
                                 s=(kk == 0), t=(kk == 3): e.matmul(o, lhsT=l, rhs=r_, start=s, stop=t), r=[wok, ("U", ti)], w=[pk])
                        p.op("dve", lambda e, o=X[:, m, t0:t0 + n], i=ps[:, 0:n]:
                             e.tensor_tensor(out=o, in0=i, in1=o, op=ALU.add), r=[pk, ("X", ti)], w=[("X", ti)])
        run_ffn(2, 40)
        cx.new_scope()
        yst = [cx.alloc("yst%d" % i, [128, 512], F32) for i in range(4)]
        ycnt = [0]

        def y_out(kc, ti, t0, n):
            s_ = ycnt[0] % 4
            ycnt[0] += 1
            y_out.last = (s_, kc, t0, n)
            return yst[s_][:, 0:n], ("yst", s_)
        p_op_orig = p.op

        def hooked(eng, emit, r=(), w=(), dsem=None):
            ins = p_op_orig(eng, emit, r=r, w=w, dsem=dsem)
            if eng == "dve" and len(w) == 1 and isinstance(w[0], tuple) and w[0][0] == "yst":
                s_, kc, t0, n = y_out.last
                p_op_orig("sp", lambda e, o=o_y[kc * 128:(kc + 1) * 128, t0:t0 + n], i=yst[s_][:, 0:n]: e.dma_start(out=o, in_=i),
                          r=[("yst", s_), "scr"], dsem=("yst", s_))
            return ins
        p.op = hooked
        rmsnorm(cx, X, vec[:, 48:56], tiles, y_out, "nfin")
        p.op = p_op_orig
        cx.p.emit_all(stack)
    return nc


def run_C(inp, x1p, x1s, op, os_, own, ns, n_cores, seq_of_core, nc_cache={}, debug=None):
    key = (own, ns, debug)
    if key not in nc_cache:
        nc_cache[key] = build_C(own, ns, debug)
    nc = nc_cache[key]
    f = lambda a: np.asarray(a, np.float32)
    vec = np.concatenate([
        lay_cols(inp["norm_ffn"][1]), lay_cols(inp["norm_mix"][2]), lay_cols(inp["pl_scale"]),
        lay_cols(inp["norm_ffn"][2]), lay_cols(inp["norm_mix"][3]), lay_cols(inp["norm_ffn"][3]),
        lay_cols(inp["norm_final"]), lay_cols(inp["sg_b_in"])], axis=1)
    fdw = np.ascontiguousarray(f(inp["ff_w_dw"])[1:4].transpose(0, 2, 1).reshape(3, NFF, 128, 3).transpose(0, 2, 1, 3))
    fb = np.stack([lay_cols(inp["ff_b_dw"][i]) for i in (1, 2, 3)])
    ws = f(inp["sg_w_s"])
    wst_p = np.ascontiguousarray(ws.transpose(2, 0, 1))
    wst_s = np.zeros((128, 4, 128), np.float32)
    msk_p = (np.arange(128)[:, None] <= np.arange(128)[None, :]).astype(np.float32)
    msk_s = np.zeros((128, 128), np.float32)
    for b in range(16):
        wst_s[b * 8:(b + 1) * 8, :, b * 8:(b + 1) * 8] = ws[:, :8, :8].transpose(2, 0, 1)
        msk_s[b * 8:(b + 1) * 8, b * 8:(b + 1) * 8] = msk_p[:8, :8]
    bs = f(inp["sg_b_s"])
    bs_p = np.broadcast_to(bs[None], (128, 4, 128))
    bs_s = np.broadcast_to(np.tile(bs[:, :8], (1, 16))[None], (128, 4, 128))
    sgrow = np.stack([np.broadcast_to(f(inp["sg_b_in"])[2 * D:][None], (128, 2 * D)),
                      np.broadcast_to(f(inp["sg_ln_g"])[None], (128, 2 * D)),
                      np.broadcast_to(f(inp["sg_ln_b"])[None], (128, 2 * D))]).astype(np.float32)
    in_maps = []
    for c in range(n_cores):
        b, h = seq_of_core(c)

        def seg(a, a_s):
            own_ = a[b, h * own:(h + 1) * own]
            halo = np.zeros((HC, D), np.float32) if h == 0 else a[b, h * own - HC:h * own]
            return np.ascontiguousarray(np.concatenate([halo, own_, a_s[c * ns:(c + 1) * ns].reshape(ns * 8, D)], 0).T)
        stp = f(inp["state_pool"][c * ns:(c + 1) * ns])
        stp = np.ascontiguousarray(stp.transpose(2, 0, 1).reshape(8, 128, ns, 15).transpose(1, 0, 2, 3))
        stf = f(inp["state_ffn"])[1:4, c * ns:(c + 1) * ns]
        stf = np.ascontiguousarray(stf.transpose(0, 3, 1, 2).reshape(3, NFF, 128, ns, 2).transpose(0, 2, 1, 3, 4))
        pos = h * own + np.arange(16)
        invc = np.stack([1.0 / np.minimum(w, pos + 1) for w in (2, 4, 8, 16)]).astype(np.float32)
        in_maps.append({
            "x1T": seg(x1p, x1s), "oT": seg(op, os_), "hmask": np.full((128, 1), float(h), np.float32),
            "vecC": vec, "w_o": f(inp["da_w_o"]), "pl_w": f(inp["pl_w"]).reshape(D, 256),
            "stpool": stp, "invc": np.ascontiguousarray(np.broadcast_to(invc[None], (128, 4, 16))),
            "ff_dw": fdw, "ff_b": fb, "ff_st": stf,
            "ff_w_gate": f(inp["ff_w_gate"])[1:4], "ff_w_up": f(inp["ff_w_up"])[1:4], "ff_w_down": f(inp["ff_w_down"])[1:4],
            "sg_w_in": f(inp["sg_w_in"]), "sg_w_out": f(inp["sg_w_out"]), "sg_rows": sgrow,
            "sg_wsT": np.stack([wst_p, wst_s]), "sg_mask": np.stack([msk_p, msk_s]),
            "sg_bs": np.ascontiguousarray(np.stack([bs_p, bs_s])),
        })
    res = run_bass_kernel_spmd(nc, in_maps, core_ids=list(range(n_cores)))
    return res.results


_B_CACHE = {}


def run_B(inp, q_p, k_p, v_p, q_s, k_s, v_s, n_heads=8):
    f = lambda a: np.asarray(a, np.float32)
    nseq, S, _ = q_p.shape
    nss = q_s.shape[0]
    ck_all = f(inp["cache_k"])
    cv_all = f(inp["cache_v"])
    n_phys = ck_all.shape[0]
    ptab = np.asarray(inp["page_table"], np.int32)
    npg = ptab.shape[1]
    key = (nseq, S, nss, npg, n_phys)
    if key not in _B_CACHE:
        _B_CACHE[key] = build_B(nseq, S, nss, npg, n_phys)
    nc = _B_CACHE[key]
    masks, smask = make_masks()
    lamp = np.concatenate([f(inp["da_lq1"]), f(inp["da_lk1"]), f(inp["da_lq2"]), f(inp["da_lk2"])]).reshape(1, 256)
    assert npg == 16
    ptabr = np.ascontiguousarray(np.repeat(ptab.T, 8, axis=0))
    iota = (np.arange(128) % 8).astype(np.float32).reshape(128, 1)
    ident = np.eye(128, dtype=np.float32)
    ng = f(inp["da_norm_g"])
    in_maps = []
    for c in range(n_heads):
        hs = slice(c * 128, (c + 1) * 128)
        in_maps.append({
            "qT": np.ascontiguousarray(q_p[:, :, hs].transpose(0, 2, 1)),
            "kT": np.ascontiguousarray(k_p[:, :, hs].transpose(0, 2, 1)),
            "v": np.ascontiguousarray(v_p[:, :, hs]),
            "qsT": np.ascontiguousarray(q_s.reshape(nss * 8, -1)[:, hs].T),
            "ksT": np.ascontiguousarray(k_s.reshape(nss * 8, -1)[:, hs].T),
            "vs": np.ascontiguousarray(v_s.reshape(nss * 8, -1)[:, hs]),
            "ck": np.ascontiguousarray(ck_all[:, :, c, :]),
            "cv": np.ascontiguousarray(cv_all[:, :, c, :]),
            "ptabr": ptabr, "iota": iota, "ident": ident, "lamp": lamp,
            "gcol": np.ascontiguousarray(ng[hs].reshape(128, 1)),
            "grow": np.ascontiguousarray(np.broadcast_to(ng[hs].reshape(1, 128), (8, 128))),
            "masks": masks, "smask": smask,
        })
    res = run_bass_kernel_spmd(nc, in_maps, core_ids=list(range(n_heads))).results
    op = np.empty((nseq, S, n_heads * 128), np.float32)
    os_ = np.empty((nss, 8, n_heads * 128), np.float32)
    for c in range(n_heads):
        hs = slice(c * 128, (c + 1) * 128)
        op[:, :, hs] = res[c]["oT"].transpose(0, 2, 1)
        os_[:, :, hs] = res[c]["os"].reshape(nss, 8, 128)
    return op, os_


def kernel(**inp):
    f = lambda a: np.asarray(a, np.float32)
    xp = f(inp["x_prompt"])
    xs = f(inp["x_sample"])
    Bn, S, _ = xp.shape
    NSS = xs.shape[0]
    n_cores = 8
    own = S * Bn // n_cores
    ns = NSS // n_cores
    halves = S // own

    def seq_of(c):
        return (c // halves, c % halves)

    rA = run_A(inp, own, ns, n_cores, seq_of)
    x1p = np.empty((Bn, S, D), np.float32)
    x1s = np.empty((NSS, 8, D), np.float32)
    qkv_p = np.empty((Bn, S, 3 * D), np.float32)
    qkv_s = np.empty((NSS, 8, 3 * D), np.float32)
    conv_p = np.empty((Bn, 30, D), np.float32)
    conv_s = np.empty((NSS, 30, D), np.float32)
    ffn_p = np.empty((4, Bn, 2, DFF), np.float32)
    ffn_s = np.empty((4, NSS, 2, DFF), np.float32)

    def put_ffn(li, c, b, h, ff):
        if h == halves - 1:
            ffn_p[li, b] = ff[:, :, :2].transpose(2, 1, 0).reshape(2, DFF)
        ffn_s[li, c * ns:(c + 1) * ns] = ff[:, :, 2:].reshape(128, NFF, ns, 2).transpose(2, 3, 1, 0).reshape(ns, 2, DFF)

    for c in range(n_cores):
        b, h = seq_of(c)
        r = rA[c]
        x1 = r["x1T"].T
        x1p[b, h * own:(h + 1) * own] = x1[HA:HA + own]
        x1s[c * ns:(c + 1) * ns] = x1[HA + own:].reshape(ns, 8, D)
        q = r["qkvT"].T
        qkv_p[b, h * own:(h + 1) * own] = q[HA:HA + own]
        qkv_s[c * ns:(c + 1) * ns] = q[HA + own:].reshape(ns, 8, 3 * D)
        cv = r["convT"]
        if h == halves - 1:
            conv_p[b] = cv[:, :, :30].transpose(2, 1, 0).reshape(30, D)
        conv_s[c * ns:(c + 1) * ns] = cv[:, :, 30:].reshape(128, 8, ns, 30).transpose(2, 3, 1, 0).reshape(ns, 30, D)
        put_ffn(0, c, b, h, r["ffnT"])
    del rA
    k_rows_p = np.ascontiguousarray(qkv_p[:, :, D:2 * D]).reshape(Bn, S, 8, 128)
    v_rows_p = np.ascontiguousarray(qkv_p[:, :, 2 * D:]).reshape(Bn, S, 8, 128)
    k_rows_s = np.ascontiguousarray(qkv_s[:, :, D:2 * D]).reshape(NSS, 8, 8, 128)
    v_rows_s = np.ascontiguousarray(qkv_s[:, :, 2 * D:]).reshape(NSS, 8, 8, 128)

    op, os_ = run_B(inp, qkv_p[:, :, :D], qkv_p[:, :, D:2 * D], qkv_p[:, :, 2 * D:],
                    qkv_s[:, :, :D], qkv_s[:, :, D:2 * D], qkv_s[:, :, 2 * D:])

    rC = run_C(inp, x1p, x1s, op, os_, own, ns, n_cores, seq_of)
    y_p = np.empty((Bn, S, D), np.float32)
    y_s = np.empty((NSS, 8, D), np.float32)
    pool_p = np.empty((Bn, 15, D), np.float32)
    pool_s = np.empty((NSS, 15, D), np.float32)
    sgv = np.empty((NSS, 8, 2 * D), np.float32)
    for c in range(n_cores):
        b, h = seq_of(c)
        r = rC[c]
        y = r["yT"].T
        y_p[b, h * own:(h + 1) * own] = y[HC:HC + own]
        y_s[c * ns:(c + 1) * ns] = y[HC + own:].reshape(ns, 8, D)
        pl = r["poolT"]
        if h == halves - 1:
            pool_p[b] = pl[:, :, :15].transpose(2, 1, 0).reshape(15, D)
        pool_s[c * ns:(c + 1) * ns] = pl[:, :, 15:].reshape(128, 8, ns, 15).transpose(2, 3, 1, 0).reshape(ns, 15, D)
        sgv[c * ns:(c + 1) * ns] = r["sgv"].reshape(ns, 8, 2 * D)
        for li in range(3):
            put_ffn(li + 1, c, b, h, r["ffnT"][li])
    return (y_p, y_s, conv_p, conv_s, k_rows_p, v_rows_p, k_rows_s, v_rows_s, pool_p, pool_s, sgv, ffn_p, ffn_s)
```

```python
import math
from contextlib import ExitStack

import numpy as np
import concourse.bass as bass
import concourse.mybir as mybir
from concourse.bass_utils import run_bass_kernel_spmd

F32 = mybir.dt.float32
BF16 = mybir.dt.bfloat16
I32 = mybir.dt.int32
AF = mybir.ActivationFunctionType
ALU = mybir.AluOpType
AX = mybir.AxisListType

D = 1024
DFF = 2816
NFF = 22
EPS = 1e-6
ENGS = ("pe", "act", "dve", "pool", "sp")


class Ins:
    __slots__ = ("eng", "emit", "deps", "dsem", "dwaits", "signal", "cnt")

    def __init__(self, eng, emit, dsem):
        self.eng = eng
        self.emit = emit
        self.dsem = dsem
        self.deps = []
        self.dwaits = {}
        self.signal = False
        self.cnt = 0


class Prog:
    def __init__(self, nc):
        self.nc = nc
        self.st = {e: [] for e in ENGS}
        self.lastw = {}
        self.rd = {}
        self.dcnt = {}

    def op(self, eng, emit, r=(), w=(), dsem=None):
        ins = Ins(eng, emit, dsem)
        deps = {}

        def need(d, kind):
            if d is None or d is ins:
                return
            if d.dsem is None and d.eng == eng and (eng == "pe" or kind == "WAR"):
                return
            deps[id(d)] = d

        for k in r:
            need(self.lastw.get(k), "RAW")
        for k in w:
            need(self.lastw.get(k), "WAW")
            for d in self.rd.get(k, {}).values():
                need(d, "WAR")
        ins.deps = list(deps.values())
        for d in ins.deps:
            if d.dsem is not None:
                ins.dwaits[d.dsem] = self.dcnt[d.dsem]
        rkey = eng if dsem is None else ("d", dsem)
        for k in r:
            self.rd.setdefault(k, {})[rkey] = ins
        for k in w:
            self.lastw[k] = ins
            self.rd[k] = {}
        if dsem is not None:
            self.dcnt[dsem] = self.dcnt.get(dsem, 0) + 16
        self.st[eng].append(ins)
        return ins

    def emit_all(self, stack):
        nc = self.nc
        for e in ENGS:
            for ins in self.st[e]:
                for d in ins.deps:
                    if d.dsem is None:
                        d.signal = True
        for e in ENGS:
            c = 0
            for ins in self.st[e]:
                if ins.dsem is None and ins.signal:
                    c += 1
                ins.cnt = c
        esem = {e: stack.enter_context(nc.semaphore("e_" + e)) for e in ENGS if e != "sp"}
        dsem = {k: stack.enter_context(nc.semaphore("d_%s" % str(k))) for k in self.dcnt}
        block = stack.enter_context(nc.Block())
        final = dict(self.dcnt)

        def run(e, eng):
            waited = {}
            for ins in self.st[e]:
                waits = {}
                for d in ins.deps:
                    if d.dsem is None:
                        key, val = ("e", d.eng), d.cnt
                    else:
                        key, val = ("d", d.dsem), ins.dwaits[d.dsem]
                    if waits.get(key, 0) < val:
                        waits[key] = val
                for key, val in waits.items():
                    if waited.get(key, 0) >= val:
                        continue
                    eng.wait_ge(esem[key[1]] if key[0] == "e" else dsem[key[1]], val)
                    waited[key] = val
                bi = ins.emit(eng)
                if ins.dsem is not None:
                    bi.then_inc(dsem[ins.dsem], 16)
                elif ins.signal:
                    bi.then_inc(esem[e], 1)
            if e == "sp":
                for k, v in final.items():
                    eng.wait_ge(dsem[k], v)

        block.tensor(lambda eng: run("pe", eng))
        block.scalar(lambda eng: run("act", eng))
        block.vector(lambda eng: run("dve", eng))
        block.gpsimd(lambda eng: run("pool", eng))
        block.sync(lambda eng: run("sp", eng))


def split_tiles(n, maxn=512):
    out = []
    t = 0
    while t < n:
        m = min(maxn, n - t)
        out.append((t, m))
        t += m
    return out


class Ctx:
    def __init__(self, nc, stack, n_wslots=4, wslot_elems=4096):
        self.nc = nc
        self.stack = stack
        self.p = Prog(nc)
        self.ps = [stack.enter_context(nc.psum_tensor("ps%d" % i, [128, 512], F32)) for i in range(8)]
        self.ps_i = 0
        self.wslots = [stack.enter_context(nc.sbuf_tensor("wslot%d" % i, [128, wslot_elems], BF16))
                       for i in range(n_wslots)]
        self.w_i = 0
        self.ones = self.sb("ones_bf", [128, 128], BF16)
        self.p.op("pool", lambda e: e.memset(self.ones[:], 1.0), w=["ones"])
        self.epsc = self.sb("epsc", [128, 1], F32)
        self.p.op("pool", lambda e: e.memset(self.epsc[:], EPS), w=["epsc"])
        self.uid = 0
        self.scr = None

    def sq_next(self):
        self.sq_i = (getattr(self, "sq_i", -1) + 1) % len(self.scr_sq)
        return self.sq_i

    def sb(self, name, shape, dt):
        return self.stack.enter_context(self.nc.sbuf_tensor("sb_" + name, shape, dt))

    def alloc(self, name, shape, dt):
        if self.scr is None:
            return self.sb(name, shape, dt)
        n = 1
        for d_ in shape[1:]:
            n *= d_
        nwords = (n * (4 if dt == F32 or dt == I32 else 2) + 3) // 4
        nwords = (nwords + 7) // 8 * 8
        assert self.scr_off + nwords <= self.scr_words, ("scratch overflow", name, self.scr_off, nwords, self.scr_words)
        v = self.scr[:, self.scr_off:self.scr_off + nwords]
        self.scr_off += nwords
        if dt != F32:
            v = v.bitcast(dt)
        v = v[:, 0:n]
        if len(shape) == 3:
            v = v.rearrange("p (a b) -> p a b", a=shape[1])
        elif len(shape) == 4:
            v = v.rearrange("p (a b c) -> p a b c", a=shape[1], b=shape[2])
        return v

    def scratch_init(self, words):
        self.scr = self.sb("scratch", [128, words], F32)
        self.scr_words = words
        self.scr_off = 0
        self.dmy = {e: self.sb("dmy_" + e, [128, 4], F32) for e in ("act", "dve", "pool")}

    def new_scope(self):
        p = self.p
        self.scr_off = 0
        self.nbar = getattr(self, "nbar", 0) + 1
        b = self.nbar
        p.op("act", lambda e: e.activation(out=self.dmy["act"][:, 0:1], in_=self.epsc[:, 0:1], func=AF.Copy),
             r=["epsc"], w=[("bar", b, "act")])
        p.op("dve", lambda e: e.memset(self.dmy["dve"][:, 0:1], 0.0), w=[("bar", b, "dve")])
        p.op("pool", lambda e: e.memset(self.dmy["pool"][:, 0:1], 0.0), w=[("bar", b, "pool")])
        allb = [("bar", b, e_) for e_ in ("act", "dve", "pool")]
        p.op("act", lambda e: e.activation(out=self.dmy["act"][:, 1:2], in_=self.epsc[:, 0:1], func=AF.Copy),
             r=["epsc"] + allb, w=[("bar2", b, "act")])
        p.op("dve", lambda e: e.memset(self.dmy["dve"][:, 1:2], 0.0), r=allb, w=[("bar2", b, "dve")])
        p.op("pool", lambda e: e.memset(self.dmy["pool"][:, 1:2], 0.0), r=allb, w=[("bar2", b, "pool"), "scr"])

    def load_s(self, dst, src, key, dsem, eng="sp"):
        self.p.op(eng, lambda e, o=dst, i=src: e.dma_start(out=o, in_=i), r=["scr"], w=[key], dsem=dsem)

    def store_s(self, dst, src, key, dsem, eng="sp"):
        self.p.op(eng, lambda e, o=dst, i=src: e.dma_start(out=o, in_=i), r=[key, "scr"], dsem=dsem)

    def dram_in(self, name, shape, dt=F32):
        return self.nc.dram_tensor(name, list(shape), dt, kind="ExternalInput").ap()

    def dram_out(self, name, shape, dt=F32):
        return self.nc.dram_tensor(name, list(shape), dt, kind="ExternalOutput").ap()

    def psum(self, exclude=0):
        i = self.ps_i
        self.ps_i = (self.ps_i + 1) % (8 - exclude)
        return self.ps[exclude + i], ("ps", exclude + i)

    def wload(self, src_ap, kc, ncols):
        s = self.w_i
        self.w_i = (self.w_i + 1) % len(self.wslots)
        view = self.wslots[s][:, 0:kc * ncols].rearrange("p (k n) -> p k n", k=kc)
        src = src_ap.rearrange("(k p) n -> p k n", p=128)
        self.p.op("pool", lambda e, o=view, i=src: e.dma_start(out=o, in_=i), w=[("w", s)], dsem=("w", s))
        return view, ("w", s)

    def load(self, dst, src, key, dsem, eng="sp"):
        self.p.op(eng, lambda e, o=dst, i=src: e.dma_start(out=o, in_=i), w=[key], dsem=dsem)

    def store(self, dst, src, key, dsem, eng="sp"):
        self.p.op(eng, lambda e, o=dst, i=src: e.dma_start(out=o, in_=i), r=[key], dsem=dsem)

    def const_cols(self, name, dram_ap, ncols):
        t = self.sb(name, [128, ncols], F32)
        self.load(t[:], dram_ap, name, "const")
        return t


def rmsnorm(cx, X, gcol, tiles, out_fn, tag, nch=8, dim=D):
    p = cx.p
    sq = cx.scr_sq
    R = cx.scr_r
    for ti, (t0, n) in enumerate(tiles):
        par = ti % 2
        ps, pk = cx.psum()
        for kc in range(nch):
            sqi = cx.sq_next()
            p.op("act", lambda e, o=sq[sqi][:, 0:n], i=X[:, kc, t0:t0 + n]:
                 e.activation(out=o, in_=i, func=AF.Square), r=[("X", ti)], w=[("sq", sqi)])
            p.op("pe", lambda e, o=ps[:, 0:n], r_=sq[sqi][:, 0:n], s=(kc == 0), t=(kc == nch - 1):
                 e.matmul(o, lhsT=cx.ones[:], rhs=r_, start=s, stop=t), r=[("sq", sqi), "ones"], w=[pk])
        p.op("act", lambda e, o=R[par][:, 0:n], i=ps[:, 0:n]:
             e.activation(out=o, in_=i, func=AF.Sqrt, bias=cx.epsc[:, 0:1], scale=1.0 / dim),
             r=[pk, "epsc"], w=[("R", par)])
        p.op("dve", lambda e, o=R[par][:, 0:n]: e.reciprocal(out=o, in_=o),
             r=[("R", par)], w=[("R", par)])
        for kc in range(nch):
            o_ap, wk = out_fn(kc, ti, t0, n)
            p.op("dve", lambda e, o=o_ap, i=X[:, kc, t0:t0 + n], g=gcol[:, kc:kc + 1], r_=R[par][:, 0:n]:
                 e.scalar_tensor_tensor(out=o, in0=i, scalar=g, in1=r_, op0=ALU.mult, op1=ALU.mult),
                 r=[("X", ti), ("R", par), gcol_key(gcol)], w=[wk])


_gk = {}


def gcol_key(t):
    return _gk.get(id(t), "const")


def xn_out(cx):
    def f(kc, ti, t0, n):
        return cx.XN[:, kc, t0:t0 + n], ("XN", ti)
    return f


def proj(cx, W, kc_n, col0, ncols_total, src, src_key, tiles, consume, group_cols=None):
    p = cx.p
    if group_cols is None:
        group_cols = max(128, (4096 // kc_n) // 128 * 128)
    c = 0
    while c < ncols_total:
        gc = min(group_cols, ncols_total - c)
        wv, wk = cx.wload(W[:, col0 + c:col0 + c + gc], kc_n, gc)
        for mm in range(gc // 128):
            for ti, (t0, n) in enumerate(tiles):
                ps, pk = cx.psum()
                for kc in range(kc_n):
                    p.op("pe", lambda e, o=ps[:, 0:n], l=wv[:, kc, mm * 128:(mm + 1) * 128],
                         r_=src[:, kc, t0:t0 + n], s=(kc == 0), t=(kc == kc_n - 1):
                         e.matmul(o, lhsT=l, rhs=r_, start=s, stop=t),
                         r=[wk, (src_key, ti)], w=[pk])
                consume(c // 128 + mm, ti, t0, n, ps, pk)
        c += gc


def conv_ffn(cx, li, W, tiles, np_tok, ns, halo, part=3):
    p = cx.p
    X, XN = cx.X, cx.XN
    p_dw = cx.ffn_dw[li]
    p_b = cx.ffn_b[li]
    stf = cx.ffn_state[li]
    outst = cx.ffn_out[li]
    j = 0
    parts = []
    while j < NFF:
        parts.append((j, min(part, NFF - j)))
        j += part
    for (j0, nj) in parts:
        wg, wgk = cx.wload(W["gate"][:, j0 * 128:(j0 + nj) * 128], 8, nj * 128)
        wu, wuk = cx.wload(W["up"][:, j0 * 128:(j0 + nj) * 128], 8, nj * 128)
        wd, wdk = cx.wload(W["down"][j0 * 128:(j0 + nj) * 128, :], nj, D)
        for jj in range(nj):
            jg = j0 + jj
            Gs = cx.Gs[jg % 2]
            p.op("pool", lambda e, o=Gs[:, :, 0:2], i=stf[:, jg, :, :]: e.tensor_copy(out=o, in_=i),
                 r=[("stf", li)], w=[("Gs", jg % 2)])
            prev = None
            for ti, (t0, n) in enumerate(tiles):
                samp = t0 >= np_tok
                psg, pgk = cx.psum()
                for kc in range(8):
                    p.op("pe", lambda e, o=psg[:, 0:n], l=wg[:, kc, jj * 128:(jj + 1) * 128],
                         r_=XN[:, kc, t0:t0 + n], s=(kc == 0), t=(kc == 7):
                         e.matmul(o, lhsT=l, rhs=r_, start=s, stop=t), r=[wgk, ("XN", ti)], w=[pgk])
                psu, puk = cx.psum()
                for kc in range(8):
                    p.op("pe", lambda e, o=psu[:, 0:n], l=wu[:, kc, jj * 128:(jj + 1) * 128],
                         r_=XN[:, kc, t0:t0 + n], s=(kc == 0), t=(kc == 7):
                         e.matmul(o, lhsT=l, rhs=r_, start=s, stop=t), r=[wuk, ("XN", ti)], w=[puk])
                cx.gt_i = (cx.gt_i + 1) % 2
                gp = cx.gt_i
                acc = cx.facc[gp]
                sil = cx.fsil[gp]
                if not samp:
                    Gt = cx.gt[gp]
                    gk = ("gt", gp)
                    if prev is None:
                        p.op("pool", lambda e, o=Gt[:, 0:2]: e.memset(o, 0.0), w=[gk])
                    else:
                        pg, pn = prev
                        p.op("pool", lambda e, o=Gt[:, 0:2], i=cx.gt[pg][:, pn:pn + 2]: e.tensor_copy(out=o, in_=i),
                             r=[("gt", pg)], w=[gk])
                    p.op("act", lambda e, o=Gt[:, 2:2 + n], i=psg[:, 0:n]:
                         e.activation(out=o, in_=i, func=AF.Copy), r=[pgk], w=[gk])
                    if ti == 0 and halo > 0:
                        assert halo <= n
                        p.op("dve", lambda e, o=Gt[:, 2:2 + halo]:
                             e.tensor_scalar(out=o, in0=o, scalar1=cx.hmask[:, 0:1], scalar2=None, op0=ALU.mult),
                             r=[gk, "const"], w=[gk])
                    prev = (gp, n)
                    v0 = Gt[:, 0:n]
                    v1 = Gt[:, 1:1 + n]
                    v2 = Gt[:, 2:2 + n]
                    a_ = acc[:, 0:n]
                    s_ = sil[:, 0:n]
                    u_ = psu[:, 0:n]
                    h_ = cx.Hb[:, jj, t0:t0 + n]
                    if t0 + n == np_tok:
                        p.op("pool", lambda e, o=outst[:, jg, 0:2], i=Gt[:, n:n + 2]: e.tensor_copy(out=o, in_=i),
                             r=[gk], w=[("ffo", li)])
                else:
                    gk = ("Gs", jg % 2)
                    p.op("act", lambda e, o=Gs[:, :, 2:10], i=psg[:, 0:n].rearrange("p (s t) -> p s t", t=8):
                         e.activation(out=o, in_=i, func=AF.Copy), r=[pgk], w=[gk])
                    v0 = Gs[:, :, 0:8]
                    v1 = Gs[:, :, 1:9]
                    v2 = Gs[:, :, 2:10]
                    a_ = acc[:, 0:n].rearrange("p (s t) -> p s t", t=8)
                    s_ = sil[:, 0:n].rearrange("p (s t) -> p s t", t=8)
                    u_ = psu[:, 0:n].rearrange("p (s t) -> p s t", t=8)
                    h_ = cx.Hb[:, jj, t0:t0 + n].rearrange("p (s t) -> p s t", t=8)
                    p.op("pool", lambda e, o=outst[:, jg, 2:2 + 2 * ns].rearrange("p (s t) -> p s t", t=2),
                         i=Gs[:, :, 8:10]: e.tensor_copy(out=o, in_=i), r=[gk], w=[("ffo", li)])
                ak = ("facc", gp)
                sk = ("fsil", gp)
                p.op("act", lambda e, o=a_, i=v0, sc=p_dw[:, jg, 0:1], b=p_b[:, jg:jg + 1]:
                     e.activation(out=o, in_=i, func=AF.Identity, bias=b, scale=sc),
                     r=[gk, "const"], w=[ak])
                p.op("dve", lambda e, o=a_, i=v1, sc=p_dw[:, jg, 1:2]:
                     e.scalar_tensor_tensor(out=o, in0=i, scalar=sc, in1=o, op0=ALU.mult, op1=ALU.add),
                     r=[gk, ak, "const"], w=[ak])
                p.op("dve", lambda e, o=a_, i=v2, sc=p_dw[:, jg, 2:3]:
                     e.scalar_tensor_tensor(out=o, in0=i, scalar=sc, in1=o, op0=ALU.mult, op1=ALU.add),
                     r=[gk, ak, "const"], w=[ak])
                p.op("act", lambda e, o=s_, i=a_: e.activation(out=o, in_=i, func=AF.Silu), r=[ak], w=[sk])
                p.op("dve", lambda e, o=h_, a=s_, b=u_: e.tensor_tensor(out=o, in0=a, in1=b, op=ALU.mult),
                     r=[sk, puk], w=[("Hb", ti)])
        for m in range(8):
            for ti, (t0, n) in enumerate(tiles):
                ps, pk = cx.psum()
                for jj in range(nj):
                    p.op("pe", lambda e, o=ps[:, 0:n], l=wd[:, jj, m * 128:(m + 1) * 128],
                         r_=cx.Hb[:, jj, t0:t0 + n], s=(jj == 0), t=(jj == nj - 1):
                         e.matmul(o, lhsT=l, rhs=r_, start=s, stop=t), r=[wdk, ("Hb", ti)], w=[pk])
                p.op("dve", lambda e, o=X[:, m, t0:t0 + n], i=ps[:, 0:n]:
                     e.tensor_tensor(out=o, in0=i, in1=o, op=ALU.add), r=[pk, ("X", ti)], w=[("X", ti)])


def ffn_setup(cx, n_layers, T, ns, dw_d, b_d, st_d, part=3):
    cx.ffn_dw, cx.ffn_b, cx.ffn_state, cx.ffn_out = [], [], [], []
    for li in range(n_layers):
        t = cx.alloc("ffdw%d" % li, [128, NFF, 3], F32)
        cx.load_s(t[:], dw_d[li], "const", "const")
        cx.ffn_dw.append(t)
        t = cx.alloc("ffb%d" % li, [128, NFF], F32)
        cx.load_s(t[:], b_d[li], "const", "const")
        cx.ffn_b.append(t)
        t = cx.alloc("ffst%d" % li, [128, NFF, ns, 2], F32)
        cx.load_s(t[:], st_d[li], ("stf", li), "const")
        cx.ffn_state.append(t)
        cx.ffn_out.append(cx.alloc("ffo%d" % li, [128, NFF, 2 + 2 * ns], F32))
    cx.gt = [cx.alloc("gt%d" % i, [128, 514], F32) for i in range(2)]
    cx.gt_i = 0
    cx.Gs = [cx.alloc("Gs%d" % i, [128, ns, 10], F32) for i in range(2)]
    cx.facc = [cx.alloc("facc%d" % i, [128, 512], F32) for i in range(2)]
    cx.fsil = [cx.alloc("fsil%d" % i, [128, 512], F32) for i in range(2)]
    cx.Hb = cx.alloc("Hb", [128, part, T], BF16)


HA = 32


def build_A(own, ns):
    np_tok = HA + own
    T = np_tok + ns * 8
    nc = bass.Bass("TRN2", target_bir_lowering=False)
    stack = ExitStack()
    with stack:
        cx = Ctx(nc, stack, n_wslots=4, wslot_elems=3072)
        p = cx.p
        d_x = cx.dram_in("xT", [D, T])
        d_hm = cx.dram_in("hmask", [128, 1])
        d_stc = cx.dram_in("stconv", [128, 8, ns, 30])
        d_vec = cx.dram_in("vecA", [128, 72])
        d_wdw = cx.dram_in("cv_wdw", [128, 8, 31])
        d_ident = cx.dram_in("identA", [128, 128])
        d_win = cx.dram_in("cv_w_in", [D, 2 * D])
        d_wout = cx.dram_in("cv_w_out", [D, D])
        d_fdw = cx.dram_in("ff_dw", [1, 128, NFF, 3])
        d_fb = cx.dram_in("ff_b", [1, 128, NFF])
        d_fst = cx.dram_in("ff_st", [1, 128, NFF, ns, 2])
        d_wg = cx.dram_in("ff_w_gate", [D, DFF])
        d_wu = cx.dram_in("ff_w_up", [D, DFF])
        d_wd = cx.dram_in("ff_w_down", [DFF, D])
        d_wqkv = cx.dram_in("w_qkv", [D, 3 * D])
        o_x1 = cx.dram_out("x1T", [D, T])
        o_qkv = cx.dram_out("qkvT", [3 * D, T])
        o_conv = cx.dram_out("convT", [128, 8, 30 + ns * 30])
        o_ffn = cx.dram_out("ffnT", [128, NFF, 2 + 2 * ns])

        tiles = split_tiles(np_tok) + [(np_tok, ns * 8)]
        ptiles = tiles[:-1]
        cx.X = cx.sb("X", [128, 8, T], F32)
        cx.XN = cx.sb("XN", [128, 8, T], BF16)
        cx.scr_sq = [cx.sb("sq%d" % i, [128, 512], BF16) for i in range(4)]
        cx.scr_r = [cx.sb("R%d" % i, [128, 512], F32) for i in range(2)]
        cx.hmask = cx.const_cols("hmask", d_hm, 1)
        vec = cx.const_cols("vecA", d_vec, 72)
        X, XN = cx.X, cx.XN
        for ti, (t0, n) in enumerate(tiles):
            for kc in range(8):
                cx.load(X[:, kc, t0:t0 + n], d_x[kc * 128:(kc + 1) * 128, t0:t0 + n], ("X", ti), ("xin", ti % 2))
        wdw = cx.sb("wdw", [128, 8, 31], F32)
        cx.load(wdw[:], d_wdw, "const", "const")
        identb = cx.sb("identb", [128, 128], BF16)
        p.op("pool", lambda e: e.dma_start(out=identb[:], in_=d_ident), w=["identb"], dsem="const2")
        cx.scratch_init(17000)
        G0 = cx.alloc("G0", [128, 8, 30 + np_tok], BF16)
        GS = cx.alloc("GS", [128, 8, ns, 38], F32)
        cx.load_s(GS[:, :, :, 0:30], d_stc, "GS", "const_s")
        glast = cx.alloc("glast", [128, 8, 32], F32)
        for c in range(8):
            p.op("pool", lambda e, o=G0[:, c, 0:30]: e.memset(o, 0.0), w=[("G0", c)])

        rmsnorm(cx, X, vec[:, 0:8], tiles, xn_out(cx), "n0")
        s1 = [cx.alloc("s1_%d" % i, [128, 512], F32) for i in range(2)]
        for (c0, ncg) in ((0, 3), (3, 3), (6, 2)):
            wa, wak = cx.wload(d_win[:, c0 * 128:(c0 + ncg) * 128], 8, ncg * 128)
            wg_, wgk = cx.wload(d_win[:, D + c0 * 128:D + (c0 + ncg) * 128], 8, ncg * 128)
            for cc in range(ncg):
                c = c0 + cc
                for ti, (t0, n) in enumerate(tiles):
                    samp = t0 >= np_tok
                    psa, pak = cx.psum()
                    for kc in range(8):
                        p.op("pe", lambda e, o=psa[:, 0:n], l=wa[:, kc, cc * 128:(cc + 1) * 128],
                             r_=XN[:, kc, t0:t0 + n], s=(kc == 0), t=(kc == 7):
                             e.matmul(o, lhsT=l, rhs=r_, start=s, stop=t), r=[wak, ("XN", ti)], w=[pak])
                    psg, pgk = cx.psum()
                    for kc in range(8):
                        p.op("pe", lambda e, o=psg[:, 0:n], l=wg_[:, kc, cc * 128:(cc + 1) * 128],
                             r_=XN[:, kc, t0:t0 + n], s=(kc == 0), t=(kc == 7):
                             e.matmul(o, lhsT=l, rhs=r_, start=s, stop=t), r=[wgk, ("XN", ti)], w=[pgk])
                    sp_ = ti % 2
                    p.op("act", lambda e, o=s1[sp_][:, 0:n], i=psg[:, 0:n], b=vec[:, 8 + 8 + c:8 + 8 + c + 1]:
                         e.activation(out=o, in_=i, func=AF.Sigmoid, bias=b), r=[pgk, "const"], w=[("s1", sp_)])
                    if not samp:
                        p.op("dve", lambda e, o=G0[:, c, 30 + t0:30 + t0 + n], i=psa[:, 0:n],
                             b=vec[:, 8 + c:8 + c + 1], s=s1[sp_][:, 0:n]:
                             e.scalar_tensor_tensor(out=o, in0=i, scalar=b, in1=s, op0=ALU.add, op1=ALU.mult),
                             r=[pak, ("s1", sp_), "const"], w=[("G0", c)])
                        if t0 + n == np_tok:
                            nl = min(n, 30)
                            p.op("dve", lambda e, o=glast[:, c, 30 - nl:30], i=psa[:, n - nl:n],
                                 b=vec[:, 8 + c:8 + c + 1], s=s1[sp_][:, n - nl:n]:
                                 e.scalar_tensor_tensor(out=o, in0=i, scalar=b, in1=s, op0=ALU.add, op1=ALU.mult),
                                 r=[pak, ("s1", sp_), "const"], w=["glast"])
                    else:
                        p.op("dve", lambda e, o=GS[:, c, :, 30:38], i=psa[:, 0:n].rearrange("p (s t) -> p s t", t=8),
                             b=vec[:, 8 + c:8 + c + 1], s=s1[sp_][:, 0:n].rearrange("p (s t) -> p s t", t=8):
                             e.scalar_tensor_tensor(out=o, in0=i, scalar=b, in1=s, op0=ALU.add, op1=ALU.mult),
                             r=[pak, ("s1", sp_), "const"], w=["GS"])
        assert ptiles[-1][1] >= 30
        cx.store_s(o_conv[:, :, 0:30], glast[:, :, 0:30], "glast", "outs")
        for c in range(8):
            cx.store_s(o_conv[:, c, 30:30 + ns * 30].rearrange("p (s t) -> p s t", t=30), GS[:, c, :, 8:38], "GS", "outs")
        accs = cx.alloc("caccs", [128, ns, 8], F32)
        DG = cx.alloc("DG", [128, 31, 128], BF16)
        for c in range(8):
            p.op("dve", lambda e, o=G0[:, c, 30:30 + HA]:
                 e.tensor_scalar(out=o, in0=o, scalar1=cx.hmask[:, 0:1], scalar2=None, op0=ALU.mult),
                 r=[("G0", c), "const"], w=[("G0", c)])
            for k in range(31):
                p.op("act", lambda e, o=DG[:, k, :], sc=wdw[:, c, k:k + 1]:
                     e.activation(out=o, in_=identb[:, :], func=AF.Copy, scale=sc), r=["identb", "const"], w=["DG"])
            for ti, (t0, n) in enumerate(ptiles):
                ps, pk = cx.psum()
                for k in range(31):
                    p.op("pe", lambda e, o=ps[:, 0:n], l=DG[:, k, :], r_=G0[:, c, t0 + k:t0 + k + n], s=(k == 0), t=(k == 30):
                         e.matmul(o, lhsT=l, rhs=r_, start=s, stop=t), r=["DG", ("G0", c)], w=[pk])
                p.op("act", lambda e, o=XN[:, c, t0:t0 + n], i=ps[:, 0:n], b=vec[:, 24 + c:25 + c]:
                     e.activation(out=o, in_=i, func=AF.Identity, bias=b), r=[pk, "const"], w=[("XN", ti)])
            for k in range(31):
                last = k == 30
                o_s = XN[:, c, np_tok:T].rearrange("p (s t) -> p s t", t=8) if last else accs[:, :, :]
                if k == 0:
                    p.op("dve", lambda e, o=o_s, i=GS[:, c, :, 0:8], sc=wdw[:, c, 0:1], b=vec[:, 24 + c:25 + c]:
                         e.tensor_scalar(out=o, in0=i, scalar1=sc, scalar2=b, op0=ALU.mult, op1=ALU.add),
                         r=["GS", "const"], w=["caccs"])
                else:
                    p.op("dve", lambda e, o=o_s, i=GS[:, c, :, k:k + 8], sc=wdw[:, c, k:k + 1], a=accs[:, :, :]:
                         e.scalar_tensor_tensor(out=o, in0=i, scalar=sc, in1=a, op0=ALU.mult, op1=ALU.add),
                         r=["GS", "caccs", "const"], w=["caccs"] + ([("XN", len(tiles) - 1)] if last else []))
        cx.new_scope()
        mu = [cx.alloc("mu%d" % i, [128, 512], F32) for i in range(2)]
        var = [cx.alloc("var%d" % i, [128, 512], F32) for i in range(2)]
        tmpc = [cx.alloc("tmpc%d" % i, [128, 512], F32) for i in range(2)]
        for ti, (t0, n) in enumerate(tiles):
            par = ti % 2
            ps1, pk1 = cx.psum()
            for kc in range(8):
                p.op("pe", lambda e, o=ps1[:, 0:n], r_=XN[:, kc, t0:t0 + n], s=(kc == 0), t=(kc == 7):
                     e.matmul(o, lhsT=cx.ones[:], rhs=r_, start=s, stop=t), r=[("XN", ti), "ones"], w=[pk1])
            ps2, pk2 = cx.psum()
            for kc in range(8):
                sqi = cx.sq_next()
                p.op("act", lambda e, o=cx.scr_sq[sqi][:, 0:n], i=XN[:, kc, t0:t0 + n]:
                     e.activation(out=o, in_=i, func=AF.Square), r=[("XN", ti)], w=[("sq", sqi)])
                p.op("pe", lambda e, o=ps2[:, 0:n], r_=cx.scr_sq[sqi][:, 0:n], s=(kc == 0), t=(kc == 7):
                     e.matmul(o, lhsT=cx.ones[:], rhs=r_, start=s, stop=t), r=[("sq", sqi), "ones"], w=[pk2])
            m_ = mu[par][:, 0:n]
            v_ = var[par][:, 0:n]
            p.op("dve", lambda e, o=m_, i=ps1[:, 0:n]:
                 e.tensor_scalar(out=o, in0=i, scalar1=1.0 / D, scalar2=None, op0=ALU.mult), r=[pk1], w=[("mu", par)])
            p.op("dve", lambda e, o=v_, a=m_: e.tensor_tensor(out=o, in0=a, in1=a, op=ALU.mult),
                 r=[("mu", par)], w=[("var", par)])
            p.op("dve", lambda e, o=v_, i=ps2[:, 0:n]:
                 e.scalar_tensor_tensor(out=o, in0=i, scalar=1.0 / D, in1=o, op0=ALU.mult, op1=ALU.subtract),
                 r=[pk2, ("var", par)], w=[("var", par)])
            p.op("act", lambda e, o=v_: e.activation(out=o, in_=o, func=AF.Sqrt, bias=cx.epsc[:, 0:1], scale=1.0),
                 r=[("var", par), "epsc"], w=[("var", par)])
            p.op("dve", lambda e, o=v_: e.reciprocal(out=o, in_=o), r=[("var", par)], w=[("var", par)])
            for kc in range(8):
                tp = (ti * 8 + kc) % 2
                t_ = tmpc[tp][:, 0:n]
                p.op("dve", lambda e, o=t_, a=XN[:, kc, t0:t0 + n], b=m_: e.tensor_tensor(out=o, in0=a, in1=b, op=ALU.subtract),
                     r=[("XN", ti), ("mu", par)], w=[("tmpc", tp)])
                p.op("dve", lambda e, o=t_, b=v_: e.tensor_tensor(out=o, in0=o, in1=b, op=ALU.mult),
                     r=[("tmpc", tp), ("var", par)], w=[("tmpc", tp)])
                p.op("act", lambda e, o=XN[:, kc, t0:t0 + n], i=t_, sc=vec[:, 32 + kc:33 + kc], b=vec[:, 40 + kc:41 + kc]:
                     e.activation(out=o, in_=i, func=AF.Silu, bias=b, scale=sc),
                     r=[("tmpc", tp), "const"], w=[("XN", ti)])
        def cons_out(m, ti, t0, n, ps, pk):
            p.op("dve", lambda e, o=X[:, m, t0:t0 + n], i=ps[:, 0:n], b=vec[:, 48 + m:49 + m]:
                 e.scalar_tensor_tensor(out=o, in0=i, scalar=b, in1=o, op0=ALU.add, op1=ALU.add),
                 r=[pk, ("X", ti), "const"], w=[("X", ti)])
        proj(cx, d_wout, 8, 0, D, XN, "XN", tiles, cons_out, group_cols=384)
        cx.new_scope()
        ffn_setup(cx, 1, T, ns, d_fdw, d_fb, d_fst)
        rmsnorm(cx, X, vec[:, 56:64], tiles, xn_out(cx), "nf0")
        conv_ffn(cx, 0, {"gate": d_wg, "up": d_wu, "down": d_wd}, tiles, np_tok, ns, HA)
        cx.store_s(o_ffn, cx.ffn_out[0], ("ffo", 0), "outs")
        for ti, (t0, n) in enumerate(tiles):
            for kc in range(8):
                cx.store(o_x1[kc * 128:(kc + 1) * 128, t0:t0 + n], X[:, kc, t0:t0 + n], ("X", ti), "outs")
        cx.new_scope()
        rmsnorm(cx, X, vec[:, 64:72], tiles, xn_out(cx), "n1")
        ost = [cx.alloc("ost%d" % i, [128, 512], F32) for i in range(3)]
        cnt = [0]

        def cons_qkv(m, ti, t0, n, ps, pk):
            s = cnt[0] % 3
            cnt[0] += 1
            p.op("act", lambda e, o=ost[s][:, 0:n], i=ps[:, 0:n]: e.activation(out=o, in_=i, func=AF.Copy),
                 r=[pk], w=[("ost", s)])
            cx.store_s(o_qkv[m * 128:(m + 1) * 128, t0:t0 + n], ost[s][:, 0:n], ("ost", s), ("ost", s))
        proj(cx, d_wqkv, 8, 0, 3 * D, XN, "XN", tiles, cons_qkv, group_cols=384)
        cx.p.emit_all(stack)
    return nc


def lay_cols(v):
    v = np.asarray(v, np.float32)
    return np.ascontiguousarray(v.reshape(-1, 128).T)


def run_A(inp, own, ns, n_cores, seq_of_core, nc_cache={}):
    key = (own, ns)
    if key not in nc_cache:
        nc_cache[key] = build_A(own, ns)
    nc = nc_cache[key]
    xp = np.asarray(inp["x_prompt"], np.float32)
    xs = np.asarray(inp["x_sample"], np.float32)
    vec = np.concatenate([
        lay_cols(inp["norm_mix"][0]), lay_cols(inp["cv_b_in"]), lay_cols(inp["cv_b_dw"]),
        lay_cols(inp["cv_ln_g"]), lay_cols(inp["cv_ln_b"]), lay_cols(inp["cv_b_out"]),
        lay_cols(inp["norm_ffn"][0]), lay_cols(inp["norm_mix"][1])], axis=1)
    wdw = np.ascontiguousarray(np.asarray(inp["cv_w_dw"], np.float32).T.reshape(8, 128, 31).transpose(1, 0, 2))
    fdw = np.ascontiguousarray(np.asarray(inp["ff_w_dw"][0], np.float32).T.reshape(NFF, 128, 3).transpose(1, 0, 2))[None]
    fb = lay_cols(inp["ff_b_dw"][0])[None]
    in_maps = []
    for c in range(n_cores):
        b, h = seq_of_core(c)
        seg = xp[b, h * own:(h + 1) * own]
        if h == 0:
            halo = np.zeros((HA, D), np.float32)
        else:
            halo = xp[b, h * own - HA:h * own]
        sm = xs[c * ns:(c + 1) * ns].reshape(ns * 8, D)
        xT = np.ascontiguousarray(np.concatenate([halo, seg, sm], 0).T)
        stc = np.asarray(inp["state_conv"][c * ns:(c + 1) * ns], np.float32)
        stc = np.ascontiguousarray(stc.transpose(2, 0, 1).reshape(8, 128, ns, 30).transpose(1, 0, 2, 3))
        stf = np.asarray(inp["state_ffn"][0, c * ns:(c + 1) * ns], np.float32)
        stf = np.ascontiguousarray(stf.transpose(2, 0, 1).reshape(NFF, 128, ns, 2).transpose(1, 0, 2, 3))[None]
        in_maps.append({
            "xT": xT, "hmask": np.full((128, 1), float(h), np.float32), "stconv": stc, "vecA": vec,
            "cv_wdw": wdw, "identA": np.eye(128, dtype=np.float32), "cv_w_in": np.asarray(inp["cv_w_in"], np.float32),
            "cv_w_out": np.asarray(inp["cv_w_out"], np.float32),
            "ff_dw": fdw, "ff_b": fb, "ff_st": stf,
            "ff_w_gate": np.asarray(inp["ff_w_gate"][0], np.float32),
            "ff_w_up": np.asarray(inp["ff_w_up"][0], np.float32),
            "ff_w_down": np.asarray(inp["ff_w_down"][0], np.float32),
            "w_qkv": np.asarray(inp["da_w_qkv"], np.float32),
        })
    res = run_bass_kernel_spmd(nc, in_maps, core_ids=list(range(n_cores)))
    return res.results


LAM_INIT = 0.8 - 0.6 * math.exp(-0.3 * 1)


def build_B(nseq, S, nss, npg, n_phys):
    nc = bass.Bass("TRN2", target_bir_lowering=False)
    stack = ExitStack()
    NG = S // 512
    NB = S // 128
    with stack:
        cx = Ctx(nc, stack, n_wslots=1, wslot_elems=64)
        p = cx.p
        d_q = cx.dram_in("qT", [nseq, 128, S])
        d_k = cx.dram_in("kT", [nseq, 128, S])
        d_v = cx.dram_in("v", [nseq, S, 128])
        d_qs = cx.dram_in("qsT", [128, nss * 8])
        d_ks = cx.dram_in("ksT", [128, nss * 8])
        d_vs = cx.dram_in("vs", [nss * 8, 128])
        d_ck = cx.dram_in("ck", [n_phys, 128, 128])
        d_cv = cx.dram_in("cv", [n_phys, 128, 128])
        d_tabr = cx.dram_in("ptabr", [128, nss], I32)
        d_iota = cx.dram_in("iota", [128, 1])
        d_ident = cx.dram_in("ident", [128, 128])
        d_lam = cx.dram_in("lamp", [1, 256])
        d_g = cx.dram_in("gcol", [128, 1])
        d_grow = cx.dram_in("grow", [8, 128])
        d_mask = cx.dram_in("masks", [128, 4, 512])
        d_smask = cx.dram_in("smask", [8, 8])
        o_p = cx.dram_out("oT", [nseq, 128, S])
        o_s = cx.dram_out("os", [nss * 8, 128])

        masks = cx.sb("masks", [128, 4, 512], BF16)
        p.op("pool", lambda e: e.dma_start(out=masks[:], in_=d_mask), w=["masks"], dsem="const2")
        smask = cx.sb("smask", [8, 8], F32)
        cx.load(smask[:], d_smask, "smask", "const")
        gcol = cx.sb("gcol", [128, 1], F32)
        cx.load(gcol[:], d_g, "gcol", "const")
        grow = cx.sb("grow", [8, 128], F32)
        cx.load(grow[:], d_grow, "grow", "const")
        lamp = cx.sb("lamp", [1, 256], F32)
        cx.load(lamp[:], d_lam, "lamp", "const")
        onesf = cx.sb("onesf", [1, 128], F32)
        p.op("pool", lambda e: e.memset(onesf[:], 1.0), w=["onesf"])
        lt = cx.sb("lt", [1, 128], F32)
        lsum = cx.sb("lsum", [1, 4], F32)
        p.op("dve", lambda e: e.tensor_tensor(out=lt[:, 0:64], in0=lamp[:, 0:64], in1=lamp[:, 64:128], op=ALU.mult),
             r=["lamp"], w=["lt"])
        p.op("dve", lambda e: e.tensor_tensor(out=lt[:, 64:128], in0=lamp[:, 128:192], in1=lamp[:, 192:256], op=ALU.mult),
             r=["lamp"], w=["lt"])
        p.op("dve", lambda e: e.reduce_sum(out=lsum[:, 0:1], in_=lt[:, 0:64], axis=AX.X), r=["lt"], w=["lsum"])
        p.op("dve", lambda e: e.reduce_sum(out=lsum[:, 1:2], in_=lt[:, 64:128], axis=AX.X), r=["lt"], w=["lsum"])
        p.op("act", lambda e: e.activation(out=lsum[:, 0:2], in_=lsum[:, 0:2], func=AF.Exp), r=["lsum"], w=["lsum"])
        p.op("dve", lambda e: e.tensor_tensor(out=lsum[:, 2:3], in0=lsum[:, 1:2], in1=lsum[:, 0:1], op=ALU.subtract),
             r=["lsum"], w=["lsum"])
        p.op("dve", lambda e: e.tensor_scalar(out=lsum[:, 2:3], in0=lsum[:, 2:3], scalar1=-LAM_INIT, scalar2=None, op0=ALU.add),
             r=["lsum"], w=["lsum"])
        neglam = cx.sb("neglam", [128, 1], F32)
        psl, plk = cx.psum()
        p.op("pe", lambda e: e.matmul(psl[:, 0:1], lhsT=onesf[:, :], rhs=lsum[:, 2:3], start=True, stop=True),
             r=["onesf", "lsum"], w=[plk])
        p.op("dve", lambda e: e.tensor_copy(out=neglam[:], in_=psl[:, 0:1]), r=[plk], w=["neglam"])
        gsc = cx.sb("gsc", [128, 1], F32)
        p.op("dve", lambda e: e.tensor_scalar(out=gsc[:], in0=gcol[:], scalar1=1.0 - LAM_INIT, scalar2=None, op0=ALU.mult),
             r=["gcol"], w=["gsc"])
        grs = cx.sb("grs", [8, 128], F32)
        p.op("dve", lambda e: e.tensor_scalar(out=grs[:], in0=grow[:], scalar1=1.0 - LAM_INIT, scalar2=None, op0=ALU.mult),
             r=["grow"], w=["grs"])

        Q = [cx.sb("Q%d" % i, [128, S], BF16) for i in range(2)]
        Kt = [cx.sb("K%d" % i, [128, S], BF16) for i in range(2)]
        V = [cx.sb("V%d" % i, [128, NB, 128], BF16) for i in range(2)]
        PT = [[cx.sb("PT%d_%d" % (m, i), [128, 512], BF16) for i in range(2)] for m in range(2)]
        ep = {n_: cx.sb("ep_" + n_, [128, 512], F32) for n_ in ("r1", "r2", "t1", "o", "rs")}
        epq = cx.sb("ep_sq", [128, 512], BF16)
        ost = [cx.sb("ostB%d" % i, [128, 512], F32) for i in range(2)]
        O1, O2, L1, L2 = cx.ps[0], cx.ps[1], cx.ps[2], cx.ps[3]
        gi_box = [0]

        def load_prompt(b):
            bp = b % 2
            for hh in range(0, S, 2048):
                he = min(S, hh + 2048)
                p.op("pool", lambda e, o=Q[bp][:, hh:he], i=d_q[b][:, hh:he]: e.dma_start(out=o, in_=i), w=[("Q", bp)], dsem=("qkv", bp))
                p.op("pool", lambda e, o=Kt[bp][:, hh:he], i=d_k[b][:, hh:he]: e.dma_start(out=o, in_=i), w=[("K", bp)], dsem=("qkv", bp))
            p.op("pool", lambda e, o=V[bp][:], i=d_v[b].rearrange("(n p) e -> p n e", p=128): e.dma_start(out=o, in_=i),
                 w=[("V", bp)], dsem=("qkv", bp))

        def unit_gen(b, G):
            bp = b % 2
            nkb = 4 * G + 4
            qs = slice(G * 512, (G + 1) * 512)

            def scores(kb):
                par = kb % 2
                ks = slice(kb * 128, (kb + 1) * 128)
                p.op("pe", lambda e, o=cx.ps[4 + 2 * par][:, :], l=Kt[bp][0:64, ks], r_=Q[bp][0:64, qs]:
                     e.matmul(o, lhsT=l, rhs=r_, start=True, stop=True),
                     r=[("K", bp), ("Q", bp)], w=[("ps", 4 + 2 * par)])
                p.op("pe", lambda e, o=cx.ps[5 + 2 * par][:, :], l=Kt[bp][64:128, ks], r_=Q[bp][64:128, qs]:
                     e.matmul(o, lhsT=l, rhs=r_, start=True, stop=True),
                     r=[("K", bp), ("Q", bp)], w=[("ps", 5 + 2 * par)])

            scores(0)
            for kb in range(nkb):
                par = kb % 2
                if kb + 1 < nkb:
                    scores(kb + 1)
                for m in range(2):
                    p.op("act", lambda e, o=PT[m][par][:, :], i=cx.ps[4 + m + 2 * par][:, :]:
                         e.activation(out=o, in_=i, func=AF.Exp, scale=0.125),
                         r=[("ps", 4 + m + 2 * par)], w=[("pt", m, par)])
                    if kb >= 4 * G:
                        p.op("dve", lambda e, o=PT[m][par][:, :], mk=masks[:, kb - 4 * G, :]:
                             e.tensor_tensor(out=o, in0=o, in1=mk, op=ALU.mult),
                             r=[("pt", m, par), "masks"], w=[("pt", m, par)])
                for m, (Ob, Lb) in enumerate(((0, 2), (1, 3))):
                    p.op("pe", lambda e, o=cx.ps[Ob][:, :], l=V[bp][:, kb, :], r_=PT[m][par][:, :], s=(kb == 0), t=(kb == nkb - 1):
                         e.matmul(o, lhsT=l, rhs=r_, start=s, stop=t), r=[("V", bp), ("pt", m, par)], w=[("ps", Ob)])
                    p.op("pe", lambda e, o=cx.ps[Lb][:, :], r_=PT[m][par][:, :], s=(kb == 0), t=(kb == nkb - 1):
                         e.matmul(o, lhsT=cx.ones[:], rhs=r_, start=s, stop=t), r=["ones", ("pt", m, par)], w=[("ps", Lb)])
                yield "kb"
            p.op("dve", lambda e: e.reciprocal(out=ep["r1"][:], in_=L1[:, :]), r=[("ps", 2)], w=["ep_r1"])
            p.op("dve", lambda e: e.reciprocal(out=ep["r2"][:], in_=L2[:, :]), r=[("ps", 3)], w=["ep_r2"])
            p.op("dve", lambda e: e.tensor_tensor(out=ep["t1"][:], in0=O1[:, :], in1=ep["r1"][:], op=ALU.mult),
                 r=[("ps", 0), "ep_r1"], w=["ep_t1"])
            p.op("dve", lambda e: e.tensor_tensor(out=ep["r2"][:], in0=O2[:, :], in1=ep["r2"][:], op=ALU.mult),
                 r=[("ps", 1), "ep_r2"], w=["ep_r2"])
            p.op("dve", lambda e: e.scalar_tensor_tensor(out=ep["o"][:], in0=ep["r2"][:], scalar=neglam[:, 0:1], in1=ep["t1"][:],
                                                         op0=ALU.mult, op1=ALU.add),
                 r=["ep_r2", "ep_t1", "neglam"], w=["ep_o"])
            p.op("act", lambda e: e.activation(out=epq[:], in_=ep["o"][:], func=AF.Square), r=["ep_o"], w=["ep_sq"])
            yield "e1"
            sp_ = 4
            p.op("pe", lambda e, o=cx.ps[sp_][:, :]: e.matmul(o, lhsT=cx.ones[:], rhs=epq[:], start=True, stop=True),
                 r=["ones", "ep_sq"], w=[("ps", sp_)])
            p.op("act", lambda e, i=cx.ps[sp_][:, :]: e.activation(out=ep["rs"][:], in_=i, func=AF.Sqrt, bias=cx.epsc[:, 0:1], scale=1.0 / 128),
                 r=[("ps", sp_), "epsc"], w=["ep_rs"])
            p.op("dve", lambda e: e.reciprocal(out=ep["rs"][:], in_=ep["rs"][:]), r=["ep_rs"], w=["ep_rs"])
            so = gi_box[0] % 2
            gi_box[0] += 1
            p.op("dve", lambda e, o=ost[so][:]: e.scalar_tensor_tensor(out=o, in0=ep["o"][:], scalar=gsc[:, 0:1], in1=ep["rs"][:],
                                                                      op0=ALU.mult, op1=ALU.mult),
                 r=["ep_o", "ep_rs", "gsc"], w=[("ostB", so)])
            cx.store(o_p[b][:, qs], ost[so][:], ("ostB", so), ("ostB", so))

        prev_gen = None
        for b in range(nseq):
            load_prompt(b)
            for G in range(NG):
                g_ = unit_gen(b, G)
                next(g_)
                if prev_gen is not None:
                    next(prev_gen, None)
                while next(g_) != "e1":
                    pass
                prev_gen = g_
        if prev_gen is not None:
            next(prev_gen, None)

        NT = nss * 8
        QS = cx.sb("QS", [128, NT], BF16)
        KN = cx.sb("KN", [128, NT], BF16)
        p.op("pool", lambda e: e.dma_start(out=QS[:], in_=d_qs), w=["QS"], dsem="const2")
        p.op("pool", lambda e: e.dma_start(out=KN[:], in_=d_ks), w=["KN"], dsem="const2")
        assert npg * 8 == 128
        NBUF = 5
        KTk = [cx.sb("KTk%d" % i, [128, 16, 128], F32) for i in range(NBUF)]
        KB = [cx.sb("KB%d" % i, [128, 16, 128], BF16) for i in range(NBUF)]
        VB = [cx.sb("VB%d" % i, [128, 17, 128], BF16) for i in range(NBUF)]
        for i in range(NBUF):
            p.op("pool", lambda e, o=VB[i][:, 16, :]: e.memset(o, 0.0), w=[("VB", i)])
        PS_ = [[cx.sb("PS%d_%d" % (m, i), [128, 17 * 8], BF16) for i in range(NBUF)] for m in range(2)]
        OS = [cx.sb("OS%d" % i, [8, 128], F32) for i in range(4)]
        sm = {n_: cx.sb("sm_" + n_, [8, 2], F32) for n_ in ("r", "ss")}
        smt = cx.sb("sm_t1", [8, 128], F32)
        smo = cx.sb("sm_o", [8, 128], F32)
        smq = cx.sb("sm_q", [8, 128], F32)
        NC_ = 17 * 8
        tabi = cx.sb("tabi", [128, nss], I32)
        tabf = cx.sb("tabf", [128, nss], F32)
        idx = cx.sb("idx", [128, nss], I32)
        iot = cx.sb("iot", [128, 1], F32)
        ident = cx.sb("ident", [128, 128], F32)
        cx.load(tabi[:], d_tabr, "tabi", "const")
        cx.load(iot[:], d_iota, "iot", "const")
        cx.load(ident[:], d_ident, "ident", "const")
        p.op("dve", lambda e: e.tensor_copy(out=tabf[:], in_=tabi[:]), r=["tabi"], w=["tabf"])
        p.op("dve", lambda e: e.tensor_scalar(out=tabf[:], in0=tabf[:], scalar1=8.0, scalar2=iot[:, 0:1], op0=ALU.mult, op1=ALU.add),
             r=["tabf", "iot"], w=["tabf"])
        p.op("dve", lambda e: e.tensor_copy(out=idx[:], in_=tabf[:]), r=["tabf"], w=["idx"])
        ckf = d_ck.rearrange("n (a b) d -> (n a) (b d)", a=8)
        cvf = d_cv.rearrange("n (a b) d -> (n a) (b d)", a=8)
        npg = 16
        def seq_gen(i):
            par = i % NBUF
            p.op("pool", lambda e, o=KTk[par][:, :, :].rearrange("p a b -> p (a b)"), c_=i: e.indirect_dma_start(
                out=o, out_offset=None, in_=ckf, in_offset=bass.IndirectOffsetOnAxis(ap=idx[:, c_:c_ + 1], axis=0)),
                r=["idx"], w=[("KTk", par)], dsem=("kb", par))
            p.op("pool", lambda e, o=VB[par][:, 0:16, :].rearrange("p a b -> p (a b)"), c_=i: e.indirect_dma_start(
                out=o, out_offset=None, in_=cvf, in_offset=bass.IndirectOffsetOnAxis(ap=idx[:, c_:c_ + 1], axis=0)),
                r=["idx"], w=[("VB", par)], dsem=("vb", par))
            p.op("pool", lambda e, o=VB[par][0:8, 16, :], i_=d_vs[i * 8:(i + 1) * 8, :]: e.dma_start(out=o, in_=i_),
                 w=[("VB", par)], dsem=("vb", par))
            yield
            for q4 in range(4):
                pst, ptk = cx.psum()
                for uu in range(4):
                    u = q4 * 4 + uu
                    p.op("pe", lambda e, o=pst[:, uu * 128:(uu + 1) * 128], a=KTk[par][:, u, :]:
                         e.transpose(o, a, ident[:, :]), r=[("KTk", par), "ident"], w=[ptk])
                eng_ = "act" if q4 % 2 == 0 else "dve"
                if eng_ == "act":
                    p.op("act", lambda e, o=KB[par][:, q4 * 4:(q4 + 1) * 4, :], a=pst[:, :].rearrange("p (u k) -> p u k", u=4):
                         e.activation(out=o, in_=a, func=AF.Copy), r=[ptk], w=[("KB", par)])
                else:
                    p.op("dve", lambda e, o=KB[par][:, q4 * 4:(q4 + 1) * 4, :], a=pst[:, :].rearrange("p (u k) -> p u k", u=4):
                         e.tensor_copy(out=o, in_=a), r=[ptk], w=[("KB", par)])
            yield
            qsl = slice(i * 8, (i + 1) * 8)
            sps = []
            for m in range(2):
                ps, pk = cx.psum()
                sps.append((ps, pk))
                pr = slice(64 * m, 64 * m + 64)
                for j in range(npg):
                    p.op("pe", lambda e, o=ps[:, j * 8:(j + 1) * 8], l=KB[par][pr, j, :], r_=QS[pr, qsl]:
                         e.matmul(o, lhsT=l, rhs=r_, start=True, stop=True), r=[("KB", par), "QS"], w=[pk])
                p.op("pe", lambda e, o=ps[0:8, npg * 8:NC_], l=KN[pr, qsl], r_=QS[pr, qsl]:
                     e.matmul(o, lhsT=l, rhs=r_, start=True, stop=True), r=["KN", "QS"], w=[pk])
            for m in range(2):
                ps, pk = sps[m]
                p.op("act", lambda e, o=PS_[m][par][:, 0:npg * 8], i_=ps[:, 0:npg * 8]:
                     e.activation(out=o, in_=i_, func=AF.Exp, scale=0.125), r=[pk], w=[("PS", m, par)])
                p.op("act", lambda e, o=PS_[m][par][0:8, npg * 8:NC_], i_=ps[0:8, npg * 8:NC_]:
                     e.activation(out=o, in_=i_, func=AF.Exp, scale=0.125), r=[pk], w=[("PS", m, par)])
                p.op("dve", lambda e, o=PS_[m][par][0:8, npg * 8:NC_]: e.tensor_tensor(out=o, in0=o, in1=smask[:, :], op=ALU.mult),
                     r=[("PS", m, par), "smask"], w=[("PS", m, par)])
            yield
            ops_ = []
            for m in range(2):
                ps, pk = cx.psum()
                ops_.append((ps, pk))
                for j in range(npg):
                    p.op("pe", lambda e, o=ps[0:8, 128:129], l=PS_[m][par][:, j * 8:(j + 1) * 8], s=(j == 0):
                         e.matmul(o, lhsT=l, rhs=cx.ones[:, 0:1], start=s, stop=False), r=[("PS", m, par), "ones"], w=[pk])
                p.op("pe", lambda e, o=ps[0:8, 128:129], l=PS_[m][par][0:8, npg * 8:NC_]:
                     e.matmul(o, lhsT=l, rhs=cx.ones[0:8, 0:1], start=False, stop=True), r=[("PS", m, par), "ones"], w=[pk])
                for j in range(npg):
                    p.op("pe", lambda e, o=ps[0:8, 0:128], l=PS_[m][par][:, j * 8:(j + 1) * 8], r_=VB[par][:, j, :], s=(j == 0):
                         e.matmul(o, lhsT=l, rhs=r_, start=s, stop=False), r=[("PS", m, par), ("VB", par)], w=[pk])
                p.op("pe", lambda e, o=ps[0:8, 0:128], l=PS_[m][par][0:8, npg * 8:NC_], r_=VB[par][0:8, npg, :]:
                     e.matmul(o, lhsT=l, rhs=r_, start=False, stop=True), r=[("PS", m, par), ("VB", par)], w=[pk])
            (p1, k1), (p2, k2) = ops_
            p.op("dve", lambda e, a=p1[0:8, 128:129]: e.reciprocal(out=sm["r"][:, 0:1], in_=a), r=[k1], w=["sm_r"])
            p.op("dve", lambda e, a=p2[0:8, 128:129]: e.reciprocal(out=sm["r"][:, 1:2], in_=a), r=[k2], w=["sm_r"])
            p.op("dve", lambda e: e.tensor_tensor(out=sm["r"][:, 1:2], in0=sm["r"][:, 1:2], in1=neglam[0:8, 0:1], op=ALU.mult),
                 r=["sm_r", "neglam"], w=["sm_r"])
            p.op("dve", lambda e, a=p1[0:8, 0:128]: e.tensor_scalar(out=smt[:], in0=a, scalar1=sm["r"][:, 0:1], scalar2=None, op0=ALU.mult),
                 r=[k1, "sm_r"], w=["sm_t1"])
            p.op("dve", lambda e, a=p2[0:8, 0:128]: e.scalar_tensor_tensor(out=smo[:], in0=a, scalar=sm["r"][:, 1:2], in1=smt[:],
                                                                            op0=ALU.mult, op1=ALU.add),
                 r=[k2, "sm_r", "sm_t1"], w=["sm_o"])
            p.op("dve", lambda e: e.tensor_tensor(out=smq[:], in0=smo[:], in1=smo[:], op=ALU.mult), r=["sm_o"], w=["sm_q"])
            p.op("dve", lambda e: e.reduce_sum(out=sm["ss"][:, 0:1], in_=smq[:], axis=AX.X), r=["sm_q"], w=["sm_ss"])
            p.op("act", lambda e: e.activation(out=sm["ss"][:, 1:2], in_=sm["ss"][:, 0:1], func=AF.Sqrt, bias=cx.epsc[0:8, 0:1], scale=1.0 / 128),
                 r=["sm_ss", "epsc"], w=["sm_ss"])
            p.op("dve", lambda e: e.reciprocal(out=sm["ss"][:, 1:2], in_=sm["ss"][:, 1:2]), r=["sm_ss"], w=["sm_ss"])
            osl = i % 4
            p.op("dve", lambda e, o=OS[osl][:, :]: e.scalar_tensor_tensor(out=o, in0=smo[:], scalar=sm["ss"][:, 1:2], in1=grs[:],
                                                                         op0=ALU.mult, op1=ALU.mult),
                 r=["sm_o", "sm_ss", "grs"], w=[("OS", osl)])
            cx.store(o_s[i * 8:(i + 1) * 8, :], OS[osl][:, :], ("OS", osl), ("OS", osl))

        gens = [seq_gen(i) for i in range(nss)]
        for step in range(nss + 3):
            for off in range(4):
                i = step - off
                if 0 <= i < nss:
                    try:
                        next(gens[i])
                    except StopIteration:
                        pass
        cx.p.emit_all(stack)
    return nc


def make_masks():
    pidx = np.arange(128)[:, None]
    j = np.arange(512)[None, :]
    m = np.stack([(j >= o * 128 + pidx) for o in range(4)], axis=1).astype(np.float32)
    sm = (np.arange(8)[:, None] <= np.arange(8)[None, :]).astype(np.float32)
    return m, sm


HC = 256
POOL_WIN = (2, 2, 4, 4, 8, 8, 16, 16)
GELU_C = 1.5957691216057308


def gelu_tile(cx, x_ap, out_ap, n_shape_keys, rk, wk):
    cx.p.op("act", lambda e: e.activation(out=out_ap, in_=x_ap, func=AF.Gelu_apprx_tanh), r=rk, w=wk)


def build_C(own, ns, debug=None):
    NP = HC + own
    NSB = ns * 8
    T = NP + NSB
    assert own % 128 == 0 and NSB <= 128
    nc = bass.Bass("TRN2", target_bir_lowering=False)
    stack = ExitStack()
    with stack:
        cx = Ctx(nc, stack, n_wslots=4, wslot_elems=3072)
        p = cx.p
        d_x = cx.dram_in("x1T", [D, T])
        d_o = cx.dram_in("oT", [D, T])
        d_hm = cx.dram_in("hmask", [128, 1])
        d_vec = cx.dram_in("vecC", [128, 88])
        d_wo = cx.dram_in("w_o", [D, D])
        d_plw = cx.dram_in("pl_w", [D, 256])
        d_stp = cx.dram_in("stpool", [128, 8, ns, 15])
        d_invc = cx.dram_in("invc", [128, 4, 16])
        d_fdw = cx.dram_in("ff_dw", [3, 128, NFF, 3])
        d_fb = cx.dram_in("ff_b", [3, 128, NFF])
        d_fst = cx.dram_in("ff_st", [3, 128, NFF, ns, 2])
        d_wg = cx.dram_in("ff_w_gate", [3, D, DFF])
        d_wu = cx.dram_in("ff_w_up", [3, D, DFF])
        d_wd = cx.dram_in("ff_w_down", [3, DFF, D])
        d_sgin = cx.dram_in("sg_w_in", [D, 4 * D])
        d_sgout = cx.dram_in("sg_w_out", [2 * D, D])
        d_sgrow = cx.dram_in("sg_rows", [3, 128, 2 * D])
        d_wst = cx.dram_in("sg_wsT", [2, 128, 4, 128])
        d_wsm = cx.dram_in("sg_mask", [2, 128, 128])
        d_bs = cx.dram_in("sg_bs", [2, 128, 4, 128])
        o_y = cx.dram_out("yT", [D, T])
        o_pool = cx.dram_out("poolT", [128, 8, 15 + ns * 15])
        o_sgv = cx.dram_out("sgv", [NSB, 2 * D])
        o_ffn = cx.dram_out("ffnT", [3, 128, NFF, 2 + 2 * ns])

        tiles = split_tiles(NP) + [(NP, NSB)]
        nt = len(tiles)
        cx.X = cx.sb("X", [128, 8, T], F32)
        cx.XN = cx.sb("XN", [128, 8, T], BF16)
        X, XN = cx.X, cx.XN
        cx.scr_sq = [cx.sb("sq%d" % i, [128, 512], BF16) for i in range(4)]
        cx.scr_r = [cx.sb("R%d" % i, [128, 512], F32) for i in range(2)]
        cx.hmask = cx.const_cols("hmask", d_hm, 1)
        vec = cx.const_cols("vecC", d_vec, 88)
        cx.scratch_init(13800)
        for ti, (t0, n) in enumerate(tiles):
            for kc in range(8):
                cx.load(X[:, kc, t0:t0 + n], d_x[kc * 128:(kc + 1) * 128, t0:t0 + n], ("X", ti), ("xin", ti % 2))
        for ti, (t0, n) in enumerate(tiles):
            p.op("pool", lambda e, o=XN[:, :, t0:t0 + n], i=d_o[:, t0:t0 + n].rearrange("(k p) n -> p k n", p=128):
                 e.dma_start(out=o, in_=i), w=[("XN", ti)], dsem=("oin", ti % 2))

        def cons_add(m, ti, t0, n, ps, pk):
            p.op("dve", lambda e, o=X[:, m, t0:t0 + n], i=ps[:, 0:n]:
                 e.tensor_tensor(out=o, in0=i, in1=o, op=ALU.add), r=[pk, ("X", ti)], w=[("X", ti)])
        proj(cx, d_wo, 8, 0, D, XN, "XN", tiles, cons_add, group_cols=384)

        def run_ffn(li, ncol):
            cx.new_scope()
            cx.ffn_dw, cx.ffn_b, cx.ffn_state, cx.ffn_out = {}, {}, {}, {}
            t = cx.alloc("ffdw", [128, NFF, 3], F32)
            cx.load_s(t, d_fdw[li], "const_s", "const_s")
            cx.ffn_dw[li] = t
            t = cx.alloc("ffb", [128, NFF], F32)
            cx.load_s(t, d_fb[li], "const_s", "const_s")
            cx.ffn_b[li] = t
            t = cx.alloc("ffst", [128, NFF, ns, 2], F32)
            cx.load_s(t, d_fst[li], ("stf", li), "const_s")
            cx.ffn_state[li] = t
            cx.ffn_out[li] = cx.alloc("ffo", [128, NFF, 2 + 2 * ns], F32)
            cx.gt = [cx.alloc("gt%d" % i, [128, 514], F32) for i in range(2)]
            cx.gt_i = 0
            cx.Gs = [cx.alloc("Gs%d" % i, [128, ns, 10], F32) for i in range(2)]
            cx.facc = [cx.alloc("facc%d" % i, [128, 512], F32) for i in range(2)]
            cx.fsil = [cx.alloc("fsil%d" % i, [128, 512], F32) for i in range(2)]
            cx.Hb = cx.alloc("Hb", [128, 3, T], BF16)
            rmsnorm(cx, X, vec[:, ncol:ncol + 8], tiles, xn_out(cx), "nf%d" % li)
            conv_ffn(cx, li, {"gate": d_wg[li], "up": d_wu[li], "down": d_wd[li]}, tiles, NP, ns, HC)
            cx.store_s(o_ffn[li], cx.ffn_out[li], ("ffo", li), "outs")

        run_ffn(0, 0)

        cx.new_scope()
        Rall = cx.alloc("Rall", [128, T], F32)
        HN = cx.alloc("HN", [128, 15 + NP], F32)
        P0 = cx.alloc("P0", [128, 15 + NP], F32)
        P1 = cx.alloc("P1", [128, 15 + NP], F32)
        HS = cx.alloc("HS", [128, ns, 23], F32)
        Q0 = cx.alloc("Q0", [128, ns, 23], F32)
        Q1 = cx.alloc("Q1", [128, ns, 23], F32)
        invc = cx.alloc("invc", [128, 4, 16], F32)
        ptm = cx.alloc("ptm", [128, 16], F32)
        cx.load_s(invc, d_invc, "invc", "const_s")
        for ti, (t0, n) in enumerate(tiles):
            ps, pk = cx.psum()
            for kc in range(8):
                sqi = cx.sq_next()
                p.op("act", lambda e, o=cx.scr_sq[sqi][:, 0:n], i=X[:, kc, t0:t0 + n]:
                     e.activation(out=o, in_=i, func=AF.Square), r=[("X", ti)], w=[("sq", sqi)])
                p.op("pe", lambda e, o=ps[:, 0:n], r_=cx.scr_sq[sqi][:, 0:n], s=(kc == 0), t=(kc == 7):
                     e.matmul(o, lhsT=cx.ones[:], rhs=r_, start=s, stop=t), r=[("sq", sqi), "ones"], w=[pk])
            p.op("act", lambda e, o=Rall[:, t0:t0 + n], i=ps[:, 0:n]:
                 e.activation(out=o, in_=i, func=AF.Sqrt, bias=cx.epsc[:, 0:1], scale=1.0 / D), r=[pk, "epsc"], w=["Rall"])
            p.op("dve", lambda e, o=Rall[:, t0:t0 + n]: e.reciprocal(out=o, in_=o), r=["Rall"], w=["Rall"])
        p.op("pool", lambda e: e.memset(HN[:, 0:15], 0.0), w=["HN"])
        allXN = [("XN", ti) for ti in range(nt)]
        allX = [("X", ti) for ti in range(nt)]
        for c in range(8):
            win = POOL_WIN[c]
            gcolc = vec[:, 8 + c:9 + c]
            p.op("dve", lambda e, o=HN[:, 15:15 + NP], i=X[:, c, 0:NP], g=gcolc, r_=Rall[:, 0:NP]:
                 e.scalar_tensor_tensor(out=o, in0=i, scalar=g, in1=r_, op0=ALU.mult, op1=ALU.mult),
                 r=allX + ["Rall", "const"], w=["HN"])
            p.op("dve", lambda e, o=HN[:, 15:15 + HC]:
                 e.tensor_scalar(out=o, in0=o, scalar1=cx.hmask[:, 0:1], scalar2=None, op0=ALU.mult), r=["HN", "const"], w=["HN"])
            p.op("dve", lambda e, o=HS[:, :, 15:23], i=X[:, c, NP:T].rearrange("p (s t) -> p s t", t=8), g=gcolc,
                 r_=Rall[:, NP:T].rearrange("p (s t) -> p s t", t=8):
                 e.scalar_tensor_tensor(out=o, in0=i, scalar=g, in1=r_, op0=ALU.mult, op1=ALU.mult),
                 r=allX + ["Rall", "const"], w=["HS"])
            p.op("sp", lambda e, o=HS[:, :, 0:15], i=d_stp[:, c, :, :]: e.dma_start(out=o, in_=i), r=["scr"], w=["HS"], dsem="stp")
            cx.store_s(o_pool[:, c, 0:15], HN[:, NP:NP + 15], "HN", "pout")
            cx.store_s(o_pool[:, c, 15:15 + ns * 15].rearrange("p (s t) -> p s t", t=15), HS[:, :, 8:23], "HS", "pout")
            src_p, src_s = HN, HS
            bufs_p, bufs_s = [P0, P1], [Q0, Q1]
            k = 1
            bi = 0
            while k < win:
                dp, ds_ = bufs_p[bi], bufs_s[bi]
                lo = 2 * k - 1
                p.op("dve", lambda e, o=dp[:, lo:15 + NP], a=src_p[:, lo:15 + NP], b=src_p[:, lo - k:15 + NP - k]:
                     e.tensor_tensor(out=o, in0=a, in1=b, op=ALU.add), r=["HN", "P0", "P1"], w=["P%d" % bi])
                p.op("dve", lambda e, o=ds_[:, :, lo:23], a=src_s[:, :, lo:23], b=src_s[:, :, lo - k:23 - k]:
                     e.tensor_tensor(out=o, in0=a, in1=b, op=ALU.add), r=["HS", "Q0", "Q1"], w=["Q%d" % bi])
                src_p, src_s = dp, ds_
                k *= 2
                bi ^= 1
            widx = {2: 0, 4: 1, 8: 2, 16: 3}[win]
            p.op("dve", lambda e, o=XN[:, c, 0:NP], a=src_p[:, 15:15 + NP], h=HN[:, 15:15 + NP], iw=1.0 / win:
                 e.scalar_tensor_tensor(out=o, in0=a, scalar=iw, in1=h, op0=ALU.mult, op1=ALU.subtract),
                 r=["HN", "P0", "P1"], w=allXN)
            p.op("dve", lambda e, a=src_p[:, 15 + HC:15 + HC + 16], iv=invc[:, widx, :]:
                 e.tensor_tensor(out=ptm[:, :], in0=a, in1=iv, op=ALU.mult), r=["P0", "P1", "invc"], w=["ptm"])
            p.op("dve", lambda e, o=XN[:, c, HC:HC + 16], h=HN[:, 15 + HC:15 + HC + 16]:
                 e.tensor_tensor(out=o, in0=ptm[:, :], in1=h, op=ALU.subtract), r=["ptm", "HN"], w=allXN)
            p.op("dve", lambda e, o=XN[:, c, NP:T].rearrange("p (s t) -> p s t", t=8), a=src_s[:, :, 15:23], h=HS[:, :, 15:23], iw=1.0 / win:
                 e.scalar_tensor_tensor(out=o, in0=a, scalar=iw, in1=h, op0=ALU.mult, op1=ALU.subtract),
                 r=["HS", "Q0", "Q1"], w=allXN)
        wv, wk = cx.wload(d_plw, 8, 256)
        for m in range(8):
            g = m // 2
            for ti, (t0, n) in enumerate(tiles):
                ps, pk = cx.psum()
                for kk in range(2):
                    p.op("pe", lambda e, o=ps[:, 0:n], l=wv[:, g * 2 + kk, (m % 2) * 128:(m % 2 + 1) * 128],
                         r_=XN[:, g * 2 + kk, t0:t0 + n], s=(kk == 0), t=(kk == 1):
                         e.matmul(o, lhsT=l, rhs=r_, start=s, stop=t), r=[wk, ("XN", ti)], w=[pk])
                p.op("dve", lambda e, o=X[:, m, t0:t0 + n], i=ps[:, 0:n], sc=vec[:, 16 + m:17 + m]:
                     e.scalar_tensor_tensor(out=o, in0=i, scalar=sc, in1=o, op0=ALU.mult, op1=ALU.add),
                     r=[pk, ("X", ti), "const"], w=[("X", ti)])
        if debug == "pool":
            for ti, (t0, n) in enumerate(tiles):
                for kc in range(8):
                    cx.store(o_y[kc * 128:(kc + 1) * 128, t0:t0 + n], X[:, kc, t0:t0 + n], ("X", ti), "outs")
            cx.p.emit_all(stack)
            return nc
        run_ffn(1, 24)

        cx.new_scope()
        rmsnorm(cx, X, vec[:, 32:40], tiles, xn_out(cx), "n3")
        NBK = NP // 128
        blocks = [(i * 128, 128) for i in range(NBK)] + [(NP, NSB)]
        nb = len(blocks)
        U = cx.alloc("U", [128, 4, T], BF16)
        rows = cx.alloc("rows", [128, 3, 512], F32)
        wst = cx.alloc("wst", [128, 2, 4, 128], F32)
        wsb = cx.alloc("wsb", [128, 2, 4, 128], BF16)
        wsm = cx.alloc("wsm", [128, 2, 128], F32)
        bsr = cx.alloc("bsr", [128, 2, 4, 128], F32)
        stat = cx.alloc("stat", [128, nb, 4, 2], F32)
        mur = cx.alloc("mur", [128, nb, 2], F32)
        rowb = cx.alloc("rowb", [1, 512], BF16)
        nmr = cx.alloc("nmr", [128, nb, 1], F32)
        zvs = [cx.alloc("zv%d" % i, [128, 512], F32) for i in range(2)]
        zqs = [cx.alloc("zq%d" % i, [128, 512], F32) for i in range(2)]
        vnb = [cx.alloc("vnb%d" % i, [128, 512], BF16) for i in range(2)]
        ssts = [cx.alloc("sst%d" % i, [128, 128], F32) for i in range(4)]
        for i in range(2):
            cx.load_s(wst[:, i], d_wst[i], "wst", "const_s")
            cx.load_s(wsm[:, i], d_wsm[i], "wsm", "const_s")
            cx.load_s(bsr[:, i], d_bs[i], "bsr", "const_s")
        for i in range(2):
            for g in range(4):
                p.op("dve", lambda e, o=wsb[:, i, g, :], a=wst[:, i, g, :], b=wsm[:, i, :]:
                     e.tensor_tensor(out=o, in0=a, in1=b, op=ALU.mult), r=["wst", "wsm"], w=["wsb"])
        for g in range(4):
            halves = []
            for hh in range(2):
                halves.append(cx.wload(d_sgin[:, 2 * D + g * 512 + hh * 256:2 * D + g * 512 + (hh + 1) * 256], 8, 256))
            p.op("sp", lambda e, o=rows[:, 0, :], i=d_sgrow[0][:, g * 512:(g + 1) * 512]: e.dma_start(out=o, in_=i),
                 r=["scr"], w=["rows"], dsem="rows")
            p.op("act", lambda e: e.activation(out=rowb[0:1, :], in_=rows[0:1, 0, :], func=AF.Copy), r=["rows"], w=["rowb"])
            for bi_, (t0, n) in enumerate(blocks):
                ti = min(t0 // 512, nt - 1) if t0 < NP else nt - 1
                ps, pk = cx.psum()
                for hh in range(2):
                    wv_, wvk = halves[hh]
                    for kc in range(8):
                        p.op("pe", lambda e, o=ps[0:n, hh * 256:(hh + 1) * 256], l=XN[:, kc, t0:t0 + n], r_=wv_[:, kc, :],
                             s=(kc == 0): e.matmul(o, lhsT=l, rhs=r_, start=s, stop=False),
                             r=[wvk, ("XN", ti)], w=[pk])
                    p.op("pe", lambda e, o=ps[0:n, hh * 256:(hh + 1) * 256], l=cx.ones[0:1, 0:n], r_=rowb[0:1, hh * 256:(hh + 1) * 256]:
                         e.matmul(o, lhsT=l, rhs=r_, start=False, stop=True), r=["ones", "rowb"], w=[pk])
                zv = zvs[bi_ % 2]
                zq = zqs[bi_ % 2]
                zk = ("zv", bi_ % 2)
                qk = ("zq", bi_ % 2)
                gelu_tile(cx, ps[0:n, :], zv[0:n, :], None, [pk], [zk])
                p.op("dve", lambda e, o=stat[0:n, bi_, g, 0:1], a=zv[0:n, :]: e.reduce_sum(out=o, in_=a, axis=AX.X), r=[zk], w=["stat"])
                p.op("act", lambda e, o=zq[0:n, :], a=zv[0:n, :]: e.activation(out=o, in_=a, func=AF.Square), r=[zk], w=[qk])
                p.op("dve", lambda e, o=stat[0:n, bi_, g, 1:2], a=zq[0:n, :]: e.reduce_sum(out=o, in_=a, axis=AX.X), r=[qk], w=["stat"])
        for bi_, (t0, n) in enumerate(blocks):
            p.op("dve", lambda e, o=mur[0:n, bi_, :], a=stat[0:n, bi_, :, :].rearrange("p g s -> p s g"):
                 e.reduce_sum(out=o, in_=a, axis=AX.X), r=["stat"], w=["mur"])
        p.op("dve", lambda e: e.tensor_scalar(out=mur[:, :, :], in0=mur[:, :, :], scalar1=1.0 / (2 * D), scalar2=None, op0=ALU.mult),
             r=["mur"], w=["mur"])
        msq = cx.alloc("msq", [128, nb, 1], F32)
        p.op("dve", lambda e: e.tensor_tensor(out=msq[:, :, :], in0=mur[:, :, 0:1], in1=mur[:, :, 0:1], op=ALU.mult), r=["mur"], w=["msq"])
        p.op("dve", lambda e: e.tensor_tensor(out=mur[:, :, 1:2], in0=mur[:, :, 1:2], in1=msq[:, :, :], op=ALU.subtract),
             r=["mur", "msq"], w=["mur"])
        p.op("act", lambda e: e.activation(out=mur[:, :, 1:2], in_=mur[:, :, 1:2], func=AF.Sqrt, bias=cx.epsc[:, 0:1], scale=1.0),
             r=["mur", "epsc"], w=["mur"])
        p.op("dve", lambda e: e.reciprocal(out=mur[:, :, 1:2], in_=mur[:, :, 1:2]), r=["mur"], w=["mur"])
        p.op("dve", lambda e: e.scalar_tensor_tensor(out=nmr[:, :, :], in0=mur[:, :, 0:1], scalar=-1.0, in1=mur[:, :, 1:2],
                                                     op0=ALU.mult, op1=ALU.mult), r=["mur"], w=["nmr"])
        for g in range(4):
            def cons_u(m, ti, t0, n, ps, pk, g=g):
                p.op("act", lambda e, o=U[:, m, t0:t0 + n], i=ps[:, 0:n], b=vec[:, 56 + g * 4 + m:57 + g * 4 + m]:
                     e.activation(out=o, in_=i, func=AF.Gelu_apprx_tanh, bias=b), r=[pk, "const"], w=[("U", ti)])
            proj(cx, d_sgin, 8, g * 512, 512, XN, "XN", tiles, cons_u, group_cols=256)
            halves = []
            for hh in range(2):
                halves.append(cx.wload(d_sgin[:, 2 * D + g * 512 + hh * 256:2 * D + g * 512 + (hh + 1) * 256], 8, 256))
            for r_i in range(3):
                p.op("sp", lambda e, o=rows[:, r_i, :], i=d_sgrow[r_i][:, g * 512:(g + 1) * 512]: e.dma_start(out=o, in_=i),
                     r=["scr"], w=["rows"], dsem="rows")
            p.op("act", lambda e: e.activation(out=rowb[0:1, :], in_=rows[0:1, 0, :], func=AF.Copy), r=["rows"], w=["rowb"])
            def blk_gen(bi_, t0, n, g=g, halves=halves):
                ti = min(t0 // 512, nt - 1) if t0 < NP else nt - 1
                samp = t0 >= NP
                ps, pk = cx.psum()
                for hh in range(2):
                    wv_, wvk = halves[hh]
                    for kc in range(8):
                        p.op("pe", lambda e, o=ps[0:n, hh * 256:(hh + 1) * 256], l=XN[:, kc, t0:t0 + n], r_=wv_[:, kc, :],
                             s=(kc == 0): e.matmul(o, lhsT=l, rhs=r_, start=s, stop=False),
                             r=[wvk, ("XN", ti)], w=[pk])
                    p.op("pe", lambda e, o=ps[0:n, hh * 256:(hh + 1) * 256], l=cx.ones[0:1, 0:n], r_=rowb[0:1, hh * 256:(hh + 1) * 256]:
                         e.matmul(o, lhsT=l, rhs=r_, start=False, stop=True), r=["ones", "rowb"], w=[pk])
                zv = zvs[bi_ % 2]
                zq = zqs[bi_ % 2]
                zk = ("zv", bi_ % 2)
                qk = ("zq", bi_ % 2)
                gelu_tile(cx, ps[0:n, :], zv[0:n, :], None, [pk], [zk])
                p.op("act", lambda e, o=zv[0:n, :], nm_=nmr[0:n, bi_, :], rs_=mur[0:n, bi_, 1:2]:
                     e.activation(out=o, in_=o, func=AF.Identity, bias=nm_, scale=rs_), r=[zk, "mur", "nmr"], w=[zk])
                p.op("dve", lambda e, o=zv[0:n, :], a=rows[0:n, 1, :]: e.tensor_tensor(out=o, in0=o, in1=a, op=ALU.mult), r=[zk, "rows"], w=[zk])
                if samp:
                    p.op("dve", lambda e, o=zq[0:n, :], a=zv[0:n, :], b=rows[0:n, 2, :]: e.tensor_tensor(out=o, in0=a, in1=b, op=ALU.add),
                         r=[zk, "rows"], w=[qk])
                    cx.store_s(o_sgv[:, g * 512:(g + 1) * 512], zq[0:n, :], qk, "sgv")
                vb = vnb[bi_ % 2]
                p.op("dve", lambda e, o=vb[0:n, :], a=zv[0:n, :], b=rows[0:n, 2, :]: e.tensor_tensor(out=o, in0=a, in1=b, op=ALU.add),
                     r=[zk, "rows"], w=[("vnb", bi_ % 2)])
                yield
                wi = 1 if samp else 0
                for cc in range(4):
                    ps2, pk2 = cx.psum()
                    p.op("pe", lambda e, o=ps2[:, 0:n], l=vb[0:n, cc * 128:(cc + 1) * 128], r_=wsb[0:n, wi, g, 0:n]:
                         e.matmul(o, lhsT=l, rhs=r_, start=True, stop=True), r=[("vnb", bi_ % 2), "wsb"], w=[pk2])
                    sst = ssts[cc]
                    p.op("dve", lambda e, o=sst[:, 0:n], a=ps2[:, 0:n], b=bsr[:, wi, g, 0:n]: e.tensor_tensor(out=o, in0=a, in1=b, op=ALU.add),
                         r=[pk2, "bsr"], w=[("sst", cc)])
                    p.op("dve", lambda e, o=U[:, cc, t0:t0 + n], a=sst[:, 0:n]: e.tensor_tensor(out=o, in0=o, in1=a, op=ALU.mult),
                         r=[("sst", cc), ("U", ti)], w=[("U", ti)])
            bgens = [blk_gen(bi_, t0, n) for bi_, (t0, n) in enumerate(blocks)]
            for step in range(nb + 1):
                for off in range(2):
                    i_ = step - off
                    if 0 <= i_ < nb:
                        try:
                            next(bgens[i_])
                        except StopIteration:
                            pass
            for half in range(2):
                if half == 1:
                    wo_, wok = cx.wload(d_sgout[g * 512:(g + 1) * 512, 512:1024], 4, 512)
                else:
                    wo_, wok = cx.wload(d_sgout[g * 512:(g + 1) * 512, 0:512], 4, 512)
                for mm in range(4):
                    m = half * 4 + mm
                    for ti, (t0, n) in enumerate(tiles):
                        ps, pk = cx.psum()
                        for kk in range(4):
                            p.op("pe", lambda e, o=ps[:, 0:n], l=wo_[:, kk, mm * 128:(mm + 1) * 128], r_=U[:, kk, t0:t0 + n],
                                 s=(kk == 0), t=(kk == 3): e.matmul(o, lhsT=l, rhs=r_, start=s, stop=t), r=[wok, ("U", ti)], w=[pk])
                        p.op("dve", lambda e, o=X[:, m, t0:t0 + n], i=ps[:, 0:n]:
                             e.tensor_tensor(out=o, in0=i, in1=o, op=ALU.add), r=[pk, ("X", ti)], w=[("X", ti)])
        run_ffn(2, 40)
        cx.new_scope()
        yst = [cx.alloc("yst%d" % i, [128, 512], F32) for i in range(4)]
        ycnt = [0]

        def y_out(kc, ti, t0, n):
            s_ = ycnt[0] % 4
            ycnt[0] += 1
            y_out.last = (s_, kc, t0, n)
            return yst[s_][:, 0:n], ("yst", s_)
        p_op_orig = p.op

        def hooked(eng, emit, r=(), w=(), dsem=None):
            ins = p_op_orig(eng, emit, r=r, w=w, dsem=dsem)
            if eng == "dve" and len(w) == 1 and isinstance(w[0], tuple) and w[0][0] == "yst":
                s_, kc, t0, n = y_out.last
                p_op_orig("sp", lambda e, o=o_y[kc * 128:(kc + 1) * 128, t0:t0 + n], i=yst[s_][:, 0:n]: e.dma_start(out=o, in_=i),
                          r=[("yst", s_), "scr"], dsem=("yst", s_))
            return ins
        p.op = hooked
        rmsnorm(cx, X, vec[:, 48:56], tiles, y_out, "nfin")
        p.op = p_op_orig
        cx.p.emit_all(stack)
    return nc


def run_C(inp, x1p, x1s, op, os_, own, ns, n_cores, seq_of_core, nc_cache={}, debug=None):
    key = (own, ns, debug)
    if key not in nc_cache:
        nc_cache[key] = build_C(own, ns, debug)
    nc = nc_cache[key]
    f = lambda a: np.asarray(a, np.float32)
    vec = np.concatenate([
        lay_cols(inp["norm_ffn"][1]), lay_cols(inp["norm_mix"][2]), lay_cols(inp["pl_scale"]),
        lay_cols(inp["norm_ffn"][2]), lay_cols(inp["norm_mix"][3]), lay_cols(inp["norm_ffn"][3]),
        lay_cols(inp["norm_final"]), lay_cols(inp["sg_b_in"])], axis=1)
    fdw = np.ascontiguousarray(f(inp["ff_w_dw"])[1:4].transpose(0, 2, 1).reshape(3, NFF, 128, 3).transpose(0, 2, 1, 3))
    fb = np.stack([lay_cols(inp["ff_b_dw"][i]) for i in (1, 2, 3)])
    ws = f(inp["sg_w_s"])
    wst_p = np.ascontiguousarray(ws.transpose(2, 0, 1))
    wst_s = np.zeros((128, 4, 128), np.float32)
    msk_p = (np.arange(128)[:, None] <= np.arange(128)[None, :]).astype(np.float32)
    msk_s = np.zeros((128, 128), np.float32)
    for b in range(16):
        wst_s[b * 8:(b + 1) * 8, :, b * 8:(b + 1) * 8] = ws[:, :8, :8].transpose(2, 0, 1)
        msk_s[b * 8:(b + 1) * 8, b * 8:(b + 1) * 8] = msk_p[:8, :8]
    bs = f(inp["sg_b_s"])
    bs_p = np.broadcast_to(bs[None], (128, 4, 128))
    bs_s = np.broadcast_to(np.tile(bs[:, :8], (1, 16))[None], (128, 4, 128))
    sgrow = np.stack([np.broadcast_to(f(inp["sg_b_in"])[2 * D:][None], (128, 2 * D)),
                      np.broadcast_to(f(inp["sg_ln_g"])[None], (128, 2 * D)),
                      np.broadcast_to(f(inp["sg_ln_b"])[None], (128, 2 * D))]).astype(np.float32)
    in_maps = []
    for c in range(n_cores):
        b, h = seq_of_core(c)

        def seg(a, a_s):
            own_ = a[b, h * own:(h + 1) * own]
            halo = np.zeros((HC, D), np.float32) if h == 0 else a[b, h * own - HC:h * own]
            return np.ascontiguousarray(np.concatenate([halo, own_, a_s[c * ns:(c + 1) * ns].reshape(ns * 8, D)], 0).T)
        stp = f(inp["state_pool"][c * ns:(c + 1) * ns])
        stp = np.ascontiguousarray(stp.transpose(2, 0, 1).reshape(8, 128, ns, 15).transpose(1, 0, 2, 3))
        stf = f(inp["state_ffn"])[1:4, c * ns:(c + 1) * ns]
        stf = np.ascontiguousarray(stf.transpose(0, 3, 1, 2).reshape(3, NFF, 128, ns, 2).transpose(0, 2, 1, 3, 4))
        pos = h * own + np.arange(16)
        invc = np.stack([1.0 / np.minimum(w, pos + 1) for w in (2, 4, 8, 16)]).astype(np.float32)
        in_maps.append({
            "x1T": seg(x1p, x1s), "oT": seg(op, os_), "hmask": np.full((128, 1), float(h), np.float32),
            "vecC": vec, "w_o": f(inp["da_w_o"]), "pl_w": f(inp["pl_w"]).reshape(D, 256),
            "stpool": stp, "invc": np.ascontiguousarray(np.broadcast_to(invc[None], (128, 4, 16))),
            "ff_dw": fdw, "ff_b": fb, "ff_st": stf,
            "ff_w_gate": f(inp["ff_w_gate"])[1:4], "ff_w_up": f(inp["ff_w_up"])[1:4], "ff_w_down": f(inp["ff_w_down"])[1:4],
            "sg_w_in": f(inp["sg_w_in"]), "sg_w_out": f(inp["sg_w_out"]), "sg_rows": sgrow,
            "sg_wsT": np.stack([wst_p, wst_s]), "sg_mask": np.stack([msk_p, msk_s]),
            "sg_bs": np.ascontiguousarray(np.stack([bs_p, bs_s])),
        })
    res = run_bass_kernel_spmd(nc, in_maps, core_ids=list(range(n_cores)))
    return res.results


_B_CACHE = {}


def run_B(inp, q_p, k_p, v_p, q_s, k_s, v_s, n_heads=8):
    f = lambda a: np.asarray(a, np.float32)
    nseq, S, _ = q_p.shape
    nss = q_s.shape[0]
    ck_all = f(inp["cache_k"])
    cv_all = f(inp["cache_v"])
    n_phys = ck_all.shape[0]
    ptab = np.asarray(inp["page_table"], np.int32)
    npg = ptab.shape[1]
    key = (nseq, S, nss, npg, n_phys)
    if key not in _B_CACHE:
        _B_CACHE[key] = build_B(nseq, S, nss, npg, n_phys)
    nc = _B_CACHE[key]
    masks, smask = make_masks()
    lamp = np.concatenate([f(inp["da_lq1"]), f(inp["da_lk1"]), f(inp["da_lq2"]), f(inp["da_lk2"])]).reshape(1, 256)
    assert npg == 16
    ptabr = np.ascontiguousarray(np.repeat(ptab.T, 8, axis=0))
    iota = (np.arange(128) % 8).astype(np.float32).reshape(128, 1)
    ident = np.eye(128, dtype=np.float32)
    ng = f(inp["da_norm_g"])
    in_maps = []
    for c in range(n_heads):
        hs = slice(c * 128, (c + 1) * 128)
        in_maps.append({
            "qT": np.ascontiguousarray(q_p[:, :, hs].transpose(0, 2, 1)),
            "kT": np.ascontiguousarray(k_p[:, :, hs].transpose(0, 2, 1)),
            "v": np.ascontiguousarray(v_p[:, :, hs]),
            "qsT": np.ascontiguousarray(q_s.reshape(nss * 8, -1)[:, hs].T),
            "ksT": np.ascontiguousarray(k_s.reshape(nss * 8, -1)[:, hs].T),
            "vs": np.ascontiguousarray(v_s.reshape(nss * 8, -1)[:, hs]),
            "ck": np.ascontiguousarray(ck_all[:, :, c, :]),
            "cv": np.ascontiguousarray(cv_all[:, :, c, :]),
            "ptabr": ptabr, "iota": iota, "ident": ident, "lamp": lamp,
            "gcol": np.ascontiguousarray(ng[hs].reshape(128, 1)),
            "grow": np.ascontiguousarray(np.broadcast_to(ng[hs].reshape(1, 128), (8, 128))),
            "masks": masks, "smask": smask,
        })
    res = run_bass_kernel_spmd(nc, in_maps, core_ids=list(range(n_heads))).results
    op = np.empty((nseq, S, n_heads * 128), np.float32)
    os_ = np.empty((nss, 8, n_heads * 128), np.float32)
    for c in range(n_heads):
        hs = slice(c * 128, (c + 1) * 128)
        op[:, :, hs] = res[c]["oT"].transpose(0, 2, 1)
        os_[:, :, hs] = res[c]["os"].reshape(nss, 8, 128)
    return op, os_


def kernel(**inp):
    f = lambda a: np.asarray(a, np.float32)
    xp = f(inp["x_prompt"])
    xs = f(inp["x_sample"])
    Bn, S, _ = xp.shape
    NSS = xs.shape[0]
    n_cores = 8
    own = S * Bn // n_cores
    ns = NSS // n_cores
    halves = S // own

    def seq_of(c):
        return (c // halves, c % halves)

    rA = run_A(inp, own, ns, n_cores, seq_of)
    x1p = np.empty((Bn, S, D), np.float32)
    x1s = np.empty((NSS, 8, D), np.float32)
    qkv_p = np.empty((Bn, S, 3 * D), np.float32)
    qkv_s = np.empty((NSS, 8, 3 * D), np.float32)
    conv_p = np.empty((Bn, 30, D), np.float32)
    conv_s = np.empty((NSS, 30, D), np.float32)
    ffn_p = np.empty((4, Bn, 2, DFF), np.float32)
    ffn_s = np.empty((4, NSS, 2, DFF), np.float32)

    def put_ffn(li, c, b, h, ff):
        if h == halves - 1:
            ffn_p[li, b] = ff[:, :, :2].transpose(2, 1, 0).reshape(2, DFF)
        ffn_s[li, c * ns:(c + 1) * ns] = ff[:, :, 2:].reshape(128, NFF, ns, 2).transpose(2, 3, 1, 0).reshape(ns, 2, DFF)

    for c in range(n_cores):
        b, h = seq_of(c)
        r = rA[c]
        x1 = r["x1T"].T
        x1p[b, h * own:(h + 1) * own] = x1[HA:HA + own]
        x1s[c * ns:(c + 1) * ns] = x1[HA + own:].reshape(ns, 8, D)
        q = r["qkvT"].T
        qkv_p[b, h * own:(h + 1) * own] = q[HA:HA + own]
        qkv_s[c * ns:(c + 1) * ns] = q[HA + own:].reshape(ns, 8, 3 * D)
        cv = r["convT"]
        if h == halves - 1:
            conv_p[b] = cv[:, :, :30].transpose(2, 1, 0).reshape(30, D)
        conv_s[c * ns:(c + 1) * ns] = cv[:, :, 30:].reshape(128, 8, ns, 30).transpose(2, 3, 1, 0).reshape(ns, 30, D)
        put_ffn(0, c, b, h, r["ffnT"])
    del rA
    k_rows_p = np.ascontiguousarray(qkv_p[:, :, D:2 * D]).reshape(Bn, S, 8, 128)
    v_rows_p = np.ascontiguousarray(qkv_p[:, :, 2 * D:]).reshape(Bn, S, 8, 128)
    k_rows_s = np.ascontiguousarray(qkv_s[:, :, D:2 * D]).reshape(NSS, 8, 8, 128)
    v_rows_s = np.ascontiguousarray(qkv_s[:, :, 2 * D:]).reshape(NSS, 8, 8, 128)

    op, os_ = run_B(inp, qkv_p[:, :, :D], qkv_p[:, :, D:2 * D], qkv_p[:, :, 2 * D:],
                    qkv_s[:, :, :D], qkv_s[:, :, D:2 * D], qkv_s[:, :, 2 * D:])

    rC = run_C(inp, x1p, x1s, op, os_, own, ns, n_cores, seq_of)
    y_p = np.empty((Bn, S, D), np.float32)
    y_s = np.empty((NSS, 8, D), np.float32)
    pool_p = np.empty((Bn, 15, D), np.float32)
    pool_s = np.empty((NSS, 15, D), np.float32)
    sgv = np.empty((NSS, 8, 2 * D), np.float32)
    for c in range(n_cores):
        b, h = seq_of(c)
        r = rC[c]
        y = r["yT"].T
        y_p[b, h * own:(h + 1) * own] = y[HC:HC + own]
        y_s[c * ns:(c + 1) * ns] = y[HC + own:].reshape(ns, 8, D)
        pl = r["poolT"]
        if h == halves - 1:
            pool_p[b] = pl[:, :, :15].transpose(2, 1, 0).reshape(15, D)
        pool_s[c * ns:(c + 1) * ns] = pl[:, :, 15:].reshape(128, 8, ns, 15).transpose(2, 3, 1, 0).reshape(ns, 15, D)
        sgv[c * ns:(c + 1) * ns] = r["sgv"].reshape(ns, 8, 2 * D)
        for li in range(3):
            put_ffn(li + 1, c, b, h, r["ffnT"][li])
    return (y_p, y_s, conv_p, conv_s, k_rows_p, v_rows_p, k_rows_s, v_rows_s, pool_p, pool_s, sgv, ffn_p, ffn_s)
```
